# Optimizing a Trainium2 kernel written in Bass

```python
import math
import jax, jax.numpy as jnp
from jax import lax
import numpy as np

D_MODEL = 2048
BATCH = 4
SEQ = 4096
DEPTH = 1
DEC_BATCH = 1
DEC_SEQ = 16384
PAST_LEN = 128

MIX_W = D_MODEL
SSM_W = MIX_W // 2
ATTN_W = MIX_W - SSM_W
SSM_H = 16
SSM_G = SSM_W // SSM_H
SSM_P = 64
HEAD_DIM = 128
N_HEADS = ATTN_W // HEAD_DIM
N_KV_HEADS = 2
GQA_GROUP = N_HEADS // N_KV_HEADS
KV_W = N_KV_HEADS * HEAD_DIM
IN_W = SSM_W + ATTN_W + 2 * KV_W
D_FF = 4 * D_MODEL
GRID_W = 64
AXIS_DIM = HEAD_DIM // 2
ROPE_THETA = 10000.0
Q_BLOCK = 128
EPS = 1e-6
DT_MIN = 0.001
DT_MAX = 0.1

kernel_name = "hymba_s5_gqa_axial_rope_encoder"


def _rms_norm(x, g):
    xf = x.astype(jnp.float32)
    y = xf * lax.rsqrt(jnp.mean(xf * xf, axis=-1, keepdims=True) + EPS)
    return (y * g.astype(jnp.float32)).astype(x.dtype)


def _grid_positions(length):
    rows = length // GRID_W
    row = jnp.repeat(jnp.arange(rows, dtype=jnp.float32), GRID_W)
    col = jnp.tile(jnp.arange(GRID_W, dtype=jnp.float32), rows)
    return row, col


def _rope_axis(x, pos):
    inv_freq = ROPE_THETA ** (-jnp.arange(0, AXIS_DIM, 2, dtype=jnp.float32) / AXIS_DIM)
    ang = pos[:, None] * inv_freq[None, :]
    cos = jnp.cos(ang)[None, :, None, :].astype(x.dtype)
    sin = jnp.sin(ang)[None, :, None, :].astype(x.dtype)
    half = AXIS_DIM // 2
    x1, x2 = x[..., :half], x[..., half:]
    return jnp.concatenate([x1 * cos - x2 * sin, x2 * cos + x1 * sin], axis=-1)


def _axial_rope(x, row, col):
    return jnp.concatenate([_rope_axis(x[..., :AXIS_DIM], row),
                            _rope_axis(x[..., AXIS_DIM:], col)], axis=-1)


def _scan_combine(e1, e2):
    a1r, a1i, b1r, b1i = e1
    a2r, a2i, b2r, b2i = e2
    return (a2r * a1r - a2i * a1i,
            a2r * a1i + a2i * a1r,
            a2r * b1r - a2i * b1i + b2r,
            a2r * b1i + a2i * b1r + b2i)


def _s5_direction(u, a_re, a_im, log_dt, b_re, b_im, c_re, c_im, reverse):
    f32 = jnp.float32
    lr, li = a_re.astype(f32), a_im.astype(f32)
    dt = jnp.exp(log_dt.astype(f32))[:, None]
    mag = jnp.exp(lr * dt)
    lb_re, lb_im = mag * jnp.cos(li * dt), mag * jnp.sin(li * dt)
    den = lr * lr + li * li
    nr, ni = lb_re - 1.0, lb_im
    w_re = (nr * lr + ni * li) / den
    w_im = (ni * lr - nr * li) / den
    br, bi = b_re.astype(f32), b_im.astype(f32)
    bb_re = w_re[..., None] * br - w_im[..., None] * bi
    bb_im = w_re[..., None] * bi + w_im[..., None] * br
    bu_re = jnp.einsum('blgh,gph->blgp', u, bb_re)
    bu_im = jnp.einsum('blgh,gph->blgp', u, bb_im)
    ar = jnp.broadcast_to(lb_re, bu_re.shape)
    ai = jnp.broadcast_to(lb_im, bu_im.shape)
    _, _, s_re, s_im = lax.associative_scan(_scan_combine, (ar, ai, bu_re, bu_im),
                                            reverse=reverse, axis=1)
    return (jnp.einsum('blgp,ghp->blgh', s_re, c_re.astype(f32))
            - jnp.einsum('blgp,ghp->blgh', s_im, c_im.astype(f32)))


def _s5_mixer(u, ssm_a_re, ssm_a_im, ssm_log_dt, ssm_b_re, ssm_b_im, ssm_c_re, ssm_c_im, ssm_d, w_glu):
    b, L, _ = u.shape
    uf = u.astype(jnp.float32).reshape(b, L, SSM_G, SSM_H)
    y = (_s5_direction(uf, ssm_a_re[0], ssm_a_im[0], ssm_log_dt[0], ssm_b_re[0], ssm_b_im[0],
                       ssm_c_re[0], ssm_c_im[0], reverse=False)
         + _s5_direction(uf, ssm_a_re[1], ssm_a_im[1], ssm_log_dt[1], ssm_b_re[1], ssm_b_im[1],
                         ssm_c_re[1], ssm_c_im[1], reverse=True)
         + ssm_d.astype(jnp.float32).reshape(SSM_G, SSM_H) * uf)
    y = jax.nn.gelu(y.reshape(b, L, SSM_W)).astype(u.dtype)
    return y * jax.nn.sigmoid(y @ w_glu)


def _attention_mixer(q, k, v, q_norm, k_norm):
    b, L, _ = q.shape
    row, col = _grid_positions(L)
    q = _rms_norm(q.reshape(b, L, N_HEADS, HEAD_DIM), q_norm)
    k = _rms_norm(k.reshape(b, L, N_KV_HEADS, HEAD_DIM), k_norm)
    v = v.reshape(b, L, N_KV_HEADS, HEAD_DIM)
    q = _axial_rope(q, row, col) * jnp.asarray(1.0 / math.sqrt(HEAD_DIM), dtype=q.dtype)
    k = _axial_rope(k, row, col)
    nblk = L // Q_BLOCK
    qb = q.reshape(b, nblk, Q_BLOCK, N_KV_HEADS, GQA_GROUP, HEAD_DIM).transpose(1, 0, 2, 3, 4, 5)

    def one_block(qi):
        s = jnp.einsum('bqkgd,bskd->bkgqs', qi, k).astype(jnp.float32)
        p = jax.nn.softmax(s, axis=-1).astype(v.dtype)
        return jnp.einsum('bkgqs,bskd->bqkgd', p, v)

    o = lax.map(one_block, qb)
    return o.transpose(1, 0, 2, 3, 4, 5).reshape(b, L, ATTN_W)


def _layer(x, pre_mix_norm, w_in, ssm_a_re, ssm_a_im, ssm_log_dt, ssm_b_re, ssm_b_im,
           ssm_c_re, ssm_c_im, ssm_d, w_glu, q_norm, k_norm, ssm_out_norm, attn_out_norm,
           w_out, post_mix_norm, pre_mlp_norm, w_up, w_down, post_mlp_norm):
    h = _rms_norm(x, pre_mix_norm)
    proj = h @ w_in
    u = proj[..., :SSM_W]
    q = proj[..., SSM_W:SSM_W + ATTN_W]
    k = proj[..., SSM_W + ATTN_W:SSM_W + ATTN_W + KV_W]
    v = proj[..., SSM_W + ATTN_W + KV_W:]
    y_ssm = _s5_mixer(u, ssm_a_re, ssm_a_im, ssm_log_dt, ssm_b_re, ssm_b_im,
                      ssm_c_re, ssm_c_im, ssm_d, w_glu)
    y_att = _attention_mixer(q, k, v, q_norm, k_norm)
    mixed = jnp.concatenate([_rms_norm(y_ssm, ssm_out_norm),
                             _rms_norm(y_att, attn_out_norm)], axis=-1) @ w_out
    x = x + _rms_norm(mixed, post_mix_norm)
    h = _rms_norm(x, pre_mlp_norm)
    m = jnp.square(jax.nn.relu(h @ w_up)) @ w_down
    return x + _rms_norm(m, post_mlp_norm)


def setup_inputs(seed: int = 0) -> dict:
    key = jax.random.key(seed)
    ks = jax.random.split(key, 32)
    f32 = jnp.float32
    nrm = lambda k, shape, s: jax.random.normal(k, shape, f32) * s
    gain = lambda k, n: 1.0 + 0.02 * jax.random.normal(k, (DEPTH, n), f32)
    a_im_base = jnp.pi * jnp.arange(SSM_P, dtype=f32)
    return {
        "x_prompt": jax.random.normal(ks[0], (BATCH, SEQ, D_MODEL), f32),
        "x_sample": jax.random.normal(ks[1], (DEC_BATCH, DEC_SEQ, D_MODEL), f32),
        "pre_mix_norm": gain(ks[2], D_MODEL),
        "w_in": nrm(ks[3], (DEPTH, D_MODEL, IN_W), D_MODEL ** -0.5),
        "ssm_a_re": -0.5 + 0.01 * jax.random.normal(ks[4], (DEPTH, 2, SSM_G, SSM_P), f32),
        "ssm_a_im": a_im_base + 0.01 * jax.random.normal(ks[5], (DEPTH, 2, SSM_G, SSM_P), f32),
        "ssm_log_dt": jax.random.uniform(ks[6], (DEPTH, 2, SSM_G), f32,
                                         minval=math.log(DT_MIN), maxval=math.log(DT_MAX)),
        "ssm_b_re": nrm(ks[7], (DEPTH, 2, SSM_G, SSM_P, SSM_H), (0.5 / SSM_H) ** 0.5),
        "ssm_b_im": nrm(ks[8], (DEPTH, 2, SSM_G, SSM_P, SSM_H), (0.5 / SSM_H) ** 0.5),
        "ssm_c_re": nrm(ks[9], (DEPTH, 2, SSM_G, SSM_H, SSM_P), (0.5 / SSM_P) ** 0.5),
        "ssm_c_im": nrm(ks[10], (DEPTH, 2, SSM_G, SSM_H, SSM_P), (0.5 / SSM_P) ** 0.5),
        "ssm_d": nrm(ks[11], (DEPTH, SSM_W), 1.0),
        "w_glu": nrm(ks[12], (DEPTH, SSM_W, SSM_W), SSM_W ** -0.5),
        "q_norm": gain(ks[13], HEAD_DIM),
        "k_norm": gain(ks[14], HEAD_DIM),
        "ssm_out_norm": gain(ks[15], SSM_W),
        "attn_out_norm": gain(ks[16], ATTN_W),
        "w_out": nrm(ks[17], (DEPTH, MIX_W, D_MODEL), MIX_W ** -0.5),
        "post_mix_norm": gain(ks[18], D_MODEL),
        "pre_mlp_norm": gain(ks[19], D_MODEL),
        "w_up": nrm(ks[20], (DEPTH, D_MODEL, D_FF), D_MODEL ** -0.5),
        "w_down": nrm(ks[21], (DEPTH, D_FF, D_MODEL), D_FF ** -0.5),
        "post_mlp_norm": gain(ks[22], D_MODEL),
    }


def reference(x_prompt, x_sample, pre_mix_norm, w_in, ssm_a_re, ssm_a_im, ssm_log_dt,
              ssm_b_re, ssm_b_im, ssm_c_re, ssm_c_im, ssm_d, w_glu, q_norm, k_norm,
              ssm_out_norm, attn_out_norm, w_out, post_mix_norm, pre_mlp_norm, w_up,
              w_down, post_mlp_norm):
    y_prompt = x_prompt
    y_sample = x_sample
    for l in range(DEPTH):
        p = (pre_mix_norm[l], w_in[l], ssm_a_re[l], ssm_a_im[l], ssm_log_dt[l],
             ssm_b_re[l], ssm_b_im[l], ssm_c_re[l], ssm_c_im[l], ssm_d[l], w_glu[l],
             q_norm[l], k_norm[l], ssm_out_norm[l], attn_out_norm[l], w_out[l],
             post_mix_norm[l], pre_mlp_norm[l], w_up[l], w_down[l], post_mlp_norm[l])
        y_prompt = _layer(y_prompt, *p)
        y_sample = _layer(y_sample, *p)
    return (y_prompt, y_sample)
```

```python
import os
import numpy as np
from contextlib import ExitStack
import concourse.bass as bass
import concourse.mybir as mybir
from concourse.bass_utils import run_bass_kernel_spmd

F32 = mybir.dt.float32
BF16 = mybir.dt.bfloat16
AF = mybir.ActivationFunctionType
ALU = mybir.AluOpType

NT = int(os.environ.get("MK_NT", "4096"))
NCTX = 4 * NT
BLK = 512
NB_OWN = NT // BLK
NB_CTX = NCTX // BLK
NKT = NCTX // 128
NROW = NCTX // 64
EPS = 1e-6
MASK_NEG = -30000.0

V_PREMIX, V_POSTMIX, V_PREMLP, V_POSTMLP, V_SSMOUT, V_ATTOUT, V_QN, V_KN, V_D = 0, 16, 32, 48, 64, 72, 80, 81, 82
NVEC = 90


class Buf:
    __slots__ = ("writers", "readers")

    def __init__(self):
        self.writers = {}
        self.readers = {}


class KB:
    ENGS = ("sync", "tensor", "vector", "scalar", "gpsimd")

    NDMA = 20

    def __init__(self, nc, es):
        self.nc = nc
        self.sems = {}
        self.counts = {}
        self.waited = {e: {} for e in self.ENGS}
        self.nops = {e: 0 for e in self.ENGS}
        self.rr = {e: 0 for e in self.ENGS}
        self.pe_prev_serial = False
        for e in self.ENGS:
            if e != "sync":
                self.sems[e] = es.enter_context(nc.semaphore("s_" + e))
                self.counts[e] = 0
        for e in ("sync", "gpsimd"):
            for i in range(self.NDMA):
                k = "%s_dma%d" % (e, i)
                self.sems[k] = es.enter_context(nc.semaphore("s_" + k))
                self.counts[k] = 0

    def emit(self, eng, fn, reads=(), writes=(), dma=False, signal=True, serial=False):
        need = {}
        for b in reads:
            for s, v in b.writers.items():
                if need.get(s, 0) < v:
                    need[s] = v
        for b in writes:
            for s, v in b.writers.items():
                if need.get(s, 0) < v:
                    need[s] = v
            for s, v in b.readers.items():
                if need.get(s, 0) < v:
                    need[s] = v
        e = getattr(self.nc, eng)
        w = self.waited[eng]
        if dma:
            key = "%s_dma%d" % (eng, self.rr[eng] % self.NDMA)
            self.rr[eng] += 1
            if self.counts[key] > need.get(key, 0):
                need[key] = self.counts[key]
        else:
            key = eng
        if eng == "tensor":
            if serial or self.pe_prev_serial:
                need["tensor"] = self.counts["tensor"]
            else:
                need.pop("tensor", None)
            self.pe_prev_serial = serial
        for s, v in need.items():
            if w.get(s, 0) < v:
                w[s] = v
                e.wait_ge(self.sems[s], v)
        inc = 16 if dma else 1
        if signal:
            self.counts[key] += inc
            val = self.counts[key]
            ins = fn(e)
            ins.then_inc(self.sems[key], inc)
        else:
            assert eng == "tensor" and not dma
            val = self.counts[key] + 1
            fn(e)
        self.nops[eng] += 1
        for b in writes:
            b.writers[key] = val
        for b in reads:
            b.readers[key] = val
        return (key, val)

    def barrier(self):
        for eng in self.ENGS:
            e = getattr(self.nc, eng)
            w = self.waited[eng]
            for k, v in self.counts.items():
                if v > 0 and w.get(k, 0) < v:
                    w[k] = v
                    e.wait_ge(self.sems[k], v)

    def finish(self):
        for eng in ("sync", "gpsimd"):
            for i in range(self.NDMA):
                k = "%s_dma%d" % (eng, i)
                if self.counts[k] > 0:
                    getattr(self.nc, eng).wait_ge(self.sems[k], self.counts[k])


def _rope_compact(pos_of_ctx_row):
    inv = (np.float32(10000.0) ** (-(np.arange(0, 64, 2, dtype=np.float32)) / np.float32(64))).astype(np.float32)
    f = np.arange(64) % 32
    cos = np.zeros((128, NROW), np.float32)
    sin = np.zeros((128, NROW), np.float32)
    rows = pos_of_ctx_row.astype(np.float32)
    ang = (rows[None, :] * inv[f][:, None]).astype(np.float32)
    cos[:64], sin[:64] = np.cos(ang), np.sin(ang)
    cols = np.arange(64, dtype=np.float32)
    angc = (cols[None, :] * inv[f][:, None]).astype(np.float32)
    cos[64:, :64], sin[64:, :64] = np.cos(angc), np.sin(angc)
    return cos, sin


def _consts():
    c = {}
    rp = np.zeros((128, 128), np.float32)
    for m in range(128):
        j = m % 64
        if j < 32:
            rp[m + 32, m] = -1.0
        else:
            rp[m - 32, m] = 1.0
    c["rperm"] = rp
    c["ident"] = np.eye(128, dtype=np.float32)
    sel = np.zeros((128, 64, 128), np.float32)
    selT = np.zeros((128, 64, 128), np.float32)
    for gl in range(8):
        for tau in range(8):
            for h in range(16):
                sel[gl * 16 + h, gl * 8 + tau, tau * 16 + h] = 1.0
                selT[tau * 16 + h, gl * 8 + tau, gl * 16 + h] = 1.0
    c["sel"] = sel
    c["selT"] = selT
    tp = np.arange(128) // 16
    c["maskL"] = (tp[:, None] <= tp[None, :]).astype(np.float32)
    c["maskU"] = (tp[:, None] >= tp[None, :]).astype(np.float32)
    return c


def _prep_shared(inp):
    f = lambda a: np.ascontiguousarray(np.asarray(a, dtype=np.float32))
    sh = {}
    w_in = f(inp["w_in"])[0]
    sh["w_in_t"] = f(w_in.reshape(16, 128, 2560).transpose(1, 0, 2))
    sh["w_glu_t"] = f(f(inp["w_glu"])[0].reshape(8, 128, 1024).transpose(1, 0, 2))
    w_out = f(inp["w_out"])[0]
    sh["w_out_t"] = f(w_out.reshape(16, 128, 16, 128).transpose(2, 1, 0, 3))
    w_up = f(inp["w_up"])[0]
    sh["w_up_t"] = f(w_up.reshape(16, 128, 64, 128).transpose(2, 1, 0, 3))
    w_dn = f(inp["w_down"])[0]
    sh["w_dn_t"] = f(w_dn.reshape(2, 32, 128, 16, 128).transpose(3, 0, 2, 1, 4))
    vec = np.zeros((128, NVEC), np.float32)
    pk = lambda v, n: f(v).reshape(n, 128).T
    vec[:, V_PREMIX:V_PREMIX + 16] = pk(inp["pre_mix_norm"], 16)
    vec[:, V_POSTMIX:V_POSTMIX + 16] = pk(inp["post_mix_norm"], 16)
    vec[:, V_PREMLP:V_PREMLP + 16] = pk(inp["pre_mlp_norm"], 16)
    vec[:, V_POSTMLP:V_POSTMLP + 16] = pk(inp["post_mlp_norm"], 16)
    vec[:, V_SSMOUT:V_SSMOUT + 8] = pk(inp["ssm_out_norm"], 8)
    vec[:, V_ATTOUT:V_ATTOUT + 8] = pk(inp["attn_out_norm"], 8)
    vec[:, V_QN] = f(inp["q_norm"])[0]
    vec[:, V_KN] = f(inp["k_norm"])[0]
    vec[:, V_D:V_D + 8] = pk(inp["ssm_d"], 8)
    sh["vecs"] = vec
    a_re = f(inp["ssm_a_re"])[0]
    a_im = f(inp["ssm_a_im"])[0]
    ldt = f(inp["ssm_log_dt"])[0]
    sh["ssm_ar"] = f(a_re.transpose(0, 2, 1).reshape(128, 64))
    sh["ssm_ai"] = f(a_im.transpose(0, 2, 1).reshape(128, 64))
    sh["ssm_ldt"] = f(np.broadcast_to(ldt[:, None, :], (2, 64, 64)).reshape(128, 64))
    sh["ssm_bre"] = f(f(inp["ssm_b_re"])[0].transpose(0, 2, 1, 3).reshape(128, 64, 16))
    sh["ssm_bim"] = f(f(inp["ssm_b_im"])[0].transpose(0, 2, 1, 3).reshape(128, 64, 16))
    sh["ssm_cre"] = f(f(inp["ssm_c_re"])[0].transpose(0, 3, 1, 2).reshape(128, 64, 16))
    sh["ssm_cim"] = f(f(inp["ssm_c_im"])[0].transpose(0, 3, 1, 2).reshape(128, 64, 16))
    d = f(inp["ssm_d"])[0].reshape(64, 16)
    sh["ssm_dug"] = f(np.tile(d.T, (8, 1)))
    sh.update(_consts())
    return sh


def _prep_core(inp, core, sh):
    m = dict(sh)
    ctx = np.zeros((2048, NCTX), np.float32)
    mask = np.zeros((NCTX,), np.float32)
    gates = np.zeros((128, 3), np.float32)
    if core < 4:
        x = np.asarray(inp["x_prompt"], np.float32)[core]
        ctx[:, :NT] = x.T
        mask[NT:] = MASK_NEG
        slots = [0, 0, 0, 0]
    else:
        slot = core - 4
        xs = np.asarray(inp["x_sample"], np.float32)[0]
        others = [j for j in range(4) if j != slot]
        slots = [slot] + others
        for i, j in enumerate(slots):
            ctx[:, i * NT:(i + 1) * NT] = xs[j * NT:(j + 1) * NT].T
        for s_ in range(3):
            gates[:64, s_] = 1.0 if s_ < slot else 0.0
            gates[64:, s_] = 1.0 if (2 - s_) >= slot else 0.0
    m["xT_ctx"] = ctx
    m["maskb"] = np.ascontiguousarray(mask.reshape(NKT, 128).T)
    rows = np.concatenate([np.arange(j * NT // 64, (j + 1) * NT // 64) for j in slots])
    m["rope_cos"], m["rope_sin"] = _rope_compact(rows)
    m["gates"] = gates
    return m


def build_program(stages="WAB", debug=()):
    nc = bass.Bass("TRN2", target_bir_lowering=False)
    dbg = set(debug)

    def din(name, shape):
        return nc.dram_tensor(name, list(shape), F32, kind="ExternalInput").ap()

    def dscr(name, shape, dt=BF16):
        kind = "ExternalOutput" if name in dbg else "Internal"
        return nc.dram_tensor(name, list(shape), dt, kind=kind).ap()

    xT_ctx = din("xT_ctx", [2048, NCTX])
    rope_cos_d, rope_sin_d = din("rope_cos", [128, NROW]), din("rope_sin", [128, NROW])
    maskb_d = din("maskb", [128, NKT])
    gates_d = din("gates", [128, 3])
    vecs_d = din("vecs", [128, NVEC])
    w_in_d = din("w_in_t", [128, 16, 2560])
    w_glu_d = din("w_glu_t", [128, 8, 1024])
    has_mlp = ("P" in stages) or ("E" in stages)
    w_out_d = din("w_out_t", [16, 128, 16, 128]) if has_mlp else None
    w_up_d = din("w_up_t", [64, 128, 16, 128]) if has_mlp else None
    w_dn_d = din("w_dn_t", [16, 2, 128, 32, 128]) if has_mlp else None
    ssm_in = {k: din(k, s) for k, s in (("ssm_ar", [128, 64]), ("ssm_ai", [128, 64]), ("ssm_ldt", [128, 64]),
                                        ("ssm_bre", [128, 64, 16]), ("ssm_bim", [128, 64, 16]),
                                        ("ssm_cre", [128, 64, 16]), ("ssm_cim", [128, 64, 16]),
                                        ("ssm_dug", [128, 64]))}
    rperm_d, ident_d = din("rperm", [128, 128]), din("ident", [128, 128])
    sel_d, selT_d = din("sel", [128, 64, 128]), din("selT", [128, 64, 128])
    maskL_d, maskU_d = din("maskL", [128, 128]), din("maskU", [128, 128])
    yT_out = nc.dram_tensor("yT_out", [2048, NT], F32, kind="ExternalOutput").ap()

    KT_s = dscr("KT_s", [2, 128, NCTX])
    V_s = dscr("V_s", [2, 128, NKT, 128])
    QT_s = dscr("QT_s", [8, 128, NT])
    WS_s = dscr("WS_s", [128, 64, 5, 128])
    UG_s = dscr("UG_s", [NB_OWN, 128, 64, 64])
    SF_s = dscr("SF_s", [NB_OWN, 64, 64, 2, 64])
    ZB_s = dscr("ZB_s", [NB_OWN, 64, 64, 2, 64])
    NS_s = dscr("NS_s", [128, 8, NT])
    YA_s = dscr("YA_s", [8, 128, NT], F32)
    WOUT_b = dscr("WOUT_b", [16, 128, 16, 128])
    WUP_b = dscr("WUP_b", [64, 128, 16, 128])
    WDN_b = dscr("WDN_b", [16, 2, 128, 32, 128])
    EST_s = dscr("EST_s", [128, 2, 64], F32)

    es = ExitStack()
    kb = KB(nc, es)
    E = kb.emit
    B = {}

    def buf(name):
        if name not in B:
            B[name] = Buf()
        return B[name]

    uid = [0]

    def sbt(st, name, shape, dt):
        uid[0] += 1
        return st.enter_context(nc.sbuf_tensor("sb%d_%s" % (uid[0], name), list(shape), dt))

    ps = [es.enter_context(nc.psum_tensor("ps%d" % i, [128, 512], F32)) for i in range(8)]
    Bps = [Buf() for _ in range(8)]
    ones_b = sbt(es, "ones_b", [128, 128], BF16)
    ones_f = sbt(es, "ones_f", [128, 128], F32)
    ident_f = sbt(es, "ident_f", [128, 128], F32)
    rperm_b = sbt(es, "rperm_b", [128, 128], BF16)
    vecs = sbt(es, "vecs", [128, NVEC], F32)
    gates = sbt(es, "gates", [128, 3], F32)
    rope_c = sbt(es, "rope_c", [128, NROW], F32)
    rope_s = sbt(es, "rope_s", [128, NROW], F32)
    A8a = sbt(es, "A8a", [128, 2, 64], F32)
    A8b = sbt(es, "A8b", [128, 2, 64], F32)
    A8c = sbt(es, "A8c", [128, 2, 64], F32)
    Sin_t = sbt(es, "Sin_t", [128, 2, 64], F32)
    Bconst = Buf()
    BA8 = Buf()
    BSin = Buf()

    E("vector", lambda e: e.memset(ones_b[:], 1.0), writes=[Bconst])
    E("vector", lambda e: e.memset(ones_f[:], 1.0), writes=[Bconst])
    E("sync", lambda e: e.dma_start(out=ident_f[:], in_=ident_d), writes=[Bconst], dma=True)
    E("gpsimd", lambda e: e.dma_start(out=rperm_b[:], in_=rperm_d), writes=[Bconst], dma=True)
    E("sync", lambda e: e.dma_start(out=vecs[:], in_=vecs_d), writes=[Bconst], dma=True)
    E("sync", lambda e: e.dma_start(out=gates[:], in_=gates_d), writes=[Bconst], dma=True)
    E("sync", lambda e: e.dma_start(out=rope_c[:], in_=rope_cos_d), writes=[Bconst], dma=True)
    E("sync", lambda e: e.dma_start(out=rope_s[:], in_=rope_sin_d), writes=[Bconst], dma=True)

    def vcol(c0, n=1):
        return vecs[:, c0:c0 + n]

    def cmul(eng, out, a, b, tmp1, tmp2, bufs_r, bufs_w, rows=slice(0, 128)):
        r = rows
        E(eng, lambda e: e.tensor_tensor(out=tmp1[r, 0, :], in0=a[r, 0, :], in1=b[r, 0, :], op=ALU.mult), reads=bufs_r, writes=bufs_w)
        E(eng, lambda e: e.tensor_tensor(out=tmp1[r, 1, :], in0=a[r, 1, :], in1=b[r, 1, :], op=ALU.mult), reads=bufs_r, writes=bufs_w)
        E(eng, lambda e: e.tensor_tensor(out=tmp2[r, 0, :], in0=a[r, 0, :], in1=b[r, 1, :], op=ALU.mult), reads=bufs_r, writes=bufs_w)
        E(eng, lambda e: e.tensor_tensor(out=tmp2[r, 1, :], in0=a[r, 1, :], in1=b[r, 0, :], op=ALU.mult), reads=bufs_r, writes=bufs_w)
        E(eng, lambda e: e.tensor_tensor(out=out[r, 0, :], in0=tmp1[r, 0, :], in1=tmp1[r, 1, :], op=ALU.subtract), reads=bufs_r, writes=bufs_w)
        E(eng, lambda e: e.tensor_tensor(out=out[r, 1, :], in0=tmp2[r, 0, :], in1=tmp2[r, 1, :], op=ALU.add), reads=bufs_r, writes=bufs_w)

    ctx = dict(nc=nc, kb=kb, E=E, es=es, ps=ps, Bps=Bps, buf=buf, sbt=sbt, vcol=vcol, cmul=cmul,
               ones_b=ones_b, ones_f=ones_f, ident_f=ident_f, rperm_b=rperm_b, vecs=vecs, gates=gates, rope_c=rope_c, rope_s=rope_s,
               A8a=A8a, A8b=A8b, A8c=A8c, Sin_t=Sin_t, Bconst=Bconst, BA8=BA8, BSin=BSin)
    dr = dict(xT_ctx=xT_ctx, maskb=maskb_d, w_in=w_in_d, w_glu=w_glu_d, w_out=w_out_d, w_up=w_up_d, w_dn=w_dn_d, ssm=ssm_in,
              sel=sel_d, selT=selT_d, maskL=maskL_d, maskU=maskU_d, yT_out=yT_out,
              KT_s=KT_s, V_s=V_s, QT_s=QT_s, WS_s=WS_s, UG_s=UG_s, SF_s=SF_s, ZB_s=ZB_s, NS_s=NS_s, YA_s=YA_s,
              WOUT_b=WOUT_b, WUP_b=WUP_b, WDN_b=WDN_b, EST_s=EST_s)

    if "P" in stages:
        phase_weight_cast(ctx, dr)
    if "W" in stages:
        phase_ssm_weights(ctx, dr)
        kb.barrier()
    if "A" in stages:
        phase_proj(ctx, dr, own=False)
        kb.barrier()
    if "B" in stages:
        phase_proj(ctx, dr, own=True)
        kb.barrier()
    if "S" in stages:
        phase_ssm_scan(ctx, dr, own=False)
        kb.barrier()
        phase_ssm_carry(ctx, dr)
        kb.barrier()
        phase_ssm_scan(ctx, dr, own=True)
        kb.barrier()
    if "C" in stages:
        phase_ssm_out(ctx, dr)
        kb.barrier()
    if "D" in stages:
        phase_attention(ctx, dr)
        kb.barrier()
    if "E" in stages:
        phase_mlp(ctx, dr)
    kb.finish()
    es.close()
    return nc, kb


def load_norm_block(c, st, xT_src, blk, xt, sq, ht, rstd, Bxt, Bsq, Bht, Brstd, gcol, ps_i=0):
    E, ps, Bps = c["E"], c["ps"], c["Bps"]
    ones_b, Bconst, vcol = c["ones_b"], c["Bconst"], c["vcol"]
    sl = slice(blk * BLK, (blk + 1) * BLK)
    src = xT_src.rearrange("(k p) t -> p k t", p=128)
    E("sync", lambda e: e.dma_start(out=xt[:, 0:8, :], in_=src[:, 0:8, sl]), writes=[Bxt], dma=True)
    E("sync", lambda e: e.dma_start(out=xt[:, 8:16, :], in_=src[:, 8:16, sl]), writes=[Bxt], dma=True)
    E("scalar", lambda e: e.activation(out=sq[:], in_=xt[:], func=AF.Square), reads=[Bxt], writes=[Bsq])
    for k in range(16):
        E("tensor", lambda e, k=k: e.matmul(ps[ps_i][:], lhsT=ones_b[:], rhs=sq[:, k, :], start=(k == 0), stop=(k == 15)),
          reads=[Bconst, Bsq], writes=[Bps[ps_i]], signal=(k == 15))
    E("scalar", lambda e: e.activation(out=rstd[:], in_=ps[ps_i][:], func=AF.Sqrt, scale=1.0 / 2048, bias=EPS),
      reads=[Bps[ps_i]], writes=[Brstd])
    E("vector", lambda e: e.reciprocal(out=rstd[:], in_=rstd[:]), reads=[Brstd], writes=[Brstd])
    for k in range(16):
        E("vector", lambda e, k=k: e.scalar_tensor_tensor(out=ht[:, k, :], in0=xt[:, k, :], scalar=vcol(gcol + k), in1=rstd[:],
                                                         op0=ALU.mult, op1=ALU.mult),
          reads=[Bxt, Brstd, Bconst], writes=[Bht])


def phase_proj(c, d, own):
    nc, E, ps, Bps, sbt, buf, vcol = c["nc"], c["E"], c["ps"], c["Bps"], c["sbt"], c["buf"], c["vcol"]
    ones_b, rperm_b, Bconst = c["ones_b"], c["rperm_b"], c["Bconst"]
    H = 8 if own else 2
    ncols = 1024 if own else 512
    col0 = 1024 if own else 2048
    nblk = NB_OWN if own else NB_CTX
    xT_src = d["xT_ctx"]
    rope_c, rope_s = c["rope_c"], c["rope_s"]
    gn = V_QN if own else V_KN
    out_s = d["QT_s"] if own else d["KT_s"]
    Bout = buf("QT_s" if own else "KT_s")
    BV = buf("V_s")
    with ExitStack() as st:
        W = sbt(st, "W_p", [128, 16, ncols], BF16)
        BW = Buf()
        for k in range(0, 16, 4):
            E("gpsimd", lambda e, k=k: e.dma_start(out=W[:, k:k + 4, :], in_=d["w_in"][:, k:k + 4, col0:col0 + ncols]), writes=[BW], dma=True)
        xt = [sbt(st, "xt%d" % i, [128, 16, BLK], F32) for i in range(2)]
        cs = [sbt(st, "cs%d" % i, [128, BLK], F32) for i in range(2)]
        sn = [sbt(st, "sn%d" % i, [128, BLK], F32) for i in range(2)]
        Bxt, Bcs = [Buf(), Buf()], [Buf(), Buf()]
        for i in range(2):
            E("gpsimd", lambda e: e.tensor_copy(out=cs[i][64:128, :].rearrange("p (r c) -> p r c", c=64),
                                                in_=rope_c[64:128, None, 0:64].to_broadcast([64, 8, 64])), reads=[Bconst], writes=[Bcs[i]])
            E("gpsimd", lambda e: e.tensor_copy(out=sn[i][64:128, :].rearrange("p (r c) -> p r c", c=64),
                                                in_=rope_s[64:128, None, 0:64].to_broadcast([64, 8, 64])), reads=[Bconst], writes=[Bcs[i]])
        sq = sbt(st, "sq", [128, 16, BLK], BF16)
        ht = sbt(st, "ht", [128, 16, BLK], BF16)
        rstd = sbt(st, "rstd", [128, BLK], F32)
        Bsq, Bht, Brstd = Buf(), Buf(), Buf()
        sqh = [sbt(st, "sqh%d" % i, [128, BLK], BF16) for i in range(2)]
        rk = [sbt(st, "rk%d" % i, [128, BLK], F32) for i in range(2)]
        kn = [sbt(st, "kn%d" % i, [128, BLK], BF16) for i in range(2)]
        t1 = [sbt(st, "t1%d" % i, [128, BLK], F32) for i in range(2)]
        t2 = [sbt(st, "t2%d" % i, [128, BLK], F32) for i in range(2)]
        ko = [sbt(st, "ko%d" % i, [128, BLK], BF16) for i in range(2)]
        Bsqh, Brk, Bkn, Bt1, Bt2, Bko = [[Buf(), Buf()] for _ in range(6)]
        vt = [sbt(st, "vt%d" % i, [128, 4, 256], BF16) for i in range(2)]
        Bvt = [Buf(), Buf()]
        for blk in range(nblk):
            b = blk % 2
            sl = slice(blk * BLK, (blk + 1) * BLK)
            E("gpsimd", lambda e: e.tensor_copy(out=cs[b][0:64, :].rearrange("p (r c) -> p r c", c=64),
                                                in_=rope_c[0:64, blk * 8:(blk + 1) * 8, None].to_broadcast([64, 8, 64])), reads=[Bconst], writes=[Bcs[b]])
            E("gpsimd", lambda e: e.tensor_copy(out=sn[b][0:64, :].rearrange("p (r c) -> p r c", c=64),
                                                in_=rope_s[0:64, blk * 8:(blk + 1) * 8, None].to_broadcast([64, 8, 64])), reads=[Bconst], writes=[Bcs[b]])
            load_norm_block(c, st, xT_src, blk, xt[b], sq, ht, rstd, Bxt[b], Bsq, Bht, Brstd, V_PREMIX, ps_i=0)
            for hd in range(H):
                i = hd % 2
                pk, pss, pr = 1 + i, 3 + i, 5 + i
                for k in range(16):
                    E("tensor", lambda e, k=k: e.matmul(ps[pk][:], lhsT=W[:, k, hd * 128:(hd + 1) * 128], rhs=ht[:, k, :],
                                                        start=(k == 0), stop=(k == 15)), reads=[BW, Bht], writes=[Bps[pk]], signal=(k == 15))
                E("scalar", lambda e: e.activation(out=sqh[i][:], in_=ps[pk][:], func=AF.Square), reads=[Bps[pk]], writes=[Bsqh[i]])
                E("tensor", lambda e: e.matmul(ps[pss][:], lhsT=ones_b[:], rhs=sqh[i][:], start=True, stop=True),
                  reads=[Bconst, Bsqh[i]], writes=[Bps[pss]])
                E("scalar", lambda e: e.activation(out=rk[i][:], in_=ps[pss][:], func=AF.Sqrt, scale=1.0 / 128, bias=EPS),
                  reads=[Bps[pss]], writes=[Brk[i]])
                E("vector", lambda e: e.reciprocal(out=rk[i][:], in_=rk[i][:]), reads=[Brk[i]], writes=[Brk[i]])
                E("vector", lambda e: e.scalar_tensor_tensor(out=kn[i][:], in0=ps[pk][:], scalar=vcol(gn), in1=rk[i][:],
                                                             op0=ALU.mult, op1=ALU.mult),
                  reads=[Bps[pk], Brk[i], Bconst], writes=[Bkn[i]])
                E("tensor", lambda e: e.matmul(ps[pr][:], lhsT=rperm_b[:], rhs=kn[i][:], start=True, stop=True),
                  reads=[Bconst, Bkn[i]], writes=[Bps[pr]])
                E("gpsimd", lambda e: e.tensor_tensor(out=t1[i][:], in0=kn[i][:], in1=cs[b][:], op=ALU.mult),
                  reads=[Bkn[i], Bcs[b]], writes=[Bt1[i]])
                E("vector", lambda e: e.tensor_tensor(out=t2[i][:], in0=ps[pr][:], in1=sn[b][:], op=ALU.mult),
                  reads=[Bps[pr], Bcs[b]], writes=[Bt2[i]])
                E("gpsimd", lambda e: e.tensor_tensor(out=ko[i][:], in0=t1[i][:], in1=t2[i][:], op=ALU.add),
                  reads=[Bt1[i], Bt2[i]], writes=[Bko[i]])
                E("sync", lambda e: e.dma_start(out=out_s[hd, :, sl], in_=ko[i][:]), reads=[Bko[i]], writes=[Bout], dma=True)
            if not own:
                for sub in range(4):
                    pv = 7
                    for k in range(16):
                        E("tensor", lambda e, k=k: e.matmul(ps[pv][:, 0:256], lhsT=ht[:, k, sub * 128:(sub + 1) * 128], rhs=W[:, k, 256:512],
                                                            start=(k == 0), stop=(k == 15)), reads=[BW, Bht], writes=[Bps[pv]], signal=(k == 15))
                    E("scalar", lambda e: e.copy(out=vt[b][:, sub, :], in_=ps[pv][:, 0:256]), reads=[Bps[pv]], writes=[Bvt[b]])
                for kvh in range(2):
                    E("sync", lambda e: e.dma_start(out=d["V_s"][kvh, :, blk * 4:(blk + 1) * 4, :], in_=vt[b][:, :, kvh * 128:(kvh + 1) * 128]),
                      reads=[Bvt[b]], writes=[BV], dma=True)


def phase_ssm_weights(c, d):
    nc, E, ps, Bps, sbt, buf, vcol = c["nc"], c["E"], c["ps"], c["Bps"], c["sbt"], c["buf"], c["vcol"]
    ident_f, Bconst = c["ident_f"], c["Bconst"]
    A8a, A8b, A8c, BA8 = c["A8a"], c["A8b"], c["A8c"], c["BA8"]
    S = d["ssm"]
    H0, H1 = slice(0, 64), slice(64, 128)
    with ExitStack() as st:
        T = lambda n, shape, dt=F32: sbt(st, n, shape, dt)
        ar, ai, ldt = T("ar", [128, 64]), T("ai", [128, 64]), T("ldt", [128, 64])
        bre, bim, cre, cim = T("bre", [128, 64, 16]), T("bim", [128, 64, 16]), T("cre", [128, 64, 16]), T("cim", [128, 64, 16])
        dug, mL, mU = T("dug", [128, 64]), T("mL", [128, 128]), T("mU", [128, 128])
        Bin = Buf()
        for t, k in ((ar, "ssm_ar"), (ai, "ssm_ai"), (ldt, "ssm_ldt"), (bre, "ssm_bre"), (bim, "ssm_bim"),
                     (cre, "ssm_cre"), (cim, "ssm_cim"), (dug, "ssm_dug")):
            E("sync", lambda e, t=t, k=k: e.dma_start(out=t[:], in_=S[k]), writes=[Bin], dma=True)
        E("sync", lambda e: e.dma_start(out=mL[:], in_=d["maskL"]), writes=[Bin], dma=True)
        E("sync", lambda e: e.dma_start(out=mU[:], in_=d["maskU"]), writes=[Bin], dma=True)
        Bw = Buf()
        V = lambda fn: E("vector", fn, reads=[Bin, Bw, Bconst], writes=[Bw])
        A = lambda fn: E("scalar", fn, reads=[Bin, Bw], writes=[Bw])
        dt_, lrdt, th, mag = T("dt_", [128, 64]), T("lrdt", [128, 64]), T("th", [128, 64]), T("mag", [128, 64])
        cc, ss, u1, u2 = T("cc", [128, 64]), T("ss", [128, 64]), T("u1", [128, 64]), T("u2", [128, 64])
        halfpi = T("halfpi", [128, 1])
        V(lambda e: e.memset(halfpi[:], float(np.pi / 2)))
        A(lambda e: e.activation(out=dt_[:], in_=ldt[:], func=AF.Exp))
        V(lambda e: e.tensor_tensor(out=lrdt[:], in0=ar[:], in1=dt_[:], op=ALU.mult))
        V(lambda e: e.tensor_tensor(out=th[:], in0=ai[:], in1=dt_[:], op=ALU.mult))
        A(lambda e: e.activation(out=mag[:], in_=lrdt[:], func=AF.Exp))
        A(lambda e: e.activation(out=ss[:], in_=th[:], func=AF.Sin, scale=1.0 / 32))
        A(lambda e: e.activation(out=cc[:], in_=th[:], func=AF.Sin, scale=1.0 / 32, bias=halfpi[:]))
        for _ in range(5):
            V(lambda e: e.tensor_tensor(out=u1[:], in0=cc[:], in1=cc[:], op=ALU.mult))
            V(lambda e: e.tensor_tensor(out=u2[:], in0=ss[:], in1=ss[:], op=ALU.mult))
            V(lambda e: e.scalar_tensor_tensor(out=ss[:], in0=cc[:], scalar=2.0, in1=ss[:], op0=ALU.mult, op1=ALU.mult))
            V(lambda e: e.tensor_tensor(out=cc[:], in0=u1[:], in1=u2[:], op=ALU.subtract))
        Lre, Lim = T("Lre", [128, 64, 9]), T("Lim", [128, 64, 9])
        Ire, Iim, inv = T("Ire", [128, 64, 9]), T("Iim", [128, 64, 9]), T("inv", [128, 64, 9])
        V(lambda e: e.memset(Lre[:, :, 0], 1.0))
        V(lambda e: e.memset(Lim[:, :, 0], 0.0))
        V(lambda e: e.tensor_tensor(out=Lre[:, :, 1], in0=mag[:], in1=cc[:], op=ALU.mult))
        V(lambda e: e.tensor_tensor(out=Lim[:, :, 1], in0=mag[:], in1=ss[:], op=ALU.mult))
        for k in range(2, 9):
            V(lambda e, k=k: e.tensor_tensor(out=u1[:], in0=Lre[:, :, k - 1], in1=Lre[:, :, 1], op=ALU.mult))
            V(lambda e, k=k: e.tensor_tensor(out=u2[:], in0=Lim[:, :, k - 1], in1=Lim[:, :, 1], op=ALU.mult))
            V(lambda e, k=k: e.tensor_tensor(out=Lre[:, :, k], in0=u1[:], in1=u2[:], op=ALU.subtract))
            V(lambda e, k=k: e.tensor_tensor(out=u1[:], in0=Lre[:, :, k - 1], in1=Lim[:, :, 1], op=ALU.mult))
            V(lambda e, k=k: e.tensor_tensor(out=u2[:], in0=Lim[:, :, k - 1], in1=Lre[:, :, 1], op=ALU.mult))
            V(lambda e, k=k: e.tensor_tensor(out=Lim[:, :, k], in0=u1[:], in1=u2[:], op=ALU.add))
        V(lambda e: e.tensor_tensor(out=inv[:], in0=Lre[:], in1=Lre[:], op=ALU.mult))
        V(lambda e: e.tensor_tensor(out=Ire[:], in0=Lim[:], in1=Lim[:], op=ALU.mult))
        V(lambda e: e.tensor_tensor(out=inv[:], in0=inv[:], in1=Ire[:], op=ALU.add))
        V(lambda e: e.reciprocal(out=inv[:], in_=inv[:]))
        V(lambda e: e.tensor_tensor(out=Ire[:], in0=Lre[:], in1=inv[:], op=ALU.mult))
        V(lambda e: e.scalar_tensor_tensor(out=Iim[:], in0=Lim[:], scalar=-1.0, in1=inv[:], op0=ALU.mult, op1=ALU.mult))
        E("vector", lambda e: e.tensor_copy(out=A8c[:, 0, :], in_=Lre[:, :, 8]), reads=[Bw], writes=[BA8])
        E("vector", lambda e: e.tensor_copy(out=A8c[:, 1, :], in_=Lim[:, :, 8]), reads=[Bw], writes=[BA8])
        E("vector", lambda e: e.tensor_copy(out=A8a[:, 0, :], in_=Lre[:, :, 8]), reads=[Bw], writes=[BA8])
        E("vector", lambda e: e.tensor_copy(out=A8a[:, 1, :], in_=Lre[:, :, 8]), reads=[Bw], writes=[BA8])
        E("vector", lambda e: e.tensor_scalar(out=A8b[:, 0, :], in0=Lim[:, :, 8], scalar1=-1.0, scalar2=None, op0=ALU.mult), reads=[Bw], writes=[BA8])
        E("vector", lambda e: e.tensor_copy(out=A8b[:, 1, :], in_=Lim[:, :, 8]), reads=[Bw], writes=[BA8])
        PCre, PCim, PGre, PGim = T("PCre", [128, 64, 8]), T("PCim", [128, 64, 8]), T("PGre", [128, 64, 8]), T("PGim", [128, 64, 8])
        for dst, src in ((PCre, Lre), (PCim, Lim), (PGre, Ire), (PGim, Iim)):
            V(lambda e, dst=dst, src=src: e.tensor_copy(out=dst[H0, :, :], in_=src[H0, :, 1:9]))
            V(lambda e, dst=dst, src=src: e.tensor_copy(out=dst[H1, :, :], in_=src[H1, :, 8:0:-1]))
        den, wre, wim = T("den", [128, 64]), T("wre", [128, 64]), T("wim", [128, 64])
        V(lambda e: e.tensor_tensor(out=den[:], in0=ar[:], in1=ar[:], op=ALU.mult))
        V(lambda e: e.tensor_tensor(out=u1[:], in0=ai[:], in1=ai[:], op=ALU.mult))
        V(lambda e: e.tensor_tensor(out=den[:], in0=den[:], in1=u1[:], op=ALU.add))
        V(lambda e: e.reciprocal(out=den[:], in_=den[:]))
        V(lambda e: e.tensor_scalar(out=u1[:], in0=Lre[:, :, 1], scalar1=-1.0, scalar2=None, op0=ALU.add))
        V(lambda e: e.tensor_tensor(out=wre[:], in0=u1[:], in1=ar[:], op=ALU.mult))
        V(lambda e: e.tensor_tensor(out=u2[:], in0=Lim[:, :, 1], in1=ai[:], op=ALU.mult))
        V(lambda e: e.tensor_tensor(out=wre[:], in0=wre[:], in1=u2[:], op=ALU.add))
        V(lambda e: e.tensor_tensor(out=wre[:], in0=wre[:], in1=den[:], op=ALU.mult))
        V(lambda e: e.tensor_tensor(out=wim[:], in0=Lim[:, :, 1], in1=ar[:], op=ALU.mult))
        V(lambda e: e.tensor_tensor(out=u2[:], in0=u1[:], in1=ai[:], op=ALU.mult))
        V(lambda e: e.tensor_tensor(out=wim[:], in0=wim[:], in1=u2[:], op=ALU.subtract))
        V(lambda e: e.tensor_tensor(out=wim[:], in0=wim[:], in1=den[:], op=ALU.mult))
        bbre, bbim, v1 = T("bbre", [128, 64, 16]), T("bbim", [128, 64, 16]), T("v1", [128, 64, 16])
        wre_b = wre[:, :, None].to_broadcast([128, 64, 16])
        wim_b = wim[:, :, None].to_broadcast([128, 64, 16])
        V(lambda e: e.tensor_tensor(out=bbre[:], in0=bre[:], in1=wre_b, op=ALU.mult))
        V(lambda e: e.tensor_tensor(out=v1[:], in0=bim[:], in1=wim_b, op=ALU.mult))
        V(lambda e: e.tensor_tensor(out=bbre[:], in0=bbre[:], in1=v1[:], op=ALU.subtract))
        V(lambda e: e.tensor_tensor(out=bbim[:], in0=bim[:], in1=wre_b, op=ALU.mult))
        V(lambda e: e.tensor_tensor(out=v1[:], in0=bre[:], in1=wim_b, op=ALU.mult))
        V(lambda e: e.tensor_tensor(out=bbim[:], in0=bbim[:], in1=v1[:], op=ALU.add))
        GB = 16
        Gre, Gim, Gimn = T("Gre", [128, GB, 8, 16]), T("Gim", [128, GB, 8, 16]), T("Gimn", [128, GB, 8, 16])
        CLre, CLim = T("CLre", [128, GB, 8, 16]), T("CLim", [128, GB, 8, 16])
        Wzre, Wzim = T("Wzre", [128, GB, 8, 16]), T("Wzim", [128, GB, 8, 16])
        w1, w2 = T("w1", [128, GB, 8, 16]), T("w2", [128, GB, 8, 16])
        stage = [T("stage%d" % i, [128, GB, 5, 128], BF16) for i in range(2)]
        Bstage = [Buf(), Buf()]
        tA, tB = T("tA", [128, 128]), T("tB", [128, 128])
        BtA = Buf()
        BWS = buf("WS_s")
        for gb in range(64 // GB):
            g0 = gb * GB
            gs = slice(g0, g0 + GB)
            sh4 = [128, GB, 8, 16]
            PGre_b = PGre[:, gs, :, None].to_broadcast(sh4)
            PGim_b = PGim[:, gs, :, None].to_broadcast(sh4)
            PCre_b = PCre[:, gs, :, None].to_broadcast(sh4)
            PCim_b = PCim[:, gs, :, None].to_broadcast(sh4)
            Bre_b = bbre[:, gs, None, :].to_broadcast(sh4)
            Bim_b = bbim[:, gs, None, :].to_broadcast(sh4)
            Cre_b = cre[:, gs, None, :].to_broadcast(sh4)
            Cim_b = cim[:, gs, None, :].to_broadcast(sh4)
            A8re_b = A8c[:, 0, gs, None, None].to_broadcast(sh4)
            A8im_b = A8c[:, 1, gs, None, None].to_broadcast(sh4)
            V2 = lambda fn: E("vector", fn, reads=[Bin, Bw, BA8, Bconst], writes=[Bw])
            V2(lambda e: e.tensor_tensor(out=w1[:], in0=PGre_b, in1=Bre_b, op=ALU.mult))
            V2(lambda e: e.tensor_tensor(out=w2[:], in0=PGim_b, in1=Bim_b, op=ALU.mult))
            V2(lambda e: e.tensor_tensor(out=Gre[:], in0=w1[:], in1=w2[:], op=ALU.subtract))
            V2(lambda e: e.tensor_tensor(out=w1[:], in0=PGre_b, in1=Bim_b, op=ALU.mult))
            V2(lambda e: e.tensor_tensor(out=w2[:], in0=PGim_b, in1=Bre_b, op=ALU.mult))
            V2(lambda e: e.tensor_tensor(out=Gim[:], in0=w1[:], in1=w2[:], op=ALU.add))
            V2(lambda e: e.tensor_scalar(out=Gimn[:], in0=Gim[:], scalar1=-1.0, scalar2=None, op0=ALU.mult))
            V2(lambda e: e.tensor_tensor(out=w1[:], in0=PCre_b, in1=Cre_b, op=ALU.mult))
            V2(lambda e: e.tensor_tensor(out=w2[:], in0=PCim_b, in1=Cim_b, op=ALU.mult))
            V2(lambda e: e.tensor_tensor(out=CLre[:], in0=w1[:], in1=w2[:], op=ALU.subtract))
            V2(lambda e: e.tensor_tensor(out=w1[:], in0=PCre_b, in1=Cim_b, op=ALU.mult))
            V2(lambda e: e.tensor_tensor(out=w2[:], in0=PCim_b, in1=Cre_b, op=ALU.mult))
            V2(lambda e: e.tensor_tensor(out=CLim[:], in0=w1[:], in1=w2[:], op=ALU.add))
            V2(lambda e: e.tensor_tensor(out=w1[:], in0=Gre[:], in1=A8re_b, op=ALU.mult))
            V2(lambda e: e.tensor_tensor(out=w2[:], in0=Gim[:], in1=A8im_b, op=ALU.mult))
            V2(lambda e: e.tensor_tensor(out=Wzre[:], in0=w1[:], in1=w2[:], op=ALU.subtract))
            V2(lambda e: e.tensor_tensor(out=w1[:], in0=Gim[:], in1=A8re_b, op=ALU.mult))
            V2(lambda e: e.tensor_tensor(out=w2[:], in0=Gre[:], in1=A8im_b, op=ALU.mult))
            V2(lambda e: e.tensor_tensor(out=Wzim[:], in0=w1[:], in1=w2[:], op=ALU.add))
            sg = stage[gb % 2]
            Bsg = Bstage[gb % 2]
            f2 = lambda t: t[:].rearrange("q g t h -> q g (t h)")
            E("gpsimd", lambda e: e.tensor_copy(out=sg[:, :, 3, :], in_=f2(CLre)), reads=[Bw], writes=[Bsg])
            E("gpsimd", lambda e: e.tensor_scalar(out=sg[:, :, 4, :], in0=f2(CLim), scalar1=-1.0, scalar2=None, op0=ALU.mult), reads=[Bw], writes=[Bsg])
            for gl in range(GB):
                g = g0 + gl
                f1 = lambda t: t[:, gl, :, :].rearrange("q t h -> q (t h)")
                E("tensor", lambda e: e.transpose(out=ps[0][:, 0:128], in_=f1(Wzre), identity=ident_f[:]), reads=[Bw, Bconst], writes=[Bps[0]], serial=True)
                E("tensor", lambda e: e.transpose(out=ps[1][:, 0:128], in_=f1(Wzim), identity=ident_f[:]), reads=[Bw, Bconst], writes=[Bps[1]], serial=True)
                E("scalar", lambda e: e.copy(out=sg[:, gl, 0, :], in_=ps[0][:, 0:128]), reads=[Bps[0]], writes=[Bsg])
                E("scalar", lambda e: e.copy(out=sg[:, gl, 1, :], in_=ps[1][:, 0:128]), reads=[Bps[1]], writes=[Bsg])
                for half, pi in ((H0, 2), (H1, 3)):
                    E("tensor", lambda e: e.matmul(ps[pi][:, 0:128], lhsT=f1(Gre)[half, :], rhs=f1(CLre)[half, :], start=True, stop=False),
                      reads=[Bw], writes=[Bps[pi]], serial=True)
                    E("tensor", lambda e: e.matmul(ps[pi][:, 0:128], lhsT=f1(Gimn)[half, :], rhs=f1(CLim)[half, :], start=False, stop=True),
                      reads=[Bw], writes=[Bps[pi]], serial=True)
                E("vector", lambda e: e.tensor_tensor(out=tA[:], in0=ps[2][:, 0:128], in1=mL[:], op=ALU.mult), reads=[Bps[2], Bin], writes=[BtA])
                E("vector", lambda e: e.tensor_tensor(out=tB[:], in0=ps[3][:, 0:128], in1=mU[:], op=ALU.mult), reads=[Bps[3], Bin], writes=[BtA])
                E("vector", lambda e: e.tensor_tensor(out=tA[:], in0=tA[:], in1=tB[:], op=ALU.add), reads=[BtA], writes=[BtA])
                E("vector", lambda e: e.scalar_tensor_tensor(out=sg[:, gl, 2, :], in0=ident_f[:], scalar=dug[:, g:g + 1], in1=tA[:],
                                                             op0=ALU.mult, op1=ALU.add), reads=[BtA, Bin, Bconst], writes=[Bsg])
            E("sync", lambda e: e.dma_start(out=d["WS_s"][:, gs, :, :], in_=sg[:]), reads=[Bsg], writes=[BWS], dma=True)


def _complex_sq(c, eng, t, tmp, bufs):
    E = c["E"]
    E(eng, lambda e: e.tensor_tensor(out=tmp[:, 0, :], in0=t[:, 0, :], in1=t[:, 0, :], op=ALU.mult), reads=bufs, writes=bufs)
    E(eng, lambda e: e.tensor_tensor(out=tmp[:, 1, :], in0=t[:, 1, :], in1=t[:, 1, :], op=ALU.mult), reads=bufs, writes=bufs)
    E(eng, lambda e: e.scalar_tensor_tensor(out=t[:, 1, :], in0=t[:, 0, :], scalar=2.0, in1=t[:, 1, :], op0=ALU.mult, op1=ALU.mult), reads=bufs, writes=bufs)
    E(eng, lambda e: e.tensor_tensor(out=t[:, 0, :], in0=tmp[:, 0, :], in1=tmp[:, 1, :], op=ALU.subtract), reads=bufs, writes=bufs)


def phase_ssm_scan(c, d, own):
    nc, E, ps, Bps, sbt, buf, vcol = c["nc"], c["E"], c["ps"], c["Bps"], c["sbt"], c["buf"], c["vcol"]
    Bconst, BA8, BSin = c["Bconst"], c["BA8"], c["BSin"]
    A8a, A8b, A8c, Sin_t = c["A8a"], c["A8b"], c["A8c"], c["Sin_t"]
    es = c["es"]
    H0, H1 = slice(0, 64), slice(64, 128)
    if "Est" not in c:
        c["Est"] = sbt(es, "Est", [128, 3, 2, 64], F32)
        c["BEst"] = Buf()
        c["A64"] = sbt(es, "A64", [128, 2, 64], F32)
        c["Aslot"] = sbt(es, "Aslot", [128, 2, 64], F32)
        c["BApow"] = Buf()
        tmpq = sbt(es, "tmpq", [128, 2, 64], F32)
        Bq = [c["BApow"], BA8]
        E("vector", lambda e: e.tensor_copy(out=c["A64"][:], in_=A8c[:]), reads=[BA8], writes=[c["BApow"]])
        for _ in range(6):
            _complex_sq(c, "vector", c["A64"], tmpq, [c["BApow"]])
        E("vector", lambda e: e.tensor_copy(out=c["Aslot"][:], in_=c["A64"][:]), reads=[c["BApow"]], writes=[c["BApow"]])
        n = NB_OWN
        while n > 1:
            _complex_sq(c, "vector", c["Aslot"], tmpq, [c["BApow"]])
            n //= 2
    Est, BEst, A64, BApow = c["Est"], c["BEst"], c["A64"], c["BApow"]
    with ExitStack() as st:
        Wu = sbt(st, "Wu", [128, 16, 1024], BF16)
        Sel = sbt(st, "Sel", [128, 64, 128], BF16)
        WZ = sbt(st, "WZ", [128, 64, 2, 128], BF16)
        BW = Buf()
        for k in range(0, 16, 4):
            E("gpsimd", lambda e, k=k: e.dma_start(out=Wu[:, k:k + 4, :], in_=d["w_in"][:, k:k + 4, 0:1024]), writes=[BW], dma=True)
        E("gpsimd", lambda e: e.dma_start(out=Sel[:], in_=d["sel"]), writes=[BW], dma=True)
        E("sync", lambda e: e.dma_start(out=WZ[:], in_=d["WS_s"][:, :, 0:2, :]), reads=[buf("WS_s")], writes=[BW], dma=True)
        xt = sbt(st, "xt", [128, 16, BLK], F32)
        sq = sbt(st, "sq", [128, 16, BLK], BF16)
        ht = sbt(st, "ht", [128, 16, BLK], BF16)
        rstd = sbt(st, "rstd", [128, BLK], F32)
        Bxt, Bsq, Bht, Brstd = Buf(), Buf(), Buf(), Buf()
        uT = sbt(st, "uT", [128, 8, BLK], BF16)
        BuT = Buf()
        ug = sbt(st, "ug", [128, 64, 64], BF16)
        Bug = Buf()
        Zb = sbt(st, "Zb", [128, 64, 2, 64], BF16)
        BZb = Buf()
        Sal = sbt(st, "Sal", [128, 64, 2, 64], BF16)
        BSal = Buf()
        S2 = sbt(st, "S2", [128, 2, 64], F32)
        q1 = sbt(st, "q1", [128, 2, 64], F32)
        q2 = sbt(st, "q2", [128, 2, 64], F32)
        BS2 = Buf()
        accb = sbt(st, "accb", [128, 2, 64], F32)
        Pw = sbt(st, "Pw", [128, 2, 64], F32)
        m1 = sbt(st, "m1", [128, 2, 64], F32)
        m2 = sbt(st, "m2", [128, 2, 64], F32)
        m3 = sbt(st, "m3", [128, 2, 64], F32)
        Bacc = Buf()
        slots = [0] if own else [1, 2, 3]
        for slot in slots:
            E("vector", lambda e: e.memset(S2[:], 0.0), writes=[BS2])
            if own:
                E("vector", lambda e: e.tensor_copy(out=S2[H0], in_=Sin_t[H0]), reads=[BSin], writes=[BS2])
            else:
                E("gpsimd", lambda e: e.memset(accb[:], 0.0), writes=[Bacc])
                E("gpsimd", lambda e: e.memset(Pw[:], 0.0), writes=[Bacc])
                E("gpsimd", lambda e: e.memset(Pw[:, 0, :], 1.0), writes=[Bacc])
            for bi in range(NB_OWN):
                blk = slot * NB_OWN + bi
                load_norm_block(c, st, d["xT_ctx"], blk, xt, sq, ht, rstd, Bxt, Bsq, Bht, Brstd, V_PREMIX, ps_i=0)
                for j in range(8):
                    pj = 1 + j % 2
                    for k in range(16):
                        E("tensor", lambda e, k=k: e.matmul(ps[pj][:], lhsT=Wu[:, k, j * 128:(j + 1) * 128], rhs=ht[:, k, :],
                                                            start=(k == 0), stop=(k == 15)), reads=[BW, Bht], writes=[Bps[pj]], signal=(k == 15))
                    E("scalar", lambda e: e.copy(out=uT[:, j, :], in_=ps[pj][:]), reads=[Bps[pj]], writes=[BuT])
                for j in range(8):
                    pj = 3 + j % 2
                    uv = uT[:, j, :].rearrange("p (c t) -> p c t", t=8)
                    for gl in range(8):
                        for tau in range(8):
                            E("tensor", lambda e: e.matmul(ps[pj][:, gl * 64:(gl + 1) * 64], lhsT=Sel[:, gl * 8 + tau, :], rhs=uv[:, :, tau],
                                                           start=(tau == 0), stop=(tau == 7)), reads=[BW, BuT], writes=[Bps[pj]], signal=(tau == 7))
                    E("vector", lambda e: e.tensor_copy(out=ug[:, j * 8:(j + 1) * 8, :], in_=ps[pj][:].rearrange("p (g c) -> p g c", c=64)),
                      reads=[Bps[pj]], writes=[Bug])
                if own:
                    E("sync", lambda e: e.dma_start(out=d["UG_s"][bi], in_=ug[:]), reads=[Bug], writes=[buf("UG_s")], dma=True)
                for j in range(8):
                    pr, pi = 5, 6
                    for ri, pz in ((0, pr), (1, pi)):
                        for gl in range(8):
                            g = j * 8 + gl
                            E("tensor", lambda e: e.matmul(ps[pz][:, gl * 64:(gl + 1) * 64], lhsT=WZ[:, g, ri, :], rhs=ug[:, g, :],
                                                           start=True, stop=True), reads=[BW, Bug], writes=[Bps[pz]])
                        pv = ps[pz][:].rearrange("q (g c) -> q g c", c=64)
                        E("scalar", lambda e: e.copy(out=Zb[H0, :, ri, j * 8:(j + 1) * 8].rearrange("q c g -> q g c"), in_=pv[H0]),
                          reads=[Bps[pz]], writes=[BZb])
                        E("vector", lambda e: e.tensor_copy(out=Zb[H1, :, ri, j * 8:(j + 1) * 8].rearrange("q c g -> q g c"), in_=pv[H1, :, ::-1]),
                          reads=[Bps[pz]], writes=[BZb])
                E("vector", lambda e: e.memset(S2[H1], 0.0), writes=[BS2])
                if own:
                    E("vector", lambda e: e.tensor_copy(out=Sal[H0, 0, :, :], in_=S2[H0]), reads=[BS2], writes=[BSal])
                for i in range(64):
                    E("vector", lambda e: e.tensor_tensor(out=q1[:], in0=S2[:], in1=A8a[:], op=ALU.mult), reads=[BS2, BA8], writes=[BS2])
                    E("vector", lambda e: e.tensor_tensor(out=q2[:], in0=S2[:, ::-1, :], in1=A8b[:], op=ALU.mult), reads=[BS2, BA8], writes=[BS2])
                    E("vector", lambda e: e.tensor_tensor(out=q1[:], in0=q1[:], in1=q2[:], op=ALU.add), reads=[BS2], writes=[BS2])
                    E("vector", lambda e: e.tensor_tensor(out=S2[:], in0=q1[:], in1=Zb[:, i, :, :], op=ALU.add), reads=[BS2, BZb], writes=[BS2])
                    if own and i < 63:
                        E("vector", lambda e: e.tensor_copy(out=Sal[H0, i + 1, :, :], in_=S2[H0]), reads=[BS2], writes=[BSal])
                if own:
                    E("sync", lambda e: e.dma_start(out=d["SF_s"][bi], in_=Sal[H0]), reads=[BSal], writes=[buf("SF_s")], dma=True)
                    E("sync", lambda e: e.dma_start(out=d["ZB_s"][bi], in_=Zb[H1]), reads=[BZb], writes=[buf("ZB_s")], dma=True)
                else:
                    rb = [BS2, Bacc, BApow]
                    c["cmul"]("gpsimd", m3, Pw, S2, m1, m2, rb, [Bacc], rows=H1)
                    E("gpsimd", lambda e: e.tensor_tensor(out=accb[H1], in0=accb[H1], in1=m3[H1], op=ALU.add), reads=rb, writes=[Bacc])
                    c["cmul"]("gpsimd", m3, Pw, A64, m1, m2, rb, [Bacc], rows=H1)
                    E("gpsimd", lambda e: e.tensor_copy(out=Pw[H1], in_=m3[H1]), reads=rb, writes=[Bacc])
            if not own:
                so = slot - 1
                E("vector", lambda e: e.tensor_copy(out=Est[H0, so, :, :], in_=S2[H0]), reads=[BS2], writes=[BEst])
                E("gpsimd", lambda e: e.tensor_copy(out=Est[H1, 2 - so, :, :], in_=accb[H1]), reads=[Bacc], writes=[BEst])


def phase_ssm_carry(c, d):
    E, sbt, es = c["E"], c["sbt"], c["es"]
    Est, BEst, Aslot, BApow, Sin_t, BSin, gates, Bconst = c["Est"], c["BEst"], c["Aslot"], c["BApow"], c["Sin_t"], c["BSin"], c["gates"], c["Bconst"]
    with ExitStack() as st:
        m1 = sbt(st, "cm1", [128, 2, 64], F32)
        m2 = sbt(st, "cm2", [128, 2, 64], F32)
        T = sbt(st, "cT", [128, 2, 64], F32)
        Bt = Buf()
        E("vector", lambda e: e.memset(Sin_t[:], 0.0), writes=[BSin])
        rb = [BEst, BApow, BSin, Bt, Bconst]
        for s_ in range(3):
            c["cmul"]("vector", T, Aslot, Sin_t, m1, m2, rb, [Bt])
            E("vector", lambda e: e.tensor_tensor(out=T[:], in0=T[:], in1=Est[:, s_, :, :], op=ALU.add), reads=rb, writes=[Bt])
            E("vector", lambda e: e.tensor_tensor(out=T[:], in0=T[:], in1=Sin_t[:], op=ALU.subtract), reads=rb, writes=[Bt])
            E("vector", lambda e: e.scalar_tensor_tensor(out=Sin_t[:], in0=T[:], scalar=gates[:, s_:s_ + 1], in1=Sin_t[:], op0=ALU.mult, op1=ALU.add),
              reads=rb, writes=[BSin])
        if d["EST_s"] is not None:
            E("sync", lambda e: e.dma_start(out=d["EST_s"], in_=Sin_t[:]), reads=[BSin], dma=True)


def phase_ssm_out(c, d):
    nc, E, ps, Bps, sbt, buf, vcol = c["nc"], c["E"], c["ps"], c["Bps"], c["sbt"], c["buf"], c["vcol"]
    Bconst, BA8, BSin = c["Bconst"], c["BA8"], c["BSin"]
    A8a, A8b, Sin_t, ones_b = c["A8a"], c["A8b"], c["Sin_t"], c["ones_b"]
    H0, H1 = slice(0, 64), slice(64, 128)
    with ExitStack() as st:
        WM = sbt(st, "WM", [128, 64, 3, 128], BF16)
        SelT = sbt(st, "SelT", [128, 64, 128], BF16)
        Wg = sbt(st, "Wg", [128, 8, 1024], BF16)
        BW = Buf()
        E("sync", lambda e: e.dma_start(out=WM[:], in_=d["WS_s"][:, :, 2:5, :]), reads=[buf("WS_s")], writes=[BW], dma=True)
        E("gpsimd", lambda e: e.dma_start(out=SelT[:], in_=d["selT"]), writes=[BW], dma=True)
        E("gpsimd", lambda e: e.dma_start(out=Wg[:], in_=d["w_glu"]), writes=[BW], dma=True)
        ug = [sbt(st, "ugc%d" % i, [128, 64, 64], BF16) for i in range(2)]
        Bug = [Buf(), Buf()]
        SalN = sbt(st, "SalN", [128, 64, 2, 64], BF16)
        SalR = sbt(st, "SalR", [128, 64, 2, 64], BF16)
        ZbR = sbt(st, "ZbR", [128, 64, 2, 64], BF16)
        BSalN, BSalR, BZbR = Buf(), Buf(), Buf()
        S2 = sbt(st, "S2c", [128, 2, 64], F32)
        q1 = sbt(st, "q1c", [128, 2, 64], F32)
        q2 = sbt(st, "q2c", [128, 2, 64], F32)
        BS2 = Buf()
        yg = [sbt(st, "yg%d" % i, [128, 8, 64], BF16) for i in range(2)]
        Byg = [Buf(), Buf()]
        ysf = sbt(st, "ysf", [128, 8, BLK], F32)
        ysb = sbt(st, "ysb", [128, 8, BLK], BF16)
        Bysf, Bysb = Buf(), Buf()
        ta = [sbt(st, "ta%d" % i, [128, BLK], F32) for i in range(2)]
        tb = [sbt(st, "tb%d" % i, [128, BLK], F32) for i in range(2)]
        Bta, Btb = [Buf(), Buf()], [Buf(), Buf()]
        sqn = sbt(st, "sqn", [128, 8, BLK], BF16)
        nsb = sbt(st, "nsb", [128, 8, BLK], BF16)
        rstd = sbt(st, "rstdc", [128, BLK], F32)
        Bsqn, Bnsb, Brstd = Buf(), Buf(), Buf()
        E("vector", lambda e: e.memset(S2[:], 0.0), writes=[BS2])
        E("vector", lambda e: e.tensor_copy(out=S2[H1], in_=Sin_t[H1]), reads=[BSin], writes=[BS2])
        for it, bi in enumerate(range(NB_OWN - 1, -1, -1)):
            b = it % 2
            tok = slice(bi * BLK, (bi + 1) * BLK)
            E("sync", lambda e: e.dma_start(out=ug[b][:], in_=d["UG_s"][bi]), reads=[buf("UG_s")], writes=[Bug[b]], dma=True)
            E("sync", lambda e: e.dma_start(out=SalN[H0], in_=d["SF_s"][bi]), reads=[buf("SF_s")], writes=[BSalN], dma=True)
            E("sync", lambda e: e.dma_start(out=ZbR[H1], in_=d["ZB_s"][bi]), reads=[buf("ZB_s")], writes=[BZbR], dma=True)
            E("vector", lambda e: e.tensor_copy(out=SalR[H1, 0, :, :], in_=S2[H1]), reads=[BS2], writes=[BSalR])
            for i in range(64):
                E("vector", lambda e: e.tensor_tensor(out=q1[H1], in0=S2[H1], in1=A8a[H1], op=ALU.mult), reads=[BS2, BA8], writes=[BS2])
                E("vector", lambda e: e.tensor_tensor(out=q2[H1], in0=S2[H1, ::-1, :], in1=A8b[H1], op=ALU.mult), reads=[BS2, BA8], writes=[BS2])
                E("vector", lambda e: e.tensor_tensor(out=q1[H1], in0=q1[H1], in1=q2[H1], op=ALU.add), reads=[BS2], writes=[BS2])
                E("vector", lambda e: e.tensor_tensor(out=S2[H1], in0=q1[H1], in1=ZbR[H1, i, :, :], op=ALU.add), reads=[BS2, BZbR], writes=[BS2])
                if i < 63:
                    E("vector", lambda e: e.tensor_copy(out=SalR[H1, i + 1, :, :], in_=S2[H1]), reads=[BS2], writes=[BSalR])
            E("gpsimd", lambda e: e.tensor_copy(out=SalN[H1], in_=SalR[H1, ::-1, :, :]), reads=[BSalR], writes=[BSalN])
            for j in range(8):
                py = 1 + j % 2
                for gl in range(8):
                    g = j * 8 + gl
                    o = ps[py][:, gl * 64:(gl + 1) * 64]
                    E("tensor", lambda e: e.matmul(o, lhsT=WM[:, g, 0, :], rhs=ug[b][:, g, :], start=True, stop=False), reads=[BW, Bug[b]], writes=[Bps[py]], signal=False)
                    E("tensor", lambda e: e.matmul(o, lhsT=WM[:, g, 1, :], rhs=SalN[:, :, 0, g], start=False, stop=False), reads=[BW, BSalN], writes=[Bps[py]], signal=False)
                    E("tensor", lambda e: e.matmul(o, lhsT=WM[:, g, 2, :], rhs=SalN[:, :, 1, g], start=False, stop=True), reads=[BW, BSalN], writes=[Bps[py]])
                E("scalar", lambda e: e.copy(out=yg[j % 2][:], in_=ps[py][:].rearrange("p (g c) -> p g c", c=64)), reads=[Bps[py]], writes=[Byg[j % 2]])
                pf = 3 + j % 2
                pfv = ps[pf][:].rearrange("p (c t) -> p c t", t=8)
                for tau in range(8):
                    for gl in range(8):
                        E("tensor", lambda e: e.matmul(pfv[:, :, tau], lhsT=SelT[:, gl * 8 + tau, :], rhs=yg[j % 2][:, gl, :],
                                                       start=(gl == 0), stop=(gl == 7)), reads=[BW, Byg[j % 2]], writes=[Bps[pf]], signal=(gl == 7))
                a_, b_ = ta[j % 2], tb[j % 2]
                Ba, Bb = Bta[j % 2], Btb[j % 2]
                E("scalar", lambda e: e.activation(out=a_[:], in_=ps[pf][:], func=AF.Square), reads=[Bps[pf]], writes=[Ba])
                E("vector", lambda e: e.tensor_scalar(out=a_[:], in0=a_[:], scalar1=0.044715, scalar2=1.0, op0=ALU.mult, op1=ALU.add), reads=[Ba], writes=[Ba])
                E("vector", lambda e: e.tensor_tensor(out=a_[:], in0=a_[:], in1=ps[pf][:], op=ALU.mult), reads=[Ba, Bps[pf]], writes=[Ba])
                E("scalar", lambda e: e.activation(out=b_[:], in_=a_[:], func=AF.Sigmoid, scale=1.5957691216057308), reads=[Ba], writes=[Bb])
                E("vector", lambda e: e.tensor_tensor(out=ysf[:, j, :], in0=b_[:], in1=ps[pf][:], op=ALU.mult), reads=[Bb, Bps[pf]], writes=[Bysf])
                E("gpsimd", lambda e: e.tensor_copy(out=ysb[:, j, :], in_=ysf[:, j, :]), reads=[Bysf], writes=[Bysb])
            for j2 in range(8):
                pg = 5 + j2 % 2
                for j in range(8):
                    E("tensor", lambda e: e.matmul(ps[pg][:], lhsT=Wg[:, j, j2 * 128:(j2 + 1) * 128], rhs=ysb[:, j, :], start=(j == 0), stop=(j == 7)),
                      reads=[BW, Bysb], writes=[Bps[pg]], signal=(j == 7))
                b_, Bb = tb[j2 % 2], Btb[j2 % 2]
                E("scalar", lambda e: e.activation(out=b_[:], in_=ps[pg][:], func=AF.Sigmoid), reads=[Bps[pg]], writes=[Bb])
                E("vector", lambda e: e.tensor_tensor(out=ysf[:, j2, :], in0=ysf[:, j2, :], in1=b_[:], op=ALU.mult), reads=[Bb, Bysf], writes=[Bysf])
            E("scalar", lambda e: e.activation(out=sqn[:], in_=ysf[:], func=AF.Square), reads=[Bysf], writes=[Bsqn])
            for j in range(8):
                E("tensor", lambda e: e.matmul(ps[7][:], lhsT=ones_b[:], rhs=sqn[:, j, :], start=(j == 0), stop=(j == 7)), reads=[Bconst, Bsqn], writes=[Bps[7]], signal=(j == 7))
            E("scalar", lambda e: e.activation(out=rstd[:], in_=ps[7][:], func=AF.Sqrt, scale=1.0 / 1024, bias=EPS), reads=[Bps[7]], writes=[Brstd])
            E("vector", lambda e: e.reciprocal(out=rstd[:], in_=rstd[:]), reads=[Brstd], writes=[Brstd])
            for j in range(8):
                E("vector", lambda e: e.scalar_tensor_tensor(out=nsb[:, j, :], in0=ysf[:, j, :], scalar=vcol(V_SSMOUT + j), in1=rstd[:],
                                                             op0=ALU.mult, op1=ALU.mult), reads=[Bysf, Brstd, Bconst], writes=[Bnsb])
            E("sync", lambda e: e.dma_start(out=d["NS_s"][:, :, tok], in_=nsb[:]), reads=[Bnsb], writes=[buf("NS_s")], dma=True)


def phase_attention(c, d):
    nc, E, ps, Bps, sbt, buf = c["nc"], c["E"], c["ps"], c["Bps"], c["sbt"], c["buf"]
    ones_f, Bconst = c["ones_f"], c["Bconst"]
    scale = 1.0 / float(np.sqrt(128.0))
    with ExitStack() as st:
        KT = sbt(st, "KT", [128, NCTX], BF16)
        Vt = sbt(st, "Vt", [128, NKT, 128], BF16)
        mk = sbt(st, "mk", [128, NKT], F32)
        BKV, Bmk = Buf(), Buf()
        E("sync", lambda e: e.dma_start(out=mk[:], in_=d["maskb"]), writes=[Bmk], dma=True)
        qt = [sbt(st, "qt%d" % i, [128, BLK], BF16) for i in range(2)]
        Bqt = [Buf(), Buf()]
        pT = [sbt(st, "pT%d" % i, [128, BLK], BF16) for i in range(4)]
        BpT = [Buf() for _ in range(4)]
        acc = [sbt(st, "acc%d" % i, [128, BLK], F32) for i in range(2)]
        Bacc = [Buf(), Buf()]
        rec = sbt(st, "rec", [128, BLK], F32)
        Brec = Buf()
        dhi = sbt(st, "dhi", [128, BLK], BF16)
        dlo = sbt(st, "dlo", [128, BLK], BF16)
        Bdh = Buf()
        ones_b = c["ones_b"]
        yo = [sbt(st, "yo%d" % i, [128, BLK], F32) for i in range(2)]
        Byo = [Buf(), Buf()]
        it = 0
        for kvh in range(2):
            nchunk = 4
            for q in range(nchunk):
                cs_ = slice(q * NCTX // nchunk, (q + 1) * NCTX // nchunk)
                ks_ = slice(q * NKT // nchunk, (q + 1) * NKT // nchunk)
                E("sync", lambda e: e.dma_start(out=KT[:, cs_], in_=d["KT_s"][kvh, :, cs_]), reads=[buf("KT_s")], writes=[BKV], dma=True)
                E("sync", lambda e: e.dma_start(out=Vt[:, ks_, :], in_=d["V_s"][kvh, :, ks_, :]), reads=[buf("V_s")], writes=[BKV], dma=True)
            for qh in range(4):
                head = kvh * 4 + qh
                for qb in range(NB_OWN):
                    b = it % 2
                    po = 4 + b
                    tok = slice(qb * BLK, (qb + 1) * BLK)
                    E("sync", lambda e: e.dma_start(out=qt[b][:], in_=d["QT_s"][head, :, tok]), reads=[buf("QT_s")], writes=[Bqt[b]], dma=True)

                    def s_mm(kt):
                        E("tensor", lambda e: e.matmul(ps[kt % 4][:], lhsT=KT[:, kt * 128:(kt + 1) * 128], rhs=qt[b][:], start=True, stop=True),
                          reads=[BKV, Bqt[b]], writes=[Bps[kt % 4]])
                    s_mm(0)
                    for kt in range(NKT):
                        if kt + 1 < NKT:
                            s_mm(kt + 1)
                        r = kt % 4
                        E("scalar", lambda e: e.activation(out=pT[r][:], in_=ps[r][:], func=AF.Exp, bias=mk[:, kt:kt + 1], scale=scale),
                          reads=[Bps[r], Bmk], writes=[BpT[r]])
                        E("tensor", lambda e: e.matmul(ps[po][:], lhsT=Vt[:, kt, :], rhs=pT[r][:], start=(kt == 0), stop=(kt == NKT - 1)),
                          reads=[BKV, BpT[r]], writes=[Bps[po]], signal=(kt == NKT - 1))
                        a = kt % 2
                        eng = "vector" if a == 0 else "gpsimd"
                        if kt < 2:
                            E(eng, lambda e: e.tensor_copy(out=acc[a][:], in_=pT[r][:]), reads=[BpT[r]], writes=[Bacc[a]])
                        else:
                            E(eng, lambda e: e.tensor_tensor(out=acc[a][:], in0=acc[a][:], in1=pT[r][:], op=ALU.add), reads=[BpT[r], Bacc[a]], writes=[Bacc[a]])
                    E("vector", lambda e: e.tensor_tensor(out=acc[0][:], in0=acc[0][:], in1=acc[1][:], op=ALU.add), reads=[Bacc[0], Bacc[1]], writes=[Bacc[0]])
                    E("vector", lambda e: e.tensor_copy(out=dhi[:], in_=acc[0][:]), reads=[Bacc[0]], writes=[Bdh])
                    E("vector", lambda e: e.tensor_tensor(out=acc[0][:], in0=acc[0][:], in1=dhi[:], op=ALU.subtract), reads=[Bdh, Bacc[0]], writes=[Bacc[0]])
                    E("vector", lambda e: e.tensor_copy(out=dlo[:], in_=acc[0][:]), reads=[Bacc[0]], writes=[Bdh])
                    E("tensor", lambda e: e.matmul(ps[6][:], lhsT=ones_b[:], rhs=dhi[:], start=True, stop=False), reads=[Bconst, Bdh], writes=[Bps[6]], signal=False)
                    E("tensor", lambda e: e.matmul(ps[6][:], lhsT=ones_b[:], rhs=dlo[:], start=False, stop=True), reads=[Bconst, Bdh], writes=[Bps[6]])
                    E("vector", lambda e: e.reciprocal(out=rec[:], in_=ps[6][:]), reads=[Bps[6]], writes=[Brec])
                    E("vector", lambda e: e.tensor_tensor(out=yo[b][:], in0=ps[po][:], in1=rec[:], op=ALU.mult), reads=[Bps[po], Brec], writes=[Byo[b]])
                    E("sync", lambda e: e.dma_start(out=d["YA_s"][head, :, tok], in_=yo[b][:]), reads=[Byo[b]], writes=[buf("YA_s")], dma=True)
                    it += 1


def phase_weight_cast(c, d):
    E, buf = c["E"], c["buf"]
    for f in range(16):
        E("gpsimd", lambda e: e.dma_start(out=d["WOUT_b"][f].rearrange("p k c -> p (k c)"), in_=d["w_out"][f].rearrange("p k c -> p (k c)")),
          writes=[buf("WOUT_b")], dma=True)
    for f in range(64):
        E("gpsimd", lambda e: e.dma_start(out=d["WUP_b"][f].rearrange("p k c -> p (k c)"), in_=d["w_up"][f].rearrange("p k c -> p (k c)")),
          writes=[buf("WUP_b")], dma=True)
    for f in range(16):
        for h in range(2):
            E("gpsimd", lambda e: e.dma_start(out=d["WDN_b"][f, h].rearrange("p k c -> p (k c)"), in_=d["w_dn"][f, h].rearrange("p k c -> p (k c)")),
              writes=[buf("WDN_b")], dma=True)


def phase_mlp(c, d):
    nc, E, ps, Bps, sbt, buf, vcol = c["nc"], c["E"], c["ps"], c["Bps"], c["sbt"], c["buf"], c["vcol"]
    ones_b, Bconst = c["ones_b"], c["Bconst"]
    with ExitStack() as st:
        XT = sbt(st, "XT", [128, 16, BLK], F32)
        MT = sbt(st, "MT", [128, 16, BLK], F32)
        ACTb = sbt(st, "ACTb", [128, 16, BLK], BF16)
        AT = sbt(st, "AT", [128, 32, BLK], BF16)
        YA = sbt(st, "YA", [128, 8, BLK], F32)
        BXT, BMT, BACT, BAT, BYA = Buf(), Buf(), Buf(), Buf(), Buf()
        rstd = sbt(st, "rstde", [128, BLK], F32)
        Brstd = Buf()
        tmp = [sbt(st, "tmpe%d" % i, [128, BLK], F32) for i in range(2)]
        Btmp = [Buf(), Buf()]
        wo = [sbt(st, "wo%d" % i, [128, 16, 128], BF16) for i in range(3)]
        Bwo = [Buf() for _ in range(3)]
        wu = [sbt(st, "wu%d" % i, [128, 16, 128], BF16) for i in range(3)]
        Bwu = [Buf() for _ in range(3)]
        wd = [sbt(st, "wd%d" % i, [128, 32, 128], BF16) for i in range(3)]
        Bwd = [Buf() for _ in range(3)]
        src = d["xT_ctx"].rearrange("(k p) t -> p k t", p=128)
        SQ = AT[:, 0:16, :]

        def stats(srct, Bsrc, nk, denom, pi):
            E("scalar", lambda e: e.activation(out=SQ[:, 0:nk, :], in_=srct, func=AF.Square), reads=[Bsrc], writes=[BAT])
            for k in range(nk):
                E("tensor", lambda e, k=k: e.matmul(ps[pi][:], lhsT=ones_b[:], rhs=SQ[:, k, :], start=(k == 0), stop=(k == nk - 1)),
                  reads=[Bconst, BAT], writes=[Bps[pi]], signal=(k == nk - 1))
            E("scalar", lambda e: e.activation(out=rstd[:], in_=ps[pi][:], func=AF.Sqrt, scale=1.0 / denom, bias=EPS), reads=[Bps[pi]], writes=[Brstd])
            E("vector", lambda e: e.reciprocal(out=rstd[:], in_=rstd[:]), reads=[Brstd], writes=[Brstd])

        for blk in range(NB_OWN):
            tok = slice(blk * BLK, (blk + 1) * BLK)
            E("sync", lambda e: e.dma_start(out=XT[:, 0:8, :], in_=src[:, 0:8, tok]), writes=[BXT], dma=True)
            E("sync", lambda e: e.dma_start(out=XT[:, 8:16, :], in_=src[:, 8:16, tok]), writes=[BXT], dma=True)
            E("sync", lambda e: e.dma_start(out=ACTb[:, 0:8, :], in_=d["NS_s"][:, :, tok]), reads=[buf("NS_s")], writes=[BACT], dma=True)
            E("sync", lambda e: e.dma_start(out=YA[:], in_=d["YA_s"][:, :, tok].rearrange("h p t -> p h t")), reads=[buf("YA_s")], writes=[BYA], dma=True)
            stats(YA[:], BYA, 8, 1024.0, 0)
            for j in range(8):
                E("vector", lambda e: e.scalar_tensor_tensor(out=ACTb[:, 8 + j, :], in0=YA[:, j, :], scalar=vcol(V_ATTOUT + j), in1=rstd[:],
                                                             op0=ALU.mult, op1=ALU.mult), reads=[BYA, Brstd, Bconst], writes=[BACT])
            for dt in range(16):
                w, Bw = wo[dt % 3], Bwo[dt % 3]
                E("sync", lambda e: e.dma_start(out=w[:], in_=d["WOUT_b"][dt]), reads=[buf("WOUT_b")], writes=[Bw], dma=True)
                pi = 1 + dt % 2
                for k in range(16):
                    E("tensor", lambda e, k=k: e.matmul(ps[pi][:], lhsT=w[:, k, :], rhs=ACTb[:, k, :], start=(k == 0), stop=(k == 15)),
                      reads=[Bw, BACT], writes=[Bps[pi]], signal=(k == 15))
                E("scalar", lambda e: e.copy(out=MT[:, dt, :], in_=ps[pi][:]), reads=[Bps[pi]], writes=[BMT])
            stats(MT[:], BMT, 16, 2048.0, 0)
            for k in range(16):
                t, Bt = tmp[k % 2], Btmp[k % 2]
                E("vector", lambda e: e.scalar_tensor_tensor(out=t[:], in0=MT[:, k, :], scalar=vcol(V_POSTMIX + k), in1=rstd[:],
                                                             op0=ALU.mult, op1=ALU.mult), reads=[BMT, Brstd, Bconst], writes=[Bt])
                E("gpsimd", lambda e: e.tensor_tensor(out=XT[:, k, :], in0=XT[:, k, :], in1=t[:], op=ALU.add), reads=[Bt, BXT], writes=[BXT])
            stats(XT[:], BXT, 16, 2048.0, 0)
            for k in range(16):
                E("vector", lambda e: e.scalar_tensor_tensor(out=ACTb[:, k, :], in0=XT[:, k, :], scalar=vcol(V_PREMLP + k), in1=rstd[:],
                                                             op0=ALU.mult, op1=ALU.mult), reads=[BXT, Brstd, Bconst], writes=[BACT])
            for half in range(2):
                for f in range(32):
                    ff = half * 32 + f
                    w, Bw = wu[ff % 3], Bwu[ff % 3]
                    E("sync", lambda e: e.dma_start(out=w[:], in_=d["WUP_b"][ff]), reads=[buf("WUP_b")], writes=[Bw], dma=True)
                    pi = 3 + ff % 2
                    for k in range(16):
                        E("tensor", lambda e, k=k: e.matmul(ps[pi][:], lhsT=w[:, k, :], rhs=ACTb[:, k, :], start=(k == 0), stop=(k == 15)),
                          reads=[Bw, BACT], writes=[Bps[pi]], signal=(k == 15))
                    t, Bt = tmp[ff % 2], Btmp[ff % 2]
                    E("scalar", lambda e: e.activation(out=t[:], in_=ps[pi][:], func=AF.Relu), reads=[Bps[pi]], writes=[Bt])
                    E("gpsimd" if ff % 2 else "vector", lambda e: e.tensor_tensor(out=AT[:, f, :], in0=t[:], in1=t[:], op=ALU.mult), reads=[Bt], writes=[BAT])
                for dt in range(16):
                    i3 = (half * 16 + dt) % 3
                    w, Bw = wd[i3], Bwd[i3]
                    E("sync", lambda e: e.dma_start(out=w[:], in_=d["WDN_b"][dt, half]), reads=[buf("WDN_b")], writes=[Bw], dma=True)
                    pi = 5 + dt % 2
                    for f in range(32):
                        E("tensor", lambda e, f=f: e.matmul(ps[pi][:], lhsT=w[:, f, :], rhs=AT[:, f, :], start=(f == 0), stop=(f == 31)),
                          reads=[Bw, BAT], writes=[Bps[pi]], signal=(f == 31))
                    if half == 0:
                        E("scalar", lambda e: e.copy(out=MT[:, dt, :], in_=ps[pi][:]), reads=[Bps[pi]], writes=[BMT])
                    else:
                        E("vector", lambda e: e.tensor_tensor(out=MT[:, dt, :], in0=MT[:, dt, :], in1=ps[pi][:], op=ALU.add), reads=[Bps[pi], BMT], writes=[BMT])
            stats(MT[:], BMT, 16, 2048.0, 0)
            for k in range(16):
                t, Bt = tmp[k % 2], Btmp[k % 2]
                E("vector", lambda e: e.scalar_tensor_tensor(out=t[:], in0=MT[:, k, :], scalar=vcol(V_POSTMLP + k), in1=rstd[:],
                                                             op0=ALU.mult, op1=ALU.mult), reads=[BMT, Brstd, Bconst], writes=[Bt])
                E("gpsimd", lambda e: e.tensor_tensor(out=XT[:, k, :], in0=XT[:, k, :], in1=t[:], op=ALU.add), reads=[Bt, BXT], writes=[BXT])
            dst = d["yT_out"].rearrange("(k p) t -> p k t", p=128)
            E("sync", lambda e: e.dma_start(out=dst[:, 0:8, tok], in_=XT[:, 0:8, :]), reads=[BXT], dma=True)
            E("sync", lambda e: e.dma_start(out=dst[:, 8:16, tok], in_=XT[:, 8:16, :]), reads=[BXT], dma=True)


_STAGES = os.environ.get("MK_STAGES", "PWABSCDE")


def run_cores(inputs, stages=_STAGES, debug=()):
    sh = _prep_shared(inputs)
    in_maps = [_prep_core(inputs, c, sh) for c in range(8)]
    nc, kb = build_program(stages, debug)
    res = run_bass_kernel_spmd(nc, in_maps, core_ids=list(range(8)))
    return res, kb


def kernel(**inputs):
    res, _ = run_cores(inputs)
    yp = np.stack([np.ascontiguousarray(res.results[c]["yT_out"].T) for c in range(4)], axis=0)
    ys = np.concatenate([res.results[4 + j]["yT_out"].T for j in range(4)], axis=0)[None]
    return (np.ascontiguousarray(yp.astype(np.float32)), np.ascontiguousarray(ys.astype(np.float32)))
```

```python
import os
import numpy as np
from contextlib import ExitStack
import concourse.bass as bass
import concourse.mybir as mybir
from concourse.bass_utils import run_bass_kernel_spmd

F32 = mybir.dt.float32
BF16 = mybir.dt.bfloat16
AF = mybir.ActivationFunctionType
ALU = mybir.AluOpType

NT = int(os.environ.get("MK_NT", "4096"))
NCTX = 4 * NT
BLK = 512
NB_OWN = NT // BLK
NB_CTX = NCTX // BLK
NKT = NCTX // 128
NROW = NCTX // 64
EPS = 1e-6
MASK_NEG = -30000.0

V_PREMIX, V_POSTMIX, V_PREMLP, V_POSTMLP, V_SSMOUT, V_ATTOUT, V_QN, V_KN, V_D = 0, 16, 32, 48, 64, 72, 80, 81, 82
NVEC = 90


class Buf:
    __slots__ = ("writers", "readers")

    def __init__(self):
        self.writers = {}
        self.readers = {}


class KB:
    ENGS = ("sync", "tensor", "vector", "scalar", "gpsimd")

    NDMA = 20

    def __init__(self, nc, es):
        self.nc = nc
        self.sems = {}
        self.counts = {}
        self.waited = {e: {} for e in self.ENGS}
        self.nops = {e: 0 for e in self.ENGS}
        self.rr = {e: 0 for e in self.ENGS}
        self.pe_prev_serial = False
        for e in self.ENGS:
            if e != "sync":
                self.sems[e] = es.enter_context(nc.semaphore("s_" + e))
                self.counts[e] = 0
        for e in ("sync", "gpsimd"):
            for i in range(self.NDMA):
                k = "%s_dma%d" % (e, i)
                self.sems[k] = es.enter_context(nc.semaphore("s_" + k))
                self.counts[k] = 0

    def emit(self, eng, fn, reads=(), writes=(), dma=False, signal=True, serial=False):
        need = {}
        for b in reads:
            for s, v in b.writers.items():
                if need.get(s, 0) < v:
                    need[s] = v
        for b in writes:
            for s, v in b.writers.items():
                if need.get(s, 0) < v:
                    need[s] = v
            for s, v in b.readers.items():
                if need.get(s, 0) < v:
                    need[s] = v
        e = getattr(self.nc, eng)
        w = self.waited[eng]
        if dma:
            key = "%s_dma%d" % (eng, self.rr[eng] % self.NDMA)
            self.rr[eng] += 1
            if self.counts[key] > need.get(key, 0):
                need[key] = self.counts[key]
        else:
            key = eng
        if eng == "tensor":
            if serial or self.pe_prev_serial:
                need["tensor"] = self.counts["tensor"]
            else:
                need.pop("tensor", None)
            self.pe_prev_serial = serial
        for s, v in need.items():
            if w.get(s, 0) < v:
                w[s] = v
                e.wait_ge(self.sems[s], v)
        inc = 16 if dma else 1
        if signal:
            self.counts[key] += inc
            val = self.counts[key]
            ins = fn(e)
            ins.then_inc(self.sems[key], inc)
        else:
            assert eng == "tensor" and not dma
            val = self.counts[key] + 1
            fn(e)
        self.nops[eng] += 1
        for b in writes:
            b.writers[key] = val
        for b in reads:
            b.readers[key] = val
        return (key, val)

    def barrier(self):
        for eng in self.ENGS:
            e = getattr(self.nc, eng)
            w = self.waited[eng]
            for k, v in self.counts.items():
                if v > 0 and w.get(k, 0) < v:
                    w[k] = v
                    e.wait_ge(self.sems[k], v)

    def finish(self):
        for eng in ("sync", "gpsimd"):
            for i in range(self.NDMA):
                k = "%s_dma%d" % (eng, i)
                if self.counts[k] > 0:
                    getattr(self.nc, eng).wait_ge(self.sems[k], self.counts[k])


def _rope_compact(pos_of_ctx_row):
    inv = (np.float32(10000.0) ** (-(np.arange(0, 64, 2, dtype=np.float32)) / np.float32(64))).astype(np.float32)
    f = np.arange(64) % 32
    cos = np.zeros((128, NROW), np.float32)
    sin = np.zeros((128, NROW), np.float32)
    rows = pos_of_ctx_row.astype(np.float32)
    ang = (rows[None, :] * inv[f][:, None]).astype(np.float32)
    cos[:64], sin[:64] = np.cos(ang), np.sin(ang)
    cols = np.arange(64, dtype=np.float32)
    angc = (cols[None, :] * inv[f][:, None]).astype(np.float32)
    cos[64:, :64], sin[64:, :64] = np.cos(angc), np.sin(angc)
    return cos, sin


def _consts():
    c = {}
    rp = np.zeros((128, 128), np.float32)
    for m in range(128):
        j = m % 64
        if j < 32:
            rp[m + 32, m] = -1.0
        else:
            rp[m - 32, m] = 1.0
    c["rperm"] = rp
    c["ident"] = np.eye(128, dtype=np.float32)
    sel = np.zeros((128, 64, 128), np.float32)
    selT = np.zeros((128, 64, 128), np.float32)
    for gl in range(8):
        for tau in range(8):
            for h in range(16):
                sel[gl * 16 + h, gl * 8 + tau, tau * 16 + h] = 1.0
                selT[tau * 16 + h, gl * 8 + tau, gl * 16 + h] = 1.0
    c["sel"] = sel
    c["selT"] = selT
    tp = np.arange(128) // 16
    c["maskL"] = (tp[:, None] <= tp[None, :]).astype(np.float32)
    c["maskU"] = (tp[:, None] >= tp[None, :]).astype(np.float32)
    return c


def _prep_shared(inp):
    f = lambda a: np.ascontiguousarray(np.asarray(a, dtype=np.float32))
    sh = {}
    w_in = f(inp["w_in"])[0]
    sh["w_in_t"] = f(w_in.reshape(16, 128, 2560).transpose(1, 0, 2))
    sh["w_glu_t"] = f(f(inp["w_glu"])[0].reshape(8, 128, 1024).transpose(1, 0, 2))
    w_out = f(inp["w_out"])[0]
    sh["w_out_t"] = f(w_out.reshape(16, 128, 16, 128).transpose(2, 1, 0, 3))
    w_up = f(inp["w_up"])[0]
    sh["w_up_t"] = f(w_up.reshape(16, 128, 64, 128).transpose(2, 1, 0, 3))
    w_dn = f(inp["w_down"])[0]
    sh["w_dn_t"] = f(w_dn.reshape(2, 32, 128, 16, 128).transpose(3, 0, 2, 1, 4))
    vec = np.zeros((128, NVEC), np.float32)
    pk = lambda v, n: f(v).reshape(n, 128).T
    vec[:, V_PREMIX:V_PREMIX + 16] = pk(inp["pre_mix_norm"], 16)
    vec[:, V_POSTMIX:V_POSTMIX + 16] = pk(inp["post_mix_norm"], 16)
    vec[:, V_PREMLP:V_PREMLP + 16] = pk(inp["pre_mlp_norm"], 16)
    vec[:, V_POSTMLP:V_POSTMLP + 16] = pk(inp["post_mlp_norm"], 16)
    vec[:, V_SSMOUT:V_SSMOUT + 8] = pk(inp["ssm_out_norm"], 8)
    vec[:, V_ATTOUT:V_ATTOUT + 8] = pk(inp["attn_out_norm"], 8)
    vec[:, V_QN] = f(inp["q_norm"])[0]
    vec[:, V_KN] = f(inp["k_norm"])[0]
    vec[:, V_D:V_D + 8] = pk(inp["ssm_d"], 8)
    sh["vecs"] = vec
    a_re = f(inp["ssm_a_re"])[0]
    a_im = f(inp["ssm_a_im"])[0]
    ldt = f(inp["ssm_log_dt"])[0]
    sh["ssm_ar"] = f(a_re.transpose(0, 2, 1).reshape(128, 64))
    sh["ssm_ai"] = f(a_im.transpose(0, 2, 1).reshape(128, 64))
    sh["ssm_ldt"] = f(np.broadcast_to(ldt[:, None, :], (2, 64, 64)).reshape(128, 64))
    sh["ssm_bre"] = f(f(inp["ssm_b_re"])[0].transpose(0, 2, 1, 3).reshape(128, 64, 16))
    sh["ssm_bim"] = f(f(inp["ssm_b_im"])[0].transpose(0, 2, 1, 3).reshape(128, 64, 16))
    sh["ssm_cre"] = f(f(inp["ssm_c_re"])[0].transpose(0, 3, 1, 2).reshape(128, 64, 16))
    sh["ssm_cim"] = f(f(inp["ssm_c_im"])[0].transpose(0, 3, 1, 2).reshape(128, 64, 16))
    d = f(inp["ssm_d"])[0].reshape(64, 16)
    sh["ssm_dug"] = f(np.tile(d.T, (8, 1)))
    sh.update(_consts())
    return sh


def _prep_core(inp, core, sh):
    m = dict(sh)
    ctx = np.zeros((2048, NCTX), np.float32)
    mask = np.zeros((NCTX,), np.float32)
    gates = np.zeros((128, 3), np.float32)
    if core < 4:
        x = np.asarray(inp["x_prompt"], np.float32)[core]
        ctx[:, :NT] = x.T
        mask[NT:] = MASK_NEG
        slots = [0, 0, 0, 0]
    else:
        slot = core - 4
        xs = np.asarray(inp["x_sample"], np.float32)[0]
        others = [j for j in range(4) if j != slot]
        slots = [slot] + others
        for i, j in enumerate(slots):
            ctx[:, i * NT:(i + 1) * NT] = xs[j * NT:(j + 1) * NT].T
        for s_ in range(3):
            gates[:64, s_] = 1.0 if s_ < slot else 0.0
            gates[64:, s_] = 1.0 if (2 - s_) >= slot else 0.0
    m["xT_ctx"] = ctx
    m["maskb"] = np.ascontiguousarray(mask.reshape(NKT, 128).T)
    rows = np.concatenate([np.arange(j * NT // 64, (j + 1) * NT // 64) for j in slots])
    m["rope_cos"], m["rope_sin"] = _rope_compact(rows)
    m["gates"] = gates
    return m


def build_program(stages="WAB", debug=()):
    nc = bass.Bass("TRN2", target_bir_lowering=False)
    dbg = set(debug)

    def din(name, shape):
        return nc.dram_tensor(name, list(shape), F32, kind="ExternalInput").ap()

    def dscr(name, shape, dt=BF16):
        kind = "ExternalOutput" if name in dbg else "Internal"
        return nc.dram_tensor(name, list(shape), dt, kind=kind).ap()

    xT_ctx = din("xT_ctx", [2048, NCTX])
    rope_cos_d, rope_sin_d = din("rope_cos", [128, NROW]), din("rope_sin", [128, NROW])
    maskb_d = din("maskb", [128, NKT])
    gates_d = din("gates", [128, 3])
    vecs_d = din("vecs", [128, NVEC])
    w_in_d = din("w_in_t", [128, 16, 2560])
    w_glu_d = din("w_glu_t", [128, 8, 1024])
    has_mlp = ("P" in stages) or ("E" in stages)
    w_out_d = din("w_out_t", [16, 128, 16, 128]) if has_mlp else None
    w_up_d = din("w_up_t", [64, 128, 16, 128]) if has_mlp else None
    w_dn_d = din("w_dn_t", [16, 2, 128, 32, 128]) if has_mlp else None
    ssm_in = {k: din(k, s) for k, s in (("ssm_ar", [128, 64]), ("ssm_ai", [128, 64]), ("ssm_ldt", [128, 64]),
                                        ("ssm_bre", [128, 64, 16]), ("ssm_bim", [128, 64, 16]),
                                        ("ssm_cre", [128, 64, 16]), ("ssm_cim", [128, 64, 16]),
                                        ("ssm_dug", [128, 64]))}
    rperm_d, ident_d = din("rperm", [128, 128]), din("ident", [128, 128])
    sel_d, selT_d = din("sel", [128, 64, 128]), din("selT", [128, 64, 128])
    maskL_d, maskU_d = din("maskL", [128, 128]), din("maskU", [128, 128])
    yT_out = nc.dram_tensor("yT_out", [2048, NT], F32, kind="ExternalOutput").ap()

    KT_s = dscr("KT_s", [2, 128, NCTX])
    V_s = dscr("V_s", [2, 128, NKT, 128])
    QT_s = dscr("QT_s", [8, 128, NT])
    WS_s = dscr("WS_s", [128, 64, 5, 128])
    UG_s = dscr("UG_s", [NB_OWN, 128, 64, 64])
    SF_s = dscr("SF_s", [NB_OWN, 64, 64, 2, 64])
    ZB_s = dscr("ZB_s", [NB_OWN, 64, 64, 2, 64])
    NS_s = dscr("NS_s", [128, 8, NT])
    YA_s = dscr("YA_s", [8, 128, NT], F32)
    WOUT_b = dscr("WOUT_b", [16, 128, 16, 128])
    WUP_b = dscr("WUP_b", [64, 128, 16, 128])
    WDN_b = dscr("WDN_b", [16, 2, 128, 32, 128])
    EST_s = dscr("EST_s", [128, 2, 64], F32)

    es = ExitStack()
    kb = KB(nc, es)
    E = kb.emit
    B = {}

    def buf(name):
        if name not in B:
            B[name] = Buf()
        return B[name]

    uid = [0]

    def sbt(st, name, shape, dt):
        uid[0] += 1
        return st.enter_context(nc.sbuf_tensor("sb%d_%s" % (uid[0], name), list(shape), dt))

    psbig = [es.enter_context(nc.psum_tensor("psb%d" % i, [128, 1024], F32)) for i in range(4)]
    ps = [psbig[i // 2][:, (i % 2) * 512:(i % 2 + 1) * 512] for i in range(8)]
    Bps = [Buf() for _ in range(8)]
    ones_b = sbt(es, "ones_b", [128, 128], BF16)
    ones_f = sbt(es, "ones_f", [128, 128], F32)
    ident_f = sbt(es, "ident_f", [128, 128], F32)
    rperm_b = sbt(es, "rperm_b", [128, 128], BF16)
    vecs = sbt(es, "vecs", [128, NVEC], F32)
    gates = sbt(es, "gates", [128, 3], F32)
    rope_c = sbt(es, "rope_c", [128, NROW], F32)
    rope_s = sbt(es, "rope_s", [128, NROW], F32)
    A8a = sbt(es, "A8a", [128, 2, 64], F32)
    A8b = sbt(es, "A8b", [128, 2, 64], F32)
    A8c = sbt(es, "A8c", [128, 2, 64], F32)
    Sin_t = sbt(es, "Sin_t", [128, 2, 64], F32)
    Bconst = Buf()
    BA8 = Buf()
    BSin = Buf()

    E("vector", lambda e: e.memset(ones_b[:], 1.0), writes=[Bconst])
    E("vector", lambda e: e.memset(ones_f[:], 1.0), writes=[Bconst])
    E("sync", lambda e: e.dma_start(out=ident_f[:], in_=ident_d), writes=[Bconst], dma=True)
    E("gpsimd", lambda e: e.dma_start(out=rperm_b[:], in_=rperm_d), writes=[Bconst], dma=True)
    E("sync", lambda e: e.dma_start(out=vecs[:], in_=vecs_d), writes=[Bconst], dma=True)
    E("sync", lambda e: e.dma_start(out=gates[:], in_=gates_d), writes=[Bconst], dma=True)
    E("sync", lambda e: e.dma_start(out=rope_c[:], in_=rope_cos_d), writes=[Bconst], dma=True)
    E("sync", lambda e: e.dma_start(out=rope_s[:], in_=rope_sin_d), writes=[Bconst], dma=True)

    def vcol(c0, n=1):
        return vecs[:, c0:c0 + n]

    def cmul(eng, out, a, b, tmp1, tmp2, bufs_r, bufs_w, rows=slice(0, 128)):
        r = rows
        E(eng, lambda e: e.tensor_tensor(out=tmp1[r, 0, :], in0=a[r, 0, :], in1=b[r, 0, :], op=ALU.mult), reads=bufs_r, writes=bufs_w)
        E(eng, lambda e: e.tensor_tensor(out=tmp1[r, 1, :], in0=a[r, 1, :], in1=b[r, 1, :], op=ALU.mult), reads=bufs_r, writes=bufs_w)
        E(eng, lambda e: e.tensor_tensor(out=tmp2[r, 0, :], in0=a[r, 0, :], in1=b[r, 1, :], op=ALU.mult), reads=bufs_r, writes=bufs_w)
        E(eng, lambda e: e.tensor_tensor(out=tmp2[r, 1, :], in0=a[r, 1, :], in1=b[r, 0, :], op=ALU.mult), reads=bufs_r, writes=bufs_w)
        E(eng, lambda e: e.tensor_tensor(out=out[r, 0, :], in0=tmp1[r, 0, :], in1=tmp1[r, 1, :], op=ALU.subtract), reads=bufs_r, writes=bufs_w)
        E(eng, lambda e: e.tensor_tensor(out=out[r, 1, :], in0=tmp2[r, 0, :], in1=tmp2[r, 1, :], op=ALU.add), reads=bufs_r, writes=bufs_w)

    ctx = dict(nc=nc, kb=kb, E=E, es=es, ps=ps, psbig=psbig, Bps=Bps, buf=buf, sbt=sbt, vcol=vcol, cmul=cmul,
               ones_b=ones_b, ones_f=ones_f, ident_f=ident_f, rperm_b=rperm_b, vecs=vecs, gates=gates, rope_c=rope_c, rope_s=rope_s,
               A8a=A8a, A8b=A8b, A8c=A8c, Sin_t=Sin_t, Bconst=Bconst, BA8=BA8, BSin=BSin)
    dr = dict(xT_ctx=xT_ctx, maskb=maskb_d, w_in=w_in_d, w_glu=w_glu_d, w_out=w_out_d, w_up=w_up_d, w_dn=w_dn_d, ssm=ssm_in,
              sel=sel_d, selT=selT_d, maskL=maskL_d, maskU=maskU_d, yT_out=yT_out,
              KT_s=KT_s, V_s=V_s, QT_s=QT_s, WS_s=WS_s, UG_s=UG_s, SF_s=SF_s, ZB_s=ZB_s, NS_s=NS_s, YA_s=YA_s,
              WOUT_b=WOUT_b, WUP_b=WUP_b, WDN_b=WDN_b, EST_s=EST_s)

    if "P" in stages:
        phase_weight_cast(ctx, dr)
    if "W" in stages:
        phase_ssm_weights(ctx, dr)
        kb.barrier()
    if "A" in stages:
        phase_proj(ctx, dr, own=False)
        kb.barrier()
    if "B" in stages:
        phase_proj(ctx, dr, own=True)
        kb.barrier()
    if "S" in stages:
        phase_ssm_scan(ctx, dr, own=False)
        kb.barrier()
        phase_ssm_carry(ctx, dr)
        kb.barrier()
        phase_ssm_scan(ctx, dr, own=True)
        kb.barrier()
    if "C" in stages:
        phase_ssm_out(ctx, dr)
        kb.barrier()
    if "D" in stages:
        phase_attention(ctx, dr)
        kb.barrier()
    if "E" in stages:
        phase_mlp(ctx, dr)
    kb.finish()
    es.close()
    return nc, kb


def load_norm_block(c, st, xT_src, blk, xt, sq, ht, rstd, Bxt, Bsq, Bht, Brstd, gcol, ps_i=0):
    E, ps, Bps = c["E"], c["ps"], c["Bps"]
    ones_b, Bconst, vcol = c["ones_b"], c["Bconst"], c["vcol"]
    sl = slice(blk * BLK, (blk + 1) * BLK)
    src = xT_src.rearrange("(k p) t -> p k t", p=128)
    Bsq = Bsq if isinstance(Bsq, list) else [Bsq]
    E("sync", lambda e: e.dma_start(out=xt[:, 0:8, :], in_=src[:, 0:8, sl]), writes=[Bxt], dma=True)
    E("sync", lambda e: e.dma_start(out=xt[:, 8:16, :], in_=src[:, 8:16, sl]), writes=[Bxt], dma=True)
    E("scalar", lambda e: e.activation(out=sq[:], in_=xt[:], func=AF.Square), reads=[Bxt], writes=Bsq)
    for k in range(16):
        E("tensor", lambda e, k=k: e.matmul(ps[ps_i][:], lhsT=ones_b[:], rhs=sq[:, k, :], start=(k == 0), stop=(k == 15)),
          reads=[Bconst] + Bsq, writes=[Bps[ps_i]], signal=(k == 15))
    E("scalar", lambda e: e.activation(out=rstd[:], in_=ps[ps_i][:], func=AF.Sqrt, scale=1.0 / 2048, bias=EPS),
      reads=[Bps[ps_i]], writes=[Brstd])
    E("vector", lambda e: e.reciprocal(out=rstd[:], in_=rstd[:]), reads=[Brstd], writes=[Brstd])
    for k in range(16):
        E("vector", lambda e, k=k: e.scalar_tensor_tensor(out=ht[:, k, :], in0=xt[:, k, :], scalar=vcol(gcol + k), in1=rstd[:],
                                                         op0=ALU.mult, op1=ALU.mult),
          reads=[Bxt, Brstd, Bconst], writes=[Bht])


def phase_proj(c, d, own):
    nc, E, ps, Bps, sbt, buf, vcol = c["nc"], c["E"], c["ps"], c["Bps"], c["sbt"], c["buf"], c["vcol"]
    ones_b, rperm_b, Bconst = c["ones_b"], c["rperm_b"], c["Bconst"]
    H = 8 if own else 2
    ncols = 1024 if own else 512
    col0 = 1024 if own else 2048
    nblk = NB_OWN if own else NB_CTX
    xT_src = d["xT_ctx"]
    rope_c, rope_s = c["rope_c"], c["rope_s"]
    gn = V_QN if own else V_KN
    out_s = d["QT_s"] if own else d["KT_s"]
    Bout = buf("QT_s" if own else "KT_s")
    BV = buf("V_s")
    with ExitStack() as st:
        W = sbt(st, "W_p", [128, 16, ncols], BF16)
        BW = Buf()
        for k in range(0, 16, 4):
            E("gpsimd", lambda e, k=k: e.dma_start(out=W[:, k:k + 4, :], in_=d["w_in"][:, k:k + 4, col0:col0 + ncols]), writes=[BW], dma=True)
        xt = [sbt(st, "xt%d" % i, [128, 16, BLK], F32) for i in range(2)]
        cs = [sbt(st, "cs%d" % i, [128, BLK], F32) for i in range(2)]
        sn = [sbt(st, "sn%d" % i, [128, BLK], F32) for i in range(2)]
        Bxt, Bcs = [Buf(), Buf()], [Buf(), Buf()]
        for i in range(2):
            E("gpsimd", lambda e: e.tensor_copy(out=cs[i][64:128, :].rearrange("p (r c) -> p r c", c=64),
                                                in_=rope_c[64:128, None, 0:64].to_broadcast([64, 8, 64])), reads=[Bconst], writes=[Bcs[i]])
            E("gpsimd", lambda e: e.tensor_copy(out=sn[i][64:128, :].rearrange("p (r c) -> p r c", c=64),
                                                in_=rope_s[64:128, None, 0:64].to_broadcast([64, 8, 64])), reads=[Bconst], writes=[Bcs[i]])
        sq = sbt(st, "sq", [128, 16, BLK], BF16)
        ht = sbt(st, "ht", [128, 16, BLK], BF16)
        rstd = sbt(st, "rstd", [128, BLK], F32)
        Bsq, Bht, Brstd = Buf(), Buf(), Buf()
        sqh = [sbt(st, "sqh%d" % i, [128, BLK], BF16) for i in range(2)]
        rk = [sbt(st, "rk%d" % i, [128, BLK], F32) for i in range(2)]
        kn = [sbt(st, "kn%d" % i, [128, BLK], BF16) for i in range(2)]
        t1 = [sbt(st, "t1%d" % i, [128, BLK], F32) for i in range(2)]
        t2 = [sbt(st, "t2%d" % i, [128, BLK], F32) for i in range(2)]
        ko = [sbt(st, "ko%d" % i, [128, BLK], BF16) for i in range(2)]
        Bsqh, Brk, Bkn, Bt1, Bt2, Bko = [[Buf(), Buf()] for _ in range(6)]
        vt = [sbt(st, "vt%d" % i, [128, 4, 256], BF16) for i in range(2)]
        Bvt = [Buf(), Buf()]
        for blk in range(nblk):
            b = blk % 2
            sl = slice(blk * BLK, (blk + 1) * BLK)
            E("gpsimd", lambda e: e.tensor_copy(out=cs[b][0:64, :].rearrange("p (r c) -> p r c", c=64),
                                                in_=rope_c[0:64, blk * 8:(blk + 1) * 8, None].to_broadcast([64, 8, 64])), reads=[Bconst], writes=[Bcs[b]])
            E("gpsimd", lambda e: e.tensor_copy(out=sn[b][0:64, :].rearrange("p (r c) -> p r c", c=64),
                                                in_=rope_s[0:64, blk * 8:(blk + 1) * 8, None].to_broadcast([64, 8, 64])), reads=[Bconst], writes=[Bcs[b]])
            load_norm_block(c, st, xT_src, blk, xt[b], sq, ht, rstd, Bxt[b], Bsq, Bht, Brstd, V_PREMIX, ps_i=0)
            for hd in range(H):
                i = hd % 2
                pk, pss, pr = 1 + i, 3 + i, 5 + i
                for k in range(16):
                    E("tensor", lambda e, k=k: e.matmul(ps[pk][:], lhsT=W[:, k, hd * 128:(hd + 1) * 128], rhs=ht[:, k, :],
                                                        start=(k == 0), stop=(k == 15)), reads=[BW, Bht], writes=[Bps[pk]], signal=(k == 15))
                E("scalar", lambda e: e.activation(out=sqh[i][:], in_=ps[pk][:], func=AF.Square), reads=[Bps[pk]], writes=[Bsqh[i]])
                E("tensor", lambda e: e.matmul(ps[pss][:], lhsT=ones_b[:], rhs=sqh[i][:], start=True, stop=True),
                  reads=[Bconst, Bsqh[i]], writes=[Bps[pss]])
                E("scalar", lambda e: e.activation(out=rk[i][:], in_=ps[pss][:], func=AF.Sqrt, scale=1.0 / 128, bias=EPS),
                  reads=[Bps[pss]], writes=[Brk[i]])
                E("vector", lambda e: e.reciprocal(out=rk[i][:], in_=rk[i][:]), reads=[Brk[i]], writes=[Brk[i]])
                E("vector", lambda e: e.scalar_tensor_tensor(out=kn[i][:], in0=ps[pk][:], scalar=vcol(gn), in1=rk[i][:],
                                                             op0=ALU.mult, op1=ALU.mult),
                  reads=[Bps[pk], Brk[i], Bconst], writes=[Bkn[i]])
                E("tensor", lambda e: e.matmul(ps[pr][:], lhsT=rperm_b[:], rhs=kn[i][:], start=True, stop=True),
                  reads=[Bconst, Bkn[i]], writes=[Bps[pr]])
                E("gpsimd", lambda e: e.tensor_tensor(out=t1[i][:], in0=kn[i][:], in1=cs[b][:], op=ALU.mult),
                  reads=[Bkn[i], Bcs[b]], writes=[Bt1[i]])
                E("vector", lambda e: e.tensor_tensor(out=t2[i][:], in0=ps[pr][:], in1=sn[b][:], op=ALU.mult),
                  reads=[Bps[pr], Bcs[b]], writes=[Bt2[i]])
                E("gpsimd", lambda e: e.tensor_tensor(out=ko[i][:], in0=t1[i][:], in1=t2[i][:], op=ALU.add),
                  reads=[Bt1[i], Bt2[i]], writes=[Bko[i]])
                E("sync", lambda e: e.dma_start(out=out_s[hd, :, sl], in_=ko[i][:]), reads=[Bko[i]], writes=[Bout], dma=True)
            if not own:
                for sub in range(4):
                    pv = 7
                    for k in range(16):
                        E("tensor", lambda e, k=k: e.matmul(ps[pv][:, 0:256], lhsT=ht[:, k, sub * 128:(sub + 1) * 128], rhs=W[:, k, 256:512],
                                                            start=(k == 0), stop=(k == 15)), reads=[BW, Bht], writes=[Bps[pv]], signal=(k == 15))
                    E("scalar", lambda e: e.copy(out=vt[b][:, sub, :], in_=ps[pv][:, 0:256]), reads=[Bps[pv]], writes=[Bvt[b]])
                for kvh in range(2):
                    E("sync", lambda e: e.dma_start(out=d["V_s"][kvh, :, blk * 4:(blk + 1) * 4, :], in_=vt[b][:, :, kvh * 128:(kvh + 1) * 128]),
                      reads=[Bvt[b]], writes=[BV], dma=True)


def phase_ssm_weights(c, d):
    nc, E, ps, Bps, sbt, buf, vcol = c["nc"], c["E"], c["ps"], c["Bps"], c["sbt"], c["buf"], c["vcol"]
    ident_f, Bconst = c["ident_f"], c["Bconst"]
    A8a, A8b, A8c, BA8 = c["A8a"], c["A8b"], c["A8c"], c["BA8"]
    S = d["ssm"]
    H0, H1 = slice(0, 64), slice(64, 128)
    with ExitStack() as st:
        T = lambda n, shape, dt=F32: sbt(st, n, shape, dt)
        ar, ai, ldt = T("ar", [128, 64]), T("ai", [128, 64]), T("ldt", [128, 64])
        bre, bim, cre, cim = T("bre", [128, 64, 16]), T("bim", [128, 64, 16]), T("cre", [128, 64, 16]), T("cim", [128, 64, 16])
        dug, mL, mU = T("dug", [128, 64]), T("mL", [128, 128]), T("mU", [128, 128])
        Bin = Buf()
        for t, k in ((ar, "ssm_ar"), (ai, "ssm_ai"), (ldt, "ssm_ldt"), (bre, "ssm_bre"), (bim, "ssm_bim"),
                     (cre, "ssm_cre"), (cim, "ssm_cim"), (dug, "ssm_dug")):
            E("sync", lambda e, t=t, k=k: e.dma_start(out=t[:], in_=S[k]), writes=[Bin], dma=True)
        E("sync", lambda e: e.dma_start(out=mL[:], in_=d["maskL"]), writes=[Bin], dma=True)
        E("sync", lambda e: e.dma_start(out=mU[:], in_=d["maskU"]), writes=[Bin], dma=True)
        Bw = Buf()
        V = lambda fn: E("vector", fn, reads=[Bin, Bw, Bconst], writes=[Bw])
        A = lambda fn: E("scalar", fn, reads=[Bin, Bw], writes=[Bw])
        dt_, lrdt, th, mag = T("dt_", [128, 64]), T("lrdt", [128, 64]), T("th", [128, 64]), T("mag", [128, 64])
        cc, ss, u1, u2 = T("cc", [128, 64]), T("ss", [128, 64]), T("u1", [128, 64]), T("u2", [128, 64])
        halfpi = T("halfpi", [128, 1])
        V(lambda e: e.memset(halfpi[:], float(np.pi / 2)))
        A(lambda e: e.activation(out=dt_[:], in_=ldt[:], func=AF.Exp))
        V(lambda e: e.tensor_tensor(out=lrdt[:], in0=ar[:], in1=dt_[:], op=ALU.mult))
        V(lambda e: e.tensor_tensor(out=th[:], in0=ai[:], in1=dt_[:], op=ALU.mult))
        A(lambda e: e.activation(out=mag[:], in_=lrdt[:], func=AF.Exp))
        A(lambda e: e.activation(out=ss[:], in_=th[:], func=AF.Sin, scale=1.0 / 32))
        A(lambda e: e.activation(out=cc[:], in_=th[:], func=AF.Sin, scale=1.0 / 32, bias=halfpi[:]))
        for _ in range(5):
            V(lambda e: e.tensor_tensor(out=u1[:], in0=cc[:], in1=cc[:], op=ALU.mult))
            V(lambda e: e.tensor_tensor(out=u2[:], in0=ss[:], in1=ss[:], op=ALU.mult))
            V(lambda e: e.scalar_tensor_tensor(out=ss[:], in0=cc[:], scalar=2.0, in1=ss[:], op0=ALU.mult, op1=ALU.mult))
            V(lambda e: e.tensor_tensor(out=cc[:], in0=u1[:], in1=u2[:], op=ALU.subtract))
        Lre, Lim = T("Lre", [128, 64, 9]), T("Lim", [128, 64, 9])
        Ire, Iim, inv = T("Ire", [128, 64, 9]), T("Iim", [128, 64, 9]), T("inv", [128, 64, 9])
        V(lambda e: e.memset(Lre[:, :, 0], 1.0))
        V(lambda e: e.memset(Lim[:, :, 0], 0.0))
        V(lambda e: e.tensor_tensor(out=Lre[:, :, 1], in0=mag[:], in1=cc[:], op=ALU.mult))
        V(lambda e: e.tensor_tensor(out=Lim[:, :, 1], in0=mag[:], in1=ss[:], op=ALU.mult))
        for k in range(2, 9):
            V(lambda e, k=k: e.tensor_tensor(out=u1[:], in0=Lre[:, :, k - 1], in1=Lre[:, :, 1], op=ALU.mult))
            V(lambda e, k=k: e.tensor_tensor(out=u2[:], in0=Lim[:, :, k - 1], in1=Lim[:, :, 1], op=ALU.mult))
            V(lambda e, k=k: e.tensor_tensor(out=Lre[:, :, k], in0=u1[:], in1=u2[:], op=ALU.subtract))
            V(lambda e, k=k: e.tensor_tensor(out=u1[:], in0=Lre[:, :, k - 1], in1=Lim[:, :, 1], op=ALU.mult))
            V(lambda e, k=k: e.tensor_tensor(out=u2[:], in0=Lim[:, :, k - 1], in1=Lre[:, :, 1], op=ALU.mult))
            V(lambda e, k=k: e.tensor_tensor(out=Lim[:, :, k], in0=u1[:], in1=u2[:], op=ALU.add))
        V(lambda e: e.tensor_tensor(out=inv[:], in0=Lre[:], in1=Lre[:], op=ALU.mult))
        V(lambda e: e.tensor_tensor(out=Ire[:], in0=Lim[:], in1=Lim[:], op=ALU.mult))
        V(lambda e: e.tensor_tensor(out=inv[:], in0=inv[:], in1=Ire[:], op=ALU.add))
        V(lambda e: e.reciprocal(out=inv[:], in_=inv[:]))
        V(lambda e: e.tensor_tensor(out=Ire[:], in0=Lre[:], in1=inv[:], op=ALU.mult))
        V(lambda e: e.scalar_tensor_tensor(out=Iim[:], in0=Lim[:], scalar=-1.0, in1=inv[:], op0=ALU.mult, op1=ALU.mult))
        E("vector", lambda e: e.tensor_copy(out=A8c[:, 0, :], in_=Lre[:, :, 8]), reads=[Bw], writes=[BA8])
        E("vector", lambda e: e.tensor_copy(out=A8c[:, 1, :], in_=Lim[:, :, 8]), reads=[Bw], writes=[BA8])
        E("vector", lambda e: e.tensor_copy(out=A8a[:, 0, :], in_=Lre[:, :, 8]), reads=[Bw], writes=[BA8])
        E("vector", lambda e: e.tensor_copy(out=A8a[:, 1, :], in_=Lre[:, :, 8]), reads=[Bw], writes=[BA8])
        E("vector", lambda e: e.tensor_scalar(out=A8b[:, 0, :], in0=Lim[:, :, 8], scalar1=-1.0, scalar2=None, op0=ALU.mult), reads=[Bw], writes=[BA8])
        E("vector", lambda e: e.tensor_copy(out=A8b[:, 1, :], in_=Lim[:, :, 8]), reads=[Bw], writes=[BA8])
        PCre, PCim, PGre, PGim = T("PCre", [128, 64, 8]), T("PCim", [128, 64, 8]), T("PGre", [128, 64, 8]), T("PGim", [128, 64, 8])
        for dst, src in ((PCre, Lre), (PCim, Lim), (PGre, Ire), (PGim, Iim)):
            V(lambda e, dst=dst, src=src: e.tensor_copy(out=dst[H0, :, :], in_=src[H0, :, 1:9]))
            V(lambda e, dst=dst, src=src: e.tensor_copy(out=dst[H1, :, :], in_=src[H1, :, 8:0:-1]))
        den, wre, wim = T("den", [128, 64]), T("wre", [128, 64]), T("wim", [128, 64])
        V(lambda e: e.tensor_tensor(out=den[:], in0=ar[:], in1=ar[:], op=ALU.mult))
        V(lambda e: e.tensor_tensor(out=u1[:], in0=ai[:], in1=ai[:], op=ALU.mult))
        V(lambda e: e.tensor_tensor(out=den[:], in0=den[:], in1=u1[:], op=ALU.add))
        V(lambda e: e.reciprocal(out=den[:], in_=den[:]))
        V(lambda e: e.tensor_scalar(out=u1[:], in0=Lre[:, :, 1], scalar1=-1.0, scalar2=None, op0=ALU.add))
        V(lambda e: e.tensor_tensor(out=wre[:], in0=u1[:], in1=ar[:], op=ALU.mult))
        V(lambda e: e.tensor_tensor(out=u2[:], in0=Lim[:, :, 1], in1=ai[:], op=ALU.mult))
        V(lambda e: e.tensor_tensor(out=wre[:], in0=wre[:], in1=u2[:], op=ALU.add))
        V(lambda e: e.tensor_tensor(out=wre[:], in0=wre[:], in1=den[:], op=ALU.mult))
        V(lambda e: e.tensor_tensor(out=wim[:], in0=Lim[:, :, 1], in1=ar[:], op=ALU.mult))
        V(lambda e: e.tensor_tensor(out=u2[:], in0=u1[:], in1=ai[:], op=ALU.mult))
        V(lambda e: e.tensor_tensor(out=wim[:], in0=wim[:], in1=u2[:], op=ALU.subtract))
        V(lambda e: e.tensor_tensor(out=wim[:], in0=wim[:], in1=den[:], op=ALU.mult))
        bbre, bbim, v1 = T("bbre", [128, 64, 16]), T("bbim", [128, 64, 16]), T("v1", [128, 64, 16])
        wre_b = wre[:, :, None].to_broadcast([128, 64, 16])
        wim_b = wim[:, :, None].to_broadcast([128, 64, 16])
        V(lambda e: e.tensor_tensor(out=bbre[:], in0=bre[:], in1=wre_b, op=ALU.mult))
        V(lambda e: e.tensor_tensor(out=v1[:], in0=bim[:], in1=wim_b, op=ALU.mult))
        V(lambda e: e.tensor_tensor(out=bbre[:], in0=bbre[:], in1=v1[:], op=ALU.subtract))
        V(lambda e: e.tensor_tensor(out=bbim[:], in0=bim[:], in1=wre_b, op=ALU.mult))
        V(lambda e: e.tensor_tensor(out=v1[:], in0=bre[:], in1=wim_b, op=ALU.mult))
        V(lambda e: e.tensor_tensor(out=bbim[:], in0=bbim[:], in1=v1[:], op=ALU.add))
        GB = 16
        Gre, Gim, Gimn = T("Gre", [128, GB, 8, 16]), T("Gim", [128, GB, 8, 16]), T("Gimn", [128, GB, 8, 16])
        CLre, CLim = T("CLre", [128, GB, 8, 16]), T("CLim", [128, GB, 8, 16])
        Wzre, Wzim = T("Wzre", [128, GB, 8, 16]), T("Wzim", [128, GB, 8, 16])
        w1, w2 = T("w1", [128, GB, 8, 16]), T("w2", [128, GB, 8, 16])
        stage = [T("stage%d" % i, [128, GB, 5, 128], BF16) for i in range(2)]
        Bstage = [Buf(), Buf()]
        tA, tB = T("tA", [128, 128]), T("tB", [128, 128])
        BtA = Buf()
        BWS = buf("WS_s")
        for gb in range(64 // GB):
            g0 = gb * GB
            gs = slice(g0, g0 + GB)
            sh4 = [128, GB, 8, 16]
            PGre_b = PGre[:, gs, :, None].to_broadcast(sh4)
            PGim_b = PGim[:, gs, :, None].to_broadcast(sh4)
            PCre_b = PCre[:, gs, :, None].to_broadcast(sh4)
            PCim_b = PCim[:, gs, :, None].to_broadcast(sh4)
            Bre_b = bbre[:, gs, None, :].to_broadcast(sh4)
            Bim_b = bbim[:, gs, None, :].to_broadcast(sh4)
            Cre_b = cre[:, gs, None, :].to_broadcast(sh4)
            Cim_b = cim[:, gs, None, :].to_broadcast(sh4)
            A8re_b = A8c[:, 0, gs, None, None].to_broadcast(sh4)
            A8im_b = A8c[:, 1, gs, None, None].to_broadcast(sh4)
            V2 = lambda fn: E("vector", fn, reads=[Bin, Bw, BA8, Bconst], writes=[Bw])
            V2(lambda e: e.tensor_tensor(out=w1[:], in0=PGre_b, in1=Bre_b, op=ALU.mult))
            V2(lambda e: e.tensor_tensor(out=w2[:], in0=PGim_b, in1=Bim_b, op=ALU.mult))
            V2(lambda e: e.tensor_tensor(out=Gre[:], in0=w1[:], in1=w2[:], op=ALU.subtract))
            V2(lambda e: e.tensor_tensor(out=w1[:], in0=PGre_b, in1=Bim_b, op=ALU.mult))
            V2(lambda e: e.tensor_tensor(out=w2[:], in0=PGim_b, in1=Bre_b, op=ALU.mult))
            V2(lambda e: e.tensor_tensor(out=Gim[:], in0=w1[:], in1=w2[:], op=ALU.add))
            V2(lambda e: e.tensor_scalar(out=Gimn[:], in0=Gim[:], scalar1=-1.0, scalar2=None, op0=ALU.mult))
            V2(lambda e: e.tensor_tensor(out=w1[:], in0=PCre_b, in1=Cre_b, op=ALU.mult))
            V2(lambda e: e.tensor_tensor(out=w2[:], in0=PCim_b, in1=Cim_b, op=ALU.mult))
            V2(lambda e: e.tensor_tensor(out=CLre[:], in0=w1[:], in1=w2[:], op=ALU.subtract))
            V2(lambda e: e.tensor_tensor(out=w1[:], in0=PCre_b, in1=Cim_b, op=ALU.mult))
            V2(lambda e: e.tensor_tensor(out=w2[:], in0=PCim_b, in1=Cre_b, op=ALU.mult))
            V2(lambda e: e.tensor_tensor(out=CLim[:], in0=w1[:], in1=w2[:], op=ALU.add))
            V2(lambda e: e.tensor_tensor(out=w1[:], in0=Gre[:], in1=A8re_b, op=ALU.mult))
            V2(lambda e: e.tensor_tensor(out=w2[:], in0=Gim[:], in1=A8im_b, op=ALU.mult))
            V2(lambda e: e.tensor_tensor(out=Wzre[:], in0=w1[:], in1=w2[:], op=ALU.subtract))
            V2(lambda e: e.tensor_tensor(out=w1[:], in0=Gim[:], in1=A8re_b, op=ALU.mult))
            V2(lambda e: e.tensor_tensor(out=w2[:], in0=Gre[:], in1=A8im_b, op=ALU.mult))
            V2(lambda e: e.tensor_tensor(out=Wzim[:], in0=w1[:], in1=w2[:], op=ALU.add))
            sg = stage[gb % 2]
            Bsg = Bstage[gb % 2]
            f2 = lambda t: t[:].rearrange("q g t h -> q g (t h)")
            E("gpsimd", lambda e: e.tensor_copy(out=sg[:, :, 3, :], in_=f2(CLre)), reads=[Bw], writes=[Bsg])
            E("gpsimd", lambda e: e.tensor_scalar(out=sg[:, :, 4, :], in0=f2(CLim), scalar1=-1.0, scalar2=None, op0=ALU.mult), reads=[Bw], writes=[Bsg])
            for gl in range(GB):
                g = g0 + gl
                f1 = lambda t: t[:, gl, :, :].rearrange("q t h -> q (t h)")
                E("tensor", lambda e: e.transpose(out=ps[0][:, 0:128], in_=f1(Wzre), identity=ident_f[:]), reads=[Bw, Bconst], writes=[Bps[0]], serial=True)
                E("tensor", lambda e: e.transpose(out=ps[1][:, 0:128], in_=f1(Wzim), identity=ident_f[:]), reads=[Bw, Bconst], writes=[Bps[1]], serial=True)
                E("scalar", lambda e: e.copy(out=sg[:, gl, 0, :], in_=ps[0][:, 0:128]), reads=[Bps[0]], writes=[Bsg])
                E("scalar", lambda e: e.copy(out=sg[:, gl, 1, :], in_=ps[1][:, 0:128]), reads=[Bps[1]], writes=[Bsg])
                for half, pi in ((H0, 2), (H1, 3)):
                    E("tensor", lambda e: e.matmul(ps[pi][:, 0:128], lhsT=f1(Gre)[half, :], rhs=f1(CLre)[half, :], start=True, stop=False),
                      reads=[Bw], writes=[Bps[pi]], serial=True)
                    E("tensor", lambda e: e.matmul(ps[pi][:, 0:128], lhsT=f1(Gimn)[half, :], rhs=f1(CLim)[half, :], start=False, stop=True),
                      reads=[Bw], writes=[Bps[pi]], serial=True)
                E("vector", lambda e: e.tensor_tensor(out=tA[:], in0=ps[2][:, 0:128], in1=mL[:], op=ALU.mult), reads=[Bps[2], Bin], writes=[BtA])
                E("vector", lambda e: e.tensor_tensor(out=tB[:], in0=ps[3][:, 0:128], in1=mU[:], op=ALU.mult), reads=[Bps[3], Bin], writes=[BtA])
                E("vector", lambda e: e.tensor_tensor(out=tA[:], in0=tA[:], in1=tB[:], op=ALU.add), reads=[BtA], writes=[BtA])
                E("vector", lambda e: e.scalar_tensor_tensor(out=sg[:, gl, 2, :], in0=ident_f[:], scalar=dug[:, g:g + 1], in1=tA[:],
                                                             op0=ALU.mult, op1=ALU.add), reads=[BtA, Bin, Bconst], writes=[Bsg])
            E("sync", lambda e: e.dma_start(out=d["WS_s"][:, gs, :, :], in_=sg[:]), reads=[Bsg], writes=[BWS], dma=True)


def _complex_sq(c, eng, t, tmp, bufs):
    E = c["E"]
    E(eng, lambda e: e.tensor_tensor(out=tmp[:, 0, :], in0=t[:, 0, :], in1=t[:, 0, :], op=ALU.mult), reads=bufs, writes=bufs)
    E(eng, lambda e: e.tensor_tensor(out=tmp[:, 1, :], in0=t[:, 1, :], in1=t[:, 1, :], op=ALU.mult), reads=bufs, writes=bufs)
    E(eng, lambda e: e.scalar_tensor_tensor(out=t[:, 1, :], in0=t[:, 0, :], scalar=2.0, in1=t[:, 1, :], op0=ALU.mult, op1=ALU.mult), reads=bufs, writes=bufs)
    E(eng, lambda e: e.tensor_tensor(out=t[:, 0, :], in0=tmp[:, 0, :], in1=tmp[:, 1, :], op=ALU.subtract), reads=bufs, writes=bufs)


def phase_ssm_scan(c, d, own):
    nc, E, ps, Bps, sbt, buf, vcol = c["nc"], c["E"], c["ps"], c["Bps"], c["sbt"], c["buf"], c["vcol"]
    Bconst, BA8, BSin = c["Bconst"], c["BA8"], c["BSin"]
    A8a, A8b, A8c, Sin_t = c["A8a"], c["A8b"], c["A8c"], c["Sin_t"]
    es = c["es"]
    H0, H1 = slice(0, 64), slice(64, 128)
    if "Est" not in c:
        c["Est"] = sbt(es, "Est", [128, 3, 2, 64], F32)
        c["BEst"] = Buf()
        c["A64"] = sbt(es, "A64", [128, 2, 64], F32)
        c["Aslot"] = sbt(es, "Aslot", [128, 2, 64], F32)
        c["BApow"] = Buf()
        tmpq = sbt(es, "tmpq", [128, 2, 64], F32)
        Bq = [c["BApow"], BA8]
        E("vector", lambda e: e.tensor_copy(out=c["A64"][:], in_=A8c[:]), reads=[BA8], writes=[c["BApow"]])
        for _ in range(6):
            _complex_sq(c, "vector", c["A64"], tmpq, [c["BApow"]])
        E("vector", lambda e: e.tensor_copy(out=c["Aslot"][:], in_=c["A64"][:]), reads=[c["BApow"]], writes=[c["BApow"]])
        n = NB_OWN
        while n > 1:
            _complex_sq(c, "vector", c["Aslot"], tmpq, [c["BApow"]])
            n //= 2
    Est, BEst, A64, BApow = c["Est"], c["BEst"], c["A64"], c["BApow"]
    with ExitStack() as st:
        Wu = sbt(st, "Wu", [128, 16, 1024], BF16)
        Sel = sbt(st, "Sel", [128, 64, 128], BF16)
        WZ = sbt(st, "WZ", [128, 64, 2, 128], BF16)
        BW = Buf()
        for k in range(0, 16, 4):
            E("gpsimd", lambda e, k=k: e.dma_start(out=Wu[:, k:k + 4, :], in_=d["w_in"][:, k:k + 4, 0:1024]), writes=[BW], dma=True)
        E("gpsimd", lambda e: e.dma_start(out=Sel[:], in_=d["sel"]), writes=[BW], dma=True)
        E("sync", lambda e: e.dma_start(out=WZ[:], in_=d["WS_s"][:, :, 0:2, :]), reads=[buf("WS_s")], writes=[BW], dma=True)
        xt = sbt(st, "xt", [128, 16, BLK], F32)
        scr = sbt(st, "scr", [128, 16, BLK], BF16)
        sq = scr
        ht = sbt(st, "ht", [128, 16, BLK], BF16)
        rstd = sbt(st, "rstd", [128, BLK], F32)
        Bxt, Bht, Brstd = Buf(), Buf(), Buf()
        uT = scr[:, 0:8, :]
        BuT = Buf()
        ug = scr[:, 8:16, :].rearrange("p a (b c) -> p (a b) c", c=64)
        Bug = Buf()
        Bsq = [BuT, Bug]
        Zbs = [sbt(st, "Zb%d" % i, [128, 64, 2, 64], BF16) for i in range(2)]
        BZbs = [Buf(), Buf()]
        nblk_done = [0]
        Sal = sbt(st, "Sal", [128, 64, 2, 64], BF16)
        BSal = Buf()
        S2 = sbt(st, "S2", [128, 2, 64], F32)
        q1 = sbt(st, "q1", [128, 2, 64], F32)
        q2 = sbt(st, "q2", [128, 2, 64], F32)
        BS2 = Buf()
        accb = sbt(st, "accb", [128, 2, 64], F32)
        Pw = sbt(st, "Pw", [128, 2, 64], F32)
        m1 = sbt(st, "m1", [128, 2, 64], F32)
        m2 = sbt(st, "m2", [128, 2, 64], F32)
        m3 = sbt(st, "m3", [128, 2, 64], F32)
        Bacc = Buf()
        slots = [0] if own else [1, 2, 3]
        for slot in slots:
            E("vector", lambda e: e.memset(S2[:], 0.0), writes=[BS2])
            if own:
                E("vector", lambda e: e.tensor_copy(out=S2[H0], in_=Sin_t[H0]), reads=[BSin], writes=[BS2])
            else:
                E("gpsimd", lambda e: e.memset(accb[:], 0.0), writes=[Bacc])
                E("gpsimd", lambda e: e.memset(Pw[:], 0.0), writes=[Bacc])
                E("gpsimd", lambda e: e.memset(Pw[:, 0, :], 1.0), writes=[Bacc])
            for bi in range(NB_OWN):
                blk = slot * NB_OWN + bi
                Zb, BZb = Zbs[nblk_done[0] % 2], BZbs[nblk_done[0] % 2]
                nblk_done[0] += 1
                load_norm_block(c, st, d["xT_ctx"], blk, xt, sq, ht, rstd, Bxt, Bsq, Bht, Brstd, V_PREMIX, ps_i=0)
                for j in range(8):
                    pj = 1 + j % 2
                    for k in range(16):
                        E("tensor", lambda e, k=k: e.matmul(ps[pj][:], lhsT=Wu[:, k, j * 128:(j + 1) * 128], rhs=ht[:, k, :],
                                                            start=(k == 0), stop=(k == 15)), reads=[BW, Bht], writes=[Bps[pj]], signal=(k == 15))
                    E("scalar", lambda e: e.copy(out=uT[:, j, :], in_=ps[pj][:]), reads=[Bps[pj]], writes=[BuT])
                for j in range(8):
                    pj = 3 + j % 2
                    uv = uT[:, j, :].rearrange("p (c t) -> p c t", t=8)
                    for gl in range(8):
                        for tau in range(8):
                            E("tensor", lambda e: e.matmul(ps[pj][:, gl * 64:(gl + 1) * 64], lhsT=Sel[:, gl * 8 + tau, :], rhs=uv[:, :, tau],
                                                           start=(tau == 0), stop=(tau == 7)), reads=[BW, BuT], writes=[Bps[pj]], signal=(tau == 7))
                    E("vector", lambda e: e.tensor_copy(out=ug[:, j * 8:(j + 1) * 8, :], in_=ps[pj][:].rearrange("p (g c) -> p g c", c=64)),
                      reads=[Bps[pj]], writes=[Bug])
                if own:
                    E("sync", lambda e: e.dma_start(out=d["UG_s"][bi], in_=ug), reads=[Bug], writes=[buf("UG_s")], dma=True)
                for j in range(8):
                    pr, pi = 5, 6
                    for ri, pz in ((0, pr), (1, pi)):
                        for gl in range(8):
                            g = j * 8 + gl
                            E("tensor", lambda e: e.matmul(ps[pz][:, gl * 64:(gl + 1) * 64], lhsT=WZ[:, g, ri, :], rhs=ug[:, g, :],
                                                           start=True, stop=True), reads=[BW, Bug], writes=[Bps[pz]])
                        pv = ps[pz][:].rearrange("q (g c) -> q g c", c=64)
                        E("scalar", lambda e: e.copy(out=Zb[H0, :, ri, j * 8:(j + 1) * 8].rearrange("q c g -> q g c"), in_=pv[H0]),
                          reads=[Bps[pz]], writes=[BZb])
                        E("vector", lambda e: e.tensor_copy(out=Zb[H1, :, ri, j * 8:(j + 1) * 8].rearrange("q c g -> q g c"), in_=pv[H1, :, ::-1]),
                          reads=[Bps[pz]], writes=[BZb])
                E("vector", lambda e: e.memset(S2[H1], 0.0), writes=[BS2])
                if own:
                    E("vector", lambda e: e.tensor_copy(out=Sal[H0, 0, :, :], in_=S2[H0]), reads=[BS2], writes=[BSal])
                for i in range(64):
                    E("vector", lambda e: e.tensor_tensor(out=q1[:], in0=S2[:], in1=A8a[:], op=ALU.mult), reads=[BS2, BA8], writes=[BS2])
                    E("vector", lambda e: e.tensor_tensor(out=q2[:], in0=S2[:, ::-1, :], in1=A8b[:], op=ALU.mult), reads=[BS2, BA8], writes=[BS2])
                    E("vector", lambda e: e.tensor_tensor(out=q1[:], in0=q1[:], in1=q2[:], op=ALU.add), reads=[BS2], writes=[BS2])
                    E("vector", lambda e: e.tensor_tensor(out=S2[:], in0=q1[:], in1=Zb[:, i, :, :], op=ALU.add), reads=[BS2, BZb], writes=[BS2])
                    if own and i < 63:
                        E("vector", lambda e: e.tensor_copy(out=Sal[H0, i + 1, :, :], in_=S2[H0]), reads=[BS2], writes=[BSal])
                if own:
                    E("sync", lambda e: e.dma_start(out=d["SF_s"][bi], in_=Sal[H0]), reads=[BSal], writes=[buf("SF_s")], dma=True)
                    E("sync", lambda e: e.dma_start(out=d["ZB_s"][bi], in_=Zb[H1]), reads=[BZb], writes=[buf("ZB_s")], dma=True)
                else:
                    rb = [BS2, Bacc, BApow]
                    c["cmul"]("gpsimd", m3, Pw, S2, m1, m2, rb, [Bacc], rows=H1)
                    E("gpsimd", lambda e: e.tensor_tensor(out=accb[H1], in0=accb[H1], in1=m3[H1], op=ALU.add), reads=rb, writes=[Bacc])
                    c["cmul"]("gpsimd", m3, Pw, A64, m1, m2, rb, [Bacc], rows=H1)
                    E("gpsimd", lambda e: e.tensor_copy(out=Pw[H1], in_=m3[H1]), reads=rb, writes=[Bacc])
            if not own:
                so = slot - 1
                E("vector", lambda e: e.tensor_copy(out=Est[H0, so, :, :], in_=S2[H0]), reads=[BS2], writes=[BEst])
                E("gpsimd", lambda e: e.tensor_copy(out=Est[H1, 2 - so, :, :], in_=accb[H1]), reads=[Bacc], writes=[BEst])


def phase_ssm_carry(c, d):
    E, sbt, es = c["E"], c["sbt"], c["es"]
    Est, BEst, Aslot, BApow, Sin_t, BSin, gates, Bconst = c["Est"], c["BEst"], c["Aslot"], c["BApow"], c["Sin_t"], c["BSin"], c["gates"], c["Bconst"]
    with ExitStack() as st:
        m1 = sbt(st, "cm1", [128, 2, 64], F32)
        m2 = sbt(st, "cm2", [128, 2, 64], F32)
        T = sbt(st, "cT", [128, 2, 64], F32)
        Bt = Buf()
        E("vector", lambda e: e.memset(Sin_t[:], 0.0), writes=[BSin])
        rb = [BEst, BApow, BSin, Bt, Bconst]
        for s_ in range(3):
            c["cmul"]("vector", T, Aslot, Sin_t, m1, m2, rb, [Bt])
            E("vector", lambda e: e.tensor_tensor(out=T[:], in0=T[:], in1=Est[:, s_, :, :], op=ALU.add), reads=rb, writes=[Bt])
            E("vector", lambda e: e.tensor_tensor(out=T[:], in0=T[:], in1=Sin_t[:], op=ALU.subtract), reads=rb, writes=[Bt])
            E("vector", lambda e: e.scalar_tensor_tensor(out=Sin_t[:], in0=T[:], scalar=gates[:, s_:s_ + 1], in1=Sin_t[:], op0=ALU.mult, op1=ALU.add),
              reads=rb, writes=[BSin])
        if d["EST_s"] is not None:
            E("sync", lambda e: e.dma_start(out=d["EST_s"], in_=Sin_t[:]), reads=[BSin], dma=True)


def phase_ssm_out(c, d):
    nc, E, ps, Bps, sbt, buf, vcol = c["nc"], c["E"], c["ps"], c["Bps"], c["sbt"], c["buf"], c["vcol"]
    Bconst, BA8, BSin = c["Bconst"], c["BA8"], c["BSin"]
    A8a, A8b, Sin_t, ones_b = c["A8a"], c["A8b"], c["Sin_t"], c["ones_b"]
    H0, H1 = slice(0, 64), slice(64, 128)
    with ExitStack() as st:
        WM = sbt(st, "WM", [128, 64, 3, 128], BF16)
        SelT = sbt(st, "SelT", [128, 64, 128], BF16)
        Wg = sbt(st, "Wg", [128, 8, 1024], BF16)
        BW = Buf()
        E("sync", lambda e: e.dma_start(out=WM[:], in_=d["WS_s"][:, :, 2:5, :]), reads=[buf("WS_s")], writes=[BW], dma=True)
        E("gpsimd", lambda e: e.dma_start(out=SelT[:], in_=d["selT"]), writes=[BW], dma=True)
        E("gpsimd", lambda e: e.dma_start(out=Wg[:], in_=d["w_glu"]), writes=[BW], dma=True)
        ug = [sbt(st, "ugc%d" % i, [128, 64, 64], BF16) for i in range(2)]
        Bug = [Buf(), Buf()]
        SalN = sbt(st, "SalN", [128, 64, 2, 64], BF16)
        SalR = sbt(st, "SalR", [128, 64, 2, 64], BF16)
        ZbR = sbt(st, "ZbR", [128, 64, 2, 64], BF16)
        BSalN, BSalR, BZbR = Buf(), Buf(), Buf()
        S2 = sbt(st, "S2c", [128, 2, 64], F32)
        q1 = sbt(st, "q1c", [128, 2, 64], F32)
        q2 = sbt(st, "q2c", [128, 2, 64], F32)
        BS2 = Buf()
        yg = [sbt(st, "yg%d" % i, [128, 8, 64], BF16) for i in range(2)]
        Byg = [Buf(), Buf()]
        ysf = sbt(st, "ysf", [128, 8, BLK], F32)
        ysb = sbt(st, "ysb", [128, 8, BLK], BF16)
        Bysf, Bysb = Buf(), Buf()
        ta = [sbt(st, "ta%d" % i, [128, BLK], F32) for i in range(2)]
        tb = [sbt(st, "tb%d" % i, [128, BLK], F32) for i in range(2)]
        Bta, Btb = [Buf(), Buf()], [Buf(), Buf()]
        sqn = sbt(st, "sqn", [128, 8, BLK], BF16)
        nsb = sbt(st, "nsb", [128, 8, BLK], BF16)
        rstd = sbt(st, "rstdc", [128, BLK], F32)
        Bsqn, Bnsb, Brstd = Buf(), Buf(), Buf()
        E("vector", lambda e: e.memset(S2[:], 0.0), writes=[BS2])
        E("vector", lambda e: e.tensor_copy(out=S2[H1], in_=Sin_t[H1]), reads=[BSin], writes=[BS2])
        for it, bi in enumerate(range(NB_OWN - 1, -1, -1)):
            b = it % 2
            tok = slice(bi * BLK, (bi + 1) * BLK)
            E("sync", lambda e: e.dma_start(out=ug[b][:], in_=d["UG_s"][bi]), reads=[buf("UG_s")], writes=[Bug[b]], dma=True)
            E("sync", lambda e: e.dma_start(out=SalN[H0], in_=d["SF_s"][bi]), reads=[buf("SF_s")], writes=[BSalN], dma=True)
            E("sync", lambda e: e.dma_start(out=ZbR[H1], in_=d["ZB_s"][bi]), reads=[buf("ZB_s")], writes=[BZbR], dma=True)
            E("vector", lambda e: e.tensor_copy(out=SalR[H1, 0, :, :], in_=S2[H1]), reads=[BS2], writes=[BSalR])
            for i in range(64):
                E("vector", lambda e: e.tensor_tensor(out=q1[H1], in0=S2[H1], in1=A8a[H1], op=ALU.mult), reads=[BS2, BA8], writes=[BS2])
                E("vector", lambda e: e.tensor_tensor(out=q2[H1], in0=S2[H1, ::-1, :], in1=A8b[H1], op=ALU.mult), reads=[BS2, BA8], writes=[BS2])
                E("vector", lambda e: e.tensor_tensor(out=q1[H1], in0=q1[H1], in1=q2[H1], op=ALU.add), reads=[BS2], writes=[BS2])
                E("vector", lambda e: e.tensor_tensor(out=S2[H1], in0=q1[H1], in1=ZbR[H1, i, :, :], op=ALU.add), reads=[BS2, BZbR], writes=[BS2])
                if i < 63:
                    E("vector", lambda e: e.tensor_copy(out=SalR[H1, i + 1, :, :], in_=S2[H1]), reads=[BS2], writes=[BSalR])
            E("gpsimd", lambda e: e.tensor_copy(out=SalN[H1], in_=SalR[H1, ::-1, :, :]), reads=[BSalR], writes=[BSalN])
            for j in range(8):
                py = 1 + j % 2
                for gl in range(8):
                    g = j * 8 + gl
                    o = ps[py][:, gl * 64:(gl + 1) * 64]
                    E("tensor", lambda e: e.matmul(o, lhsT=WM[:, g, 0, :], rhs=ug[b][:, g, :], start=True, stop=False), reads=[BW, Bug[b]], writes=[Bps[py]], signal=False)
                    E("tensor", lambda e: e.matmul(o, lhsT=WM[:, g, 1, :], rhs=SalN[:, :, 0, g], start=False, stop=False), reads=[BW, BSalN], writes=[Bps[py]], signal=False)
                    E("tensor", lambda e: e.matmul(o, lhsT=WM[:, g, 2, :], rhs=SalN[:, :, 1, g], start=False, stop=True), reads=[BW, BSalN], writes=[Bps[py]])
                E("scalar", lambda e: e.copy(out=yg[j % 2][:], in_=ps[py][:].rearrange("p (g c) -> p g c", c=64)), reads=[Bps[py]], writes=[Byg[j % 2]])
                pf = 3 + j % 2
                pfv = ps[pf][:].rearrange("p (c t) -> p c t", t=8)
                for tau in range(8):
                    for gl in range(8):
                        E("tensor", lambda e: e.matmul(pfv[:, :, tau], lhsT=SelT[:, gl * 8 + tau, :], rhs=yg[j % 2][:, gl, :],
                                                       start=(gl == 0), stop=(gl == 7)), reads=[BW, Byg[j % 2]], writes=[Bps[pf]], signal=(gl == 7))
                a_, b_ = ta[j % 2], tb[j % 2]
                Ba, Bb = Bta[j % 2], Btb[j % 2]
                E("scalar", lambda e: e.activation(out=a_[:], in_=ps[pf][:], func=AF.Square), reads=[Bps[pf]], writes=[Ba])
                E("vector", lambda e: e.tensor_scalar(out=a_[:], in0=a_[:], scalar1=0.044715, scalar2=1.0, op0=ALU.mult, op1=ALU.add), reads=[Ba], writes=[Ba])
                E("vector", lambda e: e.tensor_tensor(out=a_[:], in0=a_[:], in1=ps[pf][:], op=ALU.mult), reads=[Ba, Bps[pf]], writes=[Ba])
                E("scalar", lambda e: e.activation(out=b_[:], in_=a_[:], func=AF.Sigmoid, scale=1.5957691216057308), reads=[Ba], writes=[Bb])
                E("vector", lambda e: e.tensor_tensor(out=ysf[:, j, :], in0=b_[:], in1=ps[pf][:], op=ALU.mult), reads=[Bb, Bps[pf]], writes=[Bysf])
                E("gpsimd", lambda e: e.tensor_copy(out=ysb[:, j, :], in_=ysf[:, j, :]), reads=[Bysf], writes=[Bysb])
            for j2 in range(8):
                pg = 5 + j2 % 2
                for j in range(8):
                    E("tensor", lambda e: e.matmul(ps[pg][:], lhsT=Wg[:, j, j2 * 128:(j2 + 1) * 128], rhs=ysb[:, j, :], start=(j == 0), stop=(j == 7)),
                      reads=[BW, Bysb], writes=[Bps[pg]], signal=(j == 7))
                b_, Bb = tb[j2 % 2], Btb[j2 % 2]
                E("scalar", lambda e: e.activation(out=b_[:], in_=ps[pg][:], func=AF.Sigmoid), reads=[Bps[pg]], writes=[Bb])
                E("vector", lambda e: e.tensor_tensor(out=ysf[:, j2, :], in0=ysf[:, j2, :], in1=b_[:], op=ALU.mult), reads=[Bb, Bysf], writes=[Bysf])
            E("scalar", lambda e: e.activation(out=sqn[:], in_=ysf[:], func=AF.Square), reads=[Bysf], writes=[Bsqn])
            for j in range(8):
                E("tensor", lambda e: e.matmul(ps[7][:], lhsT=ones_b[:], rhs=sqn[:, j, :], start=(j == 0), stop=(j == 7)), reads=[Bconst, Bsqn], writes=[Bps[7]], signal=(j == 7))
            E("scalar", lambda e: e.activation(out=rstd[:], in_=ps[7][:], func=AF.Sqrt, scale=1.0 / 1024, bias=EPS), reads=[Bps[7]], writes=[Brstd])
            E("vector", lambda e: e.reciprocal(out=rstd[:], in_=rstd[:]), reads=[Brstd], writes=[Brstd])
            for j in range(8):
                E("vector", lambda e: e.scalar_tensor_tensor(out=nsb[:, j, :], in0=ysf[:, j, :], scalar=vcol(V_SSMOUT + j), in1=rstd[:],
                                                             op0=ALU.mult, op1=ALU.mult), reads=[Bysf, Brstd, Bconst], writes=[Bnsb])
            E("sync", lambda e: e.dma_start(out=d["NS_s"][:, :, tok], in_=nsb[:]), reads=[Bnsb], writes=[buf("NS_s")], dma=True)


def phase_attention(c, d):
    nc, E, ps, Bps, sbt, buf = c["nc"], c["E"], c["ps"], c["Bps"], c["sbt"], c["buf"]
    psbig, ones_b, Bconst = c["psbig"], c["ones_b"], c["Bconst"]
    scale = 1.0 / float(np.sqrt(128.0))
    QW = 2 * BLK
    with ExitStack() as st:
        KT = sbt(st, "KT", [128, NCTX], BF16)
        Vt = sbt(st, "Vt", [128, NKT, 128], BF16)
        mk = sbt(st, "mk", [128, NKT], F32)
        BKV, Bmk = Buf(), Buf()
        E("sync", lambda e: e.dma_start(out=mk[:], in_=d["maskb"]), writes=[Bmk], dma=True)
        qt = [sbt(st, "qt%d" % i, [128, QW], BF16) for i in range(2)]
        Bqt = [Buf(), Buf()]
        NP = 4
        pT = [sbt(st, "pT%d" % i, [128, QW], BF16) for i in range(NP)]
        BpT = [Buf() for _ in range(NP)]
        acc = [sbt(st, "acc%d" % i, [128, QW], F32) for i in range(2)]
        Bacc = [Buf(), Buf()]
        rec = sbt(st, "rec", [128, QW], F32)
        Brec = Buf()
        dhi = sbt(st, "dhi", [128, QW], BF16)
        dlo = sbt(st, "dlo", [128, QW], BF16)
        Bdh = Buf()
        yo = [sbt(st, "yo%d" % i, [128, QW], F32) for i in range(2)]
        Byo = [Buf(), Buf()]
        BS = [[Bps[0], Bps[1]], [Bps[2], Bps[3]]]
        BO = [Bps[4], Bps[5]]
        BD = [Bps[6], Bps[7]]
        it = 0
        for kvh in range(2):
            nchunk = 4
            for q in range(nchunk):
                cs_ = slice(q * NCTX // nchunk, (q + 1) * NCTX // nchunk)
                ks_ = slice(q * NKT // nchunk, (q + 1) * NKT // nchunk)
                E("sync", lambda e: e.dma_start(out=KT[:, cs_], in_=d["KT_s"][kvh, :, cs_]), reads=[buf("KT_s")], writes=[BKV], dma=True)
                E("sync", lambda e: e.dma_start(out=Vt[:, ks_, :], in_=d["V_s"][kvh, :, ks_, :]), reads=[buf("V_s")], writes=[BKV], dma=True)
            for qh in range(4):
                head = kvh * 4 + qh
                for qb in range(NB_OWN // 2):
                    b = it % 2
                    tok = slice(qb * QW, (qb + 1) * QW)
                    E("sync", lambda e: e.dma_start(out=qt[b][:], in_=d["QT_s"][head, :, tok]), reads=[buf("QT_s")], writes=[Bqt[b]], dma=True)

                    def s_mm(kt):
                        sb_ = kt % 2
                        for hf in range(2):
                            E("tensor", lambda e: e.matmul(psbig[sb_][:, hf * BLK:(hf + 1) * BLK], lhsT=KT[:, kt * 128:(kt + 1) * 128],
                                                           rhs=qt[b][:, hf * BLK:(hf + 1) * BLK], start=True, stop=True),
                              reads=[BKV, Bqt[b]], writes=[BS[sb_][hf]], signal=(hf == 1))
                    s_mm(0)
                    nd = [0, 0]
                    for kt in range(NKT):
                        if kt + 1 < NKT:
                            s_mm(kt + 1)
                        sb_ = kt % 2
                        r = kt % NP
                        E("scalar", lambda e: e.activation(out=pT[r][:], in_=psbig[sb_][:], func=AF.Exp, bias=mk[:, kt:kt + 1], scale=scale),
                          reads=BS[sb_] + [Bmk], writes=[BpT[r]])
                        for hf in range(2):
                            E("tensor", lambda e: e.matmul(psbig[2][:, hf * BLK:(hf + 1) * BLK], lhsT=Vt[:, kt, :], rhs=pT[r][:, hf * BLK:(hf + 1) * BLK],
                                                           start=(kt == 0), stop=(kt == NKT - 1)),
                              reads=[BKV, BpT[r]], writes=[BO[hf]], signal=(kt == NKT - 1 and hf == 1))
                        a = 1 if kt % 3 == 2 else 0
                        eng = "vector" if a == 0 else "gpsimd"
                        if nd[a] == 0:
                            E(eng, lambda e: e.tensor_copy(out=acc[a][:], in_=pT[r][:]), reads=[BpT[r]], writes=[Bacc[a]])
                        else:
                            E(eng, lambda e: e.tensor_tensor(out=acc[a][:], in0=acc[a][:], in1=pT[r][:], op=ALU.add), reads=[BpT[r], Bacc[a]], writes=[Bacc[a]])
                        nd[a] += 1
                    E("vector", lambda e: e.tensor_tensor(out=acc[0][:], in0=acc[0][:], in1=acc[1][:], op=ALU.add), reads=[Bacc[0], Bacc[1]], writes=[Bacc[0]])
                    E("vector", lambda e: e.tensor_copy(out=dhi[:], in_=acc[0][:]), reads=[Bacc[0]], writes=[Bdh])
                    E("vector", lambda e: e.tensor_tensor(out=acc[0][:], in0=acc[0][:], in1=dhi[:], op=ALU.subtract), reads=[Bdh, Bacc[0]], writes=[Bacc[0]])
                    E("vector", lambda e: e.tensor_copy(out=dlo[:], in_=acc[0][:]), reads=[Bacc[0]], writes=[Bdh])
                    for hf in range(2):
                        hs = slice(hf * BLK, (hf + 1) * BLK)
                        E("tensor", lambda e: e.matmul(psbig[3][:, hs], lhsT=ones_b[:], rhs=dhi[:, hs], start=True, stop=False), reads=[Bconst, Bdh], writes=[BD[hf]], signal=False)
                        E("tensor", lambda e: e.matmul(psbig[3][:, hs], lhsT=ones_b[:], rhs=dlo[:, hs], start=False, stop=True), reads=[Bconst, Bdh], writes=[BD[hf]])
                    E("vector", lambda e: e.reciprocal(out=rec[:], in_=psbig[3][:]), reads=BD, writes=[Brec])
                    E("vector", lambda e: e.tensor_tensor(out=yo[b][:], in0=psbig[2][:], in1=rec[:], op=ALU.mult), reads=BO + [Brec], writes=[Byo[b]])
                    E("sync", lambda e: e.dma_start(out=d["YA_s"][head, :, tok], in_=yo[b][:]), reads=[Byo[b]], writes=[buf("YA_s")], dma=True)
                    it += 1


def phase_weight_cast(c, d):
    E, buf = c["E"], c["buf"]
    for f in range(16):
        E("gpsimd", lambda e: e.dma_start(out=d["WOUT_b"][f].rearrange("p k c -> p (k c)"), in_=d["w_out"][f].rearrange("p k c -> p (k c)")),
          writes=[buf("WOUT_b")], dma=True)
    for f in range(64):
        E("gpsimd", lambda e: e.dma_start(out=d["WUP_b"][f].rearrange("p k c -> p (k c)"), in_=d["w_up"][f].rearrange("p k c -> p (k c)")),
          writes=[buf("WUP_b")], dma=True)
    for f in range(16):
        for h in range(2):
            E("gpsimd", lambda e: e.dma_start(out=d["WDN_b"][f, h].rearrange("p k c -> p (k c)"), in_=d["w_dn"][f, h].rearrange("p k c -> p (k c)")),
              writes=[buf("WDN_b")], dma=True)


def phase_mlp(c, d):
    nc, E, ps, Bps, sbt, buf, vcol = c["nc"], c["E"], c["ps"], c["Bps"], c["sbt"], c["buf"], c["vcol"]
    ones_b, Bconst = c["ones_b"], c["Bconst"]
    with ExitStack() as st:
        XT = sbt(st, "XT", [128, 16, BLK], F32)
        MT = sbt(st, "MT", [128, 16, BLK], F32)
        ACTb = sbt(st, "ACTb", [128, 16, BLK], BF16)
        AT = sbt(st, "AT", [128, 32, BLK], BF16)
        YA = sbt(st, "YA", [128, 8, BLK], F32)
        BXT, BMT, BACT, BAT, BYA = Buf(), Buf(), Buf(), Buf(), Buf()
        rstd = sbt(st, "rstde", [128, BLK], F32)
        Brstd = Buf()
        tmp = [sbt(st, "tmpe%d" % i, [128, BLK], F32) for i in range(2)]
        Btmp = [Buf(), Buf()]
        wo = [sbt(st, "wo%d" % i, [128, 16, 128], BF16) for i in range(3)]
        Bwo = [Buf() for _ in range(3)]
        wu = [sbt(st, "wu%d" % i, [128, 16, 128], BF16) for i in range(3)]
        Bwu = [Buf() for _ in range(3)]
        wd = [sbt(st, "wd%d" % i, [128, 32, 128], BF16) for i in range(3)]
        Bwd = [Buf() for _ in range(3)]
        src = d["xT_ctx"].rearrange("(k p) t -> p k t", p=128)
        SQ = AT[:, 0:16, :]

        def stats(srct, Bsrc, nk, denom, pi):
            E("scalar", lambda e: e.activation(out=SQ[:, 0:nk, :], in_=srct, func=AF.Square), reads=[Bsrc], writes=[BAT])
            for k in range(nk):
                E("tensor", lambda e, k=k: e.matmul(ps[pi][:], lhsT=ones_b[:], rhs=SQ[:, k, :], start=(k == 0), stop=(k == nk - 1)),
                  reads=[Bconst, BAT], writes=[Bps[pi]], signal=(k == nk - 1))
            E("scalar", lambda e: e.activation(out=rstd[:], in_=ps[pi][:], func=AF.Sqrt, scale=1.0 / denom, bias=EPS), reads=[Bps[pi]], writes=[Brstd])
            E("vector", lambda e: e.reciprocal(out=rstd[:], in_=rstd[:]), reads=[Brstd], writes=[Brstd])

        for blk in range(NB_OWN):
            tok = slice(blk * BLK, (blk + 1) * BLK)
            E("sync", lambda e: e.dma_start(out=XT[:, 0:8, :], in_=src[:, 0:8, tok]), writes=[BXT], dma=True)
            E("sync", lambda e: e.dma_start(out=XT[:, 8:16, :], in_=src[:, 8:16, tok]), writes=[BXT], dma=True)
            E("sync", lambda e: e.dma_start(out=ACTb[:, 0:8, :], in_=d["NS_s"][:, :, tok]), reads=[buf("NS_s")], writes=[BACT], dma=True)
            E("sync", lambda e: e.dma_start(out=YA[:], in_=d["YA_s"][:, :, tok].rearrange("h p t -> p h t")), reads=[buf("YA_s")], writes=[BYA], dma=True)
            stats(YA[:], BYA, 8, 1024.0, 0)
            for j in range(8):
                E("vector", lambda e: e.scalar_tensor_tensor(out=ACTb[:, 8 + j, :], in0=YA[:, j, :], scalar=vcol(V_ATTOUT + j), in1=rstd[:],
                                                             op0=ALU.mult, op1=ALU.mult), reads=[BYA, Brstd, Bconst], writes=[BACT])
            for dt in range(16):
                w, Bw = wo[dt % 3], Bwo[dt % 3]
                E("sync", lambda e: e.dma_start(out=w[:], in_=d["WOUT_b"][dt]), reads=[buf("WOUT_b")], writes=[Bw], dma=True)
                pi = 1 + dt % 2
                for k in range(16):
                    E("tensor", lambda e, k=k: e.matmul(ps[pi][:], lhsT=w[:, k, :], rhs=ACTb[:, k, :], start=(k == 0), stop=(k == 15)),
                      reads=[Bw, BACT], writes=[Bps[pi]], signal=(k == 15))
                E("scalar", lambda e: e.copy(out=MT[:, dt, :], in_=ps[pi][:]), reads=[Bps[pi]], writes=[BMT])
            stats(MT[:], BMT, 16, 2048.0, 0)
            for k in range(16):
                t, Bt = tmp[k % 2], Btmp[k % 2]
                E("vector", lambda e: e.scalar_tensor_tensor(out=t[:], in0=MT[:, k, :], scalar=vcol(V_POSTMIX + k), in1=rstd[:],
                                                             op0=ALU.mult, op1=ALU.mult), reads=[BMT, Brstd, Bconst], writes=[Bt])
                E("gpsimd", lambda e: e.tensor_tensor(out=XT[:, k, :], in0=XT[:, k, :], in1=t[:], op=ALU.add), reads=[Bt, BXT], writes=[BXT])
            stats(XT[:], BXT, 16, 2048.0, 0)
            for k in range(16):
                E("vector", lambda e: e.scalar_tensor_tensor(out=ACTb[:, k, :], in0=XT[:, k, :], scalar=vcol(V_PREMLP + k), in1=rstd[:],
                                                             op0=ALU.mult, op1=ALU.mult), reads=[BXT, Brstd, Bconst], writes=[BACT])
            for half in range(2):
                for f in range(32):
                    ff = half * 32 + f
                    w, Bw = wu[ff % 3], Bwu[ff % 3]
                    E("sync", lambda e: e.dma_start(out=w[:], in_=d["WUP_b"][ff]), reads=[buf("WUP_b")], writes=[Bw], dma=True)
                    pi = 3 + ff % 2
                    for k in range(16):
                        E("tensor", lambda e, k=k: e.matmul(ps[pi][:], lhsT=w[:, k, :], rhs=ACTb[:, k, :], start=(k == 0), stop=(k == 15)),
                          reads=[Bw, BACT], writes=[Bps[pi]], signal=(k == 15))
                    t, Bt = tmp[ff % 2], Btmp[ff % 2]
                    E("scalar", lambda e: e.activation(out=t[:], in_=ps[pi][:], func=AF.Relu), reads=[Bps[pi]], writes=[Bt])
                    E("gpsimd" if ff % 2 else "vector", lambda e: e.tensor_tensor(out=AT[:, f, :], in0=t[:], in1=t[:], op=ALU.mult), reads=[Bt], writes=[BAT])
                for dt in range(16):
                    i3 = (half * 16 + dt) % 3
                    w, Bw = wd[i3], Bwd[i3]
                    E("sync", lambda e: e.dma_start(out=w[:], in_=d["WDN_b"][dt, half]), reads=[buf("WDN_b")], writes=[Bw], dma=True)
                    pi = 5 + dt % 2
                    for f in range(32):
                        E("tensor", lambda e, f=f: e.matmul(ps[pi][:], lhsT=w[:, f, :], rhs=AT[:, f, :], start=(f == 0), stop=(f == 31)),
                          reads=[Bw, BAT], writes=[Bps[pi]], signal=(f == 31))
                    if half == 0:
                        E("scalar", lambda e: e.copy(out=MT[:, dt, :], in_=ps[pi][:]), reads=[Bps[pi]], writes=[BMT])
                    else:
                        E("vector", lambda e: e.tensor_tensor(out=MT[:, dt, :], in0=MT[:, dt, :], in1=ps[pi][:], op=ALU.add), reads=[Bps[pi], BMT], writes=[BMT])
            stats(MT[:], BMT, 16, 2048.0, 0)
            for k in range(16):
                t, Bt = tmp[k % 2], Btmp[k % 2]
                E("vector", lambda e: e.scalar_tensor_tensor(out=t[:], in0=MT[:, k, :], scalar=vcol(V_POSTMLP + k), in1=rstd[:],
                                                             op0=ALU.mult, op1=ALU.mult), reads=[BMT, Brstd, Bconst], writes=[Bt])
                E("gpsimd", lambda e: e.tensor_tensor(out=XT[:, k, :], in0=XT[:, k, :], in1=t[:], op=ALU.add), reads=[Bt, BXT], writes=[BXT])
            dst = d["yT_out"].rearrange("(k p) t -> p k t", p=128)
            E("sync", lambda e: e.dma_start(out=dst[:, 0:8, tok], in_=XT[:, 0:8, :]), reads=[BXT], dma=True)
            E("sync", lambda e: e.dma_start(out=dst[:, 8:16, tok], in_=XT[:, 8:16, :]), reads=[BXT], dma=True)


_STAGES = os.environ.get("MK_STAGES", "PWABSCDE")


def run_cores(inputs, stages=_STAGES, debug=()):
    sh = _prep_shared(inputs)
    in_maps = [_prep_core(inputs, c, sh) for c in range(8)]
    nc, kb = build_program(stages, debug)
    res = run_bass_kernel_spmd(nc, in_maps, core_ids=list(range(8)))
    return res, kb


def kernel(**inputs):
    res, _ = run_cores(inputs)
    yp = np.stack([np.ascontiguousarray(res.results[c]["yT_out"].T) for c in range(4)], axis=0)
    ys = np.concatenate([res.results[4 + j]["yT_out"].T for j in range(4)], axis=0)[None]
    return (np.ascontiguousarray(yp.astype(np.float32)), np.ascontiguousarray(ys.astype(np.float32)))
```

```python
import os
import numpy as np
from contextlib import ExitStack
import concourse.bass as bass
import concourse.mybir as mybir
from concourse.bass_utils import run_bass_kernel_spmd

F32 = mybir.dt.float32
BF16 = mybir.dt.bfloat16
AF = mybir.ActivationFunctionType
ALU = mybir.AluOpType

NT = int(os.environ.get("MK_NT", "4096"))
NCTX = 4 * NT
BLK = 512
NB_OWN = NT // BLK
NB_CTX = NCTX // BLK
NKT = NCTX // 128
NROW = NCTX // 64
EPS = 1e-6
MASK_NEG = -30000.0

V_PREMIX, V_POSTMIX, V_PREMLP, V_POSTMLP, V_SSMOUT, V_ATTOUT, V_QN, V_KN, V_D = 0, 16, 32, 48, 64, 72, 80, 81, 82
NVEC = 90


class Buf:
    __slots__ = ("writers", "readers")

    def __init__(self):
        self.writers = {}
        self.readers = {}


class KB:
    ENGS = ("sync", "tensor", "vector", "scalar", "gpsimd")

    NDMA = 20

    def __init__(self, nc, es):
        self.nc = nc
        self.sems = {}
        self.counts = {}
        self.waited = {e: {} for e in self.ENGS}
        self.nops = {e: 0 for e in self.ENGS}
        self.rr = {e: 0 for e in self.ENGS}
        self.pe_prev_serial = False
        for e in self.ENGS:
            if e != "sync":
                self.sems[e] = es.enter_context(nc.semaphore("s_" + e))
                self.counts[e] = 0
        for e in ("sync", "gpsimd"):
            for i in range(self.NDMA):
                k = "%s_dma%d" % (e, i)
                self.sems[k] = es.enter_context(nc.semaphore("s_" + k))
                self.counts[k] = 0

    def emit(self, eng, fn, reads=(), writes=(), dma=False, signal=True, serial=False):
        need = {}
        for b in reads:
            for s, v in b.writers.items():
                if need.get(s, 0) < v:
                    need[s] = v
        for b in writes:
            for s, v in b.writers.items():
                if need.get(s, 0) < v:
                    need[s] = v
            for s, v in b.readers.items():
                if need.get(s, 0) < v:
                    need[s] = v
        e = getattr(self.nc, eng)
        w = self.waited[eng]
        if dma:
            key = "%s_dma%d" % (eng, self.rr[eng] % self.NDMA)
            self.rr[eng] += 1
            if self.counts[key] > need.get(key, 0):
                need[key] = self.counts[key]
        else:
            key = eng
        if eng == "tensor":
            if serial or self.pe_prev_serial:
                need["tensor"] = self.counts["tensor"]
            else:
                need.pop("tensor", None)
            self.pe_prev_serial = serial
        for s, v in need.items():
            if w.get(s, 0) < v:
                w[s] = v
                e.wait_ge(self.sems[s], v)
        inc = 16 if dma else 1
        if signal:
            self.counts[key] += inc
            val = self.counts[key]
            ins = fn(e)
            ins.then_inc(self.sems[key], inc)
        else:
            assert eng == "tensor" and not dma
            val = self.counts[key] + 1
            fn(e)
        self.nops[eng] += 1
        for b in writes:
            b.writers[key] = val
        for b in reads:
            b.readers[key] = val
        return (key, val)

    def barrier(self):
        for eng in self.ENGS:
            e = getattr(self.nc, eng)
            w = self.waited[eng]
            for k, v in self.counts.items():
                if v > 0 and w.get(k, 0) < v:
                    w[k] = v
                    e.wait_ge(self.sems[k], v)

    def finish(self):
        for eng in ("sync", "gpsimd"):
            for i in range(self.NDMA):
                k = "%s_dma%d" % (eng, i)
                if self.counts[k] > 0:
                    getattr(self.nc, eng).wait_ge(self.sems[k], self.counts[k])


def _rope_compact(pos_of_ctx_row):
    inv = (np.float32(10000.0) ** (-(np.arange(0, 64, 2, dtype=np.float32)) / np.float32(64))).astype(np.float32)
    f = np.arange(64) % 32
    cos = np.zeros((128, NROW), np.float32)
    sin = np.zeros((128, NROW), np.float32)
    rows = pos_of_ctx_row.astype(np.float32)
    ang = (rows[None, :] * inv[f][:, None]).astype(np.float32)
    cos[:64], sin[:64] = np.cos(ang), np.sin(ang)
    cols = np.arange(64, dtype=np.float32)
    angc = (cols[None, :] * inv[f][:, None]).astype(np.float32)
    cos[64:, :64], sin[64:, :64] = np.cos(angc), np.sin(angc)
    return cos, sin


def _consts():
    c = {}
    rp = np.zeros((128, 128), np.float32)
    for m in range(128):
        j = m % 64
        if j < 32:
            rp[m + 32, m] = -1.0
        else:
            rp[m - 32, m] = 1.0
    c["rperm"] = rp
    c["ident"] = np.eye(128, dtype=np.float32)
    sel = np.zeros((128, 64, 128), np.float32)
    selT = np.zeros((128, 64, 128), np.float32)
    for gl in range(8):
        for tau in range(8):
            for h in range(16):
                sel[gl * 16 + h, gl * 8 + tau, tau * 16 + h] = 1.0
                selT[tau * 16 + h, gl * 8 + tau, gl * 16 + h] = 1.0
    c["sel"] = sel
    c["selT"] = selT
    tp = np.arange(128) // 16
    c["maskL"] = (tp[:, None] <= tp[None, :]).astype(np.float32)
    c["maskU"] = (tp[:, None] >= tp[None, :]).astype(np.float32)
    return c


def _prep_shared(inp):
    f = lambda a: np.ascontiguousarray(np.asarray(a, dtype=np.float32))
    sh = {}
    w_in = f(inp["w_in"])[0]
    sh["w_in_t"] = f(w_in.reshape(16, 128, 2560).transpose(1, 0, 2))
    sh["w_glu_t"] = f(f(inp["w_glu"])[0].reshape(8, 128, 1024).transpose(1, 0, 2))
    w_out = f(inp["w_out"])[0]
    sh["w_out_t"] = f(w_out.reshape(16, 128, 16, 128).transpose(2, 1, 0, 3))
    w_up = f(inp["w_up"])[0]
    sh["w_up_t"] = f(w_up.reshape(16, 128, 64, 128).transpose(2, 1, 0, 3))
    w_dn = f(inp["w_down"])[0]
    sh["w_dn_t"] = f(w_dn.reshape(2, 32, 128, 16, 128).transpose(3, 0, 2, 1, 4))
    vec = np.zeros((128, NVEC), np.float32)
    pk = lambda v, n: f(v).reshape(n, 128).T
    vec[:, V_PREMIX:V_PREMIX + 16] = pk(inp["pre_mix_norm"], 16)
    vec[:, V_POSTMIX:V_POSTMIX + 16] = pk(inp["post_mix_norm"], 16)
    vec[:, V_PREMLP:V_PREMLP + 16] = pk(inp["pre_mlp_norm"], 16)
    vec[:, V_POSTMLP:V_POSTMLP + 16] = pk(inp["post_mlp_norm"], 16)
    vec[:, V_SSMOUT:V_SSMOUT + 8] = pk(inp["ssm_out_norm"], 8)
    vec[:, V_ATTOUT:V_ATTOUT + 8] = pk(inp["attn_out_norm"], 8)
    vec[:, V_QN] = f(inp["q_norm"])[0]
    vec[:, V_KN] = f(inp["k_norm"])[0]
    vec[:, V_D:V_D + 8] = pk(inp["ssm_d"], 8)
    sh["vecs"] = vec
    a_re = f(inp["ssm_a_re"])[0]
    a_im = f(inp["ssm_a_im"])[0]
    ldt = f(inp["ssm_log_dt"])[0]
    sh["ssm_ar"] = f(a_re.transpose(0, 2, 1).reshape(128, 64))
    sh["ssm_ai"] = f(a_im.transpose(0, 2, 1).reshape(128, 64))
    sh["ssm_ldt"] = f(np.broadcast_to(ldt[:, None, :], (2, 64, 64)).reshape(128, 64))
    sh["ssm_bre"] = f(f(inp["ssm_b_re"])[0].transpose(0, 2, 1, 3).reshape(128, 64, 16))
    sh["ssm_bim"] = f(f(inp["ssm_b_im"])[0].transpose(0, 2, 1, 3).reshape(128, 64, 16))
    sh["ssm_cre"] = f(f(inp["ssm_c_re"])[0].transpose(0, 3, 1, 2).reshape(128, 64, 16))
    sh["ssm_cim"] = f(f(inp["ssm_c_im"])[0].transpose(0, 3, 1, 2).reshape(128, 64, 16))
    d = f(inp["ssm_d"])[0].reshape(64, 16)
    sh["ssm_dug"] = f(np.tile(d.T, (8, 1)))
    sh.update(_consts())
    return sh


def _prep_core(inp, core, sh):
    m = dict(sh)
    ctx = np.zeros((2048, NCTX), np.float32)
    mask = np.zeros((NCTX,), np.float32)
    gates = np.zeros((128, 3), np.float32)
    if core < 4:
        x = np.asarray(inp["x_prompt"], np.float32)[core]
        ctx[:, :NT] = x.T
        mask[NT:] = MASK_NEG
        slots = [0, 0, 0, 0]
    else:
        slot = core - 4
        xs = np.asarray(inp["x_sample"], np.float32)[0]
        others = [j for j in range(4) if j != slot]
        slots = [slot] + others
        for i, j in enumerate(slots):
            ctx[:, i * NT:(i + 1) * NT] = xs[j * NT:(j + 1) * NT].T
        for s_ in range(3):
            gates[:64, s_] = 1.0 if s_ < slot else 0.0
            gates[64:, s_] = 1.0 if (2 - s_) >= slot else 0.0
    m["xT_ctx"] = ctx
    m["maskb"] = np.ascontiguousarray(mask.reshape(NKT, 128).T)
    rows = np.concatenate([np.arange(j * NT // 64, (j + 1) * NT // 64) for j in slots])
    m["rope_cos"], m["rope_sin"] = _rope_compact(rows)
    m["gates"] = gates
    return m


def build_program(stages="WAB", debug=()):
    nc = bass.Bass("TRN2", target_bir_lowering=False)
    dbg = set(debug)

    def din(name, shape):
        return nc.dram_tensor(name, list(shape), F32, kind="ExternalInput").ap()

    def dscr(name, shape, dt=BF16):
        kind = "ExternalOutput" if name in dbg else "Internal"
        return nc.dram_tensor(name, list(shape), dt, kind=kind).ap()

    xT_ctx = din("xT_ctx", [2048, NCTX])
    rope_cos_d, rope_sin_d = din("rope_cos", [128, NROW]), din("rope_sin", [128, NROW])
    maskb_d = din("maskb", [128, NKT])
    gates_d = din("gates", [128, 3])
    vecs_d = din("vecs", [128, NVEC])
    w_in_d = din("w_in_t", [128, 16, 2560])
    w_glu_d = din("w_glu_t", [128, 8, 1024])
    has_mlp = ("P" in stages) or ("E" in stages)
    w_out_d = din("w_out_t", [16, 128, 16, 128]) if has_mlp else None
    w_up_d = din("w_up_t", [64, 128, 16, 128]) if has_mlp else None
    w_dn_d = din("w_dn_t", [16, 2, 128, 32, 128]) if has_mlp else None
    ssm_in = {k: din(k, s) for k, s in (("ssm_ar", [128, 64]), ("ssm_ai", [128, 64]), ("ssm_ldt", [128, 64]),
                                        ("ssm_bre", [128, 64, 16]), ("ssm_bim", [128, 64, 16]),
                                        ("ssm_cre", [128, 64, 16]), ("ssm_cim", [128, 64, 16]),
                                        ("ssm_dug", [128, 64]))}
    rperm_d, ident_d = din("rperm", [128, 128]), din("ident", [128, 128])
    sel_d, selT_d = din("sel", [128, 64, 128]), din("selT", [128, 64, 128])
    maskL_d, maskU_d = din("maskL", [128, 128]), din("maskU", [128, 128])
    yT_out = nc.dram_tensor("yT_out", [2048, NT], F32, kind="ExternalOutput").ap()

    KT_s = dscr("KT_s", [2, 128, NCTX])
    V_s = dscr("V_s", [2, 128, NKT, 128])
    QT_s = dscr("QT_s", [8, 128, NT])
    WS_s = dscr("WS_s", [128, 64, 5, 128])
    UG_s = dscr("UG_s", [NB_OWN, 128, 64, 64])
    SF_s = dscr("SF_s", [NB_OWN, 64, 64, 2, 64])
    ZB_s = dscr("ZB_s", [NB_OWN, 64, 64, 2, 64])
    NS_s = dscr("NS_s", [128, 8, NT])
    YA_s = dscr("YA_s", [8, 128, NT], F32)
    WOUT_b = dscr("WOUT_b", [16, 128, 16, 128])
    WUP_b = dscr("WUP_b", [64, 128, 16, 128])
    WDN_b = dscr("WDN_b", [16, 2, 128, 32, 128])
    EST_s = dscr("EST_s", [128, 2, 64], F32)

    es = ExitStack()
    kb = KB(nc, es)
    E = kb.emit
    B = {}

    def buf(name):
        if name not in B:
            B[name] = Buf()
        return B[name]

    uid = [0]

    def sbt(st, name, shape, dt):
        uid[0] += 1
        return st.enter_context(nc.sbuf_tensor("sb%d_%s" % (uid[0], name), list(shape), dt))

    psbig = [es.enter_context(nc.psum_tensor("psb%d" % i, [128, 1024], F32)) for i in range(4)]
    ps = [psbig[i // 2][:, (i % 2) * 512:(i % 2 + 1) * 512] for i in range(8)]
    Bps = [Buf() for _ in range(8)]
    ones_b = sbt(es, "ones_b", [128, 128], BF16)
    ones_f = sbt(es, "ones_f", [128, 128], F32)
    ident_f = sbt(es, "ident_f", [128, 128], F32)
    rperm_b = sbt(es, "rperm_b", [128, 128], BF16)
    vecs = sbt(es, "vecs", [128, NVEC], F32)
    gates = sbt(es, "gates", [128, 3], F32)
    rope_c = sbt(es, "rope_c", [128, NROW], F32)
    rope_s = sbt(es, "rope_s", [128, NROW], F32)
    A8a = sbt(es, "A8a", [128, 2, 64], F32)
    A8b = sbt(es, "A8b", [128, 2, 64], F32)
    A8c = sbt(es, "A8c", [128, 2, 64], F32)
    Sin_t = sbt(es, "Sin_t", [128, 2, 64], F32)
    Bconst = Buf()
    BA8 = Buf()
    BSin = Buf()

    E("vector", lambda e: e.memset(ones_b[:], 1.0), writes=[Bconst])
    E("vector", lambda e: e.memset(ones_f[:], 1.0), writes=[Bconst])
    E("sync", lambda e: e.dma_start(out=ident_f[:], in_=ident_d), writes=[Bconst], dma=True)
    E("gpsimd", lambda e: e.dma_start(out=rperm_b[:], in_=rperm_d), writes=[Bconst], dma=True)
    E("sync", lambda e: e.dma_start(out=vecs[:], in_=vecs_d), writes=[Bconst], dma=True)
    E("sync", lambda e: e.dma_start(out=gates[:], in_=gates_d), writes=[Bconst], dma=True)
    E("sync", lambda e: e.dma_start(out=rope_c[:], in_=rope_cos_d), writes=[Bconst], dma=True)
    E("sync", lambda e: e.dma_start(out=rope_s[:], in_=rope_sin_d), writes=[Bconst], dma=True)

    def vcol(c0, n=1):
        return vecs[:, c0:c0 + n]

    def cmul(eng, out, a, b, tmp1, tmp2, bufs_r, bufs_w, rows=slice(0, 128)):
        r = rows
        E(eng, lambda e: e.tensor_tensor(out=tmp1[r, 0, :], in0=a[r, 0, :], in1=b[r, 0, :], op=ALU.mult), reads=bufs_r, writes=bufs_w)
        E(eng, lambda e: e.tensor_tensor(out=tmp1[r, 1, :], in0=a[r, 1, :], in1=b[r, 1, :], op=ALU.mult), reads=bufs_r, writes=bufs_w)
        E(eng, lambda e: e.tensor_tensor(out=tmp2[r, 0, :], in0=a[r, 0, :], in1=b[r, 1, :], op=ALU.mult), reads=bufs_r, writes=bufs_w)
        E(eng, lambda e: e.tensor_tensor(out=tmp2[r, 1, :], in0=a[r, 1, :], in1=b[r, 0, :], op=ALU.mult), reads=bufs_r, writes=bufs_w)
        E(eng, lambda e: e.tensor_tensor(out=out[r, 0, :], in0=tmp1[r, 0, :], in1=tmp1[r, 1, :], op=ALU.subtract), reads=bufs_r, writes=bufs_w)
        E(eng, lambda e: e.tensor_tensor(out=out[r, 1, :], in0=tmp2[r, 0, :], in1=tmp2[r, 1, :], op=ALU.add), reads=bufs_r, writes=bufs_w)

    ctx = dict(nc=nc, kb=kb, E=E, es=es, ps=ps, psbig=psbig, Bps=Bps, buf=buf, sbt=sbt, vcol=vcol, cmul=cmul,
               ones_b=ones_b, ones_f=ones_f, ident_f=ident_f, rperm_b=rperm_b, vecs=vecs, gates=gates, rope_c=rope_c, rope_s=rope_s,
               A8a=A8a, A8b=A8b, A8c=A8c, Sin_t=Sin_t, Bconst=Bconst, BA8=BA8, BSin=BSin)
    dr = dict(xT_ctx=xT_ctx, maskb=maskb_d, w_in=w_in_d, w_glu=w_glu_d, w_out=w_out_d, w_up=w_up_d, w_dn=w_dn_d, ssm=ssm_in,
              sel=sel_d, selT=selT_d, maskL=maskL_d, maskU=maskU_d, yT_out=yT_out,
              KT_s=KT_s, V_s=V_s, QT_s=QT_s, WS_s=WS_s, UG_s=UG_s, SF_s=SF_s, ZB_s=ZB_s, NS_s=NS_s, YA_s=YA_s,
              WOUT_b=WOUT_b, WUP_b=WUP_b, WDN_b=WDN_b, EST_s=EST_s)

    if "P" in stages:
        phase_weight_cast(ctx, dr)
    if "W" in stages:
        phase_ssm_weights(ctx, dr)
        kb.barrier()
    if "A" in stages:
        phase_proj(ctx, dr, own=False)
        kb.barrier()
    if "B" in stages:
        phase_proj(ctx, dr, own=True)
        kb.barrier()
    if "S" in stages:
        phase_ssm_scan(ctx, dr, own=False)
        kb.barrier()
        phase_ssm_carry(ctx, dr)
        kb.barrier()
        phase_ssm_scan(ctx, dr, own=True)
        kb.barrier()
    if "C" in stages:
        phase_ssm_out(ctx, dr)
        kb.barrier()
    if "D" in stages:
        phase_attention(ctx, dr)
        kb.barrier()
    if "E" in stages:
        phase_mlp(ctx, dr)
    kb.finish()
    es.close()
    return nc, kb


def load_norm_block(c, st, xT_src, blk, xt, sq, ht, rstd, Bxt, Bsq, Bht, Brstd, gcol, ps_i=0):
    E, ps, Bps = c["E"], c["ps"], c["Bps"]
    ones_b, Bconst, vcol = c["ones_b"], c["Bconst"], c["vcol"]
    sl = slice(blk * BLK, (blk + 1) * BLK)
    src = xT_src.rearrange("(k p) t -> p k t", p=128)
    Bsq = Bsq if isinstance(Bsq, list) else [Bsq]
    E("sync", lambda e: e.dma_start(out=xt[:, 0:8, :], in_=src[:, 0:8, sl]), writes=[Bxt], dma=True)
    E("sync", lambda e: e.dma_start(out=xt[:, 8:16, :], in_=src[:, 8:16, sl]), writes=[Bxt], dma=True)
    E("scalar", lambda e: e.activation(out=sq[:], in_=xt[:], func=AF.Square), reads=[Bxt], writes=Bsq)
    for k in range(16):
        E("tensor", lambda e, k=k: e.matmul(ps[ps_i][:], lhsT=ones_b[:], rhs=sq[:, k, :], start=(k == 0), stop=(k == 15)),
          reads=[Bconst] + Bsq, writes=[Bps[ps_i]], signal=(k == 15))
    E("scalar", lambda e: e.activation(out=rstd[:], in_=ps[ps_i][:], func=AF.Sqrt, scale=1.0 / 2048, bias=EPS),
      reads=[Bps[ps_i]], writes=[Brstd])
    E("vector", lambda e: e.reciprocal(out=rstd[:], in_=rstd[:]), reads=[Brstd], writes=[Brstd])
    for k in range(16):
        E("vector", lambda e, k=k: e.scalar_tensor_tensor(out=ht[:, k, :], in0=xt[:, k, :], scalar=vcol(gcol + k), in1=rstd[:],
                                                         op0=ALU.mult, op1=ALU.mult),
          reads=[Bxt, Brstd, Bconst], writes=[Bht])


def phase_proj(c, d, own):
    nc, E, ps, Bps, sbt, buf, vcol = c["nc"], c["E"], c["ps"], c["Bps"], c["sbt"], c["buf"], c["vcol"]
    ones_b, rperm_b, Bconst = c["ones_b"], c["rperm_b"], c["Bconst"]
    H = 8 if own else 2
    ncols = 1024 if own else 512
    col0 = 1024 if own else 2048
    nblk = NB_OWN if own else NB_CTX
    xT_src = d["xT_ctx"]
    rope_c, rope_s = c["rope_c"], c["rope_s"]
    gn = V_QN if own else V_KN
    out_s = d["QT_s"] if own else d["KT_s"]
    Bout = buf("QT_s" if own else "KT_s")
    BV = buf("V_s")
    with ExitStack() as st:
        W = sbt(st, "W_p", [128, 16, ncols], BF16)
        BW = Buf()
        for k in range(0, 16, 4):
            E("gpsimd", lambda e, k=k: e.dma_start(out=W[:, k:k + 4, :], in_=d["w_in"][:, k:k + 4, col0:col0 + ncols]), writes=[BW], dma=True)
        xt = [sbt(st, "xt%d" % i, [128, 16, BLK], F32) for i in range(2)]
        cs = [sbt(st, "cs%d" % i, [128, BLK], F32) for i in range(2)]
        sn = [sbt(st, "sn%d" % i, [128, BLK], F32) for i in range(2)]
        Bxt, Bcs = [Buf(), Buf()], [Buf(), Buf()]
        for i in range(2):
            E("gpsimd", lambda e: e.tensor_copy(out=cs[i][64:128, :].rearrange("p (r c) -> p r c", c=64),
                                                in_=rope_c[64:128, None, 0:64].to_broadcast([64, 8, 64])), reads=[Bconst], writes=[Bcs[i]])
            E("gpsimd", lambda e: e.tensor_copy(out=sn[i][64:128, :].rearrange("p (r c) -> p r c", c=64),
                                                in_=rope_s[64:128, None, 0:64].to_broadcast([64, 8, 64])), reads=[Bconst], writes=[Bcs[i]])
        sq2 = [sbt(st, "sq%d" % i, [128, 16, BLK], BF16) for i in range(2)]
        ht2 = [sbt(st, "ht%d" % i, [128, 16, BLK], BF16) for i in range(2)]
        rstd2 = [sbt(st, "rstd%d" % i, [128, BLK], F32) for i in range(2)]
        Bsq2, Bht2, Brstd2 = [Buf(), Buf()], [Buf(), Buf()], [Buf(), Buf()]
        sqh = [sbt(st, "sqh%d" % i, [128, BLK], BF16) for i in range(2)]
        rk = [sbt(st, "rk%d" % i, [128, BLK], F32) for i in range(2)]
        kn = [sbt(st, "kn%d" % i, [128, BLK], BF16) for i in range(2)]
        t1 = [sbt(st, "t1%d" % i, [128, BLK], F32) for i in range(2)]
        t2 = [sbt(st, "t2%d" % i, [128, BLK], F32) for i in range(2)]
        ko = [sbt(st, "ko%d" % i, [128, BLK], BF16) for i in range(2)]
        Bsqh, Brk, Bkn, Bt1, Bt2, Bko = [[Buf(), Buf()] for _ in range(6)]
        vt = [sbt(st, "vt%d" % i, [128, 4, 256], BF16) for i in range(2)]
        Bvt = [Buf(), Buf()]
        for blk in range(nblk):
            b = blk % 2
            sl = slice(blk * BLK, (blk + 1) * BLK)
            E("gpsimd", lambda e: e.tensor_copy(out=cs[b][0:64, :].rearrange("p (r c) -> p r c", c=64),
                                                in_=rope_c[0:64, blk * 8:(blk + 1) * 8, None].to_broadcast([64, 8, 64])), reads=[Bconst], writes=[Bcs[b]])
            E("gpsimd", lambda e: e.tensor_copy(out=sn[b][0:64, :].rearrange("p (r c) -> p r c", c=64),
                                                in_=rope_s[0:64, blk * 8:(blk + 1) * 8, None].to_broadcast([64, 8, 64])), reads=[Bconst], writes=[Bcs[b]])
            sq, ht, rstd, Bsq, Bht, Brstd = sq2[b], ht2[b], rstd2[b], Bsq2[b], Bht2[b], Brstd2[b]
            load_norm_block(c, st, xT_src, blk, xt[b], sq, ht, rstd, Bxt[b], Bsq, Bht, Brstd, V_PREMIX, ps_i=0)
            for hd in range(H):
                i = hd % 2
                pk, pss, pr = 1 + i, 3 + i, 5 + i
                for k in range(16):
                    E("tensor", lambda e, k=k: e.matmul(ps[pk][:], lhsT=W[:, k, hd * 128:(hd + 1) * 128], rhs=ht[:, k, :],
                                                        start=(k == 0), stop=(k == 15)), reads=[BW, Bht], writes=[Bps[pk]], signal=(k == 15))
                E("scalar", lambda e: e.activation(out=sqh[i][:], in_=ps[pk][:], func=AF.Square), reads=[Bps[pk]], writes=[Bsqh[i]])
                E("tensor", lambda e: e.matmul(ps[pss][:], lhsT=ones_b[:], rhs=sqh[i][:], start=True, stop=True),
                  reads=[Bconst, Bsqh[i]], writes=[Bps[pss]])
                E("scalar", lambda e: e.activation(out=rk[i][:], in_=ps[pss][:], func=AF.Sqrt, scale=1.0 / 128, bias=EPS),
                  reads=[Bps[pss]], writes=[Brk[i]])
                E("vector", lambda e: e.reciprocal(out=rk[i][:], in_=rk[i][:]), reads=[Brk[i]], writes=[Brk[i]])
                E("vector", lambda e: e.scalar_tensor_tensor(out=kn[i][:], in0=ps[pk][:], scalar=vcol(gn), in1=rk[i][:],
                                                             op0=ALU.mult, op1=ALU.mult),
                  reads=[Bps[pk], Brk[i], Bconst], writes=[Bkn[i]])
                E("tensor", lambda e: e.matmul(ps[pr][:], lhsT=rperm_b[:], rhs=kn[i][:], start=True, stop=True),
                  reads=[Bconst, Bkn[i]], writes=[Bps[pr]])
                E("gpsimd", lambda e: e.tensor_tensor(out=t1[i][:], in0=kn[i][:], in1=cs[b][:], op=ALU.mult),
                  reads=[Bkn[i], Bcs[b]], writes=[Bt1[i]])
                E("vector", lambda e: e.tensor_tensor(out=t2[i][:], in0=ps[pr][:], in1=sn[b][:], op=ALU.mult),
                  reads=[Bps[pr], Bcs[b]], writes=[Bt2[i]])
                E("gpsimd", lambda e: e.tensor_tensor(out=ko[i][:], in0=t1[i][:], in1=t2[i][:], op=ALU.add),
                  reads=[Bt1[i], Bt2[i]], writes=[Bko[i]])
                E("sync", lambda e: e.dma_start(out=out_s[hd, :, sl], in_=ko[i][:]), reads=[Bko[i]], writes=[Bout], dma=True)
            if not own:
                for sub in range(4):
                    pv = 7
                    for k in range(16):
                        E("tensor", lambda e, k=k: e.matmul(ps[pv][:, 0:256], lhsT=ht[:, k, sub * 128:(sub + 1) * 128], rhs=W[:, k, 256:512],
                                                            start=(k == 0), stop=(k == 15)), reads=[BW, Bht], writes=[Bps[pv]], signal=(k == 15))
                    E("scalar", lambda e: e.copy(out=vt[b][:, sub, :], in_=ps[pv][:, 0:256]), reads=[Bps[pv]], writes=[Bvt[b]])
                for kvh in range(2):
                    E("sync", lambda e: e.dma_start(out=d["V_s"][kvh, :, blk * 4:(blk + 1) * 4, :], in_=vt[b][:, :, kvh * 128:(kvh + 1) * 128]),
                      reads=[Bvt[b]], writes=[BV], dma=True)


def phase_ssm_weights(c, d):
    nc, E, ps, Bps, sbt, buf, vcol = c["nc"], c["E"], c["ps"], c["Bps"], c["sbt"], c["buf"], c["vcol"]
    ident_f, Bconst = c["ident_f"], c["Bconst"]
    A8a, A8b, A8c, BA8 = c["A8a"], c["A8b"], c["A8c"], c["BA8"]
    S = d["ssm"]
    H0, H1 = slice(0, 64), slice(64, 128)
    with ExitStack() as st:
        T = lambda n, shape, dt=F32: sbt(st, n, shape, dt)
        ar, ai, ldt = T("ar", [128, 64]), T("ai", [128, 64]), T("ldt", [128, 64])
        bre, bim, cre, cim = T("bre", [128, 64, 16]), T("bim", [128, 64, 16]), T("cre", [128, 64, 16]), T("cim", [128, 64, 16])
        dug, mL, mU = T("dug", [128, 64]), T("mL", [128, 128]), T("mU", [128, 128])
        Bin = Buf()
        for t, k in ((ar, "ssm_ar"), (ai, "ssm_ai"), (ldt, "ssm_ldt"), (bre, "ssm_bre"), (bim, "ssm_bim"),
                     (cre, "ssm_cre"), (cim, "ssm_cim"), (dug, "ssm_dug")):
            E("sync", lambda e, t=t, k=k: e.dma_start(out=t[:], in_=S[k]), writes=[Bin], dma=True)
        E("sync", lambda e: e.dma_start(out=mL[:], in_=d["maskL"]), writes=[Bin], dma=True)
        E("sync", lambda e: e.dma_start(out=mU[:], in_=d["maskU"]), writes=[Bin], dma=True)
        Bw = Buf()
        V = lambda fn: E("vector", fn, reads=[Bin, Bw, Bconst], writes=[Bw])
        A = lambda fn: E("scalar", fn, reads=[Bin, Bw], writes=[Bw])
        dt_, lrdt, th, mag = T("dt_", [128, 64]), T("lrdt", [128, 64]), T("th", [128, 64]), T("mag", [128, 64])
        cc, ss, u1, u2 = T("cc", [128, 64]), T("ss", [128, 64]), T("u1", [128, 64]), T("u2", [128, 64])
        halfpi = T("halfpi", [128, 1])
        V(lambda e: e.memset(halfpi[:], float(np.pi / 2)))
        A(lambda e: e.activation(out=dt_[:], in_=ldt[:], func=AF.Exp))
        V(lambda e: e.tensor_tensor(out=lrdt[:], in0=ar[:], in1=dt_[:], op=ALU.mult))
        V(lambda e: e.tensor_tensor(out=th[:], in0=ai[:], in1=dt_[:], op=ALU.mult))
        A(lambda e: e.activation(out=mag[:], in_=lrdt[:], func=AF.Exp))
        A(lambda e: e.activation(out=ss[:], in_=th[:], func=AF.Sin, scale=1.0 / 32))
        A(lambda e: e.activation(out=cc[:], in_=th[:], func=AF.Sin, scale=1.0 / 32, bias=halfpi[:]))
        for _ in range(5):
            V(lambda e: e.tensor_tensor(out=u1[:], in0=cc[:], in1=cc[:], op=ALU.mult))
            V(lambda e: e.tensor_tensor(out=u2[:], in0=ss[:], in1=ss[:], op=ALU.mult))
            V(lambda e: e.scalar_tensor_tensor(out=ss[:], in0=cc[:], scalar=2.0, in1=ss[:], op0=ALU.mult, op1=ALU.mult))
            V(lambda e: e.tensor_tensor(out=cc[:], in0=u1[:], in1=u2[:], op=ALU.subtract))
        Lre, Lim = T("Lre", [128, 64, 9]), T("Lim", [128, 64, 9])
        Ire, Iim, inv = T("Ire", [128, 64, 9]), T("Iim", [128, 64, 9]), T("inv", [128, 64, 9])
        V(lambda e: e.memset(Lre[:, :, 0], 1.0))
        V(lambda e: e.memset(Lim[:, :, 0], 0.0))
        V(lambda e: e.tensor_tensor(out=Lre[:, :, 1], in0=mag[:], in1=cc[:], op=ALU.mult))
        V(lambda e: e.tensor_tensor(out=Lim[:, :, 1], in0=mag[:], in1=ss[:], op=ALU.mult))
        for k in range(2, 9):
            V(lambda e, k=k: e.tensor_tensor(out=u1[:], in0=Lre[:, :, k - 1], in1=Lre[:, :, 1], op=ALU.mult))
            V(lambda e, k=k: e.tensor_tensor(out=u2[:], in0=Lim[:, :, k - 1], in1=Lim[:, :, 1], op=ALU.mult))
            V(lambda e, k=k: e.tensor_tensor(out=Lre[:, :, k], in0=u1[:], in1=u2[:], op=ALU.subtract))
            V(lambda e, k=k: e.tensor_tensor(out=u1[:], in0=Lre[:, :, k - 1], in1=Lim[:, :, 1], op=ALU.mult))
            V(lambda e, k=k: e.tensor_tensor(out=u2[:], in0=Lim[:, :, k - 1], in1=Lre[:, :, 1], op=ALU.mult))
            V(lambda e, k=k: e.tensor_tensor(out=Lim[:, :, k], in0=u1[:], in1=u2[:], op=ALU.add))
        V(lambda e: e.tensor_tensor(out=inv[:], in0=Lre[:], in1=Lre[:], op=ALU.mult))
        V(lambda e: e.tensor_tensor(out=Ire[:], in0=Lim[:], in1=Lim[:], op=ALU.mult))
        V(lambda e: e.tensor_tensor(out=inv[:], in0=inv[:], in1=Ire[:], op=ALU.add))
        V(lambda e: e.reciprocal(out=inv[:], in_=inv[:]))
        V(lambda e: e.tensor_tensor(out=Ire[:], in0=Lre[:], in1=inv[:], op=ALU.mult))
        V(lambda e: e.scalar_tensor_tensor(out=Iim[:], in0=Lim[:], scalar=-1.0, in1=inv[:], op0=ALU.mult, op1=ALU.mult))
        E("vector", lambda e: e.tensor_copy(out=A8c[:, 0, :], in_=Lre[:, :, 8]), reads=[Bw], writes=[BA8])
        E("vector", lambda e: e.tensor_copy(out=A8c[:, 1, :], in_=Lim[:, :, 8]), reads=[Bw], writes=[BA8])
        E("vector", lambda e: e.tensor_copy(out=A8a[:, 0, :], in_=Lre[:, :, 8]), reads=[Bw], writes=[BA8])
        E("vector", lambda e: e.tensor_copy(out=A8a[:, 1, :], in_=Lre[:, :, 8]), reads=[Bw], writes=[BA8])
        E("vector", lambda e: e.tensor_scalar(out=A8b[:, 0, :], in0=Lim[:, :, 8], scalar1=-1.0, scalar2=None, op0=ALU.mult), reads=[Bw], writes=[BA8])
        E("vector", lambda e: e.tensor_copy(out=A8b[:, 1, :], in_=Lim[:, :, 8]), reads=[Bw], writes=[BA8])
        PCre, PCim, PGre, PGim = T("PCre", [128, 64, 8]), T("PCim", [128, 64, 8]), T("PGre", [128, 64, 8]), T("PGim", [128, 64, 8])
        for dst, src in ((PCre, Lre), (PCim, Lim), (PGre, Ire), (PGim, Iim)):
            V(lambda e, dst=dst, src=src: e.tensor_copy(out=dst[H0, :, :], in_=src[H0, :, 1:9]))
            V(lambda e, dst=dst, src=src: e.tensor_copy(out=dst[H1, :, :], in_=src[H1, :, 8:0:-1]))
        den, wre, wim = T("den", [128, 64]), T("wre", [128, 64]), T("wim", [128, 64])
        V(lambda e: e.tensor_tensor(out=den[:], in0=ar[:], in1=ar[:], op=ALU.mult))
        V(lambda e: e.tensor_tensor(out=u1[:], in0=ai[:], in1=ai[:], op=ALU.mult))
        V(lambda e: e.tensor_tensor(out=den[:], in0=den[:], in1=u1[:], op=ALU.add))
        V(lambda e: e.reciprocal(out=den[:], in_=den[:]))
        V(lambda e: e.tensor_scalar(out=u1[:], in0=Lre[:, :, 1], scalar1=-1.0, scalar2=None, op0=ALU.add))
        V(lambda e: e.tensor_tensor(out=wre[:], in0=u1[:], in1=ar[:], op=ALU.mult))
        V(lambda e: e.tensor_tensor(out=u2[:], in0=Lim[:, :, 1], in1=ai[:], op=ALU.mult))
        V(lambda e: e.tensor_tensor(out=wre[:], in0=wre[:], in1=u2[:], op=ALU.add))
        V(lambda e: e.tensor_tensor(out=wre[:], in0=wre[:], in1=den[:], op=ALU.mult))
        V(lambda e: e.tensor_tensor(out=wim[:], in0=Lim[:, :, 1], in1=ar[:], op=ALU.mult))
        V(lambda e: e.tensor_tensor(out=u2[:], in0=u1[:], in1=ai[:], op=ALU.mult))
        V(lambda e: e.tensor_tensor(out=wim[:], in0=wim[:], in1=u2[:], op=ALU.subtract))
        V(lambda e: e.tensor_tensor(out=wim[:], in0=wim[:], in1=den[:], op=ALU.mult))
        bbre, bbim, v1 = T("bbre", [128, 64, 16]), T("bbim", [128, 64, 16]), T("v1", [128, 64, 16])
        wre_b = wre[:, :, None].to_broadcast([128, 64, 16])
        wim_b = wim[:, :, None].to_broadcast([128, 64, 16])
        V(lambda e: e.tensor_tensor(out=bbre[:], in0=bre[:], in1=wre_b, op=ALU.mult))
        V(lambda e: e.tensor_tensor(out=v1[:], in0=bim[:], in1=wim_b, op=ALU.mult))
        V(lambda e: e.tensor_tensor(out=bbre[:], in0=bbre[:], in1=v1[:], op=ALU.subtract))
        V(lambda e: e.tensor_tensor(out=bbim[:], in0=bim[:], in1=wre_b, op=ALU.mult))
        V(lambda e: e.tensor_tensor(out=v1[:], in0=bre[:], in1=wim_b, op=ALU.mult))
        V(lambda e: e.tensor_tensor(out=bbim[:], in0=bbim[:], in1=v1[:], op=ALU.add))
        GB = 16
        Gre, Gim, Gimn = T("Gre", [128, GB, 8, 16]), T("Gim", [128, GB, 8, 16]), T("Gimn", [128, GB, 8, 16])
        CLre, CLim = T("CLre", [128, GB, 8, 16]), T("CLim", [128, GB, 8, 16])
        Wzre, Wzim = T("Wzre", [128, GB, 8, 16]), T("Wzim", [128, GB, 8, 16])
        w1, w2 = T("w1", [128, GB, 8, 16]), T("w2", [128, GB, 8, 16])
        stage = [T("stage%d" % i, [128, GB, 5, 128], BF16) for i in range(2)]
        Bstage = [Buf(), Buf()]
        tA, tB = T("tA", [128, 128]), T("tB", [128, 128])
        BtA = Buf()
        BWS = buf("WS_s")
        for gb in range(64 // GB):
            g0 = gb * GB
            gs = slice(g0, g0 + GB)
            sh4 = [128, GB, 8, 16]
            PGre_b = PGre[:, gs, :, None].to_broadcast(sh4)
            PGim_b = PGim[:, gs, :, None].to_broadcast(sh4)
            PCre_b = PCre[:, gs, :, None].to_broadcast(sh4)
            PCim_b = PCim[:, gs, :, None].to_broadcast(sh4)
            Bre_b = bbre[:, gs, None, :].to_broadcast(sh4)
            Bim_b = bbim[:, gs, None, :].to_broadcast(sh4)
            Cre_b = cre[:, gs, None, :].to_broadcast(sh4)
            Cim_b = cim[:, gs, None, :].to_broadcast(sh4)
            A8re_b = A8c[:, 0, gs, None, None].to_broadcast(sh4)
            A8im_b = A8c[:, 1, gs, None, None].to_broadcast(sh4)
            V2 = lambda fn: E("vector", fn, reads=[Bin, Bw, BA8, Bconst], writes=[Bw])
            V2(lambda e: e.tensor_tensor(out=w1[:], in0=PGre_b, in1=Bre_b, op=ALU.mult))
            V2(lambda e: e.tensor_tensor(out=w2[:], in0=PGim_b, in1=Bim_b, op=ALU.mult))
            V2(lambda e: e.tensor_tensor(out=Gre[:], in0=w1[:], in1=w2[:], op=ALU.subtract))
            V2(lambda e: e.tensor_tensor(out=w1[:], in0=PGre_b, in1=Bim_b, op=ALU.mult))
            V2(lambda e: e.tensor_tensor(out=w2[:], in0=PGim_b, in1=Bre_b, op=ALU.mult))
            V2(lambda e: e.tensor_tensor(out=Gim[:], in0=w1[:], in1=w2[:], op=ALU.add))
            V2(lambda e: e.tensor_scalar(out=Gimn[:], in0=Gim[:], scalar1=-1.0, scalar2=None, op0=ALU.mult))
            V2(lambda e: e.tensor_tensor(out=w1[:], in0=PCre_b, in1=Cre_b, op=ALU.mult))
            V2(lambda e: e.tensor_tensor(out=w2[:], in0=PCim_b, in1=Cim_b, op=ALU.mult))
            V2(lambda e: e.tensor_tensor(out=CLre[:], in0=w1[:], in1=w2[:], op=ALU.subtract))
            V2(lambda e: e.tensor_tensor(out=w1[:], in0=PCre_b, in1=Cim_b, op=ALU.mult))
            V2(lambda e: e.tensor_tensor(out=w2[:], in0=PCim_b, in1=Cre_b, op=ALU.mult))
            V2(lambda e: e.tensor_tensor(out=CLim[:], in0=w1[:], in1=w2[:], op=ALU.add))
            V2(lambda e: e.tensor_tensor(out=w1[:], in0=Gre[:], in1=A8re_b, op=ALU.mult))
            V2(lambda e: e.tensor_tensor(out=w2[:], in0=Gim[:], in1=A8im_b, op=ALU.mult))
            V2(lambda e: e.tensor_tensor(out=Wzre[:], in0=w1[:], in1=w2[:], op=ALU.subtract))
            V2(lambda e: e.tensor_tensor(out=w1[:], in0=Gim[:], in1=A8re_b, op=ALU.mult))
            V2(lambda e: e.tensor_tensor(out=w2[:], in0=Gre[:], in1=A8im_b, op=ALU.mult))
            V2(lambda e: e.tensor_tensor(out=Wzim[:], in0=w1[:], in1=w2[:], op=ALU.add))
            sg = stage[gb % 2]
            Bsg = Bstage[gb % 2]
            f2 = lambda t: t[:].rearrange("q g t h -> q g (t h)")
            E("gpsimd", lambda e: e.tensor_copy(out=sg[:, :, 3, :], in_=f2(CLre)), reads=[Bw], writes=[Bsg])
            E("gpsimd", lambda e: e.tensor_scalar(out=sg[:, :, 4, :], in0=f2(CLim), scalar1=-1.0, scalar2=None, op0=ALU.mult), reads=[Bw], writes=[Bsg])
            for gl in range(GB):
                g = g0 + gl
                f1 = lambda t: t[:, gl, :, :].rearrange("q t h -> q (t h)")
                E("tensor", lambda e: e.transpose(out=ps[0][:, 0:128], in_=f1(Wzre), identity=ident_f[:]), reads=[Bw, Bconst], writes=[Bps[0]], serial=True)
                E("tensor", lambda e: e.transpose(out=ps[1][:, 0:128], in_=f1(Wzim), identity=ident_f[:]), reads=[Bw, Bconst], writes=[Bps[1]], serial=True)
                E("scalar", lambda e: e.copy(out=sg[:, gl, 0, :], in_=ps[0][:, 0:128]), reads=[Bps[0]], writes=[Bsg])
                E("scalar", lambda e: e.copy(out=sg[:, gl, 1, :], in_=ps[1][:, 0:128]), reads=[Bps[1]], writes=[Bsg])
                for half, pi in ((H0, 2), (H1, 3)):
                    E("tensor", lambda e: e.matmul(ps[pi][:, 0:128], lhsT=f1(Gre)[half, :], rhs=f1(CLre)[half, :], start=True, stop=False),
                      reads=[Bw], writes=[Bps[pi]], serial=True)
                    E("tensor", lambda e: e.matmul(ps[pi][:, 0:128], lhsT=f1(Gimn)[half, :], rhs=f1(CLim)[half, :], start=False, stop=True),
                      reads=[Bw], writes=[Bps[pi]], serial=True)
                E("vector", lambda e: e.tensor_tensor(out=tA[:], in0=ps[2][:, 0:128], in1=mL[:], op=ALU.mult), reads=[Bps[2], Bin], writes=[BtA])
                E("vector", lambda e: e.tensor_tensor(out=tB[:], in0=ps[3][:, 0:128], in1=mU[:], op=ALU.mult), reads=[Bps[3], Bin], writes=[BtA])
                E("vector", lambda e: e.tensor_tensor(out=tA[:], in0=tA[:], in1=tB[:], op=ALU.add), reads=[BtA], writes=[BtA])
                E("vector", lambda e: e.scalar_tensor_tensor(out=sg[:, gl, 2, :], in0=ident_f[:], scalar=dug[:, g:g + 1], in1=tA[:],
                                                             op0=ALU.mult, op1=ALU.add), reads=[BtA, Bin, Bconst], writes=[Bsg])
            E("sync", lambda e: e.dma_start(out=d["WS_s"][:, gs, :, :], in_=sg[:]), reads=[Bsg], writes=[BWS], dma=True)


def _complex_sq(c, eng, t, tmp, bufs):
    E = c["E"]
    E(eng, lambda e: e.tensor_tensor(out=tmp[:, 0, :], in0=t[:, 0, :], in1=t[:, 0, :], op=ALU.mult), reads=bufs, writes=bufs)
    E(eng, lambda e: e.tensor_tensor(out=tmp[:, 1, :], in0=t[:, 1, :], in1=t[:, 1, :], op=ALU.mult), reads=bufs, writes=bufs)
    E(eng, lambda e: e.scalar_tensor_tensor(out=t[:, 1, :], in0=t[:, 0, :], scalar=2.0, in1=t[:, 1, :], op0=ALU.mult, op1=ALU.mult), reads=bufs, writes=bufs)
    E(eng, lambda e: e.tensor_tensor(out=t[:, 0, :], in0=tmp[:, 0, :], in1=tmp[:, 1, :], op=ALU.subtract), reads=bufs, writes=bufs)


def phase_ssm_scan(c, d, own):
    nc, E, ps, Bps, sbt, buf, vcol = c["nc"], c["E"], c["ps"], c["Bps"], c["sbt"], c["buf"], c["vcol"]
    Bconst, BA8, BSin = c["Bconst"], c["BA8"], c["BSin"]
    A8a, A8b, A8c, Sin_t = c["A8a"], c["A8b"], c["A8c"], c["Sin_t"]
    es = c["es"]
    H0, H1 = slice(0, 64), slice(64, 128)
    if "Est" not in c:
        c["Est"] = sbt(es, "Est", [128, 3, 2, 64], F32)
        c["BEst"] = Buf()
        c["A64"] = sbt(es, "A64", [128, 2, 64], F32)
        c["Aslot"] = sbt(es, "Aslot", [128, 2, 64], F32)
        c["BApow"] = Buf()
        tmpq = sbt(es, "tmpq", [128, 2, 64], F32)
        Bq = [c["BApow"], BA8]
        E("vector", lambda e: e.tensor_copy(out=c["A64"][:], in_=A8c[:]), reads=[BA8], writes=[c["BApow"]])
        for _ in range(6):
            _complex_sq(c, "vector", c["A64"], tmpq, [c["BApow"]])
        E("vector", lambda e: e.tensor_copy(out=c["Aslot"][:], in_=c["A64"][:]), reads=[c["BApow"]], writes=[c["BApow"]])
        n = NB_OWN
        while n > 1:
            _complex_sq(c, "vector", c["Aslot"], tmpq, [c["BApow"]])
            n //= 2
    Est, BEst, A64, BApow = c["Est"], c["BEst"], c["A64"], c["BApow"]
    with ExitStack() as st:
        Wu = sbt(st, "Wu", [128, 16, 1024], BF16)
        Sel = sbt(st, "Sel", [128, 64, 128], BF16)
        WZ = sbt(st, "WZ", [128, 64, 2, 128], BF16)
        BW = Buf()
        for k in range(0, 16, 4):
            E("gpsimd", lambda e, k=k: e.dma_start(out=Wu[:, k:k + 4, :], in_=d["w_in"][:, k:k + 4, 0:1024]), writes=[BW], dma=True)
        E("gpsimd", lambda e: e.dma_start(out=Sel[:], in_=d["sel"]), writes=[BW], dma=True)
        E("sync", lambda e: e.dma_start(out=WZ[:], in_=d["WS_s"][:, :, 0:2, :]), reads=[buf("WS_s")], writes=[BW], dma=True)
        xt = sbt(st, "xt", [128, 16, BLK], F32)
        scr = sbt(st, "scr", [128, 16, BLK], BF16)
        sq = scr
        ht = sbt(st, "ht", [128, 16, BLK], BF16)
        rstd = sbt(st, "rstd", [128, BLK], F32)
        Bxt, Bht, Brstd = Buf(), Buf(), Buf()
        uT = scr[:, 0:8, :]
        BuT = Buf()
        ug = scr[:, 8:16, :].rearrange("p a (b c) -> p (a b) c", c=64)
        Bug = Buf()
        Bsq = [BuT, Bug]
        Zbs = [sbt(st, "Zb%d" % i, [128, 64, 2, 64], BF16) for i in range(2)]
        BZbs = [Buf(), Buf()]
        nblk_done = [0]
        Sal = sbt(st, "Sal", [128, 64, 2, 64], BF16)
        BSal = Buf()
        S2 = sbt(st, "S2", [128, 2, 64], F32)
        q1 = sbt(st, "q1", [128, 2, 64], F32)
        q2 = sbt(st, "q2", [128, 2, 64], F32)
        BS2 = Buf()
        accb = sbt(st, "accb", [128, 2, 64], F32)
        Pw = sbt(st, "Pw", [128, 2, 64], F32)
        m1 = sbt(st, "m1", [128, 2, 64], F32)
        m2 = sbt(st, "m2", [128, 2, 64], F32)
        m3 = sbt(st, "m3", [128, 2, 64], F32)
        Bacc = Buf()
        slots = [0] if own else [1, 2, 3]
        jobs = [(slot, bi) for slot in slots for bi in range(NB_OWN)]

        def stage_a(n):
            slot, bi = jobs[n]
            load_norm_block(c, st, d["xT_ctx"], slot * NB_OWN + bi, xt, sq, ht, rstd, Bxt, Bsq, Bht, Brstd, V_PREMIX, ps_i=0)

        def stage_b(n):
            slot, bi = jobs[n]
            Zb, BZb = Zbs[n % 2], BZbs[n % 2]
            for j in range(8):
                pj = 1 + j % 2
                for k in range(16):
                    E("tensor", lambda e, k=k: e.matmul(ps[pj][:], lhsT=Wu[:, k, j * 128:(j + 1) * 128], rhs=ht[:, k, :],
                                                        start=(k == 0), stop=(k == 15)), reads=[BW, Bht], writes=[Bps[pj]], signal=(k == 15))
                E("scalar", lambda e: e.copy(out=uT[:, j, :], in_=ps[pj][:]), reads=[Bps[pj]], writes=[BuT])
            for j in range(8):
                pj = 3 + j % 2
                uv = uT[:, j, :].rearrange("p (c t) -> p c t", t=8)
                for gl in range(8):
                    for tau in range(8):
                        E("tensor", lambda e: e.matmul(ps[pj][:, gl * 64:(gl + 1) * 64], lhsT=Sel[:, gl * 8 + tau, :], rhs=uv[:, :, tau],
                                                       start=(tau == 0), stop=(tau == 7)), reads=[BW, BuT], writes=[Bps[pj]], signal=(tau == 7))
                E("scalar", lambda e: e.copy(out=ug[:, j * 8:(j + 1) * 8, :], in_=ps[pj][:].rearrange("p (g c) -> p g c", c=64)),
                  reads=[Bps[pj]], writes=[Bug])
            if own:
                E("sync", lambda e: e.dma_start(out=d["UG_s"][bi], in_=ug), reads=[Bug], writes=[buf("UG_s")], dma=True)
            for j in range(8):
                for ri in range(2):
                    pz = 5 + ((2 * j + ri) % 3)
                    for gl in range(8):
                        g = j * 8 + gl
                        E("tensor", lambda e: e.matmul(ps[pz][:, gl * 64:(gl + 1) * 64], lhsT=WZ[:, g, ri, :], rhs=ug[:, g, :],
                                                       start=True, stop=True), reads=[BW, Bug], writes=[Bps[pz]], signal=(gl == 7))
                    pv = ps[pz][:].rearrange("q (g c) -> q g c", c=64)
                    E("scalar", lambda e: e.copy(out=Zb[H0, :, ri, j * 8:(j + 1) * 8].rearrange("q c g -> q g c"), in_=pv[H0]),
                      reads=[Bps[pz]], writes=[BZb])
                    E("scalar", lambda e: e.copy(out=Zb[H1, :, ri, j * 8:(j + 1) * 8].rearrange("q c g -> q g c"), in_=pv[H1, :, ::-1]),
                      reads=[Bps[pz]], writes=[BZb])

        def stage_scan(n):
            slot, bi = jobs[n]
            Zb, BZb = Zbs[n % 2], BZbs[n % 2]
            if bi == 0:
                E("vector", lambda e: e.memset(S2[:], 0.0), writes=[BS2])
                if own:
                    E("vector", lambda e: e.tensor_copy(out=S2[H0], in_=Sin_t[H0]), reads=[BSin], writes=[BS2])
                else:
                    E("gpsimd", lambda e: e.memset(accb[:], 0.0), writes=[Bacc])
                    E("gpsimd", lambda e: e.memset(Pw[:], 0.0), writes=[Bacc])
                    E("gpsimd", lambda e: e.memset(Pw[:, 0, :], 1.0), writes=[Bacc])
            E("vector", lambda e: e.memset(S2[H1], 0.0), writes=[BS2])
            if own:
                E("vector", lambda e: e.tensor_copy(out=Sal[H0, 0, :, :], in_=S2[H0]), reads=[BS2], writes=[BSal])
            for i in range(64):
                E("vector", lambda e: e.tensor_tensor(out=q1[:], in0=S2[:], in1=A8a[:], op=ALU.mult), reads=[BS2, BA8], writes=[BS2])
                E("vector", lambda e: e.tensor_tensor(out=q2[:], in0=S2[:, ::-1, :], in1=A8b[:], op=ALU.mult), reads=[BS2, BA8], writes=[BS2])
                E("vector", lambda e: e.tensor_tensor(out=q1[:], in0=q1[:], in1=q2[:], op=ALU.add), reads=[BS2], writes=[BS2])
                E("vector", lambda e: e.tensor_tensor(out=S2[:], in0=q1[:], in1=Zb[:, i, :, :], op=ALU.add), reads=[BS2, BZb], writes=[BS2])
                if own and i < 63:
                    E("vector", lambda e: e.tensor_copy(out=Sal[H0, i + 1, :, :], in_=S2[H0]), reads=[BS2], writes=[BSal])
            if own:
                E("sync", lambda e: e.dma_start(out=d["SF_s"][bi], in_=Sal[H0]), reads=[BSal], writes=[buf("SF_s")], dma=True)
                E("sync", lambda e: e.dma_start(out=d["ZB_s"][bi], in_=Zb[H1]), reads=[BZb], writes=[buf("ZB_s")], dma=True)
            else:
                rb = [BS2, Bacc, BApow]
                c["cmul"]("gpsimd", m3, Pw, S2, m1, m2, rb, [Bacc], rows=H1)
                E("gpsimd", lambda e: e.tensor_tensor(out=accb[H1], in0=accb[H1], in1=m3[H1], op=ALU.add), reads=rb, writes=[Bacc])
                c["cmul"]("gpsimd", m3, Pw, A64, m1, m2, rb, [Bacc], rows=H1)
                E("gpsimd", lambda e: e.tensor_copy(out=Pw[H1], in_=m3[H1]), reads=rb, writes=[Bacc])
                if bi == NB_OWN - 1:
                    so = slot - 1
                    E("vector", lambda e: e.tensor_copy(out=Est[H0, so, :, :], in_=S2[H0]), reads=[BS2], writes=[BEst])
                    E("gpsimd", lambda e: e.tensor_copy(out=Est[H1, 2 - so, :, :], in_=accb[H1]), reads=[Bacc], writes=[BEst])

        for n in range(len(jobs)):
            stage_a(n)
            if n > 0:
                stage_scan(n - 1)
            stage_b(n)
        stage_scan(len(jobs) - 1)


def phase_ssm_carry(c, d):
    E, sbt, es = c["E"], c["sbt"], c["es"]
    Est, BEst, Aslot, BApow, Sin_t, BSin, gates, Bconst = c["Est"], c["BEst"], c["Aslot"], c["BApow"], c["Sin_t"], c["BSin"], c["gates"], c["Bconst"]
    with ExitStack() as st:
        m1 = sbt(st, "cm1", [128, 2, 64], F32)
        m2 = sbt(st, "cm2", [128, 2, 64], F32)
        T = sbt(st, "cT", [128, 2, 64], F32)
        Bt = Buf()
        E("vector", lambda e: e.memset(Sin_t[:], 0.0), writes=[BSin])
        rb = [BEst, BApow, BSin, Bt, Bconst]
        for s_ in range(3):
            c["cmul"]("vector", T, Aslot, Sin_t, m1, m2, rb, [Bt])
            E("vector", lambda e: e.tensor_tensor(out=T[:], in0=T[:], in1=Est[:, s_, :, :], op=ALU.add), reads=rb, writes=[Bt])
            E("vector", lambda e: e.tensor_tensor(out=T[:], in0=T[:], in1=Sin_t[:], op=ALU.subtract), reads=rb, writes=[Bt])
            E("vector", lambda e: e.scalar_tensor_tensor(out=Sin_t[:], in0=T[:], scalar=gates[:, s_:s_ + 1], in1=Sin_t[:], op0=ALU.mult, op1=ALU.add),
              reads=rb, writes=[BSin])
        if d["EST_s"] is not None:
            E("sync", lambda e: e.dma_start(out=d["EST_s"], in_=Sin_t[:]), reads=[BSin], dma=True)


def phase_ssm_out(c, d):
    nc, E, ps, Bps, sbt, buf, vcol = c["nc"], c["E"], c["ps"], c["Bps"], c["sbt"], c["buf"], c["vcol"]
    Bconst, BA8, BSin = c["Bconst"], c["BA8"], c["BSin"]
    A8a, A8b, Sin_t, ones_b = c["A8a"], c["A8b"], c["Sin_t"], c["ones_b"]
    H0, H1 = slice(0, 64), slice(64, 128)
    with ExitStack() as st:
        WM = sbt(st, "WM", [128, 64, 3, 128], BF16)
        SelT = sbt(st, "SelT", [128, 64, 128], BF16)
        Wg = sbt(st, "Wg", [128, 8, 1024], BF16)
        BW = Buf()
        E("sync", lambda e: e.dma_start(out=WM[:], in_=d["WS_s"][:, :, 2:5, :]), reads=[buf("WS_s")], writes=[BW], dma=True)
        E("gpsimd", lambda e: e.dma_start(out=SelT[:], in_=d["selT"]), writes=[BW], dma=True)
        E("gpsimd", lambda e: e.dma_start(out=Wg[:], in_=d["w_glu"]), writes=[BW], dma=True)
        ug = [sbt(st, "ugc%d" % i, [128, 64, 64], BF16) for i in range(2)]
        Bug = [Buf(), Buf()]
        SalN = sbt(st, "SalN", [128, 64, 2, 64], BF16)
        SalR = sbt(st, "SalR", [128, 64, 2, 64], BF16)
        ZbR = sbt(st, "ZbR", [128, 64, 2, 64], BF16)
        BSalN, BSalR, BZbR = Buf(), Buf(), Buf()
        S2 = sbt(st, "S2c", [128, 2, 64], F32)
        q1 = sbt(st, "q1c", [128, 2, 64], F32)
        q2 = sbt(st, "q2c", [128, 2, 64], F32)
        BS2 = Buf()
        yg = [sbt(st, "yg%d" % i, [128, 8, 64], BF16) for i in range(2)]
        Byg = [Buf(), Buf()]
        ysf = sbt(st, "ysf", [128, 8, BLK], F32)
        ysb = sbt(st, "ysb", [128, 8, BLK], BF16)
        Bysf, Bysb = Buf(), Buf()
        ta = [sbt(st, "ta%d" % i, [128, BLK], F32) for i in range(2)]
        tb = [sbt(st, "tb%d" % i, [128, BLK], F32) for i in range(2)]
        Bta, Btb = [Buf(), Buf()], [Buf(), Buf()]
        sqn = sbt(st, "sqn", [128, 8, BLK], BF16)
        nsb = sbt(st, "nsb", [128, 8, BLK], BF16)
        rstd = sbt(st, "rstdc", [128, BLK], F32)
        Bsqn, Bnsb, Brstd = Buf(), Buf(), Buf()
        E("vector", lambda e: e.memset(S2[:], 0.0), writes=[BS2])
        E("vector", lambda e: e.tensor_copy(out=S2[H1], in_=Sin_t[H1]), reads=[BSin], writes=[BS2])
        for it, bi in enumerate(range(NB_OWN - 1, -1, -1)):
            b = it % 2
            tok = slice(bi * BLK, (bi + 1) * BLK)
            E("sync", lambda e: e.dma_start(out=ug[b][:], in_=d["UG_s"][bi]), reads=[buf("UG_s")], writes=[Bug[b]], dma=True)
            E("sync", lambda e: e.dma_start(out=SalN[H0], in_=d["SF_s"][bi]), reads=[buf("SF_s")], writes=[BSalN], dma=True)
            E("sync", lambda e: e.dma_start(out=ZbR[H1], in_=d["ZB_s"][bi]), reads=[buf("ZB_s")], writes=[BZbR], dma=True)
            E("vector", lambda e: e.tensor_copy(out=SalR[H1, 0, :, :], in_=S2[H1]), reads=[BS2], writes=[BSalR])
            for i in range(64):
                E("vector", lambda e: e.tensor_tensor(out=q1[H1], in0=S2[H1], in1=A8a[H1], op=ALU.mult), reads=[BS2, BA8], writes=[BS2])
                E("vector", lambda e: e.tensor_tensor(out=q2[H1], in0=S2[H1, ::-1, :], in1=A8b[H1], op=ALU.mult), reads=[BS2, BA8], writes=[BS2])
                E("vector", lambda e: e.tensor_tensor(out=q1[H1], in0=q1[H1], in1=q2[H1], op=ALU.add), reads=[BS2], writes=[BS2])
                E("vector", lambda e: e.tensor_tensor(out=S2[H1], in0=q1[H1], in1=ZbR[H1, i, :, :], op=ALU.add), reads=[BS2, BZbR], writes=[BS2])
                if i < 63:
                    E("vector", lambda e: e.tensor_copy(out=SalR[H1, i + 1, :, :], in_=S2[H1]), reads=[BS2], writes=[BSalR])
            E("gpsimd", lambda e: e.tensor_copy(out=SalN[H1], in_=SalR[H1, ::-1, :, :]), reads=[BSalR], writes=[BSalN])
            for j in range(8):
                py = 1 + j % 2
                for gl in range(8):
                    g = j * 8 + gl
                    o = ps[py][:, gl * 64:(gl + 1) * 64]
                    E("tensor", lambda e: e.matmul(o, lhsT=WM[:, g, 0, :], rhs=ug[b][:, g, :], start=True, stop=False), reads=[BW, Bug[b]], writes=[Bps[py]], signal=False)
                    E("tensor", lambda e: e.matmul(o, lhsT=WM[:, g, 1, :], rhs=SalN[:, :, 0, g], start=False, stop=False), reads=[BW, BSalN], writes=[Bps[py]], signal=False)
                    E("tensor", lambda e: e.matmul(o, lhsT=WM[:, g, 2, :], rhs=SalN[:, :, 1, g], start=False, stop=True), reads=[BW, BSalN], writes=[Bps[py]])
                E("scalar", lambda e: e.copy(out=yg[j % 2][:], in_=ps[py][:].rearrange("p (g c) -> p g c", c=64)), reads=[Bps[py]], writes=[Byg[j % 2]])
                pf = 3 + j % 2
                pfv = ps[pf][:].rearrange("p (c t) -> p c t", t=8)
                for tau in range(8):
                    for gl in range(8):
                        E("tensor", lambda e: e.matmul(pfv[:, :, tau], lhsT=SelT[:, gl * 8 + tau, :], rhs=yg[j % 2][:, gl, :],
                                                       start=(gl == 0), stop=(gl == 7)), reads=[BW, Byg[j % 2]], writes=[Bps[pf]], signal=(gl == 7))
                a_, b_ = ta[j % 2], tb[j % 2]
                Ba, Bb = Bta[j % 2], Btb[j % 2]
                E("scalar", lambda e: e.activation(out=a_[:], in_=ps[pf][:], func=AF.Square), reads=[Bps[pf]], writes=[Ba])
                E("vector", lambda e: e.tensor_scalar(out=a_[:], in0=a_[:], scalar1=0.044715, scalar2=1.0, op0=ALU.mult, op1=ALU.add), reads=[Ba], writes=[Ba])
                E("vector", lambda e: e.tensor_tensor(out=a_[:], in0=a_[:], in1=ps[pf][:], op=ALU.mult), reads=[Ba, Bps[pf]], writes=[Ba])
                E("scalar", lambda e: e.activation(out=b_[:], in_=a_[:], func=AF.Sigmoid, scale=1.5957691216057308), reads=[Ba], writes=[Bb])
                E("vector", lambda e: e.tensor_tensor(out=ysf[:, j, :], in0=b_[:], in1=ps[pf][:], op=ALU.mult), reads=[Bb, Bps[pf]], writes=[Bysf])
                E("gpsimd", lambda e: e.tensor_copy(out=ysb[:, j, :], in_=ysf[:, j, :]), reads=[Bysf], writes=[Bysb])
            for j2 in range(8):
                pg = 5 + j2 % 2
                for j in range(8):
                    E("tensor", lambda e: e.matmul(ps[pg][:], lhsT=Wg[:, j, j2 * 128:(j2 + 1) * 128], rhs=ysb[:, j, :], start=(j == 0), stop=(j == 7)),
                      reads=[BW, Bysb], writes=[Bps[pg]], signal=(j == 7))
                b_, Bb = tb[j2 % 2], Btb[j2 % 2]
                E("scalar", lambda e: e.activation(out=b_[:], in_=ps[pg][:], func=AF.Sigmoid), reads=[Bps[pg]], writes=[Bb])
                E("vector", lambda e: e.tensor_tensor(out=ysf[:, j2, :], in0=ysf[:, j2, :], in1=b_[:], op=ALU.mult), reads=[Bb, Bysf], writes=[Bysf])
            E("scalar", lambda e: e.activation(out=sqn[:], in_=ysf[:], func=AF.Square), reads=[Bysf], writes=[Bsqn])
            for j in range(8):
                E("tensor", lambda e: e.matmul(ps[7][:], lhsT=ones_b[:], rhs=sqn[:, j, :], start=(j == 0), stop=(j == 7)), reads=[Bconst, Bsqn], writes=[Bps[7]], signal=(j == 7))
            E("scalar", lambda e: e.activation(out=rstd[:], in_=ps[7][:], func=AF.Sqrt, scale=1.0 / 1024, bias=EPS), reads=[Bps[7]], writes=[Brstd])
            E("vector", lambda e: e.reciprocal(out=rstd[:], in_=rstd[:]), reads=[Brstd], writes=[Brstd])
            for j in range(8):
                E("vector", lambda e: e.scalar_tensor_tensor(out=nsb[:, j, :], in0=ysf[:, j, :], scalar=vcol(V_SSMOUT + j), in1=rstd[:],
                                                             op0=ALU.mult, op1=ALU.mult), reads=[Bysf, Brstd, Bconst], writes=[Bnsb])
            E("sync", lambda e: e.dma_start(out=d["NS_s"][:, :, tok], in_=nsb[:]), reads=[Bnsb], writes=[buf("NS_s")], dma=True)


def phase_attention(c, d):
    nc, E, ps, Bps, sbt, buf = c["nc"], c["E"], c["ps"], c["Bps"], c["sbt"], c["buf"]
    psbig, ones_b, Bconst = c["psbig"], c["ones_b"], c["Bconst"]
    scale = 1.0 / float(np.sqrt(128.0))
    QW = 2 * BLK
    with ExitStack() as st:
        KT = sbt(st, "KT", [128, NCTX], BF16)
        Vt = sbt(st, "Vt", [128, NKT, 128], BF16)
        mk = sbt(st, "mk", [128, NKT], F32)
        BKV, Bmk = Buf(), Buf()
        E("sync", lambda e: e.dma_start(out=mk[:], in_=d["maskb"]), writes=[Bmk], dma=True)
        qt = [sbt(st, "qt%d" % i, [128, QW], BF16) for i in range(2)]
        Bqt = [Buf(), Buf()]
        NP = 4
        pT = [sbt(st, "pT%d" % i, [128, QW], BF16) for i in range(NP)]
        BpT = [Buf() for _ in range(NP)]
        acc = [sbt(st, "acc%d" % i, [128, QW], F32) for i in range(2)]
        Bacc = [Buf(), Buf()]
        rec = sbt(st, "rec", [128, QW], F32)
        Brec = Buf()
        dhi = sbt(st, "dhi", [128, QW], BF16)
        dlo = sbt(st, "dlo", [128, QW], BF16)
        Bdh = Buf()
        yo = [sbt(st, "yo%d" % i, [128, QW], F32) for i in range(2)]
        Byo = [Buf(), Buf()]
        BS = [[Bps[0], Bps[1]], [Bps[2], Bps[3]]]
        BO = [Bps[4], Bps[5]]
        BD = [Bps[6], Bps[7]]
        it = 0
        for kvh in range(2):
            nchunk = 4
            for q in range(nchunk):
                cs_ = slice(q * NCTX // nchunk, (q + 1) * NCTX // nchunk)
                ks_ = slice(q * NKT // nchunk, (q + 1) * NKT // nchunk)
                E("sync", lambda e: e.dma_start(out=KT[:, cs_], in_=d["KT_s"][kvh, :, cs_]), reads=[buf("KT_s")], writes=[BKV], dma=True)
                E("sync", lambda e: e.dma_start(out=Vt[:, ks_, :], in_=d["V_s"][kvh, :, ks_, :]), reads=[buf("V_s")], writes=[BKV], dma=True)
            for qh in range(4):
                head = kvh * 4 + qh
                for qb in range(NB_OWN // 2):
                    b = it % 2
                    tok = slice(qb * QW, (qb + 1) * QW)
                    E("sync", lambda e: e.dma_start(out=qt[b][:], in_=d["QT_s"][head, :, tok]), reads=[buf("QT_s")], writes=[Bqt[b]], dma=True)

                    def s_mm(kt):
                        sb_ = kt % 2
                        for hf in range(2):
                            E("tensor", lambda e: e.matmul(psbig[sb_][:, hf * BLK:(hf + 1) * BLK], lhsT=KT[:, kt * 128:(kt + 1) * 128],
                                                           rhs=qt[b][:, hf * BLK:(hf + 1) * BLK], start=True, stop=True),
                              reads=[BKV, Bqt[b]], writes=[BS[sb_][hf]], signal=(hf == 1))
                    s_mm(0)
                    nd = [0, 0]
                    for kt in range(NKT):
                        if kt + 1 < NKT:
                            s_mm(kt + 1)
                        sb_ = kt % 2
                        r = kt % NP
                        E("scalar", lambda e: e.activation(out=pT[r][:], in_=psbig[sb_][:], func=AF.Exp, bias=mk[:, kt:kt + 1], scale=scale),
                          reads=BS[sb_] + [Bmk], writes=[BpT[r]])
                        for hf in range(2):
                            E("tensor", lambda e: e.matmul(psbig[2][:, hf * BLK:(hf + 1) * BLK], lhsT=Vt[:, kt, :], rhs=pT[r][:, hf * BLK:(hf + 1) * BLK],
                                                           start=(kt == 0), stop=(kt == NKT - 1)),
                              reads=[BKV, BpT[r]], writes=[BO[hf]], signal=(kt == NKT - 1 and hf == 1))
                        a = kt % 2
                        eng = "vector" if a == 0 else "gpsimd"
                        if nd[a] == 0:
                            E(eng, lambda e: e.tensor_copy(out=acc[a][:], in_=pT[r][:]), reads=[BpT[r]], writes=[Bacc[a]])
                        else:
                            E(eng, lambda e: e.tensor_tensor(out=acc[a][:], in0=acc[a][:], in1=pT[r][:], op=ALU.add), reads=[BpT[r], Bacc[a]], writes=[Bacc[a]])
                        nd[a] += 1
                    E("vector", lambda e: e.tensor_tensor(out=acc[0][:], in0=acc[0][:], in1=acc[1][:], op=ALU.add), reads=[Bacc[0], Bacc[1]], writes=[Bacc[0]])
                    E("vector", lambda e: e.tensor_copy(out=dhi[:], in_=acc[0][:]), reads=[Bacc[0]], writes=[Bdh])
                    E("vector", lambda e: e.tensor_tensor(out=acc[0][:], in0=acc[0][:], in1=dhi[:], op=ALU.subtract), reads=[Bdh, Bacc[0]], writes=[Bacc[0]])
                    E("vector", lambda e: e.tensor_copy(out=dlo[:], in_=acc[0][:]), reads=[Bacc[0]], writes=[Bdh])
                    for hf in range(2):
                        hs = slice(hf * BLK, (hf + 1) * BLK)
                        E("tensor", lambda e: e.matmul(psbig[3][:, hs], lhsT=ones_b[:], rhs=dhi[:, hs], start=True, stop=False), reads=[Bconst, Bdh], writes=[BD[hf]], signal=False)
                        E("tensor", lambda e: e.matmul(psbig[3][:, hs], lhsT=ones_b[:], rhs=dlo[:, hs], start=False, stop=True), reads=[Bconst, Bdh], writes=[BD[hf]])
                    E("vector", lambda e: e.reciprocal(out=rec[:], in_=psbig[3][:]), reads=BD, writes=[Brec])
                    E("vector", lambda e: e.tensor_tensor(out=yo[b][:], in0=psbig[2][:], in1=rec[:], op=ALU.mult), reads=BO + [Brec], writes=[Byo[b]])
                    E("sync", lambda e: e.dma_start(out=d["YA_s"][head, :, tok], in_=yo[b][:]), reads=[Byo[b]], writes=[buf("YA_s")], dma=True)
                    it += 1


def phase_weight_cast(c, d):
    E, buf = c["E"], c["buf"]
    for f in range(16):
        E("gpsimd", lambda e: e.dma_start(out=d["WOUT_b"][f].rearrange("p k c -> p (k c)"), in_=d["w_out"][f].rearrange("p k c -> p (k c)")),
          writes=[buf("WOUT_b")], dma=True)
    for f in range(64):
        E("gpsimd", lambda e: e.dma_start(out=d["WUP_b"][f].rearrange("p k c -> p (k c)"), in_=d["w_up"][f].rearrange("p k c -> p (k c)")),
          writes=[buf("WUP_b")], dma=True)
    for f in range(16):
        for h in range(2):
            E("gpsimd", lambda e: e.dma_start(out=d["WDN_b"][f, h].rearrange("p k c -> p (k c)"), in_=d["w_dn"][f, h].rearrange("p k c -> p (k c)")),
              writes=[buf("WDN_b")], dma=True)


def phase_mlp(c, d):
    nc, E, ps, Bps, sbt, buf, vcol = c["nc"], c["E"], c["ps"], c["Bps"], c["sbt"], c["buf"], c["vcol"]
    ones_b, Bconst = c["ones_b"], c["Bconst"]
    with ExitStack() as st:
        XT = sbt(st, "XT", [128, 16, BLK], F32)
        MT = sbt(st, "MT", [128, 16, BLK], F32)
        ACTb = sbt(st, "ACTb", [128, 16, BLK], BF16)
        AT = sbt(st, "AT", [128, 32, BLK], BF16)
        YA = sbt(st, "YA", [128, 8, BLK], F32)
        BXT, BMT, BACT, BAT, BYA = Buf(), Buf(), Buf(), Buf(), Buf()
        rstd = sbt(st, "rstde", [128, BLK], F32)
        Brstd = Buf()
        tmp = [sbt(st, "tmpe%d" % i, [128, BLK], F32) for i in range(2)]
        Btmp = [Buf(), Buf()]
        wo = [sbt(st, "wo%d" % i, [128, 16, 128], BF16) for i in range(3)]
        Bwo = [Buf() for _ in range(3)]
        wu = [sbt(st, "wu%d" % i, [128, 16, 128], BF16) for i in range(3)]
        Bwu = [Buf() for _ in range(3)]
        wd = [sbt(st, "wd%d" % i, [128, 32, 128], BF16) for i in range(3)]
        Bwd = [Buf() for _ in range(3)]
        src = d["xT_ctx"].rearrange("(k p) t -> p k t", p=128)
        SQ = AT[:, 0:16, :]

        def stats(srct, Bsrc, nk, denom, pi):
            E("scalar", lambda e: e.activation(out=SQ[:, 0:nk, :], in_=srct, func=AF.Square), reads=[Bsrc], writes=[BAT])
            for k in range(nk):
                E("tensor", lambda e, k=k: e.matmul(ps[pi][:], lhsT=ones_b[:], rhs=SQ[:, k, :], start=(k == 0), stop=(k == nk - 1)),
                  reads=[Bconst, BAT], writes=[Bps[pi]], signal=(k == nk - 1))
            E("scalar", lambda e: e.activation(out=rstd[:], in_=ps[pi][:], func=AF.Sqrt, scale=1.0 / denom, bias=EPS), reads=[Bps[pi]], writes=[Brstd])
            E("vector", lambda e: e.reciprocal(out=rstd[:], in_=rstd[:]), reads=[Brstd], writes=[Brstd])

        for blk in range(NB_OWN):
            tok = slice(blk * BLK, (blk + 1) * BLK)
            E("sync", lambda e: e.dma_start(out=XT[:, 0:8, :], in_=src[:, 0:8, tok]), writes=[BXT], dma=True)
            E("sync", lambda e: e.dma_start(out=XT[:, 8:16, :], in_=src[:, 8:16, tok]), writes=[BXT], dma=True)
            E("sync", lambda e: e.dma_start(out=ACTb[:, 0:8, :], in_=d["NS_s"][:, :, tok]), reads=[buf("NS_s")], writes=[BACT], dma=True)
            E("sync", lambda e: e.dma_start(out=YA[:], in_=d["YA_s"][:, :, tok].rearrange("h p t -> p h t")), reads=[buf("YA_s")], writes=[BYA], dma=True)
            stats(YA[:], BYA, 8, 1024.0, 0)
            for j in range(8):
                E("vector", lambda e: e.scalar_tensor_tensor(out=ACTb[:, 8 + j, :], in0=YA[:, j, :], scalar=vcol(V_ATTOUT + j), in1=rstd[:],
                                                             op0=ALU.mult, op1=ALU.mult), reads=[BYA, Brstd, Bconst], writes=[BACT])
            for dt in range(16):
                w, Bw = wo[dt % 3], Bwo[dt % 3]
                E("sync", lambda e: e.dma_start(out=w[:], in_=d["WOUT_b"][dt]), reads=[buf("WOUT_b")], writes=[Bw], dma=True)
                pi = 1 + dt % 2
                for k in range(16):
                    E("tensor", lambda e, k=k: e.matmul(ps[pi][:], lhsT=w[:, k, :], rhs=ACTb[:, k, :], start=(k == 0), stop=(k == 15)),
                      reads=[Bw, BACT], writes=[Bps[pi]], signal=(k == 15))
                E("scalar", lambda e: e.copy(out=MT[:, dt, :], in_=ps[pi][:]), reads=[Bps[pi]], writes=[BMT])
            stats(MT[:], BMT, 16, 2048.0, 0)
            for k in range(16):
                t, Bt = tmp[k % 2], Btmp[k % 2]
                E("vector", lambda e: e.scalar_tensor_tensor(out=t[:], in0=MT[:, k, :], scalar=vcol(V_POSTMIX + k), in1=rstd[:],
                                                             op0=ALU.mult, op1=ALU.mult), reads=[BMT, Brstd, Bconst], writes=[Bt])
                E("gpsimd", lambda e: e.tensor_tensor(out=XT[:, k, :], in0=XT[:, k, :], in1=t[:], op=ALU.add), reads=[Bt, BXT], writes=[BXT])
            stats(XT[:], BXT, 16, 2048.0, 0)
            for k in range(16):
                E("vector", lambda e: e.scalar_tensor_tensor(out=ACTb[:, k, :], in0=XT[:, k, :], scalar=vcol(V_PREMLP + k), in1=rstd[:],
                                                             op0=ALU.mult, op1=ALU.mult), reads=[BXT, Brstd, Bconst], writes=[BACT])
            for half in range(2):
                for f in range(32):
                    ff = half * 32 + f
                    w, Bw = wu[ff % 3], Bwu[ff % 3]
                    E("sync", lambda e: e.dma_start(out=w[:], in_=d["WUP_b"][ff]), reads=[buf("WUP_b")], writes=[Bw], dma=True)
                    pi = 3 + ff % 2
                    for k in range(16):
                        E("tensor", lambda e, k=k: e.matmul(ps[pi][:], lhsT=w[:, k, :], rhs=ACTb[:, k, :], start=(k == 0), stop=(k == 15)),
                          reads=[Bw, BACT], writes=[Bps[pi]], signal=(k == 15))
                    t, Bt = tmp[ff % 2], Btmp[ff % 2]
                    E("scalar", lambda e: e.activation(out=t[:], in_=ps[pi][:], func=AF.Relu), reads=[Bps[pi]], writes=[Bt])
                    E("gpsimd" if ff % 2 else "vector", lambda e: e.tensor_tensor(out=AT[:, f, :], in0=t[:], in1=t[:], op=ALU.mult), reads=[Bt], writes=[BAT])
                for dt in range(16):
                    i3 = (half * 16 + dt) % 3
                    w, Bw = wd[i3], Bwd[i3]
                    E("sync", lambda e: e.dma_start(out=w[:], in_=d["WDN_b"][dt, half]), reads=[buf("WDN_b")], writes=[Bw], dma=True)
                    pi = 5 + dt % 2
                    for f in range(32):
                        E("tensor", lambda e, f=f: e.matmul(ps[pi][:], lhsT=w[:, f, :], rhs=AT[:, f, :], start=(f == 0), stop=(f == 31)),
                          reads=[Bw, BAT], writes=[Bps[pi]], signal=(f == 31))
                    if half == 0:
                        E("scalar", lambda e: e.copy(out=MT[:, dt, :], in_=ps[pi][:]), reads=[Bps[pi]], writes=[BMT])
                    else:
                        E("vector", lambda e: e.tensor_tensor(out=MT[:, dt, :], in0=MT[:, dt, :], in1=ps[pi][:], op=ALU.add), reads=[Bps[pi], BMT], writes=[BMT])
            stats(MT[:], BMT, 16, 2048.0, 0)
            for k in range(16):
                t, Bt = tmp[k % 2], Btmp[k % 2]
                E("vector", lambda e: e.scalar_tensor_tensor(out=t[:], in0=MT[:, k, :], scalar=vcol(V_POSTMLP + k), in1=rstd[:],
                                                             op0=ALU.mult, op1=ALU.mult), reads=[BMT, Brstd, Bconst], writes=[Bt])
                E("gpsimd", lambda e: e.tensor_tensor(out=XT[:, k, :], in0=XT[:, k, :], in1=t[:], op=ALU.add), reads=[Bt, BXT], writes=[BXT])
            dst = d["yT_out"].rearrange("(k p) t -> p k t", p=128)
            E("sync", lambda e: e.dma_start(out=dst[:, 0:8, tok], in_=XT[:, 0:8, :]), reads=[BXT], dma=True)
            E("sync", lambda e: e.dma_start(out=dst[:, 8:16, tok], in_=XT[:, 8:16, :]), reads=[BXT], dma=True)


_STAGES = os.environ.get("MK_STAGES", "PWABSCDE")


def run_cores(inputs, stages=_STAGES, debug=()):
    sh = _prep_shared(inputs)
    in_maps = [_prep_core(inputs, c, sh) for c in range(8)]
    nc, kb = build_program(stages, debug)
    res = run_bass_kernel_spmd(nc, in_maps, core_ids=list(range(8)))
    return res, kb


def kernel(**inputs):
    res, _ = run_cores(inputs)
    yp = np.stack([np.ascontiguousarray(res.results[c]["yT_out"].T) for c in range(4)], axis=0)
    ys = np.concatenate([res.results[4 + j]["yT_out"].T for j in range(4)], axis=0)[None]
    return (np.ascontiguousarray(yp.astype(np.float32)), np.ascontiguousarray(ys.astype(np.float32)))
```

```python
import os
import numpy as np
from contextlib import ExitStack
import concourse.bass as bass
import concourse.mybir as mybir
from concourse.bass_utils import run_bass_kernel_spmd

F32 = mybir.dt.float32
BF16 = mybir.dt.bfloat16
AF = mybir.ActivationFunctionType
ALU = mybir.AluOpType

NT = int(os.environ.get("MK_NT", "4096"))
NCTX = 4 * NT
BLK = 512
NB_OWN = NT // BLK
NB_CTX = NCTX // BLK
NKT = NCTX // 128
NROW = NCTX // 64
EPS = 1e-6
MASK_NEG = -30000.0

V_PREMIX, V_POSTMIX, V_PREMLP, V_POSTMLP, V_SSMOUT, V_ATTOUT, V_QN, V_KN, V_D = 0, 16, 32, 48, 64, 72, 80, 81, 82
NVEC = 90


class Buf:
    __slots__ = ("writers", "readers")

    def __init__(self):
        self.writers = {}
        self.readers = {}


class KB:
    ENGS = ("sync", "tensor", "vector", "scalar", "gpsimd")

    NDMA = 20

    def __init__(self, nc, es):
        self.nc = nc
        self.sems = {}
        self.counts = {}
        self.waited = {e: {} for e in self.ENGS}
        self.nops = {e: 0 for e in self.ENGS}
        self.rr = {e: 0 for e in self.ENGS}
        self.pe_prev_serial = False
        for e in self.ENGS:
            if e != "sync":
                self.sems[e] = es.enter_context(nc.semaphore("s_" + e))
                self.counts[e] = 0
        for e in ("sync", "gpsimd"):
            for i in range(self.NDMA):
                k = "%s_dma%d" % (e, i)
                self.sems[k] = es.enter_context(nc.semaphore("s_" + k))
                self.counts[k] = 0

    def emit(self, eng, fn, reads=(), writes=(), dma=False, signal=True, serial=False):
        need = {}
        for b in reads:
            for s, v in b.writers.items():
                if need.get(s, 0) < v:
                    need[s] = v
        for b in writes:
            for s, v in b.writers.items():
                if need.get(s, 0) < v:
                    need[s] = v
            for s, v in b.readers.items():
                if need.get(s, 0) < v:
                    need[s] = v
        e = getattr(self.nc, eng)
        w = self.waited[eng]
        if dma:
            key = "%s_dma%d" % (eng, self.rr[eng] % self.NDMA)
            self.rr[eng] += 1
            if self.counts[key] > need.get(key, 0):
                need[key] = self.counts[key]
        else:
            key = eng
        if eng == "tensor":
            if serial or self.pe_prev_serial:
                need["tensor"] = self.counts["tensor"]
            else:
                need.pop("tensor", None)
            self.pe_prev_serial = serial
        for s, v in need.items():
            if w.get(s, 0) < v:
                w[s] = v
                e.wait_ge(self.sems[s], v)
        inc = 16 if dma else 1
        if signal:
            self.counts[key] += inc
            val = self.counts[key]
            ins = fn(e)
            ins.then_inc(self.sems[key], inc)
        else:
            assert eng == "tensor" and not dma
            val = self.counts[key] + 1
            fn(e)
        self.nops[eng] += 1
        for b in writes:
            b.writers[key] = val
        for b in reads:
            b.readers[key] = val
        return (key, val)

    def barrier(self):
        for eng in self.ENGS:
            e = getattr(self.nc, eng)
            w = self.waited[eng]
            for k, v in self.counts.items():
                if v > 0 and w.get(k, 0) < v:
                    w[k] = v
                    e.wait_ge(self.sems[k], v)

    def finish(self):
        for eng in ("sync", "gpsimd"):
            for i in range(self.NDMA):
                k = "%s_dma%d" % (eng, i)
                if self.counts[k] > 0:
                    getattr(self.nc, eng).wait_ge(self.sems[k], self.counts[k])


def _rope_compact(pos_of_ctx_row):
    inv = (np.float32(10000.0) ** (-(np.arange(0, 64, 2, dtype=np.float32)) / np.float32(64))).astype(np.float32)
    f = np.arange(64) % 32
    cos = np.zeros((128, NROW), np.float32)
    sin = np.zeros((128, NROW), np.float32)
    rows = pos_of_ctx_row.astype(np.float32)
    ang = (rows[None, :] * inv[f][:, None]).astype(np.float32)
    cos[:64], sin[:64] = np.cos(ang), np.sin(ang)
    cols = np.arange(64, dtype=np.float32)
    angc = (cols[None, :] * inv[f][:, None]).astype(np.float32)
    cos[64:, :64], sin[64:, :64] = np.cos(angc), np.sin(angc)
    return cos, sin


def _consts():
    c = {}
    rp = np.zeros((128, 128), np.float32)
    for m in range(128):
        j = m % 64
        if j < 32:
            rp[m + 32, m] = -1.0
        else:
            rp[m - 32, m] = 1.0
    c["rperm"] = rp
    c["ident"] = np.eye(128, dtype=np.float32)
    sel = np.zeros((128, 64, 128), np.float32)
    selT = np.zeros((128, 64, 128), np.float32)
    for gl in range(8):
        for tau in range(8):
            for h in range(16):
                sel[gl * 16 + h, gl * 8 + tau, tau * 16 + h] = 1.0
                selT[tau * 16 + h, gl * 8 + tau, gl * 16 + h] = 1.0
    c["sel"] = sel
    c["selT"] = selT
    tp = np.arange(128) // 16
    c["maskL"] = (tp[:, None] <= tp[None, :]).astype(np.float32)
    c["maskU"] = (tp[:, None] >= tp[None, :]).astype(np.float32)
    return c


def _prep_shared(inp):
    f = lambda a: np.ascontiguousarray(np.asarray(a, dtype=np.float32))
    sh = {}
    w_in = f(inp["w_in"])[0]
    sh["w_in_t"] = f(w_in.reshape(16, 128, 2560).transpose(1, 0, 2))
    sh["w_glu_t"] = f(f(inp["w_glu"])[0].reshape(8, 128, 1024).transpose(1, 0, 2))
    w_out = f(inp["w_out"])[0]
    sh["w_out_t"] = f(w_out.reshape(16, 128, 16, 128).transpose(2, 1, 0, 3))
    w_up = f(inp["w_up"])[0]
    sh["w_up_t"] = f(w_up.reshape(16, 128, 64, 128).transpose(2, 1, 0, 3))
    w_dn = f(inp["w_down"])[0]
    sh["w_dn_t"] = f(w_dn.reshape(2, 32, 128, 16, 128).transpose(3, 0, 2, 1, 4))
    vec = np.zeros((128, NVEC), np.float32)
    pk = lambda v, n: f(v).reshape(n, 128).T
    vec[:, V_PREMIX:V_PREMIX + 16] = pk(inp["pre_mix_norm"], 16)
    vec[:, V_POSTMIX:V_POSTMIX + 16] = pk(inp["post_mix_norm"], 16)
    vec[:, V_PREMLP:V_PREMLP + 16] = pk(inp["pre_mlp_norm"], 16)
    vec[:, V_POSTMLP:V_POSTMLP + 16] = pk(inp["post_mlp_norm"], 16)
    vec[:, V_SSMOUT:V_SSMOUT + 8] = pk(inp["ssm_out_norm"], 8)
    vec[:, V_ATTOUT:V_ATTOUT + 8] = pk(inp["attn_out_norm"], 8)
    vec[:, V_QN] = f(inp["q_norm"])[0]
    vec[:, V_KN] = f(inp["k_norm"])[0]
    vec[:, V_D:V_D + 8] = pk(inp["ssm_d"], 8)
    sh["vecs"] = vec
    a_re = f(inp["ssm_a_re"])[0]
    a_im = f(inp["ssm_a_im"])[0]
    ldt = f(inp["ssm_log_dt"])[0]
    sh["ssm_ar"] = f(a_re.transpose(0, 2, 1).reshape(128, 64))
    sh["ssm_ai"] = f(a_im.transpose(0, 2, 1).reshape(128, 64))
    sh["ssm_ldt"] = f(np.broadcast_to(ldt[:, None, :], (2, 64, 64)).reshape(128, 64))
    sh["ssm_bre"] = f(f(inp["ssm_b_re"])[0].transpose(0, 2, 1, 3).reshape(128, 64, 16))
    sh["ssm_bim"] = f(f(inp["ssm_b_im"])[0].transpose(0, 2, 1, 3).reshape(128, 64, 16))
    sh["ssm_cre"] = f(f(inp["ssm_c_re"])[0].transpose(0, 3, 1, 2).reshape(128, 64, 16))
    sh["ssm_cim"] = f(f(inp["ssm_c_im"])[0].transpose(0, 3, 1, 2).reshape(128, 64, 16))
    d = f(inp["ssm_d"])[0].reshape(64, 16)
    sh["ssm_dug"] = f(np.tile(d.T, (8, 1)))
    sh.update(_consts())
    return sh


def _prep_core(inp, core, sh):
    m = dict(sh)
    ctx = np.zeros((2048, NCTX), np.float32)
    mask = np.zeros((NCTX,), np.float32)
    gates = np.zeros((128, 3), np.float32)
    if core < 4:
        x = np.asarray(inp["x_prompt"], np.float32)[core]
        ctx[:, :NT] = x.T
        mask[NT:] = MASK_NEG
        slots = [0, 0, 0, 0]
    else:
        slot = core - 4
        xs = np.asarray(inp["x_sample"], np.float32)[0]
        others = [j for j in range(4) if j != slot]
        slots = [slot] + others
        for i, j in enumerate(slots):
            ctx[:, i * NT:(i + 1) * NT] = xs[j * NT:(j + 1) * NT].T
        for s_ in range(3):
            gates[:64, s_] = 1.0 if s_ < slot else 0.0
            gates[64:, s_] = 1.0 if (2 - s_) >= slot else 0.0
    m["xT_ctx"] = ctx
    m["maskb"] = np.ascontiguousarray(mask.reshape(NKT, 128).T)
    rows = np.concatenate([np.arange(j * NT // 64, (j + 1) * NT // 64) for j in slots])
    m["rope_cos"], m["rope_sin"] = _rope_compact(rows)
    m["gates"] = gates
    return m


def build_program(stages="WAB", debug=()):
    nc = bass.Bass("TRN2", target_bir_lowering=False)
    dbg = set(debug)

    def din(name, shape):
        return nc.dram_tensor(name, list(shape), F32, kind="ExternalInput").ap()

    def dscr(name, shape, dt=BF16):
        kind = "ExternalOutput" if name in dbg else "Internal"
        return nc.dram_tensor(name, list(shape), dt, kind=kind).ap()

    xT_ctx = din("xT_ctx", [2048, NCTX])
    rope_cos_d, rope_sin_d = din("rope_cos", [128, NROW]), din("rope_sin", [128, NROW])
    maskb_d = din("maskb", [128, NKT])
    gates_d = din("gates", [128, 3])
    vecs_d = din("vecs", [128, NVEC])
    w_in_d = din("w_in_t", [128, 16, 2560])
    w_glu_d = din("w_glu_t", [128, 8, 1024])
    has_mlp = ("P" in stages) or ("E" in stages)
    w_out_d = din("w_out_t", [16, 128, 16, 128]) if has_mlp else None
    w_up_d = din("w_up_t", [64, 128, 16, 128]) if has_mlp else None
    w_dn_d = din("w_dn_t", [16, 2, 128, 32, 128]) if has_mlp else None
    ssm_in = {k: din(k, s) for k, s in (("ssm_ar", [128, 64]), ("ssm_ai", [128, 64]), ("ssm_ldt", [128, 64]),
                                        ("ssm_bre", [128, 64, 16]), ("ssm_bim", [128, 64, 16]),
                                        ("ssm_cre", [128, 64, 16]), ("ssm_cim", [128, 64, 16]),
                                        ("ssm_dug", [128, 64]))}
    rperm_d, ident_d = din("rperm", [128, 128]), din("ident", [128, 128])
    sel_d, selT_d = din("sel", [128, 64, 128]), din("selT", [128, 64, 128])
    maskL_d, maskU_d = din("maskL", [128, 128]), din("maskU", [128, 128])
    yT_out = nc.dram_tensor("yT_out", [2048, NT], F32, kind="ExternalOutput").ap()

    KT_s = dscr("KT_s", [2, 128, NCTX])
    V_s = dscr("V_s", [2, 128, NKT, 128])
    QT_s = dscr("QT_s", [8, 128, NT])
    WS_s = dscr("WS_s", [128, 64, 5, 128])
    UG_s = dscr("UG_s", [NB_OWN, 128, 64, 64])
    SF_s = dscr("SF_s", [NB_OWN, 64, 64, 2, 64])
    ZB_s = dscr("ZB_s", [NB_OWN, 64, 64, 2, 64])
    NS_s = dscr("NS_s", [128, 8, NT])
    YA_s = dscr("YA_s", [8, 128, NT], F32)
    WOUT_b = dscr("WOUT_b", [16, 128, 16, 128])
    WUP_b = dscr("WUP_b", [64, 128, 16, 128])
    WDN_b = dscr("WDN_b", [16, 2, 128, 32, 128])
    EST_s = dscr("EST_s", [128, 2, 64], F32)

    es = ExitStack()
    kb = KB(nc, es)
    E = kb.emit
    B = {}

    def buf(name):
        if name not in B:
            B[name] = Buf()
        return B[name]

    uid = [0]

    def sbt(st, name, shape, dt):
        uid[0] += 1
        return st.enter_context(nc.sbuf_tensor("sb%d_%s" % (uid[0], name), list(shape), dt))

    psbig = [es.enter_context(nc.psum_tensor("psb%d" % i, [128, 1024], F32)) for i in range(4)]
    ps = [psbig[i // 2][:, (i % 2) * 512:(i % 2 + 1) * 512] for i in range(8)]
    Bps = [Buf() for _ in range(8)]
    ones_b = sbt(es, "ones_b", [128, 128], BF16)
    ones_f = sbt(es, "ones_f", [128, 128], F32)
    ident_f = sbt(es, "ident_f", [128, 128], F32)
    rperm_b = sbt(es, "rperm_b", [128, 128], BF16)
    vecs = sbt(es, "vecs", [128, NVEC], F32)
    gates = sbt(es, "gates", [128, 3], F32)
    rope_c = sbt(es, "rope_c", [128, NROW], F32)
    rope_s = sbt(es, "rope_s", [128, NROW], F32)
    A8a = sbt(es, "A8a", [128, 2, 64], F32)
    A8b = sbt(es, "A8b", [128, 2, 64], F32)
    A8c = sbt(es, "A8c", [128, 2, 64], F32)
    Sin_t = sbt(es, "Sin_t", [128, 2, 64], F32)
    Bconst = Buf()
    BA8 = Buf()
    BSin = Buf()

    E("vector", lambda e: e.memset(ones_b[:], 1.0), writes=[Bconst])
    E("vector", lambda e: e.memset(ones_f[:], 1.0), writes=[Bconst])
    E("sync", lambda e: e.dma_start(out=ident_f[:], in_=ident_d), writes=[Bconst], dma=True)
    E("gpsimd", lambda e: e.dma_start(out=rperm_b[:], in_=rperm_d), writes=[Bconst], dma=True)
    E("sync", lambda e: e.dma_start(out=vecs[:], in_=vecs_d), writes=[Bconst], dma=True)
    E("sync", lambda e: e.dma_start(out=gates[:], in_=gates_d), writes=[Bconst], dma=True)
    E("sync", lambda e: e.dma_start(out=rope_c[:], in_=rope_cos_d), writes=[Bconst], dma=True)
    E("sync", lambda e: e.dma_start(out=rope_s[:], in_=rope_sin_d), writes=[Bconst], dma=True)

    def vcol(c0, n=1):
        return vecs[:, c0:c0 + n]

    def cmul(eng, out, a, b, tmp1, tmp2, bufs_r, bufs_w, rows=slice(0, 128)):
        r = rows
        E(eng, lambda e: e.tensor_tensor(out=tmp1[r, 0, :], in0=a[r, 0, :], in1=b[r, 0, :], op=ALU.mult), reads=bufs_r, writes=bufs_w)
        E(eng, lambda e: e.tensor_tensor(out=tmp1[r, 1, :], in0=a[r, 1, :], in1=b[r, 1, :], op=ALU.mult), reads=bufs_r, writes=bufs_w)
        E(eng, lambda e: e.tensor_tensor(out=tmp2[r, 0, :], in0=a[r, 0, :], in1=b[r, 1, :], op=ALU.mult), reads=bufs_r, writes=bufs_w)
        E(eng, lambda e: e.tensor_tensor(out=tmp2[r, 1, :], in0=a[r, 1, :], in1=b[r, 0, :], op=ALU.mult), reads=bufs_r, writes=bufs_w)
        E(eng, lambda e: e.tensor_tensor(out=out[r, 0, :], in0=tmp1[r, 0, :], in1=tmp1[r, 1, :], op=ALU.subtract), reads=bufs_r, writes=bufs_w)
        E(eng, lambda e: e.tensor_tensor(out=out[r, 1, :], in0=tmp2[r, 0, :], in1=tmp2[r, 1, :], op=ALU.add), reads=bufs_r, writes=bufs_w)

    ctx = dict(nc=nc, kb=kb, E=E, es=es, ps=ps, psbig=psbig, Bps=Bps, buf=buf, sbt=sbt, vcol=vcol, cmul=cmul,
               ones_b=ones_b, ones_f=ones_f, ident_f=ident_f, rperm_b=rperm_b, vecs=vecs, gates=gates, rope_c=rope_c, rope_s=rope_s,
               A8a=A8a, A8b=A8b, A8c=A8c, Sin_t=Sin_t, Bconst=Bconst, BA8=BA8, BSin=BSin)
    dr = dict(xT_ctx=xT_ctx, maskb=maskb_d, w_in=w_in_d, w_glu=w_glu_d, w_out=w_out_d, w_up=w_up_d, w_dn=w_dn_d, ssm=ssm_in,
              sel=sel_d, selT=selT_d, maskL=maskL_d, maskU=maskU_d, yT_out=yT_out,
              KT_s=KT_s, V_s=V_s, QT_s=QT_s, WS_s=WS_s, UG_s=UG_s, SF_s=SF_s, ZB_s=ZB_s, NS_s=NS_s, YA_s=YA_s,
              WOUT_b=WOUT_b, WUP_b=WUP_b, WDN_b=WDN_b, EST_s=EST_s)

    if "P" in stages:
        phase_weight_cast(ctx, dr)
    if "W" in stages:
        phase_ssm_weights(ctx, dr)
        kb.barrier()
    if "A" in stages:
        phase_proj(ctx, dr, own=False)
        kb.barrier()
    if "B" in stages:
        phase_proj(ctx, dr, own=True)
        kb.barrier()
    if "S" in stages:
        phase_ssm_scan(ctx, dr, own=False)
        kb.barrier()
        phase_ssm_carry(ctx, dr)
        kb.barrier()
        phase_ssm_scan(ctx, dr, own=True)
        kb.barrier()
    if "C" in stages:
        phase_ssm_out(ctx, dr)
        kb.barrier()
    if "D" in stages:
        phase_attention(ctx, dr)
        kb.barrier()
    if "E" in stages:
        phase_mlp(ctx, dr)
    kb.finish()
    es.close()
    return nc, kb


def load_norm_block(c, st, xT_src, blk, xt, sq, ht, rstd, Bxt, Bsq, Bht, Brstd, gcol, ps_i=0):
    E, ps, Bps = c["E"], c["ps"], c["Bps"]
    ones_b, Bconst, vcol = c["ones_b"], c["Bconst"], c["vcol"]
    sl = slice(blk * BLK, (blk + 1) * BLK)
    src = xT_src.rearrange("(k p) t -> p k t", p=128)
    Bsq = Bsq if isinstance(Bsq, list) else [Bsq]
    E("sync", lambda e: e.dma_start(out=xt[:, 0:8, :], in_=src[:, 0:8, sl]), writes=[Bxt], dma=True)
    E("sync", lambda e: e.dma_start(out=xt[:, 8:16, :], in_=src[:, 8:16, sl]), writes=[Bxt], dma=True)
    E("scalar", lambda e: e.activation(out=sq[:], in_=xt[:], func=AF.Square), reads=[Bxt], writes=Bsq)
    for k in range(16):
        E("tensor", lambda e, k=k: e.matmul(ps[ps_i][:], lhsT=ones_b[:], rhs=sq[:, k, :], start=(k == 0), stop=(k == 15)),
          reads=[Bconst] + Bsq, writes=[Bps[ps_i]], signal=(k == 15))
    E("scalar", lambda e: e.activation(out=rstd[:], in_=ps[ps_i][:], func=AF.Sqrt, scale=1.0 / 2048, bias=EPS),
      reads=[Bps[ps_i]], writes=[Brstd])
    E("vector", lambda e: e.reciprocal(out=rstd[:], in_=rstd[:]), reads=[Brstd], writes=[Brstd])
    for k in range(16):
        E("vector", lambda e, k=k: e.scalar_tensor_tensor(out=ht[:, k, :], in0=xt[:, k, :], scalar=vcol(gcol + k), in1=rstd[:],
                                                         op0=ALU.mult, op1=ALU.mult),
          reads=[Bxt, Brstd, Bconst], writes=[Bht])


def phase_proj(c, d, own):
    nc, E, ps, Bps, sbt, buf, vcol = c["nc"], c["E"], c["ps"], c["Bps"], c["sbt"], c["buf"], c["vcol"]
    ones_b, rperm_b, Bconst = c["ones_b"], c["rperm_b"], c["Bconst"]
    H = 8 if own else 2
    ncols = 1024 if own else 512
    col0 = 1024 if own else 2048
    nblk = NB_OWN if own else NB_CTX
    xT_src = d["xT_ctx"]
    rope_c, rope_s = c["rope_c"], c["rope_s"]
    gn = V_QN if own else V_KN
    out_s = d["QT_s"] if own else d["KT_s"]
    Bout = buf("QT_s" if own else "KT_s")
    BV = buf("V_s")
    with ExitStack() as st:
        W = sbt(st, "W_p", [128, 16, ncols], BF16)
        BW = Buf()
        for k in range(0, 16, 4):
            E("gpsimd", lambda e, k=k: e.dma_start(out=W[:, k:k + 4, :], in_=d["w_in"][:, k:k + 4, col0:col0 + ncols]), writes=[BW], dma=True)
        xt = [sbt(st, "xt%d" % i, [128, 16, BLK], F32) for i in range(2)]
        cs = [sbt(st, "cs%d" % i, [128, BLK], F32) for i in range(2)]
        sn = [sbt(st, "sn%d" % i, [128, BLK], F32) for i in range(2)]
        Bxt, Bcs = [Buf(), Buf()], [Buf(), Buf()]
        for i in range(2):
            E("gpsimd", lambda e: e.tensor_copy(out=cs[i][64:128, :].rearrange("p (r c) -> p r c", c=64),
                                                in_=rope_c[64:128, None, 0:64].to_broadcast([64, 8, 64])), reads=[Bconst], writes=[Bcs[i]])
            E("gpsimd", lambda e: e.tensor_copy(out=sn[i][64:128, :].rearrange("p (r c) -> p r c", c=64),
                                                in_=rope_s[64:128, None, 0:64].to_broadcast([64, 8, 64])), reads=[Bconst], writes=[Bcs[i]])
        sq2 = [sbt(st, "sq%d" % i, [128, 16, BLK], BF16) for i in range(2)]
        ht2 = [sbt(st, "ht%d" % i, [128, 16, BLK], BF16) for i in range(2)]
        rstd2 = [sbt(st, "rstd%d" % i, [128, BLK], F32) for i in range(2)]
        Bsq2, Bht2, Brstd2 = [Buf(), Buf()], [Buf(), Buf()], [Buf(), Buf()]
        sqh = [sbt(st, "sqh%d" % i, [128, BLK], BF16) for i in range(2)]
        rk = [sbt(st, "rk%d" % i, [128, BLK], F32) for i in range(2)]
        kn = [sbt(st, "kn%d" % i, [128, BLK], BF16) for i in range(2)]
        t1 = [sbt(st, "t1%d" % i, [128, BLK], F32) for i in range(2)]
        t2 = [sbt(st, "t2%d" % i, [128, BLK], F32) for i in range(2)]
        ko = [sbt(st, "ko%d" % i, [128, BLK], BF16) for i in range(2)]
        Bsqh, Brk, Bkn, Bt1, Bt2, Bko = [[Buf(), Buf()] for _ in range(6)]
        vt = [sbt(st, "vt%d" % i, [128, 4, 256], BF16) for i in range(2)]
        Bvt = [Buf(), Buf()]
        for blk in range(nblk):
            b = blk % 2
            sl = slice(blk * BLK, (blk + 1) * BLK)
            E("gpsimd", lambda e: e.tensor_copy(out=cs[b][0:64, :].rearrange("p (r c) -> p r c", c=64),
                                                in_=rope_c[0:64, blk * 8:(blk + 1) * 8, None].to_broadcast([64, 8, 64])), reads=[Bconst], writes=[Bcs[b]])
            E("gpsimd", lambda e: e.tensor_copy(out=sn[b][0:64, :].rearrange("p (r c) -> p r c", c=64),
                                                in_=rope_s[0:64, blk * 8:(blk + 1) * 8, None].to_broadcast([64, 8, 64])), reads=[Bconst], writes=[Bcs[b]])
            sq, ht, rstd, Bsq, Bht, Brstd = sq2[b], ht2[b], rstd2[b], Bsq2[b], Bht2[b], Brstd2[b]
            if blk == 0:
                load_norm_block(c, st, xT_src, 0, xt[0], sq2[0], ht2[0], rstd2[0], Bxt[0], Bsq2[0], Bht2[0], Brstd2[0], V_PREMIX, ps_i=0)
            if blk + 1 < nblk:
                nb_ = (blk + 1) % 2
                load_norm_block(c, st, xT_src, blk + 1, xt[nb_], sq2[nb_], ht2[nb_], rstd2[nb_], Bxt[nb_], Bsq2[nb_], Bht2[nb_], Brstd2[nb_], V_PREMIX, ps_i=0)
            for hd in range(H):
                i = hd % 2
                pk, pss, pr = 1 + i, 3 + i, 5 + i
                for k in range(16):
                    E("tensor", lambda e, k=k: e.matmul(ps[pk][:], lhsT=W[:, k, hd * 128:(hd + 1) * 128], rhs=ht[:, k, :],
                                                        start=(k == 0), stop=(k == 15)), reads=[BW, Bht], writes=[Bps[pk]], signal=(k == 15))
                E("scalar", lambda e: e.activation(out=sqh[i][:], in_=ps[pk][:], func=AF.Square), reads=[Bps[pk]], writes=[Bsqh[i]])
                E("tensor", lambda e: e.matmul(ps[pss][:], lhsT=ones_b[:], rhs=sqh[i][:], start=True, stop=True),
                  reads=[Bconst, Bsqh[i]], writes=[Bps[pss]])
                E("scalar", lambda e: e.activation(out=rk[i][:], in_=ps[pss][:], func=AF.Sqrt, scale=1.0 / 128, bias=EPS),
                  reads=[Bps[pss]], writes=[Brk[i]])
                E("vector", lambda e: e.reciprocal(out=rk[i][:], in_=rk[i][:]), reads=[Brk[i]], writes=[Brk[i]])
                E("vector", lambda e: e.scalar_tensor_tensor(out=kn[i][:], in0=ps[pk][:], scalar=vcol(gn), in1=rk[i][:],
                                                             op0=ALU.mult, op1=ALU.mult),
                  reads=[Bps[pk], Brk[i], Bconst], writes=[Bkn[i]])
                E("tensor", lambda e: e.matmul(ps[pr][:], lhsT=rperm_b[:], rhs=kn[i][:], start=True, stop=True),
                  reads=[Bconst, Bkn[i]], writes=[Bps[pr]])
                E("gpsimd", lambda e: e.tensor_tensor(out=t1[i][:], in0=kn[i][:], in1=cs[b][:], op=ALU.mult),
                  reads=[Bkn[i], Bcs[b]], writes=[Bt1[i]])
                E("vector", lambda e: e.tensor_tensor(out=t2[i][:], in0=ps[pr][:], in1=sn[b][:], op=ALU.mult),
                  reads=[Bps[pr], Bcs[b]], writes=[Bt2[i]])
                E("gpsimd", lambda e: e.tensor_tensor(out=ko[i][:], in0=t1[i][:], in1=t2[i][:], op=ALU.add),
                  reads=[Bt1[i], Bt2[i]], writes=[Bko[i]])
                E("sync", lambda e: e.dma_start(out=out_s[hd, :, sl], in_=ko[i][:]), reads=[Bko[i]], writes=[Bout], dma=True)
            if not own:
                for sub in range(4):
                    pv = 7
                    for k in range(16):
                        E("tensor", lambda e, k=k: e.matmul(ps[pv][:, 0:256], lhsT=ht[:, k, sub * 128:(sub + 1) * 128], rhs=W[:, k, 256:512],
                                                            start=(k == 0), stop=(k == 15)), reads=[BW, Bht], writes=[Bps[pv]], signal=(k == 15))
                    E("scalar", lambda e: e.copy(out=vt[b][:, sub, :], in_=ps[pv][:, 0:256]), reads=[Bps[pv]], writes=[Bvt[b]])
                for kvh in range(2):
                    E("sync", lambda e: e.dma_start(out=d["V_s"][kvh, :, blk * 4:(blk + 1) * 4, :], in_=vt[b][:, :, kvh * 128:(kvh + 1) * 128]),
                      reads=[Bvt[b]], writes=[BV], dma=True)


def phase_ssm_weights(c, d):
    nc, E, ps, Bps, sbt, buf, vcol = c["nc"], c["E"], c["ps"], c["Bps"], c["sbt"], c["buf"], c["vcol"]
    ident_f, Bconst = c["ident_f"], c["Bconst"]
    A8a, A8b, A8c, BA8 = c["A8a"], c["A8b"], c["A8c"], c["BA8"]
    S = d["ssm"]
    H0, H1 = slice(0, 64), slice(64, 128)
    with ExitStack() as st:
        T = lambda n, shape, dt=F32: sbt(st, n, shape, dt)
        ar, ai, ldt = T("ar", [128, 64]), T("ai", [128, 64]), T("ldt", [128, 64])
        bre, bim, cre, cim = T("bre", [128, 64, 16]), T("bim", [128, 64, 16]), T("cre", [128, 64, 16]), T("cim", [128, 64, 16])
        dug, mL, mU = T("dug", [128, 64]), T("mL", [128, 128]), T("mU", [128, 128])
        Bin = Buf()
        for t, k in ((ar, "ssm_ar"), (ai, "ssm_ai"), (ldt, "ssm_ldt"), (bre, "ssm_bre"), (bim, "ssm_bim"),
                     (cre, "ssm_cre"), (cim, "ssm_cim"), (dug, "ssm_dug")):
            E("sync", lambda e, t=t, k=k: e.dma_start(out=t[:], in_=S[k]), writes=[Bin], dma=True)
        E("sync", lambda e: e.dma_start(out=mL[:], in_=d["maskL"]), writes=[Bin], dma=True)
        E("sync", lambda e: e.dma_start(out=mU[:], in_=d["maskU"]), writes=[Bin], dma=True)
        Bw = Buf()
        V = lambda fn: E("vector", fn, reads=[Bin, Bw, Bconst], writes=[Bw])
        A = lambda fn: E("scalar", fn, reads=[Bin, Bw], writes=[Bw])
        dt_, lrdt, th, mag = T("dt_", [128, 64]), T("lrdt", [128, 64]), T("th", [128, 64]), T("mag", [128, 64])
        cc, ss, u1, u2 = T("cc", [128, 64]), T("ss", [128, 64]), T("u1", [128, 64]), T("u2", [128, 64])
        halfpi = T("halfpi", [128, 1])
        V(lambda e: e.memset(halfpi[:], float(np.pi / 2)))
        A(lambda e: e.activation(out=dt_[:], in_=ldt[:], func=AF.Exp))
        V(lambda e: e.tensor_tensor(out=lrdt[:], in0=ar[:], in1=dt_[:], op=ALU.mult))
        V(lambda e: e.tensor_tensor(out=th[:], in0=ai[:], in1=dt_[:], op=ALU.mult))
        A(lambda e: e.activation(out=mag[:], in_=lrdt[:], func=AF.Exp))
        A(lambda e: e.activation(out=ss[:], in_=th[:], func=AF.Sin, scale=1.0 / 32))
        A(lambda e: e.activation(out=cc[:], in_=th[:], func=AF.Sin, scale=1.0 / 32, bias=halfpi[:]))
        for _ in range(5):
            V(lambda e: e.tensor_tensor(out=u1[:], in0=cc[:], in1=cc[:], op=ALU.mult))
            V(lambda e: e.tensor_tensor(out=u2[:], in0=ss[:], in1=ss[:], op=ALU.mult))
            V(lambda e: e.scalar_tensor_tensor(out=ss[:], in0=cc[:], scalar=2.0, in1=ss[:], op0=ALU.mult, op1=ALU.mult))
            V(lambda e: e.tensor_tensor(out=cc[:], in0=u1[:], in1=u2[:], op=ALU.subtract))
        Lre, Lim = T("Lre", [128, 64, 9]), T("Lim", [128, 64, 9])
        Ire, Iim, inv = T("Ire", [128, 64, 9]), T("Iim", [128, 64, 9]), T("inv", [128, 64, 9])
        V(lambda e: e.memset(Lre[:, :, 0], 1.0))
        V(lambda e: e.memset(Lim[:, :, 0], 0.0))
        V(lambda e: e.tensor_tensor(out=Lre[:, :, 1], in0=mag[:], in1=cc[:], op=ALU.mult))
        V(lambda e: e.tensor_tensor(out=Lim[:, :, 1], in0=mag[:], in1=ss[:], op=ALU.mult))
        for k in range(2, 9):
            V(lambda e, k=k: e.tensor_tensor(out=u1[:], in0=Lre[:, :, k - 1], in1=Lre[:, :, 1], op=ALU.mult))
            V(lambda e, k=k: e.tensor_tensor(out=u2[:], in0=Lim[:, :, k - 1], in1=Lim[:, :, 1], op=ALU.mult))
            V(lambda e, k=k: e.tensor_tensor(out=Lre[:, :, k], in0=u1[:], in1=u2[:], op=ALU.subtract))
            V(lambda e, k=k: e.tensor_tensor(out=u1[:], in0=Lre[:, :, k - 1], in1=Lim[:, :, 1], op=ALU.mult))
            V(lambda e, k=k: e.tensor_tensor(out=u2[:], in0=Lim[:, :, k - 1], in1=Lre[:, :, 1], op=ALU.mult))
            V(lambda e, k=k: e.tensor_tensor(out=Lim[:, :, k], in0=u1[:], in1=u2[:], op=ALU.add))
        V(lambda e: e.tensor_tensor(out=inv[:], in0=Lre[:], in1=Lre[:], op=ALU.mult))
        V(lambda e: e.tensor_tensor(out=Ire[:], in0=Lim[:], in1=Lim[:], op=ALU.mult))
        V(lambda e: e.tensor_tensor(out=inv[:], in0=inv[:], in1=Ire[:], op=ALU.add))
        V(lambda e: e.reciprocal(out=inv[:], in_=inv[:]))
        V(lambda e: e.tensor_tensor(out=Ire[:], in0=Lre[:], in1=inv[:], op=ALU.mult))
        V(lambda e: e.scalar_tensor_tensor(out=Iim[:], in0=Lim[:], scalar=-1.0, in1=inv[:], op0=ALU.mult, op1=ALU.mult))
        E("vector", lambda e: e.tensor_copy(out=A8c[:, 0, :], in_=Lre[:, :, 8]), reads=[Bw], writes=[BA8])
        E("vector", lambda e: e.tensor_copy(out=A8c[:, 1, :], in_=Lim[:, :, 8]), reads=[Bw], writes=[BA8])
        E("vector", lambda e: e.tensor_copy(out=A8a[:, 0, :], in_=Lre[:, :, 8]), reads=[Bw], writes=[BA8])
        E("vector", lambda e: e.tensor_copy(out=A8a[:, 1, :], in_=Lre[:, :, 8]), reads=[Bw], writes=[BA8])
        E("vector", lambda e: e.tensor_scalar(out=A8b[:, 0, :], in0=Lim[:, :, 8], scalar1=-1.0, scalar2=None, op0=ALU.mult), reads=[Bw], writes=[BA8])
        E("vector", lambda e: e.tensor_copy(out=A8b[:, 1, :], in_=Lim[:, :, 8]), reads=[Bw], writes=[BA8])
        PCre, PCim, PGre, PGim = T("PCre", [128, 64, 8]), T("PCim", [128, 64, 8]), T("PGre", [128, 64, 8]), T("PGim", [128, 64, 8])
        for dst, src in ((PCre, Lre), (PCim, Lim), (PGre, Ire), (PGim, Iim)):
            V(lambda e, dst=dst, src=src: e.tensor_copy(out=dst[H0, :, :], in_=src[H0, :, 1:9]))
            V(lambda e, dst=dst, src=src: e.tensor_copy(out=dst[H1, :, :], in_=src[H1, :, 8:0:-1]))
        den, wre, wim = T("den", [128, 64]), T("wre", [128, 64]), T("wim", [128, 64])
        V(lambda e: e.tensor_tensor(out=den[:], in0=ar[:], in1=ar[:], op=ALU.mult))
        V(lambda e: e.tensor_tensor(out=u1[:], in0=ai[:], in1=ai[:], op=ALU.mult))
        V(lambda e: e.tensor_tensor(out=den[:], in0=den[:], in1=u1[:], op=ALU.add))
        V(lambda e: e.reciprocal(out=den[:], in_=den[:]))
        V(lambda e: e.tensor_scalar(out=u1[:], in0=Lre[:, :, 1], scalar1=-1.0, scalar2=None, op0=ALU.add))
        V(lambda e: e.tensor_tensor(out=wre[:], in0=u1[:], in1=ar[:], op=ALU.mult))
        V(lambda e: e.tensor_tensor(out=u2[:], in0=Lim[:, :, 1], in1=ai[:], op=ALU.mult))
        V(lambda e: e.tensor_tensor(out=wre[:], in0=wre[:], in1=u2[:], op=ALU.add))
        V(lambda e: e.tensor_tensor(out=wre[:], in0=wre[:], in1=den[:], op=ALU.mult))
        V(lambda e: e.tensor_tensor(out=wim[:], in0=Lim[:, :, 1], in1=ar[:], op=ALU.mult))
        V(lambda e: e.tensor_tensor(out=u2[:], in0=u1[:], in1=ai[:], op=ALU.mult))
        V(lambda e: e.tensor_tensor(out=wim[:], in0=wim[:], in1=u2[:], op=ALU.subtract))
        V(lambda e: e.tensor_tensor(out=wim[:], in0=wim[:], in1=den[:], op=ALU.mult))
        bbre, bbim, v1 = T("bbre", [128, 64, 16]), T("bbim", [128, 64, 16]), T("v1", [128, 64, 16])
        wre_b = wre[:, :, None].to_broadcast([128, 64, 16])
        wim_b = wim[:, :, None].to_broadcast([128, 64, 16])
        V(lambda e: e.tensor_tensor(out=bbre[:], in0=bre[:], in1=wre_b, op=ALU.mult))
        V(lambda e: e.tensor_tensor(out=v1[:], in0=bim[:], in1=wim_b, op=ALU.mult))
        V(lambda e: e.tensor_tensor(out=bbre[:], in0=bbre[:], in1=v1[:], op=ALU.subtract))
        V(lambda e: e.tensor_tensor(out=bbim[:], in0=bim[:], in1=wre_b, op=ALU.mult))
        V(lambda e: e.tensor_tensor(out=v1[:], in0=bre[:], in1=wim_b, op=ALU.mult))
        V(lambda e: e.tensor_tensor(out=bbim[:], in0=bbim[:], in1=v1[:], op=ALU.add))
        GB = 16
        Gre, Gim, Gimn = T("Gre", [128, GB, 8, 16]), T("Gim", [128, GB, 8, 16]), T("Gimn", [128, GB, 8, 16])
        CLre, CLim = T("CLre", [128, GB, 8, 16]), T("CLim", [128, GB, 8, 16])
        Wzre, Wzim = T("Wzre", [128, GB, 8, 16]), T("Wzim", [128, GB, 8, 16])
        w1, w2 = T("w1", [128, GB, 8, 16]), T("w2", [128, GB, 8, 16])
        stage = [T("stage%d" % i, [128, GB, 5, 128], BF16) for i in range(2)]
        Bstage = [Buf(), Buf()]
        tA, tB = T("tA", [128, 128]), T("tB", [128, 128])
        BtA = Buf()
        BWS = buf("WS_s")
        for gb in range(64 // GB):
            g0 = gb * GB
            gs = slice(g0, g0 + GB)
            sh4 = [128, GB, 8, 16]
            PGre_b = PGre[:, gs, :, None].to_broadcast(sh4)
            PGim_b = PGim[:, gs, :, None].to_broadcast(sh4)
            PCre_b = PCre[:, gs, :, None].to_broadcast(sh4)
            PCim_b = PCim[:, gs, :, None].to_broadcast(sh4)
            Bre_b = bbre[:, gs, None, :].to_broadcast(sh4)
            Bim_b = bbim[:, gs, None, :].to_broadcast(sh4)
            Cre_b = cre[:, gs, None, :].to_broadcast(sh4)
            Cim_b = cim[:, gs, None, :].to_broadcast(sh4)
            A8re_b = A8c[:, 0, gs, None, None].to_broadcast(sh4)
            A8im_b = A8c[:, 1, gs, None, None].to_broadcast(sh4)
            V2 = lambda fn: E("vector", fn, reads=[Bin, Bw, BA8, Bconst], writes=[Bw])
            V2(lambda e: e.tensor_tensor(out=w1[:], in0=PGre_b, in1=Bre_b, op=ALU.mult))
            V2(lambda e: e.tensor_tensor(out=w2[:], in0=PGim_b, in1=Bim_b, op=ALU.mult))
            V2(lambda e: e.tensor_tensor(out=Gre[:], in0=w1[:], in1=w2[:], op=ALU.subtract))
            V2(lambda e: e.tensor_tensor(out=w1[:], in0=PGre_b, in1=Bim_b, op=ALU.mult))
            V2(lambda e: e.tensor_tensor(out=w2[:], in0=PGim_b, in1=Bre_b, op=ALU.mult))
            V2(lambda e: e.tensor_tensor(out=Gim[:], in0=w1[:], in1=w2[:], op=ALU.add))
            V2(lambda e: e.tensor_scalar(out=Gimn[:], in0=Gim[:], scalar1=-1.0, scalar2=None, op0=ALU.mult))
            V2(lambda e: e.tensor_tensor(out=w1[:], in0=PCre_b, in1=Cre_b, op=ALU.mult))
            V2(lambda e: e.tensor_tensor(out=w2[:], in0=PCim_b, in1=Cim_b, op=ALU.mult))
            V2(lambda e: e.tensor_tensor(out=CLre[:], in0=w1[:], in1=w2[:], op=ALU.subtract))
            V2(lambda e: e.tensor_tensor(out=w1[:], in0=PCre_b, in1=Cim_b, op=ALU.mult))
            V2(lambda e: e.tensor_tensor(out=w2[:], in0=PCim_b, in1=Cre_b, op=ALU.mult))
            V2(lambda e: e.tensor_tensor(out=CLim[:], in0=w1[:], in1=w2[:], op=ALU.add))
            V2(lambda e: e.tensor_tensor(out=w1[:], in0=Gre[:], in1=A8re_b, op=ALU.mult))
            V2(lambda e: e.tensor_tensor(out=w2[:], in0=Gim[:], in1=A8im_b, op=ALU.mult))
            V2(lambda e: e.tensor_tensor(out=Wzre[:], in0=w1[:], in1=w2[:], op=ALU.subtract))
            V2(lambda e: e.tensor_tensor(out=w1[:], in0=Gim[:], in1=A8re_b, op=ALU.mult))
            V2(lambda e: e.tensor_tensor(out=w2[:], in0=Gre[:], in1=A8im_b, op=ALU.mult))
            V2(lambda e: e.tensor_tensor(out=Wzim[:], in0=w1[:], in1=w2[:], op=ALU.add))
            sg = stage[gb % 2]
            Bsg = Bstage[gb % 2]
            f2 = lambda t: t[:].rearrange("q g t h -> q g (t h)")
            E("gpsimd", lambda e: e.tensor_copy(out=sg[:, :, 3, :], in_=f2(CLre)), reads=[Bw], writes=[Bsg])
            E("gpsimd", lambda e: e.tensor_scalar(out=sg[:, :, 4, :], in0=f2(CLim), scalar1=-1.0, scalar2=None, op0=ALU.mult), reads=[Bw], writes=[Bsg])
            for gl in range(GB):
                g = g0 + gl
                f1 = lambda t: t[:, gl, :, :].rearrange("q t h -> q (t h)")
                E("tensor", lambda e: e.transpose(out=ps[0][:, 0:128], in_=f1(Wzre), identity=ident_f[:]), reads=[Bw, Bconst], writes=[Bps[0]], serial=True)
                E("tensor", lambda e: e.transpose(out=ps[1][:, 0:128], in_=f1(Wzim), identity=ident_f[:]), reads=[Bw, Bconst], writes=[Bps[1]], serial=True)
                E("scalar", lambda e: e.copy(out=sg[:, gl, 0, :], in_=ps[0][:, 0:128]), reads=[Bps[0]], writes=[Bsg])
                E("scalar", lambda e: e.copy(out=sg[:, gl, 1, :], in_=ps[1][:, 0:128]), reads=[Bps[1]], writes=[Bsg])
                for half, pi in ((H0, 2), (H1, 3)):
                    E("tensor", lambda e: e.matmul(ps[pi][:, 0:128], lhsT=f1(Gre)[half, :], rhs=f1(CLre)[half, :], start=True, stop=False),
                      reads=[Bw], writes=[Bps[pi]], serial=True)
                    E("tensor", lambda e: e.matmul(ps[pi][:, 0:128], lhsT=f1(Gimn)[half, :], rhs=f1(CLim)[half, :], start=False, stop=True),
                      reads=[Bw], writes=[Bps[pi]], serial=True)
                E("vector", lambda e: e.tensor_tensor(out=tA[:], in0=ps[2][:, 0:128], in1=mL[:], op=ALU.mult), reads=[Bps[2], Bin], writes=[BtA])
                E("vector", lambda e: e.tensor_tensor(out=tB[:], in0=ps[3][:, 0:128], in1=mU[:], op=ALU.mult), reads=[Bps[3], Bin], writes=[BtA])
                E("vector", lambda e: e.tensor_tensor(out=tA[:], in0=tA[:], in1=tB[:], op=ALU.add), reads=[BtA], writes=[BtA])
                E("vector", lambda e: e.scalar_tensor_tensor(out=sg[:, gl, 2, :], in0=ident_f[:], scalar=dug[:, g:g + 1], in1=tA[:],
                                                             op0=ALU.mult, op1=ALU.add), reads=[BtA, Bin, Bconst], writes=[Bsg])
            E("sync", lambda e: e.dma_start(out=d["WS_s"][:, gs, :, :], in_=sg[:]), reads=[Bsg], writes=[BWS], dma=True)


def _complex_sq(c, eng, t, tmp, bufs):
    E = c["E"]
    E(eng, lambda e: e.tensor_tensor(out=tmp[:, 0, :], in0=t[:, 0, :], in1=t[:, 0, :], op=ALU.mult), reads=bufs, writes=bufs)
    E(eng, lambda e: e.tensor_tensor(out=tmp[:, 1, :], in0=t[:, 1, :], in1=t[:, 1, :], op=ALU.mult), reads=bufs, writes=bufs)
    E(eng, lambda e: e.scalar_tensor_tensor(out=t[:, 1, :], in0=t[:, 0, :], scalar=2.0, in1=t[:, 1, :], op0=ALU.mult, op1=ALU.mult), reads=bufs, writes=bufs)
    E(eng, lambda e: e.tensor_tensor(out=t[:, 0, :], in0=tmp[:, 0, :], in1=tmp[:, 1, :], op=ALU.subtract), reads=bufs, writes=bufs)


def phase_ssm_scan(c, d, own):
    nc, E, ps, Bps, sbt, buf, vcol = c["nc"], c["E"], c["ps"], c["Bps"], c["sbt"], c["buf"], c["vcol"]
    Bconst, BA8, BSin = c["Bconst"], c["BA8"], c["BSin"]
    A8a, A8b, A8c, Sin_t = c["A8a"], c["A8b"], c["A8c"], c["Sin_t"]
    es = c["es"]
    H0, H1 = slice(0, 64), slice(64, 128)
    if "Est" not in c:
        c["Est"] = sbt(es, "Est", [128, 3, 2, 64], F32)
        c["BEst"] = Buf()
        c["A64"] = sbt(es, "A64", [128, 2, 64], F32)
        c["Aslot"] = sbt(es, "Aslot", [128, 2, 64], F32)
        c["BApow"] = Buf()
        tmpq = sbt(es, "tmpq", [128, 2, 64], F32)
        Bq = [c["BApow"], BA8]
        E("vector", lambda e: e.tensor_copy(out=c["A64"][:], in_=A8c[:]), reads=[BA8], writes=[c["BApow"]])
        for _ in range(6):
            _complex_sq(c, "vector", c["A64"], tmpq, [c["BApow"]])
        E("vector", lambda e: e.tensor_copy(out=c["Aslot"][:], in_=c["A64"][:]), reads=[c["BApow"]], writes=[c["BApow"]])
        n = NB_OWN
        while n > 1:
            _complex_sq(c, "vector", c["Aslot"], tmpq, [c["BApow"]])
            n //= 2
    Est, BEst, A64, BApow = c["Est"], c["BEst"], c["A64"], c["BApow"]
    with ExitStack() as st:
        Wu = sbt(st, "Wu", [128, 16, 1024], BF16)
        Sel = sbt(st, "Sel", [128, 64, 128], BF16)
        WZ = sbt(st, "WZ", [128, 64, 2, 128], BF16)
        BW = Buf()
        for k in range(0, 16, 4):
            E("gpsimd", lambda e, k=k: e.dma_start(out=Wu[:, k:k + 4, :], in_=d["w_in"][:, k:k + 4, 0:1024]), writes=[BW], dma=True)
        E("gpsimd", lambda e: e.dma_start(out=Sel[:], in_=d["sel"]), writes=[BW], dma=True)
        E("sync", lambda e: e.dma_start(out=WZ[:], in_=d["WS_s"][:, :, 0:2, :]), reads=[buf("WS_s")], writes=[BW], dma=True)
        xt = sbt(st, "xt", [128, 16, BLK], F32)
        scr = sbt(st, "scr", [128, 16, BLK], BF16)
        sq = scr
        ht = sbt(st, "ht", [128, 16, BLK], BF16)
        rstd = sbt(st, "rstd", [128, BLK], F32)
        Bxt, Bht, Brstd = Buf(), Buf(), Buf()
        uT = scr[:, 0:8, :]
        BuT = Buf()
        ug = scr[:, 8:16, :].rearrange("p a (b c) -> p (a b) c", c=64)
        Bug = Buf()
        Bsq = [BuT, Bug]
        Zbs = [sbt(st, "Zb%d" % i, [128, 64, 2, 64], BF16) for i in range(2)]
        BZbs = [Buf(), Buf()]
        nblk_done = [0]
        Sal = sbt(st, "Sal", [128, 64, 2, 64], BF16)
        BSal = Buf()
        S2 = sbt(st, "S2", [128, 2, 64], F32)
        q1 = sbt(st, "q1", [128, 2, 64], F32)
        q2 = sbt(st, "q2", [128, 2, 64], F32)
        BS2 = Buf()
        accb = sbt(st, "accb", [128, 2, 64], F32)
        Pw = sbt(st, "Pw", [128, 2, 64], F32)
        m1 = sbt(st, "m1", [128, 2, 64], F32)
        m2 = sbt(st, "m2", [128, 2, 64], F32)
        m3 = sbt(st, "m3", [128, 2, 64], F32)
        Bacc = Buf()
        slots = [0] if own else [1, 2, 3]
        jobs = [(slot, bi) for slot in slots for bi in range(NB_OWN)]

        def stage_a(n):
            slot, bi = jobs[n]
            load_norm_block(c, st, d["xT_ctx"], slot * NB_OWN + bi, xt, sq, ht, rstd, Bxt, Bsq, Bht, Brstd, V_PREMIX, ps_i=0)

        def stage_b(n):
            slot, bi = jobs[n]
            Zb, BZb = Zbs[n % 2], BZbs[n % 2]
            for j in range(8):
                pj = 1 + j % 2
                for k in range(16):
                    E("tensor", lambda e, k=k: e.matmul(ps[pj][:], lhsT=Wu[:, k, j * 128:(j + 1) * 128], rhs=ht[:, k, :],
                                                        start=(k == 0), stop=(k == 15)), reads=[BW, Bht], writes=[Bps[pj]], signal=(k == 15))
                E("scalar", lambda e: e.copy(out=uT[:, j, :], in_=ps[pj][:]), reads=[Bps[pj]], writes=[BuT])
            for j in range(8):
                pj = 3 + j % 2
                uv = uT[:, j, :].rearrange("p (c t) -> p c t", t=8)
                for gl in range(8):
                    for tau in range(8):
                        E("tensor", lambda e: e.matmul(ps[pj][:, gl * 64:(gl + 1) * 64], lhsT=Sel[:, gl * 8 + tau, :], rhs=uv[:, :, tau],
                                                       start=(tau == 0), stop=(tau == 7)), reads=[BW, BuT], writes=[Bps[pj]], signal=(tau == 7))
                E("scalar", lambda e: e.copy(out=ug[:, j * 8:(j + 1) * 8, :], in_=ps[pj][:].rearrange("p (g c) -> p g c", c=64)),
                  reads=[Bps[pj]], writes=[Bug])
            if own:
                E("sync", lambda e: e.dma_start(out=d["UG_s"][bi], in_=ug), reads=[Bug], writes=[buf("UG_s")], dma=True)
            for j in range(8):
                for ri in range(2):
                    pz = 5 + ((2 * j + ri) % 3)
                    for gl in range(8):
                        g = j * 8 + gl
                        E("tensor", lambda e: e.matmul(ps[pz][:, gl * 64:(gl + 1) * 64], lhsT=WZ[:, g, ri, :], rhs=ug[:, g, :],
                                                       start=True, stop=True), reads=[BW, Bug], writes=[Bps[pz]], signal=(gl == 7))
                    pv = ps[pz][:].rearrange("q (g c) -> q g c", c=64)
                    E("scalar", lambda e: e.copy(out=Zb[H0, :, ri, j * 8:(j + 1) * 8].rearrange("q c g -> q g c"), in_=pv[H0]),
                      reads=[Bps[pz]], writes=[BZb])
                    E("scalar", lambda e: e.copy(out=Zb[H1, :, ri, j * 8:(j + 1) * 8].rearrange("q c g -> q g c"), in_=pv[H1, :, ::-1]),
                      reads=[Bps[pz]], writes=[BZb])

        def stage_scan(n):
            slot, bi = jobs[n]
            Zb, BZb = Zbs[n % 2], BZbs[n % 2]
            if bi == 0:
                E("vector", lambda e: e.memset(S2[:], 0.0), writes=[BS2])
                if own:
                    E("vector", lambda e: e.tensor_copy(out=S2[H0], in_=Sin_t[H0]), reads=[BSin], writes=[BS2])
                else:
                    E("gpsimd", lambda e: e.memset(accb[:], 0.0), writes=[Bacc])
                    E("gpsimd", lambda e: e.memset(Pw[:], 0.0), writes=[Bacc])
                    E("gpsimd", lambda e: e.memset(Pw[:, 0, :], 1.0), writes=[Bacc])
            E("vector", lambda e: e.memset(S2[H1], 0.0), writes=[BS2])
            if own:
                E("vector", lambda e: e.tensor_copy(out=Sal[H0, 0, :, :], in_=S2[H0]), reads=[BS2], writes=[BSal])
            for i in range(64):
                E("vector", lambda e: e.tensor_tensor(out=q1[:], in0=S2[:], in1=A8a[:], op=ALU.mult), reads=[BS2, BA8], writes=[BS2])
                E("vector", lambda e: e.tensor_tensor(out=q2[:], in0=S2[:, ::-1, :], in1=A8b[:], op=ALU.mult), reads=[BS2, BA8], writes=[BS2])
                E("vector", lambda e: e.tensor_tensor(out=q1[:], in0=q1[:], in1=q2[:], op=ALU.add), reads=[BS2], writes=[BS2])
                E("vector", lambda e: e.tensor_tensor(out=S2[:], in0=q1[:], in1=Zb[:, i, :, :], op=ALU.add), reads=[BS2, BZb], writes=[BS2])
                if own and i < 63:
                    E("vector", lambda e: e.tensor_copy(out=Sal[H0, i + 1, :, :], in_=S2[H0]), reads=[BS2], writes=[BSal])
            if own:
                E("sync", lambda e: e.dma_start(out=d["SF_s"][bi], in_=Sal[H0]), reads=[BSal], writes=[buf("SF_s")], dma=True)
                E("sync", lambda e: e.dma_start(out=d["ZB_s"][bi], in_=Zb[H1]), reads=[BZb], writes=[buf("ZB_s")], dma=True)
            else:
                rb = [BS2, Bacc, BApow]
                c["cmul"]("gpsimd", m3, Pw, S2, m1, m2, rb, [Bacc], rows=H1)
                E("gpsimd", lambda e: e.tensor_tensor(out=accb[H1], in0=accb[H1], in1=m3[H1], op=ALU.add), reads=rb, writes=[Bacc])
                c["cmul"]("gpsimd", m3, Pw, A64, m1, m2, rb, [Bacc], rows=H1)
                E("gpsimd", lambda e: e.tensor_copy(out=Pw[H1], in_=m3[H1]), reads=rb, writes=[Bacc])
                if bi == NB_OWN - 1:
                    so = slot - 1
                    E("vector", lambda e: e.tensor_copy(out=Est[H0, so, :, :], in_=S2[H0]), reads=[BS2], writes=[BEst])
                    E("gpsimd", lambda e: e.tensor_copy(out=Est[H1, 2 - so, :, :], in_=accb[H1]), reads=[Bacc], writes=[BEst])

        for n in range(len(jobs)):
            stage_a(n)
            if n > 0:
                stage_scan(n - 1)
            stage_b(n)
        stage_scan(len(jobs) - 1)


def phase_ssm_carry(c, d):
    E, sbt, es = c["E"], c["sbt"], c["es"]
    Est, BEst, Aslot, BApow, Sin_t, BSin, gates, Bconst = c["Est"], c["BEst"], c["Aslot"], c["BApow"], c["Sin_t"], c["BSin"], c["gates"], c["Bconst"]
    with ExitStack() as st:
        m1 = sbt(st, "cm1", [128, 2, 64], F32)
        m2 = sbt(st, "cm2", [128, 2, 64], F32)
        T = sbt(st, "cT", [128, 2, 64], F32)
        Bt = Buf()
        E("vector", lambda e: e.memset(Sin_t[:], 0.0), writes=[BSin])
        rb = [BEst, BApow, BSin, Bt, Bconst]
        for s_ in range(3):
            c["cmul"]("vector", T, Aslot, Sin_t, m1, m2, rb, [Bt])
            E("vector", lambda e: e.tensor_tensor(out=T[:], in0=T[:], in1=Est[:, s_, :, :], op=ALU.add), reads=rb, writes=[Bt])
            E("vector", lambda e: e.tensor_tensor(out=T[:], in0=T[:], in1=Sin_t[:], op=ALU.subtract), reads=rb, writes=[Bt])
            E("vector", lambda e: e.scalar_tensor_tensor(out=Sin_t[:], in0=T[:], scalar=gates[:, s_:s_ + 1], in1=Sin_t[:], op0=ALU.mult, op1=ALU.add),
              reads=rb, writes=[BSin])
        if d["EST_s"] is not None:
            E("sync", lambda e: e.dma_start(out=d["EST_s"], in_=Sin_t[:]), reads=[BSin], dma=True)


def phase_ssm_out(c, d):
    nc, E, ps, Bps, sbt, buf, vcol = c["nc"], c["E"], c["ps"], c["Bps"], c["sbt"], c["buf"], c["vcol"]
    Bconst, BA8, BSin = c["Bconst"], c["BA8"], c["BSin"]
    A8a, A8b, Sin_t, ones_b = c["A8a"], c["A8b"], c["Sin_t"], c["ones_b"]
    H0, H1 = slice(0, 64), slice(64, 128)
    with ExitStack() as st:
        WM = sbt(st, "WM", [128, 64, 3, 128], BF16)
        SelT = sbt(st, "SelT", [128, 64, 128], BF16)
        Wg = sbt(st, "Wg", [128, 8, 1024], BF16)
        BW = Buf()
        E("sync", lambda e: e.dma_start(out=WM[:], in_=d["WS_s"][:, :, 2:5, :]), reads=[buf("WS_s")], writes=[BW], dma=True)
        E("gpsimd", lambda e: e.dma_start(out=SelT[:], in_=d["selT"]), writes=[BW], dma=True)
        E("gpsimd", lambda e: e.dma_start(out=Wg[:], in_=d["w_glu"]), writes=[BW], dma=True)
        ug = [sbt(st, "ugc%d" % i, [128, 64, 64], BF16) for i in range(2)]
        Bug = [Buf(), Buf()]
        SalN = sbt(st, "SalN", [128, 64, 2, 64], BF16)
        SalR = sbt(st, "SalR", [128, 64, 2, 64], BF16)
        ZbR = sbt(st, "ZbR", [128, 64, 2, 64], BF16)
        BSalN, BSalR, BZbR = Buf(), Buf(), Buf()
        S2 = sbt(st, "S2c", [128, 2, 64], F32)
        q1 = sbt(st, "q1c", [128, 2, 64], F32)
        q2 = sbt(st, "q2c", [128, 2, 64], F32)
        BS2 = Buf()
        yg = [sbt(st, "yg%d" % i, [128, 8, 64], BF16) for i in range(2)]
        Byg = [Buf(), Buf()]
        ysf = sbt(st, "ysf", [128, 8, BLK], F32)
        ysb = sbt(st, "ysb", [128, 8, BLK], BF16)
        Bysf, Bysb = Buf(), Buf()
        ta = [sbt(st, "ta%d" % i, [128, BLK], F32) for i in range(2)]
        tb = [sbt(st, "tb%d" % i, [128, BLK], F32) for i in range(2)]
        Bta, Btb = [Buf(), Buf()], [Buf(), Buf()]
        sqn = sbt(st, "sqn", [128, 8, BLK], BF16)
        nsb = sbt(st, "nsb", [128, 8, BLK], BF16)
        rstd = sbt(st, "rstdc", [128, BLK], F32)
        Bsqn, Bnsb, Brstd = Buf(), Buf(), Buf()
        E("vector", lambda e: e.memset(S2[:], 0.0), writes=[BS2])
        E("vector", lambda e: e.tensor_copy(out=S2[H1], in_=Sin_t[H1]), reads=[BSin], writes=[BS2])
        for it, bi in enumerate(range(NB_OWN - 1, -1, -1)):
            b = it % 2
            tok = slice(bi * BLK, (bi + 1) * BLK)
            E("sync", lambda e: e.dma_start(out=ug[b][:], in_=d["UG_s"][bi]), reads=[buf("UG_s")], writes=[Bug[b]], dma=True)
            E("sync", lambda e: e.dma_start(out=SalN[H0], in_=d["SF_s"][bi]), reads=[buf("SF_s")], writes=[BSalN], dma=True)
            E("sync", lambda e: e.dma_start(out=ZbR[H1], in_=d["ZB_s"][bi]), reads=[buf("ZB_s")], writes=[BZbR], dma=True)
            E("vector", lambda e: e.tensor_copy(out=SalR[H1, 0, :, :], in_=S2[H1]), reads=[BS2], writes=[BSalR])
            for i in range(64):
                E("vector", lambda e: e.tensor_tensor(out=q1[H1], in0=S2[H1], in1=A8a[H1], op=ALU.mult), reads=[BS2, BA8], writes=[BS2])
                E("vector", lambda e: e.tensor_tensor(out=q2[H1], in0=S2[H1, ::-1, :], in1=A8b[H1], op=ALU.mult), reads=[BS2, BA8], writes=[BS2])
                E("vector", lambda e: e.tensor_tensor(out=q1[H1], in0=q1[H1], in1=q2[H1], op=ALU.add), reads=[BS2], writes=[BS2])
                E("vector", lambda e: e.tensor_tensor(out=S2[H1], in0=q1[H1], in1=ZbR[H1, i, :, :], op=ALU.add), reads=[BS2, BZbR], writes=[BS2])
                if i < 63:
                    E("vector", lambda e: e.tensor_copy(out=SalR[H1, i + 1, :, :], in_=S2[H1]), reads=[BS2], writes=[BSalR])
            E("gpsimd", lambda e: e.tensor_copy(out=SalN[H1], in_=SalR[H1, ::-1, :, :]), reads=[BSalR], writes=[BSalN])
            for j in range(8):
                py = 1 + j % 2
                for gl in range(8):
                    g = j * 8 + gl
                    o = ps[py][:, gl * 64:(gl + 1) * 64]
                    E("tensor", lambda e: e.matmul(o, lhsT=WM[:, g, 0, :], rhs=ug[b][:, g, :], start=True, stop=False), reads=[BW, Bug[b]], writes=[Bps[py]], signal=False)
                    E("tensor", lambda e: e.matmul(o, lhsT=WM[:, g, 1, :], rhs=SalN[:, :, 0, g], start=False, stop=False), reads=[BW, BSalN], writes=[Bps[py]], signal=False)
                    E("tensor", lambda e: e.matmul(o, lhsT=WM[:, g, 2, :], rhs=SalN[:, :, 1, g], start=False, stop=True), reads=[BW, BSalN], writes=[Bps[py]])
                E("scalar", lambda e: e.copy(out=yg[j % 2][:], in_=ps[py][:].rearrange("p (g c) -> p g c", c=64)), reads=[Bps[py]], writes=[Byg[j % 2]])
                pf = 3 + j % 2
                pfv = ps[pf][:].rearrange("p (c t) -> p c t", t=8)
                for tau in range(8):
                    for gl in range(8):
                        E("tensor", lambda e: e.matmul(pfv[:, :, tau], lhsT=SelT[:, gl * 8 + tau, :], rhs=yg[j % 2][:, gl, :],
                                                       start=(gl == 0), stop=(gl == 7)), reads=[BW, Byg[j % 2]], writes=[Bps[pf]], signal=(gl == 7))
                a_, b_ = ta[j % 2], tb[j % 2]
                Ba, Bb = Bta[j % 2], Btb[j % 2]
                E("scalar", lambda e: e.activation(out=a_[:], in_=ps[pf][:], func=AF.Square), reads=[Bps[pf]], writes=[Ba])
                E("vector", lambda e: e.tensor_scalar(out=a_[:], in0=a_[:], scalar1=0.044715, scalar2=1.0, op0=ALU.mult, op1=ALU.add), reads=[Ba], writes=[Ba])
                E("vector", lambda e: e.tensor_tensor(out=a_[:], in0=a_[:], in1=ps[pf][:], op=ALU.mult), reads=[Ba, Bps[pf]], writes=[Ba])
                E("scalar", lambda e: e.activation(out=b_[:], in_=a_[:], func=AF.Sigmoid, scale=1.5957691216057308), reads=[Ba], writes=[Bb])
                E("vector", lambda e: e.tensor_tensor(out=ysf[:, j, :], in0=b_[:], in1=ps[pf][:], op=ALU.mult), reads=[Bb, Bps[pf]], writes=[Bysf])
                E("gpsimd", lambda e: e.tensor_copy(out=ysb[:, j, :], in_=ysf[:, j, :]), reads=[Bysf], writes=[Bysb])
            for j2 in range(8):
                pg = 5 + j2 % 2
                for j in range(8):
                    E("tensor", lambda e: e.matmul(ps[pg][:], lhsT=Wg[:, j, j2 * 128:(j2 + 1) * 128], rhs=ysb[:, j, :], start=(j == 0), stop=(j == 7)),
                      reads=[BW, Bysb], writes=[Bps[pg]], signal=(j == 7))
                b_, Bb = tb[j2 % 2], Btb[j2 % 2]
                E("scalar", lambda e: e.activation(out=b_[:], in_=ps[pg][:], func=AF.Sigmoid), reads=[Bps[pg]], writes=[Bb])
                E("vector", lambda e: e.tensor_tensor(out=ysf[:, j2, :], in0=ysf[:, j2, :], in1=b_[:], op=ALU.mult), reads=[Bb, Bysf], writes=[Bysf])
            E("scalar", lambda e: e.activation(out=sqn[:], in_=ysf[:], func=AF.Square), reads=[Bysf], writes=[Bsqn])
            for j in range(8):
                E("tensor", lambda e: e.matmul(ps[7][:], lhsT=ones_b[:], rhs=sqn[:, j, :], start=(j == 0), stop=(j == 7)), reads=[Bconst, Bsqn], writes=[Bps[7]], signal=(j == 7))
            E("scalar", lambda e: e.activation(out=rstd[:], in_=ps[7][:], func=AF.Sqrt, scale=1.0 / 1024, bias=EPS), reads=[Bps[7]], writes=[Brstd])
            E("vector", lambda e: e.reciprocal(out=rstd[:], in_=rstd[:]), reads=[Brstd], writes=[Brstd])
            for j in range(8):
                E("vector", lambda e: e.scalar_tensor_tensor(out=nsb[:, j, :], in0=ysf[:, j, :], scalar=vcol(V_SSMOUT + j), in1=rstd[:],
                                                             op0=ALU.mult, op1=ALU.mult), reads=[Bysf, Brstd, Bconst], writes=[Bnsb])
            E("sync", lambda e: e.dma_start(out=d["NS_s"][:, :, tok], in_=nsb[:]), reads=[Bnsb], writes=[buf("NS_s")], dma=True)


def phase_attention(c, d):
    nc, E, ps, Bps, sbt, buf = c["nc"], c["E"], c["ps"], c["Bps"], c["sbt"], c["buf"]
    psbig, ones_b, Bconst = c["psbig"], c["ones_b"], c["Bconst"]
    scale = 1.0 / float(np.sqrt(128.0))
    QW = 2 * BLK
    with ExitStack() as st:
        KT = sbt(st, "KT", [128, NCTX], BF16)
        Vt = sbt(st, "Vt", [128, NKT, 128], BF16)
        mk = sbt(st, "mk", [128, NKT], F32)
        BKV, Bmk = Buf(), Buf()
        E("sync", lambda e: e.dma_start(out=mk[:], in_=d["maskb"]), writes=[Bmk], dma=True)
        qt = [sbt(st, "qt%d" % i, [128, QW], BF16) for i in range(2)]
        Bqt = [Buf(), Buf()]
        NP = 4
        pT = [sbt(st, "pT%d" % i, [128, QW], BF16) for i in range(NP)]
        BpT = [Buf() for _ in range(NP)]
        acc = [sbt(st, "acc%d" % i, [128, QW], F32) for i in range(2)]
        Bacc = [Buf(), Buf()]
        rec = sbt(st, "rec", [128, QW], F32)
        Brec = Buf()
        dhi = sbt(st, "dhi", [128, QW], BF16)
        dlo = sbt(st, "dlo", [128, QW], BF16)
        Bdh = Buf()
        yo = [sbt(st, "yo%d" % i, [128, QW], F32) for i in range(2)]
        Byo = [Buf(), Buf()]
        BS = [[Bps[0], Bps[1]], [Bps[2], Bps[3]]]
        BO = [Bps[4], Bps[5]]
        BD = [Bps[6], Bps[7]]
        it = 0
        for kvh in range(2):
            nchunk = 4
            for q in range(nchunk):
                cs_ = slice(q * NCTX // nchunk, (q + 1) * NCTX // nchunk)
                ks_ = slice(q * NKT // nchunk, (q + 1) * NKT // nchunk)
                E("sync", lambda e: e.dma_start(out=KT[:, cs_], in_=d["KT_s"][kvh, :, cs_]), reads=[buf("KT_s")], writes=[BKV], dma=True)
                E("sync", lambda e: e.dma_start(out=Vt[:, ks_, :], in_=d["V_s"][kvh, :, ks_, :]), reads=[buf("V_s")], writes=[BKV], dma=True)
            for qh in range(4):
                head = kvh * 4 + qh
                for qb in range(NB_OWN // 2):
                    b = it % 2
                    tok = slice(qb * QW, (qb + 1) * QW)
                    E("sync", lambda e: e.dma_start(out=qt[b][:], in_=d["QT_s"][head, :, tok]), reads=[buf("QT_s")], writes=[Bqt[b]], dma=True)

                    def s_mm(kt):
                        sb_ = kt % 2
                        for hf in range(2):
                            E("tensor", lambda e: e.matmul(psbig[sb_][:, hf * BLK:(hf + 1) * BLK], lhsT=KT[:, kt * 128:(kt + 1) * 128],
                                                           rhs=qt[b][:, hf * BLK:(hf + 1) * BLK], start=True, stop=True),
                              reads=[BKV, Bqt[b]], writes=[BS[sb_][hf]], signal=(hf == 1))
                    s_mm(0)
                    nd = [0, 0]
                    for kt in range(NKT):
                        if kt + 1 < NKT:
                            s_mm(kt + 1)
                        sb_ = kt % 2
                        r = kt % NP
                        E("scalar", lambda e: e.activation(out=pT[r][:], in_=psbig[sb_][:], func=AF.Exp, bias=mk[:, kt:kt + 1], scale=scale),
                          reads=BS[sb_] + [Bmk], writes=[BpT[r]])
                        for hf in range(2):
                            E("tensor", lambda e: e.matmul(psbig[2][:, hf * BLK:(hf + 1) * BLK], lhsT=Vt[:, kt, :], rhs=pT[r][:, hf * BLK:(hf + 1) * BLK],
                                                           start=(kt == 0), stop=(kt == NKT - 1)),
                              reads=[BKV, BpT[r]], writes=[BO[hf]], signal=(kt == NKT - 1 and hf == 1))
                        if kt % 2 == 0:
                            for hf in range(2):
                                hs = slice(hf * BLK, (hf + 1) * BLK)
                                E("tensor", lambda e: e.matmul(psbig[3][:, hs], lhsT=ones_b[:], rhs=pT[r][:, hs], start=(kt == 0), stop=False),
                                  reads=[Bconst, BpT[r]], writes=[BD[hf]], signal=False)
                        else:
                            a = (kt // 2) % 2
                            eng = "vector" if a == 0 else "gpsimd"
                            if nd[a] == 0:
                                E(eng, lambda e: e.tensor_copy(out=acc[a][:], in_=pT[r][:]), reads=[BpT[r]], writes=[Bacc[a]])
                            else:
                                E(eng, lambda e: e.tensor_tensor(out=acc[a][:], in0=acc[a][:], in1=pT[r][:], op=ALU.add), reads=[BpT[r], Bacc[a]], writes=[Bacc[a]])
                            nd[a] += 1
                    E("vector", lambda e: e.tensor_tensor(out=acc[0][:], in0=acc[0][:], in1=acc[1][:], op=ALU.add), reads=[Bacc[0], Bacc[1]], writes=[Bacc[0]])
                    E("vector", lambda e: e.tensor_copy(out=dhi[:], in_=acc[0][:]), reads=[Bacc[0]], writes=[Bdh])
                    E("vector", lambda e: e.tensor_tensor(out=acc[0][:], in0=acc[0][:], in1=dhi[:], op=ALU.subtract), reads=[Bdh, Bacc[0]], writes=[Bacc[0]])
                    E("vector", lambda e: e.tensor_copy(out=dlo[:], in_=acc[0][:]), reads=[Bacc[0]], writes=[Bdh])
                    for hf in range(2):
                        hs = slice(hf * BLK, (hf + 1) * BLK)
                        E("tensor", lambda e: e.matmul(psbig[3][:, hs], lhsT=ones_b[:], rhs=dhi[:, hs], start=False, stop=False), reads=[Bconst, Bdh], writes=[BD[hf]], signal=False)
                        E("tensor", lambda e: e.matmul(psbig[3][:, hs], lhsT=ones_b[:], rhs=dlo[:, hs], start=False, stop=True), reads=[Bconst, Bdh], writes=[BD[hf]])
                    E("vector", lambda e: e.reciprocal(out=rec[:], in_=psbig[3][:]), reads=BD, writes=[Brec])
                    E("vector", lambda e: e.tensor_tensor(out=yo[b][:], in0=psbig[2][:], in1=rec[:], op=ALU.mult), reads=BO + [Brec], writes=[Byo[b]])
                    E("sync", lambda e: e.dma_start(out=d["YA_s"][head, :, tok], in_=yo[b][:]), reads=[Byo[b]], writes=[buf("YA_s")], dma=True)
                    it += 1


def phase_weight_cast(c, d):
    E, buf = c["E"], c["buf"]
    for f in range(16):
        E("gpsimd", lambda e: e.dma_start(out=d["WOUT_b"][f].rearrange("p k c -> p (k c)"), in_=d["w_out"][f].rearrange("p k c -> p (k c)")),
          writes=[buf("WOUT_b")], dma=True)
    for f in range(64):
        E("gpsimd", lambda e: e.dma_start(out=d["WUP_b"][f].rearrange("p k c -> p (k c)"), in_=d["w_up"][f].rearrange("p k c -> p (k c)")),
          writes=[buf("WUP_b")], dma=True)
    for f in range(16):
        for h in range(2):
            E("gpsimd", lambda e: e.dma_start(out=d["WDN_b"][f, h].rearrange("p k c -> p (k c)"), in_=d["w_dn"][f, h].rearrange("p k c -> p (k c)")),
              writes=[buf("WDN_b")], dma=True)


def phase_mlp(c, d):
    nc, E, ps, Bps, sbt, buf, vcol = c["nc"], c["E"], c["ps"], c["Bps"], c["sbt"], c["buf"], c["vcol"]
    ones_b, Bconst = c["ones_b"], c["Bconst"]
    with ExitStack() as st:
        XT = sbt(st, "XT", [128, 16, BLK], F32)
        MT = sbt(st, "MT", [128, 16, BLK], F32)
        ACTb = sbt(st, "ACTb", [128, 16, BLK], BF16)
        AT = sbt(st, "AT", [128, 32, BLK], BF16)
        YA = sbt(st, "YA", [128, 8, BLK], F32)
        BXT, BMT, BACT, BAT, BYA = Buf(), Buf(), Buf(), Buf(), Buf()
        rstd = sbt(st, "rstde", [128, BLK], F32)
        Brstd = Buf()
        tmp = [sbt(st, "tmpe%d" % i, [128, BLK], F32) for i in range(2)]
        Btmp = [Buf(), Buf()]
        wo = [sbt(st, "wo%d" % i, [128, 16, 128], BF16) for i in range(3)]
        Bwo = [Buf() for _ in range(3)]
        wu = [sbt(st, "wu%d" % i, [128, 16, 128], BF16) for i in range(3)]
        Bwu = [Buf() for _ in range(3)]
        wd = [sbt(st, "wd%d" % i, [128, 32, 128], BF16) for i in range(3)]
        Bwd = [Buf() for _ in range(3)]
        src = d["xT_ctx"].rearrange("(k p) t -> p k t", p=128)
        SQ = AT[:, 0:16, :]

        def stats(srct, Bsrc, nk, denom, pi):
            E("scalar", lambda e: e.activation(out=SQ[:, 0:nk, :], in_=srct, func=AF.Square), reads=[Bsrc], writes=[BAT])
            for k in range(nk):
                E("tensor", lambda e, k=k: e.matmul(ps[pi][:], lhsT=ones_b[:], rhs=SQ[:, k, :], start=(k == 0), stop=(k == nk - 1)),
                  reads=[Bconst, BAT], writes=[Bps[pi]], signal=(k == nk - 1))
            E("scalar", lambda e: e.activation(out=rstd[:], in_=ps[pi][:], func=AF.Sqrt, scale=1.0 / denom, bias=EPS), reads=[Bps[pi]], writes=[Brstd])
            E("vector", lambda e: e.reciprocal(out=rstd[:], in_=rstd[:]), reads=[Brstd], writes=[Brstd])

        for blk in range(NB_OWN):
            tok = slice(blk * BLK, (blk + 1) * BLK)
            E("sync", lambda e: e.dma_start(out=XT[:, 0:8, :], in_=src[:, 0:8, tok]), writes=[BXT], dma=True)
            E("sync", lambda e: e.dma_start(out=XT[:, 8:16, :], in_=src[:, 8:16, tok]), writes=[BXT], dma=True)
            E("sync", lambda e: e.dma_start(out=ACTb[:, 0:8, :], in_=d["NS_s"][:, :, tok]), reads=[buf("NS_s")], writes=[BACT], dma=True)
            E("sync", lambda e: e.dma_start(out=YA[:], in_=d["YA_s"][:, :, tok].rearrange("h p t -> p h t")), reads=[buf("YA_s")], writes=[BYA], dma=True)
            stats(YA[:], BYA, 8, 1024.0, 0)
            for j in range(8):
                E("vector", lambda e: e.scalar_tensor_tensor(out=ACTb[:, 8 + j, :], in0=YA[:, j, :], scalar=vcol(V_ATTOUT + j), in1=rstd[:],
                                                             op0=ALU.mult, op1=ALU.mult), reads=[BYA, Brstd, Bconst], writes=[BACT])
            for dt in range(16):
                w, Bw = wo[dt % 3], Bwo[dt % 3]
                E("sync", lambda e: e.dma_start(out=w[:], in_=d["WOUT_b"][dt]), reads=[buf("WOUT_b")], writes=[Bw], dma=True)
                pi = 1 + dt % 2
                for k in range(16):
                    E("tensor", lambda e, k=k: e.matmul(ps[pi][:], lhsT=w[:, k, :], rhs=ACTb[:, k, :], start=(k == 0), stop=(k == 15)),
                      reads=[Bw, BACT], writes=[Bps[pi]], signal=(k == 15))
                E("scalar", lambda e: e.copy(out=MT[:, dt, :], in_=ps[pi][:]), reads=[Bps[pi]], writes=[BMT])
            stats(MT[:], BMT, 16, 2048.0, 0)
            for k in range(16):
                t, Bt = tmp[k % 2], Btmp[k % 2]
                E("vector", lambda e: e.scalar_tensor_tensor(out=t[:], in0=MT[:, k, :], scalar=vcol(V_POSTMIX + k), in1=rstd[:],
                                                             op0=ALU.mult, op1=ALU.mult), reads=[BMT, Brstd, Bconst], writes=[Bt])
                E("gpsimd", lambda e: e.tensor_tensor(out=XT[:, k, :], in0=XT[:, k, :], in1=t[:], op=ALU.add), reads=[Bt, BXT], writes=[BXT])
            stats(XT[:], BXT, 16, 2048.0, 0)
            for k in range(16):
                E("vector", lambda e: e.scalar_tensor_tensor(out=ACTb[:, k, :], in0=XT[:, k, :], scalar=vcol(V_PREMLP + k), in1=rstd[:],
                                                             op0=ALU.mult, op1=ALU.mult), reads=[BXT, Brstd, Bconst], writes=[BACT])
            for half in range(2):
                for f in range(32):
                    ff = half * 32 + f
                    w, Bw = wu[ff % 3], Bwu[ff % 3]
                    E("sync", lambda e: e.dma_start(out=w[:], in_=d["WUP_b"][ff]), reads=[buf("WUP_b")], writes=[Bw], dma=True)
                    pi = 3 + ff % 2
                    for k in range(16):
                        E("tensor", lambda e, k=k: e.matmul(ps[pi][:], lhsT=w[:, k, :], rhs=ACTb[:, k, :], start=(k == 0), stop=(k == 15)),
                          reads=[Bw, BACT], writes=[Bps[pi]], signal=(k == 15))
                    t, Bt = tmp[ff % 2], Btmp[ff % 2]
                    E("scalar", lambda e: e.activation(out=t[:], in_=ps[pi][:], func=AF.Relu), reads=[Bps[pi]], writes=[Bt])
                    E("gpsimd" if ff % 2 else "vector", lambda e: e.tensor_tensor(out=AT[:, f, :], in0=t[:], in1=t[:], op=ALU.mult), reads=[Bt], writes=[BAT])
                for dt in range(16):
                    i3 = (half * 16 + dt) % 3
                    w, Bw = wd[i3], Bwd[i3]
                    E("sync", lambda e: e.dma_start(out=w[:], in_=d["WDN_b"][dt, half]), reads=[buf("WDN_b")], writes=[Bw], dma=True)
                    pi = 5 + dt % 2
                    for f in range(32):
                        E("tensor", lambda e, f=f: e.matmul(ps[pi][:], lhsT=w[:, f, :], rhs=AT[:, f, :], start=(f == 0), stop=(f == 31)),
                          reads=[Bw, BAT], writes=[Bps[pi]], signal=(f == 31))
                    if half == 0:
                        E("scalar", lambda e: e.copy(out=MT[:, dt, :], in_=ps[pi][:]), reads=[Bps[pi]], writes=[BMT])
                    else:
                        E("vector", lambda e: e.tensor_tensor(out=MT[:, dt, :], in0=MT[:, dt, :], in1=ps[pi][:], op=ALU.add), reads=[Bps[pi], BMT], writes=[BMT])
            stats(MT[:], BMT, 16, 2048.0, 0)
            for k in range(16):
                t, Bt = tmp[k % 2], Btmp[k % 2]
                E("vector", lambda e: e.scalar_tensor_tensor(out=t[:], in0=MT[:, k, :], scalar=vcol(V_POSTMLP + k), in1=rstd[:],
                                                             op0=ALU.mult, op1=ALU.mult), reads=[BMT, Brstd, Bconst], writes=[Bt])
                E("gpsimd", lambda e: e.tensor_tensor(out=XT[:, k, :], in0=XT[:, k, :], in1=t[:], op=ALU.add), reads=[Bt, BXT], writes=[BXT])
            dst = d["yT_out"].rearrange("(k p) t -> p k t", p=128)
            E("sync", lambda e: e.dma_start(out=dst[:, 0:8, tok], in_=XT[:, 0:8, :]), reads=[BXT], dma=True)
            E("sync", lambda e: e.dma_start(out=dst[:, 8:16, tok], in_=XT[:, 8:16, :]), reads=[BXT], dma=True)


_STAGES = os.environ.get("MK_STAGES", "PWABSCDE")


def run_cores(inputs, stages=_STAGES, debug=()):
    sh = _prep_shared(inputs)
    in_maps = [_prep_core(inputs, c, sh) for c in range(8)]
    nc, kb = build_program(stages, debug)
    res = run_bass_kernel_spmd(nc, in_maps, core_ids=list(range(8)))
    return res, kb


def kernel(**inputs):
    res, _ = run_cores(inputs)
    yp = np.stack([np.ascontiguousarray(res.results[c]["yT_out"].T) for c in range(4)], axis=0)
    ys = np.concatenate([res.results[4 + j]["yT_out"].T for j in range(4)], axis=0)[None]
    return (np.ascontiguousarray(yp.astype(np.float32)), np.ascontiguousarray(ys.astype(np.float32)))
```

```python
import os
import numpy as np
from contextlib import ExitStack
import concourse.bass as bass
import concourse.mybir as mybir
from concourse.bass_utils import run_bass_kernel_spmd

F32 = mybir.dt.float32
BF16 = mybir.dt.bfloat16
AF = mybir.ActivationFunctionType
ALU = mybir.AluOpType

NT = int(os.environ.get("MK_NT", "4096"))
NCTX = 4 * NT
BLK = 512
NB_OWN = NT // BLK
NB_CTX = NCTX // BLK
NKT = NCTX // 128
NROW = NCTX // 64
EPS = 1e-6
MASK_NEG = -30000.0

V_PREMIX, V_POSTMIX, V_PREMLP, V_POSTMLP, V_SSMOUT, V_ATTOUT, V_QN, V_KN, V_D = 0, 16, 32, 48, 64, 72, 80, 81, 82
NVEC = 90


class Buf:
    __slots__ = ("writers", "readers")

    def __init__(self):
        self.writers = {}
        self.readers = {}


class KB:
    ENGS = ("sync", "tensor", "vector", "scalar", "gpsimd")

    NDMA = 20

    def __init__(self, nc, es):
        self.nc = nc
        self.sems = {}
        self.counts = {}
        self.waited = {e: {} for e in self.ENGS}
        self.nops = {e: 0 for e in self.ENGS}
        self.rr = {e: 0 for e in self.ENGS}
        self.pe_prev_serial = False
        for e in self.ENGS:
            if e != "sync":
                self.sems[e] = es.enter_context(nc.semaphore("s_" + e))
                self.counts[e] = 0
        for e in ("sync", "gpsimd"):
            for i in range(self.NDMA):
                k = "%s_dma%d" % (e, i)
                self.sems[k] = es.enter_context(nc.semaphore("s_" + k))
                self.counts[k] = 0

    def emit(self, eng, fn, reads=(), writes=(), dma=False, signal=True, serial=False):
        need = {}
        for b in reads:
            for s, v in b.writers.items():
                if need.get(s, 0) < v:
                    need[s] = v
        for b in writes:
            for s, v in b.writers.items():
                if need.get(s, 0) < v:
                    need[s] = v
            for s, v in b.readers.items():
                if need.get(s, 0) < v:
                    need[s] = v
        e = getattr(self.nc, eng)
        w = self.waited[eng]
        if dma:
            key = "%s_dma%d" % (eng, self.rr[eng] % self.NDMA)
            self.rr[eng] += 1
            if self.counts[key] > need.get(key, 0):
                need[key] = self.counts[key]
        else:
            key = eng
        if eng == "tensor":
            if serial or self.pe_prev_serial:
                need["tensor"] = self.counts["tensor"]
            else:
                need.pop("tensor", None)
            self.pe_prev_serial = serial
        for s, v in need.items():
            if w.get(s, 0) < v:
                w[s] = v
                e.wait_ge(self.sems[s], v)
        inc = 16 if dma else 1
        if signal:
            self.counts[key] += inc
            val = self.counts[key]
            ins = fn(e)
            ins.then_inc(self.sems[key], inc)
        else:
            assert eng == "tensor" and not dma
            val = self.counts[key] + 1
            fn(e)
        self.nops[eng] += 1
        for b in writes:
            b.writers[key] = val
        for b in reads:
            b.readers[key] = val
        return (key, val)

    def barrier(self):
        for eng in self.ENGS:
            e = getattr(self.nc, eng)
            w = self.waited[eng]
            for k, v in self.counts.items():
                if v > 0 and w.get(k, 0) < v:
                    w[k] = v
                    e.wait_ge(self.sems[k], v)

    def finish(self):
        for eng in ("sync", "gpsimd"):
            for i in range(self.NDMA):
                k = "%s_dma%d" % (eng, i)
                if self.counts[k] > 0:
                    getattr(self.nc, eng).wait_ge(self.sems[k], self.counts[k])


def _rope_compact(pos_of_ctx_row):
    inv = (np.float32(10000.0) ** (-(np.arange(0, 64, 2, dtype=np.float32)) / np.float32(64))).astype(np.float32)
    f = np.arange(64) % 32
    cos = np.zeros((128, NROW), np.float32)
    sin = np.zeros((128, NROW), np.float32)
    rows = pos_of_ctx_row.astype(np.float32)
    ang = (rows[None, :] * inv[f][:, None]).astype(np.float32)
    cos[:64], sin[:64] = np.cos(ang), np.sin(ang)
    cols = np.arange(64, dtype=np.float32)
    angc = (cols[None, :] * inv[f][:, None]).astype(np.float32)
    cos[64:, :64], sin[64:, :64] = np.cos(angc), np.sin(angc)
    return cos, sin


def _consts():
    c = {}
    rp = np.zeros((128, 128), np.float32)
    for m in range(128):
        j = m % 64
        if j < 32:
            rp[m + 32, m] = -1.0
        else:
            rp[m - 32, m] = 1.0
    c["rperm"] = rp
    c["ident"] = np.eye(128, dtype=np.float32)
    sel = np.zeros((128, 64, 128), np.float32)
    selT = np.zeros((128, 64, 128), np.float32)
    for gl in range(8):
        for tau in range(8):
            for h in range(16):
                sel[gl * 16 + h, gl * 8 + tau, tau * 16 + h] = 1.0
                selT[tau * 16 + h, gl * 8 + tau, gl * 16 + h] = 1.0
    c["sel"] = sel
    c["selT"] = selT
    tp = np.arange(128) // 16
    c["maskL"] = (tp[:, None] <= tp[None, :]).astype(np.float32)
    c["maskU"] = (tp[:, None] >= tp[None, :]).astype(np.float32)
    return c


def _prep_shared(inp):
    f = lambda a: np.ascontiguousarray(np.asarray(a, dtype=np.float32))
    sh = {}
    w_in = f(inp["w_in"])[0]
    sh["w_in_t"] = f(w_in.reshape(16, 128, 2560).transpose(1, 0, 2))
    sh["w_glu_t"] = f(f(inp["w_glu"])[0].reshape(8, 128, 1024).transpose(1, 0, 2))
    w_out = f(inp["w_out"])[0]
    sh["w_out_t"] = f(w_out.reshape(16, 128, 16, 128).transpose(2, 1, 0, 3))
    w_up = f(inp["w_up"])[0]
    sh["w_up_t"] = f(w_up.reshape(16, 128, 64, 128).transpose(2, 1, 0, 3))
    w_dn = f(inp["w_down"])[0]
    sh["w_dn_t"] = f(w_dn.reshape(2, 32, 128, 16, 128).transpose(3, 0, 2, 1, 4))
    vec = np.zeros((128, NVEC), np.float32)
    pk = lambda v, n: f(v).reshape(n, 128).T
    vec[:, V_PREMIX:V_PREMIX + 16] = pk(inp["pre_mix_norm"], 16)
    vec[:, V_POSTMIX:V_POSTMIX + 16] = pk(inp["post_mix_norm"], 16)
    vec[:, V_PREMLP:V_PREMLP + 16] = pk(inp["pre_mlp_norm"], 16)
    vec[:, V_POSTMLP:V_POSTMLP + 16] = pk(inp["post_mlp_norm"], 16)
    vec[:, V_SSMOUT:V_SSMOUT + 8] = pk(inp["ssm_out_norm"], 8)
    vec[:, V_ATTOUT:V_ATTOUT + 8] = pk(inp["attn_out_norm"], 8)
    vec[:, V_QN] = f(inp["q_norm"])[0]
    vec[:, V_KN] = f(inp["k_norm"])[0]
    vec[:, V_D:V_D + 8] = pk(inp["ssm_d"], 8)
    sh["vecs"] = vec
    a_re = f(inp["ssm_a_re"])[0]
    a_im = f(inp["ssm_a_im"])[0]
    ldt = f(inp["ssm_log_dt"])[0]
    sh["ssm_ar"] = f(a_re.transpose(0, 2, 1).reshape(128, 64))
    sh["ssm_ai"] = f(a_im.transpose(0, 2, 1).reshape(128, 64))
    sh["ssm_ldt"] = f(np.broadcast_to(ldt[:, None, :], (2, 64, 64)).reshape(128, 64))
    sh["ssm_bre"] = f(f(inp["ssm_b_re"])[0].transpose(0, 2, 1, 3).reshape(128, 64, 16))
    sh["ssm_bim"] = f(f(inp["ssm_b_im"])[0].transpose(0, 2, 1, 3).reshape(128, 64, 16))
    sh["ssm_cre"] = f(f(inp["ssm_c_re"])[0].transpose(0, 3, 1, 2).reshape(128, 64, 16))
    sh["ssm_cim"] = f(f(inp["ssm_c_im"])[0].transpose(0, 3, 1, 2).reshape(128, 64, 16))
    d = f(inp["ssm_d"])[0].reshape(64, 16)
    sh["ssm_dug"] = f(np.tile(d.T, (8, 1)))
    sh.update(_consts())
    return sh


def _prep_core(inp, core, sh):
    m = dict(sh)
    ctx = np.zeros((2048, NCTX), np.float32)
    mask = np.zeros((NCTX,), np.float32)
    gates = np.zeros((128, 3), np.float32)
    if core < 4:
        x = np.asarray(inp["x_prompt"], np.float32)[core]
        ctx[:, :NT] = x.T
        mask[NT:] = MASK_NEG
        slots = [0, 0, 0, 0]
    else:
        slot = core - 4
        xs = np.asarray(inp["x_sample"], np.float32)[0]
        others = [j for j in range(4) if j != slot]
        slots = [slot] + others
        for i, j in enumerate(slots):
            ctx[:, i * NT:(i + 1) * NT] = xs[j * NT:(j + 1) * NT].T
        for s_ in range(3):
            gates[:64, s_] = 1.0 if s_ < slot else 0.0
            gates[64:, s_] = 1.0 if (2 - s_) >= slot else 0.0
    m["xT_ctx"] = ctx
    m["maskb"] = np.ascontiguousarray(mask.reshape(NKT, 128).T)
    rows = np.concatenate([np.arange(j * NT // 64, (j + 1) * NT // 64) for j in slots])
    m["rope_cos"], m["rope_sin"] = _rope_compact(rows)
    m["gates"] = gates
    return m


def build_program(stages="WAB", debug=()):
    nc = bass.Bass("TRN2", target_bir_lowering=False)
    dbg = set(debug)

    def din(name, shape):
        return nc.dram_tensor(name, list(shape), F32, kind="ExternalInput").ap()

    def dscr(name, shape, dt=BF16):
        kind = "ExternalOutput" if name in dbg else "Internal"
        return nc.dram_tensor(name, list(shape), dt, kind=kind).ap()

    xT_ctx = din("xT_ctx", [2048, NCTX])
    rope_cos_d, rope_sin_d = din("rope_cos", [128, NROW]), din("rope_sin", [128, NROW])
    maskb_d = din("maskb", [128, NKT])
    gates_d = din("gates", [128, 3])
    vecs_d = din("vecs", [128, NVEC])
    w_in_d = din("w_in_t", [128, 16, 2560])
    w_glu_d = din("w_glu_t", [128, 8, 1024])
    has_mlp = ("P" in stages) or ("E" in stages)
    w_out_d = din("w_out_t", [16, 128, 16, 128]) if has_mlp else None
    w_up_d = din("w_up_t", [64, 128, 16, 128]) if has_mlp else None
    w_dn_d = din("w_dn_t", [16, 2, 128, 32, 128]) if has_mlp else None
    ssm_in = {k: din(k, s) for k, s in (("ssm_ar", [128, 64]), ("ssm_ai", [128, 64]), ("ssm_ldt", [128, 64]),
                                        ("ssm_bre", [128, 64, 16]), ("ssm_bim", [128, 64, 16]),
                                        ("ssm_cre", [128, 64, 16]), ("ssm_cim", [128, 64, 16]),
                                        ("ssm_dug", [128, 64]))}
    rperm_d, ident_d = din("rperm", [128, 128]), din("ident", [128, 128])
    sel_d, selT_d = din("sel", [128, 64, 128]), din("selT", [128, 64, 128])
    maskL_d, maskU_d = din("maskL", [128, 128]), din("maskU", [128, 128])
    yT_out = nc.dram_tensor("yT_out", [2048, NT], F32, kind="ExternalOutput").ap()

    KT_s = dscr("KT_s", [2, 128, NCTX])
    V_s = dscr("V_s", [2, 128, NKT, 128])
    QT_s = dscr("QT_s", [8, 128, NT])
    WS_s = dscr("WS_s", [128, 64, 5, 128])
    UG_s = dscr("UG_s", [NB_OWN, 128, 64, 64])
    SF_s = dscr("SF_s", [NB_OWN, 64, 64, 2, 64])
    ZB_s = dscr("ZB_s", [NB_OWN, 64, 64, 2, 64])
    NS_s = dscr("NS_s", [128, 8, NT])
    YA_s = dscr("YA_s", [8, 128, NT], F32)
    WOUT_b = dscr("WOUT_b", [16, 128, 16, 128])
    WUP_b = dscr("WUP_b", [64, 128, 16, 128])
    WDN_b = dscr("WDN_b", [16, 2, 128, 32, 128])
    EST_s = dscr("EST_s", [128, 2, 64], F32)

    es = ExitStack()
    kb = KB(nc, es)
    E = kb.emit
    B = {}

    def buf(name):
        if name not in B:
            B[name] = Buf()
        return B[name]

    uid = [0]

    def sbt(st, name, shape, dt):
        uid[0] += 1
        return st.enter_context(nc.sbuf_tensor("sb%d_%s" % (uid[0], name), list(shape), dt))

    psbig = [es.enter_context(nc.psum_tensor("psb%d" % i, [128, 1024], F32)) for i in range(4)]
    ps = [psbig[i // 2][:, (i % 2) * 512:(i % 2 + 1) * 512] for i in range(8)]
    Bps = [Buf() for _ in range(8)]
    ones_b = sbt(es, "ones_b", [128, 128], BF16)
    ones_f = sbt(es, "ones_f", [128, 128], F32)
    ident_f = sbt(es, "ident_f", [128, 128], F32)
    rperm_b = sbt(es, "rperm_b", [128, 128], BF16)
    vecs = sbt(es, "vecs", [128, NVEC], F32)
    gates = sbt(es, "gates", [128, 3], F32)
    rope_c = sbt(es, "rope_c", [128, NROW], F32)
    rope_s = sbt(es, "rope_s", [128, NROW], F32)
    A8a = sbt(es, "A8a", [128, 2, 64], F32)
    A8b = sbt(es, "A8b", [128, 2, 64], F32)
    A8c = sbt(es, "A8c", [128, 2, 64], F32)
    Sin_t = sbt(es, "Sin_t", [128, 2, 64], F32)
    Bconst = Buf()
    BA8 = Buf()
    BSin = Buf()

    E("vector", lambda e: e.memset(ones_b[:], 1.0), writes=[Bconst])
    E("vector", lambda e: e.memset(ones_f[:], 1.0), writes=[Bconst])
    E("sync", lambda e: e.dma_start(out=ident_f[:], in_=ident_d), writes=[Bconst], dma=True)
    E("gpsimd", lambda e: e.dma_start(out=rperm_b[:], in_=rperm_d), writes=[Bconst], dma=True)
    E("sync", lambda e: e.dma_start(out=vecs[:], in_=vecs_d), writes=[Bconst], dma=True)
    E("sync", lambda e: e.dma_start(out=gates[:], in_=gates_d), writes=[Bconst], dma=True)
    E("sync", lambda e: e.dma_start(out=rope_c[:], in_=rope_cos_d), writes=[Bconst], dma=True)
    E("sync", lambda e: e.dma_start(out=rope_s[:], in_=rope_sin_d), writes=[Bconst], dma=True)

    def vcol(c0, n=1):
        return vecs[:, c0:c0 + n]

    def cmul(eng, out, a, b, tmp1, tmp2, bufs_r, bufs_w, rows=slice(0, 128)):
        r = rows
        E(eng, lambda e: e.tensor_tensor(out=tmp1[r, 0, :], in0=a[r, 0, :], in1=b[r, 0, :], op=ALU.mult), reads=bufs_r, writes=bufs_w)
        E(eng, lambda e: e.tensor_tensor(out=tmp1[r, 1, :], in0=a[r, 1, :], in1=b[r, 1, :], op=ALU.mult), reads=bufs_r, writes=bufs_w)
        E(eng, lambda e: e.tensor_tensor(out=tmp2[r, 0, :], in0=a[r, 0, :], in1=b[r, 1, :], op=ALU.mult), reads=bufs_r, writes=bufs_w)
        E(eng, lambda e: e.tensor_tensor(out=tmp2[r, 1, :], in0=a[r, 1, :], in1=b[r, 0, :], op=ALU.mult), reads=bufs_r, writes=bufs_w)
        E(eng, lambda e: e.tensor_tensor(out=out[r, 0, :], in0=tmp1[r, 0, :], in1=tmp1[r, 1, :], op=ALU.subtract), reads=bufs_r, writes=bufs_w)
        E(eng, lambda e: e.tensor_tensor(out=out[r, 1, :], in0=tmp2[r, 0, :], in1=tmp2[r, 1, :], op=ALU.add), reads=bufs_r, writes=bufs_w)

    ctx = dict(nc=nc, kb=kb, E=E, es=es, ps=ps, psbig=psbig, Bps=Bps, buf=buf, sbt=sbt, vcol=vcol, cmul=cmul,
               ones_b=ones_b, ones_f=ones_f, ident_f=ident_f, rperm_b=rperm_b, vecs=vecs, gates=gates, rope_c=rope_c, rope_s=rope_s,
               A8a=A8a, A8b=A8b, A8c=A8c, Sin_t=Sin_t, Bconst=Bconst, BA8=BA8, BSin=BSin)
    dr = dict(xT_ctx=xT_ctx, maskb=maskb_d, w_in=w_in_d, w_glu=w_glu_d, w_out=w_out_d, w_up=w_up_d, w_dn=w_dn_d, ssm=ssm_in,
              sel=sel_d, selT=selT_d, maskL=maskL_d, maskU=maskU_d, yT_out=yT_out,
              KT_s=KT_s, V_s=V_s, QT_s=QT_s, WS_s=WS_s, UG_s=UG_s, SF_s=SF_s, ZB_s=ZB_s, NS_s=NS_s, YA_s=YA_s,
              WOUT_b=WOUT_b, WUP_b=WUP_b, WDN_b=WDN_b, EST_s=EST_s)

    if "P" in stages:
        phase_weight_cast(ctx, dr)
    if "W" in stages:
        phase_ssm_weights(ctx, dr)
        kb.barrier()
    if "A" in stages:
        phase_proj(ctx, dr, own=False)
        kb.barrier()
    if "B" in stages:
        phase_proj(ctx, dr, own=True)
        kb.barrier()
    if "S" in stages:
        phase_ssm_scan(ctx, dr, own=False)
        kb.barrier()
        phase_ssm_carry(ctx, dr)
        kb.barrier()
        phase_ssm_scan(ctx, dr, own=True)
        kb.barrier()
    if "C" in stages:
        phase_ssm_out(ctx, dr)
        kb.barrier()
    if "D" in stages:
        phase_attention(ctx, dr)
        kb.barrier()
    if "E" in stages:
        phase_mlp(ctx, dr)
    kb.finish()
    es.close()
    return nc, kb


def load_x_block(c, xT_src, blk, xt, Bxt):
    E = c["E"]
    sl = slice(blk * BLK, (blk + 1) * BLK)
    src = xT_src.rearrange("(k p) t -> p k t", p=128)
    E("sync", lambda e: e.dma_start(out=xt[:, 0:8, :], in_=src[:, 0:8, sl]), writes=[Bxt], dma=True)
    E("sync", lambda e: e.dma_start(out=xt[:, 8:16, :], in_=src[:, 8:16, sl]), writes=[Bxt], dma=True)


def load_norm_block(c, st, xT_src, blk, xt, sq, ht, rstd, Bxt, Bsq, Bht, Brstd, gcol, ps_i=0, do_load=True):
    E, ps, Bps = c["E"], c["ps"], c["Bps"]
    ones_b, Bconst, vcol = c["ones_b"], c["Bconst"], c["vcol"]
    sl = slice(blk * BLK, (blk + 1) * BLK)
    src = xT_src.rearrange("(k p) t -> p k t", p=128)
    Bsq = Bsq if isinstance(Bsq, list) else [Bsq]
    if do_load:
        E("sync", lambda e: e.dma_start(out=xt[:, 0:8, :], in_=src[:, 0:8, sl]), writes=[Bxt], dma=True)
        E("sync", lambda e: e.dma_start(out=xt[:, 8:16, :], in_=src[:, 8:16, sl]), writes=[Bxt], dma=True)
    E("scalar", lambda e: e.activation(out=sq[:], in_=xt[:], func=AF.Square), reads=[Bxt], writes=Bsq)
    for k in range(16):
        E("tensor", lambda e, k=k: e.matmul(ps[ps_i][:], lhsT=ones_b[:], rhs=sq[:, k, :], start=(k == 0), stop=(k == 15)),
          reads=[Bconst] + Bsq, writes=[Bps[ps_i]], signal=(k == 15))
    E("scalar", lambda e: e.activation(out=rstd[:], in_=ps[ps_i][:], func=AF.Sqrt, scale=1.0 / 2048, bias=EPS),
      reads=[Bps[ps_i]], writes=[Brstd])
    E("vector", lambda e: e.reciprocal(out=rstd[:], in_=rstd[:]), reads=[Brstd], writes=[Brstd])
    for k in range(16):
        E("vector", lambda e, k=k: e.scalar_tensor_tensor(out=ht[:, k, :], in0=xt[:, k, :], scalar=vcol(gcol + k), in1=rstd[:],
                                                         op0=ALU.mult, op1=ALU.mult),
          reads=[Bxt, Brstd, Bconst], writes=[Bht])


def phase_proj(c, d, own):
    nc, E, ps, Bps, sbt, buf, vcol = c["nc"], c["E"], c["ps"], c["Bps"], c["sbt"], c["buf"], c["vcol"]
    ones_b, rperm_b, Bconst = c["ones_b"], c["rperm_b"], c["Bconst"]
    H = 8 if own else 2
    ncols = 1024 if own else 512
    col0 = 1024 if own else 2048
    nblk = NB_OWN if own else NB_CTX
    xT_src = d["xT_ctx"]
    rope_c, rope_s = c["rope_c"], c["rope_s"]
    gn = V_QN if own else V_KN
    out_s = d["QT_s"] if own else d["KT_s"]
    Bout = buf("QT_s" if own else "KT_s")
    BV = buf("V_s")
    with ExitStack() as st:
        W = sbt(st, "W_p", [128, 16, ncols], BF16)
        BW = Buf()
        for k in range(0, 16, 4):
            E("gpsimd", lambda e, k=k: e.dma_start(out=W[:, k:k + 4, :], in_=d["w_in"][:, k:k + 4, col0:col0 + ncols]), writes=[BW], dma=True)
        nx = 2 if own else 3
        xt = [sbt(st, "xt%d" % i, [128, 16, BLK], F32) for i in range(nx)]
        cs = [sbt(st, "cs%d" % i, [128, BLK], F32) for i in range(2)]
        sn = [sbt(st, "sn%d" % i, [128, BLK], F32) for i in range(2)]
        Bxt, Bcs = [Buf(), Buf(), Buf()], [Buf(), Buf()]
        for i in range(2):
            E("gpsimd", lambda e: e.tensor_copy(out=cs[i][64:128, :].rearrange("p (r c) -> p r c", c=64),
                                                in_=rope_c[64:128, None, 0:64].to_broadcast([64, 8, 64])), reads=[Bconst], writes=[Bcs[i]])
            E("gpsimd", lambda e: e.tensor_copy(out=sn[i][64:128, :].rearrange("p (r c) -> p r c", c=64),
                                                in_=rope_s[64:128, None, 0:64].to_broadcast([64, 8, 64])), reads=[Bconst], writes=[Bcs[i]])
        sq_one = sbt(st, "sq", [128, 16, BLK], BF16)
        sq2 = [sq_one, sq_one]
        ht2 = [sbt(st, "ht%d" % i, [128, 16, BLK], BF16) for i in range(2)]
        rstd2 = [sbt(st, "rstd%d" % i, [128, BLK], F32) for i in range(2)]
        Bsq_one = Buf()
        Bsq2, Bht2, Brstd2 = [Bsq_one, Bsq_one], [Buf(), Buf()], [Buf(), Buf()]
        sqh = [sbt(st, "sqh%d" % i, [128, BLK], BF16) for i in range(2)]
        rk = [sbt(st, "rk%d" % i, [128, BLK], F32) for i in range(2)]
        kn = [sbt(st, "kn%d" % i, [128, BLK], BF16) for i in range(2)]
        t1 = [sbt(st, "t1%d" % i, [128, BLK], F32) for i in range(2)]
        t2 = [sbt(st, "t2%d" % i, [128, BLK], F32) for i in range(2)]
        ko = [sbt(st, "ko%d" % i, [128, BLK], BF16) for i in range(2)]
        Bsqh, Brk, Bkn, Bt1, Bt2, Bko = [[Buf(), Buf()] for _ in range(6)]
        vt = [sbt(st, "vt%d" % i, [128, 4, 256], BF16) for i in range(2)] if not own else None
        Bvt = [Buf(), Buf()]
        for blk in range(nblk):
            b = blk % 2
            sl = slice(blk * BLK, (blk + 1) * BLK)
            E("gpsimd", lambda e: e.tensor_copy(out=cs[b][0:64, :].rearrange("p (r c) -> p r c", c=64),
                                                in_=rope_c[0:64, blk * 8:(blk + 1) * 8, None].to_broadcast([64, 8, 64])), reads=[Bconst], writes=[Bcs[b]])
            E("gpsimd", lambda e: e.tensor_copy(out=sn[b][0:64, :].rearrange("p (r c) -> p r c", c=64),
                                                in_=rope_s[0:64, blk * 8:(blk + 1) * 8, None].to_broadcast([64, 8, 64])), reads=[Bconst], writes=[Bcs[b]])
            sq, ht, rstd, Bsq, Bht, Brstd = sq2[b], ht2[b], rstd2[b], Bsq2[b], Bht2[b], Brstd2[b]
            if blk == 0:
                load_x_block(c, xT_src, 0, xt[0], Bxt[0])
                if nblk > 1:
                    load_x_block(c, xT_src, 1, xt[1], Bxt[1])
                load_norm_block(c, st, xT_src, 0, xt[0], sq2[0], ht2[0], rstd2[0], Bxt[0], Bsq2[0], Bht2[0], Brstd2[0], V_PREMIX, ps_i=0, do_load=False)
            if blk + 2 < nblk:
                load_x_block(c, xT_src, blk + 2, xt[(blk + 2) % nx], Bxt[(blk + 2) % nx])
            if blk + 1 < nblk:
                nb_ = (blk + 1) % 2
                load_norm_block(c, st, xT_src, blk + 1, xt[(blk + 1) % nx], sq2[nb_], ht2[nb_], rstd2[nb_], Bxt[(blk + 1) % nx], Bsq2[nb_], Bht2[nb_], Brstd2[nb_],
                                V_PREMIX, ps_i=0, do_load=False)
            for hd in range(H):
                i = hd % 2
                pk, pss, pr = 1 + i, 3 + i, 5 + i
                for k in range(16):
                    E("tensor", lambda e, k=k: e.matmul(ps[pk][:], lhsT=W[:, k, hd * 128:(hd + 1) * 128], rhs=ht[:, k, :],
                                                        start=(k == 0), stop=(k == 15)), reads=[BW, Bht], writes=[Bps[pk]], signal=(k == 15))
                E("scalar", lambda e: e.activation(out=sqh[i][:], in_=ps[pk][:], func=AF.Square), reads=[Bps[pk]], writes=[Bsqh[i]])
                E("tensor", lambda e: e.matmul(ps[pss][:], lhsT=ones_b[:], rhs=sqh[i][:], start=True, stop=True),
                  reads=[Bconst, Bsqh[i]], writes=[Bps[pss]])
                E("scalar", lambda e: e.activation(out=rk[i][:], in_=ps[pss][:], func=AF.Sqrt, scale=1.0 / 128, bias=EPS),
                  reads=[Bps[pss]], writes=[Brk[i]])
                E("vector", lambda e: e.reciprocal(out=rk[i][:], in_=rk[i][:]), reads=[Brk[i]], writes=[Brk[i]])
                E("vector", lambda e: e.scalar_tensor_tensor(out=kn[i][:], in0=ps[pk][:], scalar=vcol(gn), in1=rk[i][:],
                                                             op0=ALU.mult, op1=ALU.mult),
                  reads=[Bps[pk], Brk[i], Bconst], writes=[Bkn[i]])
                E("tensor", lambda e: e.matmul(ps[pr][:], lhsT=rperm_b[:], rhs=kn[i][:], start=True, stop=True),
                  reads=[Bconst, Bkn[i]], writes=[Bps[pr]])
                E("gpsimd", lambda e: e.tensor_tensor(out=t1[i][:], in0=kn[i][:], in1=cs[b][:], op=ALU.mult),
                  reads=[Bkn[i], Bcs[b]], writes=[Bt1[i]])
                E("vector", lambda e: e.tensor_tensor(out=t2[i][:], in0=ps[pr][:], in1=sn[b][:], op=ALU.mult),
                  reads=[Bps[pr], Bcs[b]], writes=[Bt2[i]])
                E("gpsimd", lambda e: e.tensor_tensor(out=ko[i][:], in0=t1[i][:], in1=t2[i][:], op=ALU.add),
                  reads=[Bt1[i], Bt2[i]], writes=[Bko[i]])
                E("sync", lambda e: e.dma_start(out=out_s[hd, :, sl], in_=ko[i][:]), reads=[Bko[i]], writes=[Bout], dma=True)
            if not own:
                for sub in range(4):
                    pv = 7
                    for k in range(16):
                        E("tensor", lambda e, k=k: e.matmul(ps[pv][:, 0:256], lhsT=ht[:, k, sub * 128:(sub + 1) * 128], rhs=W[:, k, 256:512],
                                                            start=(k == 0), stop=(k == 15)), reads=[BW, Bht], writes=[Bps[pv]], signal=(k == 15))
                    E("scalar", lambda e: e.copy(out=vt[b][:, sub, :], in_=ps[pv][:, 0:256]), reads=[Bps[pv]], writes=[Bvt[b]])
                for kvh in range(2):
                    E("sync", lambda e: e.dma_start(out=d["V_s"][kvh, :, blk * 4:(blk + 1) * 4, :], in_=vt[b][:, :, kvh * 128:(kvh + 1) * 128]),
                      reads=[Bvt[b]], writes=[BV], dma=True)


def phase_ssm_weights(c, d):
    nc, E, ps, Bps, sbt, buf, vcol = c["nc"], c["E"], c["ps"], c["Bps"], c["sbt"], c["buf"], c["vcol"]
    ident_f, Bconst = c["ident_f"], c["Bconst"]
    A8a, A8b, A8c, BA8 = c["A8a"], c["A8b"], c["A8c"], c["BA8"]
    S = d["ssm"]
    H0, H1 = slice(0, 64), slice(64, 128)
    with ExitStack() as st:
        T = lambda n, shape, dt=F32: sbt(st, n, shape, dt)
        ar, ai, ldt = T("ar", [128, 64]), T("ai", [128, 64]), T("ldt", [128, 64])
        bre, bim, cre, cim = T("bre", [128, 64, 16]), T("bim", [128, 64, 16]), T("cre", [128, 64, 16]), T("cim", [128, 64, 16])
        dug, mL, mU = T("dug", [128, 64]), T("mL", [128, 128]), T("mU", [128, 128])
        Bin = Buf()
        for t, k in ((ar, "ssm_ar"), (ai, "ssm_ai"), (ldt, "ssm_ldt"), (bre, "ssm_bre"), (bim, "ssm_bim"),
                     (cre, "ssm_cre"), (cim, "ssm_cim"), (dug, "ssm_dug")):
            E("sync", lambda e, t=t, k=k: e.dma_start(out=t[:], in_=S[k]), writes=[Bin], dma=True)
        E("sync", lambda e: e.dma_start(out=mL[:], in_=d["maskL"]), writes=[Bin], dma=True)
        E("sync", lambda e: e.dma_start(out=mU[:], in_=d["maskU"]), writes=[Bin], dma=True)
        Bw = Buf()
        V = lambda fn: E("vector", fn, reads=[Bin, Bw, Bconst], writes=[Bw])
        A = lambda fn: E("scalar", fn, reads=[Bin, Bw], writes=[Bw])
        dt_, lrdt, th, mag = T("dt_", [128, 64]), T("lrdt", [128, 64]), T("th", [128, 64]), T("mag", [128, 64])
        cc, ss, u1, u2 = T("cc", [128, 64]), T("ss", [128, 64]), T("u1", [128, 64]), T("u2", [128, 64])
        halfpi = T("halfpi", [128, 1])
        V(lambda e: e.memset(halfpi[:], float(np.pi / 2)))
        A(lambda e: e.activation(out=dt_[:], in_=ldt[:], func=AF.Exp))
        V(lambda e: e.tensor_tensor(out=lrdt[:], in0=ar[:], in1=dt_[:], op=ALU.mult))
        V(lambda e: e.tensor_tensor(out=th[:], in0=ai[:], in1=dt_[:], op=ALU.mult))
        A(lambda e: e.activation(out=mag[:], in_=lrdt[:], func=AF.Exp))
        A(lambda e: e.activation(out=ss[:], in_=th[:], func=AF.Sin, scale=1.0 / 32))
        A(lambda e: e.activation(out=cc[:], in_=th[:], func=AF.Sin, scale=1.0 / 32, bias=halfpi[:]))
        for _ in range(5):
            V(lambda e: e.tensor_tensor(out=u1[:], in0=cc[:], in1=cc[:], op=ALU.mult))
            V(lambda e: e.tensor_tensor(out=u2[:], in0=ss[:], in1=ss[:], op=ALU.mult))
            V(lambda e: e.scalar_tensor_tensor(out=ss[:], in0=cc[:], scalar=2.0, in1=ss[:], op0=ALU.mult, op1=ALU.mult))
            V(lambda e: e.tensor_tensor(out=cc[:], in0=u1[:], in1=u2[:], op=ALU.subtract))
        Lre, Lim = T("Lre", [128, 64, 9]), T("Lim", [128, 64, 9])
        Ire, Iim, inv = T("Ire", [128, 64, 9]), T("Iim", [128, 64, 9]), T("inv", [128, 64, 9])
        V(lambda e: e.memset(Lre[:, :, 0], 1.0))
        V(lambda e: e.memset(Lim[:, :, 0], 0.0))
        V(lambda e: e.tensor_tensor(out=Lre[:, :, 1], in0=mag[:], in1=cc[:], op=ALU.mult))
        V(lambda e: e.tensor_tensor(out=Lim[:, :, 1], in0=mag[:], in1=ss[:], op=ALU.mult))
        for k in range(2, 9):
            V(lambda e, k=k: e.tensor_tensor(out=u1[:], in0=Lre[:, :, k - 1], in1=Lre[:, :, 1], op=ALU.mult))
            V(lambda e, k=k: e.tensor_tensor(out=u2[:], in0=Lim[:, :, k - 1], in1=Lim[:, :, 1], op=ALU.mult))
            V(lambda e, k=k: e.tensor_tensor(out=Lre[:, :, k], in0=u1[:], in1=u2[:], op=ALU.subtract))
            V(lambda e, k=k: e.tensor_tensor(out=u1[:], in0=Lre[:, :, k - 1], in1=Lim[:, :, 1], op=ALU.mult))
            V(lambda e, k=k: e.tensor_tensor(out=u2[:], in0=Lim[:, :, k - 1], in1=Lre[:, :, 1], op=ALU.mult))
            V(lambda e, k=k: e.tensor_tensor(out=Lim[:, :, k], in0=u1[:], in1=u2[:], op=ALU.add))
        V(lambda e: e.tensor_tensor(out=inv[:], in0=Lre[:], in1=Lre[:], op=ALU.mult))
        V(lambda e: e.tensor_tensor(out=Ire[:], in0=Lim[:], in1=Lim[:], op=ALU.mult))
        V(lambda e: e.tensor_tensor(out=inv[:], in0=inv[:], in1=Ire[:], op=ALU.add))
        V(lambda e: e.reciprocal(out=inv[:], in_=inv[:]))
        V(lambda e: e.tensor_tensor(out=Ire[:], in0=Lre[:], in1=inv[:], op=ALU.mult))
        V(lambda e: e.scalar_tensor_tensor(out=Iim[:], in0=Lim[:], scalar=-1.0, in1=inv[:], op0=ALU.mult, op1=ALU.mult))
        E("vector", lambda e: e.tensor_copy(out=A8c[:, 0, :], in_=Lre[:, :, 8]), reads=[Bw], writes=[BA8])
        E("vector", lambda e: e.tensor_copy(out=A8c[:, 1, :], in_=Lim[:, :, 8]), reads=[Bw], writes=[BA8])
        E("vector", lambda e: e.tensor_copy(out=A8a[:, 0, :], in_=Lre[:, :, 8]), reads=[Bw], writes=[BA8])
        E("vector", lambda e: e.tensor_copy(out=A8a[:, 1, :], in_=Lre[:, :, 8]), reads=[Bw], writes=[BA8])
        E("vector", lambda e: e.tensor_scalar(out=A8b[:, 0, :], in0=Lim[:, :, 8], scalar1=-1.0, scalar2=None, op0=ALU.mult), reads=[Bw], writes=[BA8])
        E("vector", lambda e: e.tensor_copy(out=A8b[:, 1, :], in_=Lim[:, :, 8]), reads=[Bw], writes=[BA8])
        PCre, PCim, PGre, PGim = T("PCre", [128, 64, 8]), T("PCim", [128, 64, 8]), T("PGre", [128, 64, 8]), T("PGim", [128, 64, 8])
        for dst, src in ((PCre, Lre), (PCim, Lim), (PGre, Ire), (PGim, Iim)):
            V(lambda e, dst=dst, src=src: e.tensor_copy(out=dst[H0, :, :], in_=src[H0, :, 1:9]))
            V(lambda e, dst=dst, src=src: e.tensor_copy(out=dst[H1, :, :], in_=src[H1, :, 8:0:-1]))
        den, wre, wim = T("den", [128, 64]), T("wre", [128, 64]), T("wim", [128, 64])
        V(lambda e: e.tensor_tensor(out=den[:], in0=ar[:], in1=ar[:], op=ALU.mult))
        V(lambda e: e.tensor_tensor(out=u1[:], in0=ai[:], in1=ai[:], op=ALU.mult))
        V(lambda e: e.tensor_tensor(out=den[:], in0=den[:], in1=u1[:], op=ALU.add))
        V(lambda e: e.reciprocal(out=den[:], in_=den[:]))
        V(lambda e: e.tensor_scalar(out=u1[:], in0=Lre[:, :, 1], scalar1=-1.0, scalar2=None, op0=ALU.add))
        V(lambda e: e.tensor_tensor(out=wre[:], in0=u1[:], in1=ar[:], op=ALU.mult))
        V(lambda e: e.tensor_tensor(out=u2[:], in0=Lim[:, :, 1], in1=ai[:], op=ALU.mult))
        V(lambda e: e.tensor_tensor(out=wre[:], in0=wre[:], in1=u2[:], op=ALU.add))
        V(lambda e: e.tensor_tensor(out=wre[:], in0=wre[:], in1=den[:], op=ALU.mult))
        V(lambda e: e.tensor_tensor(out=wim[:], in0=Lim[:, :, 1], in1=ar[:], op=ALU.mult))
        V(lambda e: e.tensor_tensor(out=u2[:], in0=u1[:], in1=ai[:], op=ALU.mult))
        V(lambda e: e.tensor_tensor(out=wim[:], in0=wim[:], in1=u2[:], op=ALU.subtract))
        V(lambda e: e.tensor_tensor(out=wim[:], in0=wim[:], in1=den[:], op=ALU.mult))
        bbre, bbim, v1 = T("bbre", [128, 64, 16]), T("bbim", [128, 64, 16]), T("v1", [128, 64, 16])
        wre_b = wre[:, :, None].to_broadcast([128, 64, 16])
        wim_b = wim[:, :, None].to_broadcast([128, 64, 16])
        V(lambda e: e.tensor_tensor(out=bbre[:], in0=bre[:], in1=wre_b, op=ALU.mult))
        V(lambda e: e.tensor_tensor(out=v1[:], in0=bim[:], in1=wim_b, op=ALU.mult))
        V(lambda e: e.tensor_tensor(out=bbre[:], in0=bbre[:], in1=v1[:], op=ALU.subtract))
        V(lambda e: e.tensor_tensor(out=bbim[:], in0=bim[:], in1=wre_b, op=ALU.mult))
        V(lambda e: e.tensor_tensor(out=v1[:], in0=bre[:], in1=wim_b, op=ALU.mult))
        V(lambda e: e.tensor_tensor(out=bbim[:], in0=bbim[:], in1=v1[:], op=ALU.add))
        GB = 16
        Gre, Gim, Gimn = T("Gre", [128, GB, 8, 16]), T("Gim", [128, GB, 8, 16]), T("Gimn", [128, GB, 8, 16])
        CLre, CLim = T("CLre", [128, GB, 8, 16]), T("CLim", [128, GB, 8, 16])
        Wzre, Wzim = T("Wzre", [128, GB, 8, 16]), T("Wzim", [128, GB, 8, 16])
        w1, w2 = T("w1", [128, GB, 8, 16]), T("w2", [128, GB, 8, 16])
        stage = [T("stage%d" % i, [128, GB, 5, 128], BF16) for i in range(2)]
        Bstage = [Buf(), Buf()]
        tA, tB = T("tA", [128, 128]), T("tB", [128, 128])
        BtA = Buf()
        BWS = buf("WS_s")
        for gb in range(64 // GB):
            g0 = gb * GB
            gs = slice(g0, g0 + GB)
            sh4 = [128, GB, 8, 16]
            PGre_b = PGre[:, gs, :, None].to_broadcast(sh4)
            PGim_b = PGim[:, gs, :, None].to_broadcast(sh4)
            PCre_b = PCre[:, gs, :, None].to_broadcast(sh4)
            PCim_b = PCim[:, gs, :, None].to_broadcast(sh4)
            Bre_b = bbre[:, gs, None, :].to_broadcast(sh4)
            Bim_b = bbim[:, gs, None, :].to_broadcast(sh4)
            Cre_b = cre[:, gs, None, :].to_broadcast(sh4)
            Cim_b = cim[:, gs, None, :].to_broadcast(sh4)
            A8re_b = A8c[:, 0, gs, None, None].to_broadcast(sh4)
            A8im_b = A8c[:, 1, gs, None, None].to_broadcast(sh4)
            V2 = lambda fn: E("vector", fn, reads=[Bin, Bw, BA8, Bconst], writes=[Bw])
            V2(lambda e: e.tensor_tensor(out=w1[:], in0=PGre_b, in1=Bre_b, op=ALU.mult))
            V2(lambda e: e.tensor_tensor(out=w2[:], in0=PGim_b, in1=Bim_b, op=ALU.mult))
            V2(lambda e: e.tensor_tensor(out=Gre[:], in0=w1[:], in1=w2[:], op=ALU.subtract))
            V2(lambda e: e.tensor_tensor(out=w1[:], in0=PGre_b, in1=Bim_b, op=ALU.mult))
            V2(lambda e: e.tensor_tensor(out=w2[:], in0=PGim_b, in1=Bre_b, op=ALU.mult))
            V2(lambda e: e.tensor_tensor(out=Gim[:], in0=w1[:], in1=w2[:], op=ALU.add))
            V2(lambda e: e.tensor_scalar(out=Gimn[:], in0=Gim[:], scalar1=-1.0, scalar2=None, op0=ALU.mult))
            V2(lambda e: e.tensor_tensor(out=w1[:], in0=PCre_b, in1=Cre_b, op=ALU.mult))
            V2(lambda e: e.tensor_tensor(out=w2[:], in0=PCim_b, in1=Cim_b, op=ALU.mult))
            V2(lambda e: e.tensor_tensor(out=CLre[:], in0=w1[:], in1=w2[:], op=ALU.subtract))
            V2(lambda e: e.tensor_tensor(out=w1[:], in0=PCre_b, in1=Cim_b, op=ALU.mult))
            V2(lambda e: e.tensor_tensor(out=w2[:], in0=PCim_b, in1=Cre_b, op=ALU.mult))
            V2(lambda e: e.tensor_tensor(out=CLim[:], in0=w1[:], in1=w2[:], op=ALU.add))
            V2(lambda e: e.tensor_tensor(out=w1[:], in0=Gre[:], in1=A8re_b, op=ALU.mult))
            V2(lambda e: e.tensor_tensor(out=w2[:], in0=Gim[:], in1=A8im_b, op=ALU.mult))
            V2(lambda e: e.tensor_tensor(out=Wzre[:], in0=w1[:], in1=w2[:], op=ALU.subtract))
            V2(lambda e: e.tensor_tensor(out=w1[:], in0=Gim[:], in1=A8re_b, op=ALU.mult))
            V2(lambda e: e.tensor_tensor(out=w2[:], in0=Gre[:], in1=A8im_b, op=ALU.mult))
            V2(lambda e: e.tensor_tensor(out=Wzim[:], in0=w1[:], in1=w2[:], op=ALU.add))
            sg = stage[gb % 2]
            Bsg = Bstage[gb % 2]
            f2 = lambda t: t[:].rearrange("q g t h -> q g (t h)")
            E("gpsimd", lambda e: e.tensor_copy(out=sg[:, :, 3, :], in_=f2(CLre)), reads=[Bw], writes=[Bsg])
            E("gpsimd", lambda e: e.tensor_scalar(out=sg[:, :, 4, :], in0=f2(CLim), scalar1=-1.0, scalar2=None, op0=ALU.mult), reads=[Bw], writes=[Bsg])
            for gl in range(GB):
                g = g0 + gl
                f1 = lambda t: t[:, gl, :, :].rearrange("q t h -> q (t h)")
                E("tensor", lambda e: e.transpose(out=ps[0][:, 0:128], in_=f1(Wzre), identity=ident_f[:]), reads=[Bw, Bconst], writes=[Bps[0]], serial=True)
                E("tensor", lambda e: e.transpose(out=ps[1][:, 0:128], in_=f1(Wzim), identity=ident_f[:]), reads=[Bw, Bconst], writes=[Bps[1]], serial=True)
                E("scalar", lambda e: e.copy(out=sg[:, gl, 0, :], in_=ps[0][:, 0:128]), reads=[Bps[0]], writes=[Bsg])
                E("scalar", lambda e: e.copy(out=sg[:, gl, 1, :], in_=ps[1][:, 0:128]), reads=[Bps[1]], writes=[Bsg])
                for half, pi in ((H0, 2), (H1, 3)):
                    E("tensor", lambda e: e.matmul(ps[pi][:, 0:128], lhsT=f1(Gre)[half, :], rhs=f1(CLre)[half, :], start=True, stop=False),
                      reads=[Bw], writes=[Bps[pi]], serial=True)
                    E("tensor", lambda e: e.matmul(ps[pi][:, 0:128], lhsT=f1(Gimn)[half, :], rhs=f1(CLim)[half, :], start=False, stop=True),
                      reads=[Bw], writes=[Bps[pi]], serial=True)
                E("vector", lambda e: e.tensor_tensor(out=tA[:], in0=ps[2][:, 0:128], in1=mL[:], op=ALU.mult), reads=[Bps[2], Bin], writes=[BtA])
                E("vector", lambda e: e.tensor_tensor(out=tB[:], in0=ps[3][:, 0:128], in1=mU[:], op=ALU.mult), reads=[Bps[3], Bin], writes=[BtA])
                E("vector", lambda e: e.tensor_tensor(out=tA[:], in0=tA[:], in1=tB[:], op=ALU.add), reads=[BtA], writes=[BtA])
                E("vector", lambda e: e.scalar_tensor_tensor(out=sg[:, gl, 2, :], in0=ident_f[:], scalar=dug[:, g:g + 1], in1=tA[:],
                                                             op0=ALU.mult, op1=ALU.add), reads=[BtA, Bin, Bconst], writes=[Bsg])
            E("sync", lambda e: e.dma_start(out=d["WS_s"][:, gs, :, :], in_=sg[:]), reads=[Bsg], writes=[BWS], dma=True)


def _complex_sq(c, eng, t, tmp, bufs):
    E = c["E"]
    E(eng, lambda e: e.tensor_tensor(out=tmp[:, 0, :], in0=t[:, 0, :], in1=t[:, 0, :], op=ALU.mult), reads=bufs, writes=bufs)
    E(eng, lambda e: e.tensor_tensor(out=tmp[:, 1, :], in0=t[:, 1, :], in1=t[:, 1, :], op=ALU.mult), reads=bufs, writes=bufs)
    E(eng, lambda e: e.scalar_tensor_tensor(out=t[:, 1, :], in0=t[:, 0, :], scalar=2.0, in1=t[:, 1, :], op0=ALU.mult, op1=ALU.mult), reads=bufs, writes=bufs)
    E(eng, lambda e: e.tensor_tensor(out=t[:, 0, :], in0=tmp[:, 0, :], in1=tmp[:, 1, :], op=ALU.subtract), reads=bufs, writes=bufs)


def phase_ssm_scan(c, d, own):
    nc, E, ps, Bps, sbt, buf, vcol = c["nc"], c["E"], c["ps"], c["Bps"], c["sbt"], c["buf"], c["vcol"]
    Bconst, BA8, BSin = c["Bconst"], c["BA8"], c["BSin"]
    A8a, A8b, A8c, Sin_t = c["A8a"], c["A8b"], c["A8c"], c["Sin_t"]
    es = c["es"]
    H0, H1 = slice(0, 64), slice(64, 128)
    if "Est" not in c:
        c["Est"] = sbt(es, "Est", [128, 3, 2, 64], F32)
        c["BEst"] = Buf()
        c["A64"] = sbt(es, "A64", [128, 2, 64], F32)
        c["Aslot"] = sbt(es, "Aslot", [128, 2, 64], F32)
        c["BApow"] = Buf()
        tmpq = sbt(es, "tmpq", [128, 2, 64], F32)
        Bq = [c["BApow"], BA8]
        E("vector", lambda e: e.tensor_copy(out=c["A64"][:], in_=A8c[:]), reads=[BA8], writes=[c["BApow"]])
        for _ in range(6):
            _complex_sq(c, "vector", c["A64"], tmpq, [c["BApow"]])
        E("vector", lambda e: e.tensor_copy(out=c["Aslot"][:], in_=c["A64"][:]), reads=[c["BApow"]], writes=[c["BApow"]])
        n = NB_OWN
        while n > 1:
            _complex_sq(c, "vector", c["Aslot"], tmpq, [c["BApow"]])
            n //= 2
    Est, BEst, A64, BApow = c["Est"], c["BEst"], c["A64"], c["BApow"]
    with ExitStack() as st:
        Wu = sbt(st, "Wu", [128, 16, 1024], BF16)
        Sel = sbt(st, "Sel", [128, 64, 128], BF16)
        WZ = sbt(st, "WZ", [128, 64, 2, 128], BF16)
        BW = Buf()
        for k in range(0, 16, 4):
            E("gpsimd", lambda e, k=k: e.dma_start(out=Wu[:, k:k + 4, :], in_=d["w_in"][:, k:k + 4, 0:1024]), writes=[BW], dma=True)
        E("gpsimd", lambda e: e.dma_start(out=Sel[:], in_=d["sel"]), writes=[BW], dma=True)
        E("sync", lambda e: e.dma_start(out=WZ[:], in_=d["WS_s"][:, :, 0:2, :]), reads=[buf("WS_s")], writes=[BW], dma=True)
        xt = sbt(st, "xt", [128, 16, BLK], F32)
        scr = sbt(st, "scr", [128, 16, BLK], BF16)
        sq = scr
        ht = sbt(st, "ht", [128, 16, BLK], BF16)
        rstd = sbt(st, "rstd", [128, BLK], F32)
        Bxt, Bht, Brstd = Buf(), Buf(), Buf()
        uT = scr[:, 0:8, :]
        BuT = Buf()
        ug = scr[:, 8:16, :].rearrange("p a (b c) -> p (a b) c", c=64)
        Bug = Buf()
        Bsq = [BuT, Bug]
        Zbs = [sbt(st, "Zb%d" % i, [128, 64, 2, 64], BF16) for i in range(2)]
        BZbs = [Buf(), Buf()]
        nblk_done = [0]
        Sal = sbt(st, "Sal", [128, 64, 2, 64], BF16)
        BSal = Buf()
        S2 = sbt(st, "S2", [128, 2, 64], F32)
        q1 = sbt(st, "q1", [128, 2, 64], F32)
        q2 = sbt(st, "q2", [128, 2, 64], F32)
        BS2 = Buf()
        accb = sbt(st, "accb", [128, 2, 64], F32)
        Pw = sbt(st, "Pw", [128, 2, 64], F32)
        m1 = sbt(st, "m1", [128, 2, 64], F32)
        m2 = sbt(st, "m2", [128, 2, 64], F32)
        m3 = sbt(st, "m3", [128, 2, 64], F32)
        Bacc = Buf()
        slots = [0] if own else [1, 2, 3]
        jobs = [(slot, bi) for slot in slots for bi in range(NB_OWN)]

        def stage_a(n):
            slot, bi = jobs[n]
            load_norm_block(c, st, d["xT_ctx"], slot * NB_OWN + bi, xt, sq, ht, rstd, Bxt, Bsq, Bht, Brstd, V_PREMIX, ps_i=0)

        def stage_b(n):
            slot, bi = jobs[n]
            Zb, BZb = Zbs[n % 2], BZbs[n % 2]
            for j in range(8):
                pj = 1 + j % 2
                for k in range(16):
                    E("tensor", lambda e, k=k: e.matmul(ps[pj][:], lhsT=Wu[:, k, j * 128:(j + 1) * 128], rhs=ht[:, k, :],
                                                        start=(k == 0), stop=(k == 15)), reads=[BW, Bht], writes=[Bps[pj]], signal=(k == 15))
                E("scalar", lambda e: e.copy(out=uT[:, j, :], in_=ps[pj][:]), reads=[Bps[pj]], writes=[BuT])
            for j in range(8):
                pj = 3 + j % 2
                uv = uT[:, j, :].rearrange("p (c t) -> p c t", t=8)
                for gl in range(8):
                    for tau in range(8):
                        E("tensor", lambda e: e.matmul(ps[pj][:, gl * 64:(gl + 1) * 64], lhsT=Sel[:, gl * 8 + tau, :], rhs=uv[:, :, tau],
                                                       start=(tau == 0), stop=(tau == 7)), reads=[BW, BuT], writes=[Bps[pj]], signal=(tau == 7))
                E("scalar", lambda e: e.copy(out=ug[:, j * 8:(j + 1) * 8, :], in_=ps[pj][:].rearrange("p (g c) -> p g c", c=64)),
                  reads=[Bps[pj]], writes=[Bug])
            if own:
                E("sync", lambda e: e.dma_start(out=d["UG_s"][bi], in_=ug), reads=[Bug], writes=[buf("UG_s")], dma=True)
            for j in range(8):
                for ri in range(2):
                    pz = 5 + ((2 * j + ri) % 3)
                    for gl in range(8):
                        g = j * 8 + gl
                        E("tensor", lambda e: e.matmul(ps[pz][:, gl * 64:(gl + 1) * 64], lhsT=WZ[:, g, ri, :], rhs=ug[:, g, :],
                                                       start=True, stop=True), reads=[BW, Bug], writes=[Bps[pz]], signal=(gl == 7))
                    pv = ps[pz][:].rearrange("q (g c) -> q g c", c=64)
                    E("scalar", lambda e: e.copy(out=Zb[H0, :, ri, j * 8:(j + 1) * 8].rearrange("q c g -> q g c"), in_=pv[H0]),
                      reads=[Bps[pz]], writes=[BZb])
                    E("scalar", lambda e: e.copy(out=Zb[H1, :, ri, j * 8:(j + 1) * 8].rearrange("q c g -> q g c"), in_=pv[H1, :, ::-1]),
                      reads=[Bps[pz]], writes=[BZb])

        def stage_scan(n):
            slot, bi = jobs[n]
            Zb, BZb = Zbs[n % 2], BZbs[n % 2]
            if bi == 0:
                E("vector", lambda e: e.memset(S2[:], 0.0), writes=[BS2])
                if own:
                    E("vector", lambda e: e.tensor_copy(out=S2[H0], in_=Sin_t[H0]), reads=[BSin], writes=[BS2])
                else:
                    E("gpsimd", lambda e: e.memset(accb[:], 0.0), writes=[Bacc])
                    E("gpsimd", lambda e: e.memset(Pw[:], 0.0), writes=[Bacc])
                    E("gpsimd", lambda e: e.memset(Pw[:, 0, :], 1.0), writes=[Bacc])
            E("vector", lambda e: e.memset(S2[H1], 0.0), writes=[BS2])
            if own:
                E("vector", lambda e: e.tensor_copy(out=Sal[H0, 0, :, :], in_=S2[H0]), reads=[BS2], writes=[BSal])
            for i in range(64):
                E("vector", lambda e: e.tensor_tensor(out=q1[:], in0=S2[:], in1=A8a[:], op=ALU.mult), reads=[BS2, BA8], writes=[BS2])
                E("vector", lambda e: e.tensor_tensor(out=q2[:], in0=S2[:, ::-1, :], in1=A8b[:], op=ALU.mult), reads=[BS2, BA8], writes=[BS2])
                E("vector", lambda e: e.tensor_tensor(out=q1[:], in0=q1[:], in1=q2[:], op=ALU.add), reads=[BS2], writes=[BS2])
                E("vector", lambda e: e.tensor_tensor(out=S2[:], in0=q1[:], in1=Zb[:, i, :, :], op=ALU.add), reads=[BS2, BZb], writes=[BS2])
                if own and i < 63:
                    E("vector", lambda e: e.tensor_copy(out=Sal[H0, i + 1, :, :], in_=S2[H0]), reads=[BS2], writes=[BSal])
            if own:
                E("sync", lambda e: e.dma_start(out=d["SF_s"][bi], in_=Sal[H0]), reads=[BSal], writes=[buf("SF_s")], dma=True)
                E("sync", lambda e: e.dma_start(out=d["ZB_s"][bi], in_=Zb[H1]), reads=[BZb], writes=[buf("ZB_s")], dma=True)
            else:
                rb = [BS2, Bacc, BApow]
                c["cmul"]("gpsimd", m3, Pw, S2, m1, m2, rb, [Bacc], rows=H1)
                E("gpsimd", lambda e: e.tensor_tensor(out=accb[H1], in0=accb[H1], in1=m3[H1], op=ALU.add), reads=rb, writes=[Bacc])
                c["cmul"]("gpsimd", m3, Pw, A64, m1, m2, rb, [Bacc], rows=H1)
                E("gpsimd", lambda e: e.tensor_copy(out=Pw[H1], in_=m3[H1]), reads=rb, writes=[Bacc])
                if bi == NB_OWN - 1:
                    so = slot - 1
                    E("vector", lambda e: e.tensor_copy(out=Est[H0, so, :, :], in_=S2[H0]), reads=[BS2], writes=[BEst])
                    E("gpsimd", lambda e: e.tensor_copy(out=Est[H1, 2 - so, :, :], in_=accb[H1]), reads=[Bacc], writes=[BEst])

        for n in range(len(jobs)):
            stage_a(n)
            if n > 0:
                stage_scan(n - 1)
            stage_b(n)
        stage_scan(len(jobs) - 1)


def phase_ssm_carry(c, d):
    E, sbt, es = c["E"], c["sbt"], c["es"]
    Est, BEst, Aslot, BApow, Sin_t, BSin, gates, Bconst = c["Est"], c["BEst"], c["Aslot"], c["BApow"], c["Sin_t"], c["BSin"], c["gates"], c["Bconst"]
    with ExitStack() as st:
        m1 = sbt(st, "cm1", [128, 2, 64], F32)
        m2 = sbt(st, "cm2", [128, 2, 64], F32)
        T = sbt(st, "cT", [128, 2, 64], F32)
        Bt = Buf()
        E("vector", lambda e: e.memset(Sin_t[:], 0.0), writes=[BSin])
        rb = [BEst, BApow, BSin, Bt, Bconst]
        for s_ in range(3):
            c["cmul"]("vector", T, Aslot, Sin_t, m1, m2, rb, [Bt])
            E("vector", lambda e: e.tensor_tensor(out=T[:], in0=T[:], in1=Est[:, s_, :, :], op=ALU.add), reads=rb, writes=[Bt])
            E("vector", lambda e: e.tensor_tensor(out=T[:], in0=T[:], in1=Sin_t[:], op=ALU.subtract), reads=rb, writes=[Bt])
            E("vector", lambda e: e.scalar_tensor_tensor(out=Sin_t[:], in0=T[:], scalar=gates[:, s_:s_ + 1], in1=Sin_t[:], op0=ALU.mult, op1=ALU.add),
              reads=rb, writes=[BSin])
        if d["EST_s"] is not None:
            E("sync", lambda e: e.dma_start(out=d["EST_s"], in_=Sin_t[:]), reads=[BSin], dma=True)


def phase_ssm_out(c, d):
    nc, E, ps, Bps, sbt, buf, vcol = c["nc"], c["E"], c["ps"], c["Bps"], c["sbt"], c["buf"], c["vcol"]
    Bconst, BA8, BSin = c["Bconst"], c["BA8"], c["BSin"]
    A8a, A8b, Sin_t, ones_b = c["A8a"], c["A8b"], c["Sin_t"], c["ones_b"]
    H0, H1 = slice(0, 64), slice(64, 128)
    with ExitStack() as st:
        WM = sbt(st, "WM", [128, 64, 3, 128], BF16)
        SelT = sbt(st, "SelT", [128, 64, 128], BF16)
        Wg = sbt(st, "Wg", [128, 8, 1024], BF16)
        BW = Buf()
        E("sync", lambda e: e.dma_start(out=WM[:], in_=d["WS_s"][:, :, 2:5, :]), reads=[buf("WS_s")], writes=[BW], dma=True)
        E("gpsimd", lambda e: e.dma_start(out=SelT[:], in_=d["selT"]), writes=[BW], dma=True)
        E("gpsimd", lambda e: e.dma_start(out=Wg[:], in_=d["w_glu"]), writes=[BW], dma=True)
        ug = [sbt(st, "ugc%d" % i, [128, 64, 64], BF16) for i in range(2)]
        Bug = [Buf(), Buf()]
        SalN = sbt(st, "SalN", [128, 64, 2, 64], BF16)
        SalR = sbt(st, "SalR", [128, 64, 2, 64], BF16)
        ZbR = sbt(st, "ZbR", [128, 64, 2, 64], BF16)
        BSalN, BSalR, BZbR = Buf(), Buf(), Buf()
        S2 = sbt(st, "S2c", [128, 2, 64], F32)
        q1 = sbt(st, "q1c", [128, 2, 64], F32)
        q2 = sbt(st, "q2c", [128, 2, 64], F32)
        BS2 = Buf()
        yg = [sbt(st, "yg%d" % i, [128, 8, 64], BF16) for i in range(2)]
        Byg = [Buf(), Buf()]
        ysf = sbt(st, "ysf", [128, 8, BLK], F32)
        ysb = sbt(st, "ysb", [128, 8, BLK], BF16)
        Bysf, Bysb = Buf(), Buf()
        ta = [sbt(st, "ta%d" % i, [128, BLK], F32) for i in range(2)]
        tb = [sbt(st, "tb%d" % i, [128, BLK], F32) for i in range(2)]
        Bta, Btb = [Buf(), Buf()], [Buf(), Buf()]
        sqn = sbt(st, "sqn", [128, 8, BLK], BF16)
        nsb = sbt(st, "nsb", [128, 8, BLK], BF16)
        rstd = sbt(st, "rstdc", [128, BLK], F32)
        Bsqn, Bnsb, Brstd = Buf(), Buf(), Buf()
        E("vector", lambda e: e.memset(S2[:], 0.0), writes=[BS2])
        E("vector", lambda e: e.tensor_copy(out=S2[H1], in_=Sin_t[H1]), reads=[BSin], writes=[BS2])
        for it, bi in enumerate(range(NB_OWN - 1, -1, -1)):
            b = it % 2
            tok = slice(bi * BLK, (bi + 1) * BLK)
            E("sync", lambda e: e.dma_start(out=ug[b][:], in_=d["UG_s"][bi]), reads=[buf("UG_s")], writes=[Bug[b]], dma=True)
            E("sync", lambda e: e.dma_start(out=SalN[H0], in_=d["SF_s"][bi]), reads=[buf("SF_s")], writes=[BSalN], dma=True)
            E("sync", lambda e: e.dma_start(out=ZbR[H1], in_=d["ZB_s"][bi]), reads=[buf("ZB_s")], writes=[BZbR], dma=True)
            E("vector", lambda e: e.tensor_copy(out=SalR[H1, 0, :, :], in_=S2[H1]), reads=[BS2], writes=[BSalR])
            for i in range(64):
                E("vector", lambda e: e.tensor_tensor(out=q1[H1], in0=S2[H1], in1=A8a[H1], op=ALU.mult), reads=[BS2, BA8], writes=[BS2])
                E("vector", lambda e: e.tensor_tensor(out=q2[H1], in0=S2[H1, ::-1, :], in1=A8b[H1], op=ALU.mult), reads=[BS2, BA8], writes=[BS2])
                E("vector", lambda e: e.tensor_tensor(out=q1[H1], in0=q1[H1], in1=q2[H1], op=ALU.add), reads=[BS2], writes=[BS2])
                E("vector", lambda e: e.tensor_tensor(out=S2[H1], in0=q1[H1], in1=ZbR[H1, i, :, :], op=ALU.add), reads=[BS2, BZbR], writes=[BS2])
                if i < 63:
                    E("vector", lambda e: e.tensor_copy(out=SalR[H1, i + 1, :, :], in_=S2[H1]), reads=[BS2], writes=[BSalR])
            E("gpsimd", lambda e: e.tensor_copy(out=SalN[H1], in_=SalR[H1, ::-1, :, :]), reads=[BSalR], writes=[BSalN])
            for j in range(8):
                py = 1 + j % 2
                for gl in range(8):
                    g = j * 8 + gl
                    o = ps[py][:, gl * 64:(gl + 1) * 64]
                    E("tensor", lambda e: e.matmul(o, lhsT=WM[:, g, 0, :], rhs=ug[b][:, g, :], start=True, stop=False), reads=[BW, Bug[b]], writes=[Bps[py]], signal=False)
                    E("tensor", lambda e: e.matmul(o, lhsT=WM[:, g, 1, :], rhs=SalN[:, :, 0, g], start=False, stop=False), reads=[BW, BSalN], writes=[Bps[py]], signal=False)
                    E("tensor", lambda e: e.matmul(o, lhsT=WM[:, g, 2, :], rhs=SalN[:, :, 1, g], start=False, stop=True), reads=[BW, BSalN], writes=[Bps[py]])
                E("scalar", lambda e: e.copy(out=yg[j % 2][:], in_=ps[py][:].rearrange("p (g c) -> p g c", c=64)), reads=[Bps[py]], writes=[Byg[j % 2]])
                pf = 3 + j % 2
                pfv = ps[pf][:].rearrange("p (c t) -> p c t", t=8)
                for tau in range(8):
                    for gl in range(8):
                        E("tensor", lambda e: e.matmul(pfv[:, :, tau], lhsT=SelT[:, gl * 8 + tau, :], rhs=yg[j % 2][:, gl, :],
                                                       start=(gl == 0), stop=(gl == 7)), reads=[BW, Byg[j % 2]], writes=[Bps[pf]], signal=(gl == 7))
                a_, b_ = ta[j % 2], tb[j % 2]
                Ba, Bb = Bta[j % 2], Btb[j % 2]
                E("scalar", lambda e: e.activation(out=a_[:], in_=ps[pf][:], func=AF.Square), reads=[Bps[pf]], writes=[Ba])
                E("vector", lambda e: e.tensor_scalar(out=a_[:], in0=a_[:], scalar1=0.044715, scalar2=1.0, op0=ALU.mult, op1=ALU.add), reads=[Ba], writes=[Ba])
                E("vector", lambda e: e.tensor_tensor(out=a_[:], in0=a_[:], in1=ps[pf][:], op=ALU.mult), reads=[Ba, Bps[pf]], writes=[Ba])
                E("scalar", lambda e: e.activation(out=b_[:], in_=a_[:], func=AF.Sigmoid, scale=1.5957691216057308), reads=[Ba], writes=[Bb])
                E("vector", lambda e: e.tensor_tensor(out=ysf[:, j, :], in0=b_[:], in1=ps[pf][:], op=ALU.mult), reads=[Bb, Bps[pf]], writes=[Bysf])
                E("gpsimd", lambda e: e.tensor_copy(out=ysb[:, j, :], in_=ysf[:, j, :]), reads=[Bysf], writes=[Bysb])
            for j2 in range(8):
                pg = 5 + j2 % 2
                for j in range(8):
                    E("tensor", lambda e: e.matmul(ps[pg][:], lhsT=Wg[:, j, j2 * 128:(j2 + 1) * 128], rhs=ysb[:, j, :], start=(j == 0), stop=(j == 7)),
                      reads=[BW, Bysb], writes=[Bps[pg]], signal=(j == 7))
                b_, Bb = tb[j2 % 2], Btb[j2 % 2]
                E("scalar", lambda e: e.activation(out=b_[:], in_=ps[pg][:], func=AF.Sigmoid), reads=[Bps[pg]], writes=[Bb])
                E("vector", lambda e: e.tensor_tensor(out=ysf[:, j2, :], in0=ysf[:, j2, :], in1=b_[:], op=ALU.mult), reads=[Bb, Bysf], writes=[Bysf])
            E("scalar", lambda e: e.activation(out=sqn[:], in_=ysf[:], func=AF.Square), reads=[Bysf], writes=[Bsqn])
            for j in range(8):
                E("tensor", lambda e: e.matmul(ps[7][:], lhsT=ones_b[:], rhs=sqn[:, j, :], start=(j == 0), stop=(j == 7)), reads=[Bconst, Bsqn], writes=[Bps[7]], signal=(j == 7))
            E("scalar", lambda e: e.activation(out=rstd[:], in_=ps[7][:], func=AF.Sqrt, scale=1.0 / 1024, bias=EPS), reads=[Bps[7]], writes=[Brstd])
            E("vector", lambda e: e.reciprocal(out=rstd[:], in_=rstd[:]), reads=[Brstd], writes=[Brstd])
            for j in range(8):
                E("vector", lambda e: e.scalar_tensor_tensor(out=nsb[:, j, :], in0=ysf[:, j, :], scalar=vcol(V_SSMOUT + j), in1=rstd[:],
                                                             op0=ALU.mult, op1=ALU.mult), reads=[Bysf, Brstd, Bconst], writes=[Bnsb])
            E("sync", lambda e: e.dma_start(out=d["NS_s"][:, :, tok], in_=nsb[:]), reads=[Bnsb], writes=[buf("NS_s")], dma=True)


def phase_attention(c, d):
    nc, E, ps, Bps, sbt, buf = c["nc"], c["E"], c["ps"], c["Bps"], c["sbt"], c["buf"]
    psbig, ones_b, Bconst = c["psbig"], c["ones_b"], c["Bconst"]
    scale = 1.0 / float(np.sqrt(128.0))
    QW = 2 * BLK
    with ExitStack() as st:
        KT = sbt(st, "KT", [128, NCTX], BF16)
        Vt = sbt(st, "Vt", [128, NKT, 128], BF16)
        mk = sbt(st, "mk", [128, NKT], F32)
        BKV, Bmk = Buf(), Buf()
        E("sync", lambda e: e.dma_start(out=mk[:], in_=d["maskb"]), writes=[Bmk], dma=True)
        qt = [sbt(st, "qt%d" % i, [128, QW], BF16) for i in range(2)]
        Bqt = [Buf(), Buf()]
        NP = 4
        pT = [sbt(st, "pT%d" % i, [128, QW], BF16) for i in range(NP)]
        BpT = [Buf() for _ in range(NP)]
        acc = [sbt(st, "acc%d" % i, [128, QW], F32) for i in range(2)]
        Bacc = [Buf(), Buf()]
        rec = sbt(st, "rec", [128, QW], F32)
        Brec = Buf()
        dhi = sbt(st, "dhi", [128, QW], BF16)
        dlo = sbt(st, "dlo", [128, QW], BF16)
        Bdh = Buf()
        yo = [sbt(st, "yo%d" % i, [128, QW], F32) for i in range(2)]
        Byo = [Buf(), Buf()]
        BS = [[Bps[0], Bps[1]], [Bps[2], Bps[3]]]
        BO = [Bps[4], Bps[5]]
        BD = [Bps[6], Bps[7]]
        it = 0
        for kvh in range(2):
            nchunk = 4
            for q in range(nchunk):
                cs_ = slice(q * NCTX // nchunk, (q + 1) * NCTX // nchunk)
                ks_ = slice(q * NKT // nchunk, (q + 1) * NKT // nchunk)
                E("sync", lambda e: e.dma_start(out=KT[:, cs_], in_=d["KT_s"][kvh, :, cs_]), reads=[buf("KT_s")], writes=[BKV], dma=True)
                E("sync", lambda e: e.dma_start(out=Vt[:, ks_, :], in_=d["V_s"][kvh, :, ks_, :]), reads=[buf("V_s")], writes=[BKV], dma=True)
            for qh in range(4):
                head = kvh * 4 + qh
                for qb in range(NB_OWN // 2):
                    b = it % 2
                    tok = slice(qb * QW, (qb + 1) * QW)
                    E("sync", lambda e: e.dma_start(out=qt[b][:], in_=d["QT_s"][head, :, tok]), reads=[buf("QT_s")], writes=[Bqt[b]], dma=True)

                    def s_mm(kt):
                        sb_ = kt % 2
                        for hf in range(2):
                            E("tensor", lambda e: e.matmul(psbig[sb_][:, hf * BLK:(hf + 1) * BLK], lhsT=KT[:, kt * 128:(kt + 1) * 128],
                                                           rhs=qt[b][:, hf * BLK:(hf + 1) * BLK], start=True, stop=True),
                              reads=[BKV, Bqt[b]], writes=[BS[sb_][hf]], signal=(hf == 1))
                    s_mm(0)
                    nd = [0, 0]
                    for kt in range(NKT):
                        if kt + 1 < NKT:
                            s_mm(kt + 1)
                        sb_ = kt % 2
                        r = kt % NP
                        E("scalar", lambda e: e.activation(out=pT[r][:], in_=psbig[sb_][:], func=AF.Exp, bias=mk[:, kt:kt + 1], scale=scale),
                          reads=BS[sb_] + [Bmk], writes=[BpT[r]])
                        for hf in range(2):
                            E("tensor", lambda e: e.matmul(psbig[2][:, hf * BLK:(hf + 1) * BLK], lhsT=Vt[:, kt, :], rhs=pT[r][:, hf * BLK:(hf + 1) * BLK],
                                                           start=(kt == 0), stop=(kt == NKT - 1)),
                              reads=[BKV, BpT[r]], writes=[BO[hf]], signal=(kt == NKT - 1 and hf == 1))
                        if kt % 2 == 0:
                            for hf in range(2):
                                hs = slice(hf * BLK, (hf + 1) * BLK)
                                E("tensor", lambda e: e.matmul(psbig[3][:, hs], lhsT=ones_b[:], rhs=pT[r][:, hs], start=(kt == 0), stop=False),
                                  reads=[Bconst, BpT[r]], writes=[BD[hf]], signal=False)
                        else:
                            a = (kt // 2) % 2
                            eng = "vector" if a == 0 else "gpsimd"
                            if nd[a] == 0:
                                E(eng, lambda e: e.tensor_copy(out=acc[a][:], in_=pT[r][:]), reads=[BpT[r]], writes=[Bacc[a]])
                            else:
                                E(eng, lambda e: e.tensor_tensor(out=acc[a][:], in0=acc[a][:], in1=pT[r][:], op=ALU.add), reads=[BpT[r], Bacc[a]], writes=[Bacc[a]])
                            nd[a] += 1
                    E("vector", lambda e: e.tensor_tensor(out=acc[0][:], in0=acc[0][:], in1=acc[1][:], op=ALU.add), reads=[Bacc[0], Bacc[1]], writes=[Bacc[0]])
                    E("vector", lambda e: e.tensor_copy(out=dhi[:], in_=acc[0][:]), reads=[Bacc[0]], writes=[Bdh])
                    E("vector", lambda e: e.tensor_tensor(out=acc[0][:], in0=acc[0][:], in1=dhi[:], op=ALU.subtract), reads=[Bdh, Bacc[0]], writes=[Bacc[0]])
                    E("vector", lambda e: e.tensor_copy(out=dlo[:], in_=acc[0][:]), reads=[Bacc[0]], writes=[Bdh])
                    for hf in range(2):
                        hs = slice(hf * BLK, (hf + 1) * BLK)
                        E("tensor", lambda e: e.matmul(psbig[3][:, hs], lhsT=ones_b[:], rhs=dhi[:, hs], start=False, stop=False), reads=[Bconst, Bdh], writes=[BD[hf]], signal=False)
                        E("tensor", lambda e: e.matmul(psbig[3][:, hs], lhsT=ones_b[:], rhs=dlo[:, hs], start=False, stop=True), reads=[Bconst, Bdh], writes=[BD[hf]])
                    E("vector", lambda e: e.reciprocal(out=rec[:], in_=psbig[3][:]), reads=BD, writes=[Brec])
                    E("vector", lambda e: e.tensor_tensor(out=yo[b][:], in0=psbig[2][:], in1=rec[:], op=ALU.mult), reads=BO + [Brec], writes=[Byo[b]])
                    E("sync", lambda e: e.dma_start(out=d["YA_s"][head, :, tok], in_=yo[b][:]), reads=[Byo[b]], writes=[buf("YA_s")], dma=True)
                    it += 1


def phase_weight_cast(c, d):
    E, buf = c["E"], c["buf"]
    for f in range(16):
        E("gpsimd", lambda e: e.dma_start(out=d["WOUT_b"][f].rearrange("p k c -> p (k c)"), in_=d["w_out"][f].rearrange("p k c -> p (k c)")),
          writes=[buf("WOUT_b")], dma=True)
    for f in range(64):
        E("gpsimd", lambda e: e.dma_start(out=d["WUP_b"][f].rearrange("p k c -> p (k c)"), in_=d["w_up"][f].rearrange("p k c -> p (k c)")),
          writes=[buf("WUP_b")], dma=True)
    for f in range(16):
        for h in range(2):
            E("gpsimd", lambda e: e.dma_start(out=d["WDN_b"][f, h].rearrange("p k c -> p (k c)"), in_=d["w_dn"][f, h].rearrange("p k c -> p (k c)")),
              writes=[buf("WDN_b")], dma=True)


def phase_mlp(c, d):
    nc, E, ps, Bps, sbt, buf, vcol = c["nc"], c["E"], c["ps"], c["Bps"], c["sbt"], c["buf"], c["vcol"]
    ones_b, Bconst = c["ones_b"], c["Bconst"]
    with ExitStack() as st:
        XT = sbt(st, "XT", [128, 16, BLK], F32)
        MT = sbt(st, "MT", [128, 16, BLK], F32)
        ACTb = sbt(st, "ACTb", [128, 16, BLK], BF16)
        AT = sbt(st, "AT", [128, 32, BLK], BF16)
        YA = sbt(st, "YA", [128, 8, BLK], F32)
        BXT, BMT, BACT, BAT, BYA = Buf(), Buf(), Buf(), Buf(), Buf()
        rstd = sbt(st, "rstde", [128, BLK], F32)
        Brstd = Buf()
        tmp = [sbt(st, "tmpe%d" % i, [128, BLK], F32) for i in range(2)]
        Btmp = [Buf(), Buf()]
        wo = [sbt(st, "wo%d" % i, [128, 16, 128], BF16) for i in range(3)]
        Bwo = [Buf() for _ in range(3)]
        wu = [sbt(st, "wu%d" % i, [128, 16, 128], BF16) for i in range(3)]
        Bwu = [Buf() for _ in range(3)]
        wd = [sbt(st, "wd%d" % i, [128, 32, 128], BF16) for i in range(3)]
        Bwd = [Buf() for _ in range(3)]
        src = d["xT_ctx"].rearrange("(k p) t -> p k t", p=128)
        SQ = AT[:, 0:16, :]

        def stats(srct, Bsrc, nk, denom, pi):
            E("scalar", lambda e: e.activation(out=SQ[:, 0:nk, :], in_=srct, func=AF.Square), reads=[Bsrc], writes=[BAT])
            for k in range(nk):
                E("tensor", lambda e, k=k: e.matmul(ps[pi][:], lhsT=ones_b[:], rhs=SQ[:, k, :], start=(k == 0), stop=(k == nk - 1)),
                  reads=[Bconst, BAT], writes=[Bps[pi]], signal=(k == nk - 1))
            E("scalar", lambda e: e.activation(out=rstd[:], in_=ps[pi][:], func=AF.Sqrt, scale=1.0 / denom, bias=EPS), reads=[Bps[pi]], writes=[Brstd])
            E("vector", lambda e: e.reciprocal(out=rstd[:], in_=rstd[:]), reads=[Brstd], writes=[Brstd])

        for blk in range(NB_OWN):
            tok = slice(blk * BLK, (blk + 1) * BLK)
            E("sync", lambda e: e.dma_start(out=XT[:, 0:8, :], in_=src[:, 0:8, tok]), writes=[BXT], dma=True)
            E("sync", lambda e: e.dma_start(out=XT[:, 8:16, :], in_=src[:, 8:16, tok]), writes=[BXT], dma=True)
            E("sync", lambda e: e.dma_start(out=ACTb[:, 0:8, :], in_=d["NS_s"][:, :, tok]), reads=[buf("NS_s")], writes=[BACT], dma=True)
            E("sync", lambda e: e.dma_start(out=YA[:], in_=d["YA_s"][:, :, tok].rearrange("h p t -> p h t")), reads=[buf("YA_s")], writes=[BYA], dma=True)
            stats(YA[:], BYA, 8, 1024.0, 0)
            for j in range(8):
                E("vector", lambda e: e.scalar_tensor_tensor(out=ACTb[:, 8 + j, :], in0=YA[:, j, :], scalar=vcol(V_ATTOUT + j), in1=rstd[:],
                                                             op0=ALU.mult, op1=ALU.mult), reads=[BYA, Brstd, Bconst], writes=[BACT])
            for dt in range(16):
                w, Bw = wo[dt % 3], Bwo[dt % 3]
                E("sync", lambda e: e.dma_start(out=w[:], in_=d["WOUT_b"][dt]), reads=[buf("WOUT_b")], writes=[Bw], dma=True)
                pi = 1 + dt % 2
                for k in range(16):
                    E("tensor", lambda e, k=k: e.matmul(ps[pi][:], lhsT=w[:, k, :], rhs=ACTb[:, k, :], start=(k == 0), stop=(k == 15)),
                      reads=[Bw, BACT], writes=[Bps[pi]], signal=(k == 15))
                E("scalar", lambda e: e.copy(out=MT[:, dt, :], in_=ps[pi][:]), reads=[Bps[pi]], writes=[BMT])
            stats(MT[:], BMT, 16, 2048.0, 0)
            for k in range(16):
                t, Bt = tmp[k % 2], Btmp[k % 2]
                E("vector", lambda e: e.scalar_tensor_tensor(out=t[:], in0=MT[:, k, :], scalar=vcol(V_POSTMIX + k), in1=rstd[:],
                                                             op0=ALU.mult, op1=ALU.mult), reads=[BMT, Brstd, Bconst], writes=[Bt])
                E("gpsimd", lambda e: e.tensor_tensor(out=XT[:, k, :], in0=XT[:, k, :], in1=t[:], op=ALU.add), reads=[Bt, BXT], writes=[BXT])
            stats(XT[:], BXT, 16, 2048.0, 0)
            for k in range(16):
                E("vector", lambda e: e.scalar_tensor_tensor(out=ACTb[:, k, :], in0=XT[:, k, :], scalar=vcol(V_PREMLP + k), in1=rstd[:],
                                                             op0=ALU.mult, op1=ALU.mult), reads=[BXT, Brstd, Bconst], writes=[BACT])
            for half in range(2):
                for f in range(32):
                    ff = half * 32 + f
                    w, Bw = wu[ff % 3], Bwu[ff % 3]
                    E("sync", lambda e: e.dma_start(out=w[:], in_=d["WUP_b"][ff]), reads=[buf("WUP_b")], writes=[Bw], dma=True)
                    pi = 3 + ff % 2
                    for k in range(16):
                        E("tensor", lambda e, k=k: e.matmul(ps[pi][:], lhsT=w[:, k, :], rhs=ACTb[:, k, :], start=(k == 0), stop=(k == 15)),
                          reads=[Bw, BACT], writes=[Bps[pi]], signal=(k == 15))
                    t, Bt = tmp[ff % 2], Btmp[ff % 2]
                    E("scalar", lambda e: e.activation(out=t[:], in_=ps[pi][:], func=AF.Relu), reads=[Bps[pi]], writes=[Bt])
                    E("gpsimd" if ff % 2 else "vector", lambda e: e.tensor_tensor(out=AT[:, f, :], in0=t[:], in1=t[:], op=ALU.mult), reads=[Bt], writes=[BAT])
                for dt in range(16):
                    i3 = (half * 16 + dt) % 3
                    w, Bw = wd[i3], Bwd[i3]
                    E("sync", lambda e: e.dma_start(out=w[:], in_=d["WDN_b"][dt, half]), reads=[buf("WDN_b")], writes=[Bw], dma=True)
                    pi = 5 + dt % 2
                    for f in range(32):
                        E("tensor", lambda e, f=f: e.matmul(ps[pi][:], lhsT=w[:, f, :], rhs=AT[:, f, :], start=(f == 0), stop=(f == 31)),
                          reads=[Bw, BAT], writes=[Bps[pi]], signal=(f == 31))
                    if half == 0:
                        E("scalar", lambda e: e.copy(out=MT[:, dt, :], in_=ps[pi][:]), reads=[Bps[pi]], writes=[BMT])
                    else:
                        E("vector", lambda e: e.tensor_tensor(out=MT[:, dt, :], in0=MT[:, dt, :], in1=ps[pi][:], op=ALU.add), reads=[Bps[pi], BMT], writes=[BMT])
            stats(MT[:], BMT, 16, 2048.0, 0)
            for k in range(16):
                t, Bt = tmp[k % 2], Btmp[k % 2]
                E("vector", lambda e: e.scalar_tensor_tensor(out=t[:], in0=MT[:, k, :], scalar=vcol(V_POSTMLP + k), in1=rstd[:],
                                                             op0=ALU.mult, op1=ALU.mult), reads=[BMT, Brstd, Bconst], writes=[Bt])
                E("gpsimd", lambda e: e.tensor_tensor(out=XT[:, k, :], in0=XT[:, k, :], in1=t[:], op=ALU.add), reads=[Bt, BXT], writes=[BXT])
            dst = d["yT_out"].rearrange("(k p) t -> p k t", p=128)
            E("sync", lambda e: e.dma_start(out=dst[:, 0:8, tok], in_=XT[:, 0:8, :]), reads=[BXT], dma=True)
            E("sync", lambda e: e.dma_start(out=dst[:, 8:16, tok], in_=XT[:, 8:16, :]), reads=[BXT], dma=True)


_STAGES = os.environ.get("MK_STAGES", "PWABSCDE")


def run_cores(inputs, stages=_STAGES, debug=()):
    sh = _prep_shared(inputs)
    in_maps = [_prep_core(inputs, c, sh) for c in range(8)]
    nc, kb = build_program(stages, debug)
    res = run_bass_kernel_spmd(nc, in_maps, core_ids=list(range(8)))
    return res, kb


def kernel(**inputs):
    res, _ = run_cores(inputs)
    yp = np.stack([np.ascontiguousarray(res.results[c]["yT_out"].T) for c in range(4)], axis=0)
    ys = np.concatenate([res.results[4 + j]["yT_out"].T for j in range(4)], axis=0)[None]
    return (np.ascontiguousarray(yp.astype(np.float32)), np.ascontiguousarray(ys.astype(np.float32)))
```

```python
import os
import numpy as np
from contextlib import ExitStack
import concourse.bass as bass
import concourse.mybir as mybir
from concourse.bass_utils import run_bass_kernel_spmd

F32 = mybir.dt.float32
BF16 = mybir.dt.bfloat16
AF = mybir.ActivationFunctionType
ALU = mybir.AluOpType

NT = int(os.environ.get("MK_NT", "4096"))
NCTX = 4 * NT
BLK = 512
NB_OWN = NT // BLK
NB_CTX = NCTX // BLK
NKT = NCTX // 128
NROW = NCTX // 64
EPS = 1e-6
MASK_NEG = -30000.0

V_PREMIX, V_POSTMIX, V_PREMLP, V_POSTMLP, V_SSMOUT, V_ATTOUT, V_QN, V_KN, V_D = 0, 16, 32, 48, 64, 72, 80, 81, 82
NVEC = 90


class Buf:
    __slots__ = ("writers", "readers")

    def __init__(self):
        self.writers = {}
        self.readers = {}


class KB:
    ENGS = ("sync", "tensor", "vector", "scalar", "gpsimd")

    NDMA = 20

    def __init__(self, nc, es):
        self.nc = nc
        self.sems = {}
        self.counts = {}
        self.waited = {e: {} for e in self.ENGS}
        self.nops = {e: 0 for e in self.ENGS}
        self.rr = {e: 0 for e in self.ENGS}
        self.pe_prev_serial = False
        for e in self.ENGS:
            if e != "sync":
                self.sems[e] = es.enter_context(nc.semaphore("s_" + e))
                self.counts[e] = 0
        for e in ("sync", "gpsimd"):
            for i in range(self.NDMA):
                k = "%s_dma%d" % (e, i)
                self.sems[k] = es.enter_context(nc.semaphore("s_" + k))
                self.counts[k] = 0

    def emit(self, eng, fn, reads=(), writes=(), dma=False, signal=True, serial=False):
        need = {}
        for b in reads:
            for s, v in b.writers.items():
                if need.get(s, 0) < v:
                    need[s] = v
        for b in writes:
            for s, v in b.writers.items():
                if need.get(s, 0) < v:
                    need[s] = v
            for s, v in b.readers.items():
                if need.get(s, 0) < v:
                    need[s] = v
        e = getattr(self.nc, eng)
        w = self.waited[eng]
        if dma:
            key = "%s_dma%d" % (eng, self.rr[eng] % self.NDMA)
            self.rr[eng] += 1
            if self.counts[key] > need.get(key, 0):
                need[key] = self.counts[key]
        else:
            key = eng
        if eng == "tensor":
            if serial or self.pe_prev_serial:
                need["tensor"] = self.counts["tensor"]
            else:
                need.pop("tensor", None)
            self.pe_prev_serial = serial
        for s, v in need.items():
            if w.get(s, 0) < v:
                w[s] = v
                e.wait_ge(self.sems[s], v)
        inc = 16 if dma else 1
        if signal:
            self.counts[key] += inc
            val = self.counts[key]
            ins = fn(e)
            ins.then_inc(self.sems[key], inc)
        else:
            assert eng == "tensor" and not dma
            val = self.counts[key] + 1
            fn(e)
        self.nops[eng] += 1
        for b in writes:
            b.writers[key] = val
        for b in reads:
            b.readers[key] = val
        return (key, val)

    def barrier(self):
        for eng in self.ENGS:
            e = getattr(self.nc, eng)
            w = self.waited[eng]
            for k, v in self.counts.items():
                if v > 0 and w.get(k, 0) < v:
                    w[k] = v
                    e.wait_ge(self.sems[k], v)

    def finish(self):
        for eng in ("sync", "gpsimd"):
            for i in range(self.NDMA):
                k = "%s_dma%d" % (eng, i)
                if self.counts[k] > 0:
                    getattr(self.nc, eng).wait_ge(self.sems[k], self.counts[k])


def _rope_compact(pos_of_ctx_row):
    inv = (np.float32(10000.0) ** (-(np.arange(0, 64, 2, dtype=np.float32)) / np.float32(64))).astype(np.float32)
    f = np.arange(64) % 32
    cos = np.zeros((128, NROW), np.float32)
    sin = np.zeros((128, NROW), np.float32)
    rows = pos_of_ctx_row.astype(np.float32)
    ang = (rows[None, :] * inv[f][:, None]).astype(np.float32)
    cos[:64], sin[:64] = np.cos(ang), np.sin(ang)
    cols = np.arange(64, dtype=np.float32)
    angc = (cols[None, :] * inv[f][:, None]).astype(np.float32)
    cos[64:, :64], sin[64:, :64] = np.cos(angc), np.sin(angc)
    return cos, sin


def _consts():
    c = {}
    rp = np.zeros((128, 128), np.float32)
    for m in range(128):
        j = m % 64
        if j < 32:
            rp[m + 32, m] = -1.0
        else:
            rp[m - 32, m] = 1.0
    c["rperm"] = rp
    c["ident"] = np.eye(128, dtype=np.float32)
    sel = np.zeros((128, 64, 128), np.float32)
    selT = np.zeros((128, 64, 128), np.float32)
    for gl in range(8):
        for tau in range(8):
            for h in range(16):
                sel[gl * 16 + h, gl * 8 + tau, tau * 16 + h] = 1.0
                selT[tau * 16 + h, gl * 8 + tau, gl * 16 + h] = 1.0
    c["sel"] = sel
    c["selT"] = selT
    tp = np.arange(128) // 16
    c["maskL"] = (tp[:, None] <= tp[None, :]).astype(np.float32)
    c["maskU"] = (tp[:, None] >= tp[None, :]).astype(np.float32)
    return c


def _prep_shared(inp):
    f = lambda a: np.ascontiguousarray(np.asarray(a, dtype=np.float32))
    sh = {}
    w_in = f(inp["w_in"])[0]
    sh["w_in_t"] = f(w_in.reshape(16, 128, 2560).transpose(1, 0, 2))
    sh["w_glu_t"] = f(f(inp["w_glu"])[0].reshape(8, 128, 1024).transpose(1, 0, 2))
    w_out = f(inp["w_out"])[0]
    sh["w_out_t"] = f(w_out.reshape(16, 128, 16, 128).transpose(2, 1, 0, 3))
    w_up = f(inp["w_up"])[0]
    sh["w_up_t"] = f(w_up.reshape(16, 128, 64, 128).transpose(2, 1, 0, 3))
    w_dn = f(inp["w_down"])[0]
    sh["w_dn_t"] = f(w_dn.reshape(2, 32, 128, 16, 128).transpose(3, 0, 2, 1, 4))
    vec = np.zeros((128, NVEC), np.float32)
    pk = lambda v, n: f(v).reshape(n, 128).T
    vec[:, V_PREMIX:V_PREMIX + 16] = pk(inp["pre_mix_norm"], 16)
    vec[:, V_POSTMIX:V_POSTMIX + 16] = pk(inp["post_mix_norm"], 16)
    vec[:, V_PREMLP:V_PREMLP + 16] = pk(inp["pre_mlp_norm"], 16)
    vec[:, V_POSTMLP:V_POSTMLP + 16] = pk(inp["post_mlp_norm"], 16)
    vec[:, V_SSMOUT:V_SSMOUT + 8] = pk(inp["ssm_out_norm"], 8)
    vec[:, V_ATTOUT:V_ATTOUT + 8] = pk(inp["attn_out_norm"], 8)
    vec[:, V_QN] = f(inp["q_norm"])[0]
    vec[:, V_KN] = f(inp["k_norm"])[0]
    vec[:, V_D:V_D + 8] = pk(inp["ssm_d"], 8)
    sh["vecs"] = vec
    a_re = f(inp["ssm_a_re"])[0]
    a_im = f(inp["ssm_a_im"])[0]
    ldt = f(inp["ssm_log_dt"])[0]
    sh["ssm_ar"] = f(a_re.transpose(0, 2, 1).reshape(128, 64))
    sh["ssm_ai"] = f(a_im.transpose(0, 2, 1).reshape(128, 64))
    sh["ssm_ldt"] = f(np.broadcast_to(ldt[:, None, :], (2, 64, 64)).reshape(128, 64))
    sh["ssm_bre"] = f(f(inp["ssm_b_re"])[0].transpose(0, 2, 1, 3).reshape(128, 64, 16))
    sh["ssm_bim"] = f(f(inp["ssm_b_im"])[0].transpose(0, 2, 1, 3).reshape(128, 64, 16))
    sh["ssm_cre"] = f(f(inp["ssm_c_re"])[0].transpose(0, 3, 1, 2).reshape(128, 64, 16))
    sh["ssm_cim"] = f(f(inp["ssm_c_im"])[0].transpose(0, 3, 1, 2).reshape(128, 64, 16))
    d = f(inp["ssm_d"])[0].reshape(64, 16)
    sh["ssm_dug"] = f(np.tile(d.T, (8, 1)))
    sh.update(_consts())
    return sh


def _prep_core(inp, core, sh):
    m = dict(sh)
    ctx = np.zeros((2048, NCTX), np.float32)
    mask = np.zeros((NCTX,), np.float32)
    gates = np.zeros((128, 3), np.float32)
    if core < 4:
        x = np.asarray(inp["x_prompt"], np.float32)[core]
        ctx[:, :NT] = x.T
        mask[NT:] = MASK_NEG
        slots = [0, 0, 0, 0]
    else:
        slot = core - 4
        xs = np.asarray(inp["x_sample"], np.float32)[0]
        others = [j for j in range(4) if j != slot]
        slots = [slot] + others
        for i, j in enumerate(slots):
            ctx[:, i * NT:(i + 1) * NT] = xs[j * NT:(j + 1) * NT].T
        for s_ in range(3):
            gates[:64, s_] = 1.0 if s_ < slot else 0.0
            gates[64:, s_] = 1.0 if (2 - s_) >= slot else 0.0
    m["xT_ctx"] = ctx
    m["maskb"] = np.ascontiguousarray(mask.reshape(NKT, 128).T)
    rows = np.concatenate([np.arange(j * NT // 64, (j + 1) * NT // 64) for j in slots])
    m["rope_cos"], m["rope_sin"] = _rope_compact(rows)
    m["gates"] = gates
    return m


def build_program(stages="WAB", debug=()):
    nc = bass.Bass("TRN2", target_bir_lowering=False)
    dbg = set(debug)

    def din(name, shape):
        return nc.dram_tensor(name, list(shape), F32, kind="ExternalInput").ap()

    def dscr(name, shape, dt=BF16):
        kind = "ExternalOutput" if name in dbg else "Internal"
        return nc.dram_tensor(name, list(shape), dt, kind=kind).ap()

    xT_ctx = din("xT_ctx", [2048, NCTX])
    rope_cos_d, rope_sin_d = din("rope_cos", [128, NROW]), din("rope_sin", [128, NROW])
    maskb_d = din("maskb", [128, NKT])
    gates_d = din("gates", [128, 3])
    vecs_d = din("vecs", [128, NVEC])
    w_in_d = din("w_in_t", [128, 16, 2560])
    w_glu_d = din("w_glu_t", [128, 8, 1024])
    has_mlp = ("P" in stages) or ("E" in stages)
    w_out_d = din("w_out_t", [16, 128, 16, 128]) if has_mlp else None
    w_up_d = din("w_up_t", [64, 128, 16, 128]) if has_mlp else None
    w_dn_d = din("w_dn_t", [16, 2, 128, 32, 128]) if has_mlp else None
    ssm_in = {k: din(k, s) for k, s in (("ssm_ar", [128, 64]), ("ssm_ai", [128, 64]), ("ssm_ldt", [128, 64]),
                                        ("ssm_bre", [128, 64, 16]), ("ssm_bim", [128, 64, 16]),
                                        ("ssm_cre", [128, 64, 16]), ("ssm_cim", [128, 64, 16]),
                                        ("ssm_dug", [128, 64]))}
    rperm_d, ident_d = din("rperm", [128, 128]), din("ident", [128, 128])
    sel_d, selT_d = din("sel", [128, 64, 128]), din("selT", [128, 64, 128])
    maskL_d, maskU_d = din("maskL", [128, 128]), din("maskU", [128, 128])
    yT_out = nc.dram_tensor("yT_out", [2048, NT], F32, kind="ExternalOutput").ap()

    KT_s = dscr("KT_s", [2, 128, NCTX])
    V_s = dscr("V_s", [2, 128, NKT, 128])
    QT_s = dscr("QT_s", [8, 128, NT])
    WS_s = dscr("WS_s", [128, 64, 5, 128])
    UG_s = dscr("UG_s", [NB_OWN, 128, 64, 64])
    SF_s = dscr("SF_s", [NB_OWN, 64, 64, 2, 64])
    ZB_s = dscr("ZB_s", [NB_OWN, 64, 64, 2, 64])
    NS_s = dscr("NS_s", [128, 8, NT])
    YA_s = dscr("YA_s", [8, 128, NT], F32)
    WOUT_b = dscr("WOUT_b", [16, 128, 16, 128])
    WUP_b = dscr("WUP_b", [64, 128, 16, 128])
    WDN_b = dscr("WDN_b", [16, 2, 128, 32, 128])
    EST_s = dscr("EST_s", [128, 2, 64], F32)

    es = ExitStack()
    kb = KB(nc, es)
    E = kb.emit
    B = {}

    def buf(name):
        if name not in B:
            B[name] = Buf()
        return B[name]

    uid = [0]

    def sbt(st, name, shape, dt):
        uid[0] += 1
        return st.enter_context(nc.sbuf_tensor("sb%d_%s" % (uid[0], name), list(shape), dt))

    psbig = [es.enter_context(nc.psum_tensor("psb%d" % i, [128, 1024], F32)) for i in range(4)]
    ps = [psbig[i // 2][:, (i % 2) * 512:(i % 2 + 1) * 512] for i in range(8)]
    Bps = [Buf() for _ in range(8)]
    ones_b = sbt(es, "ones_b", [128, 128], BF16)
    ones_f = sbt(es, "ones_f", [128, 128], F32)
    ident_f = sbt(es, "ident_f", [128, 128], F32)
    rperm_b = sbt(es, "rperm_b", [128, 128], BF16)
    vecs = sbt(es, "vecs", [128, NVEC], F32)
    gates = sbt(es, "gates", [128, 3], F32)
    rope_c = sbt(es, "rope_c", [128, NROW], F32)
    rope_s = sbt(es, "rope_s", [128, NROW], F32)
    A8a = sbt(es, "A8a", [128, 2, 64], F32)
    A8b = sbt(es, "A8b", [128, 2, 64], F32)
    A8c = sbt(es, "A8c", [128, 2, 64], F32)
    Sin_t = sbt(es, "Sin_t", [128, 2, 64], F32)
    Bconst = Buf()
    BA8 = Buf()
    BSin = Buf()

    E("vector", lambda e: e.memset(ones_b[:], 1.0), writes=[Bconst])
    E("vector", lambda e: e.memset(ones_f[:], 1.0), writes=[Bconst])
    E("sync", lambda e: e.dma_start(out=ident_f[:], in_=ident_d), writes=[Bconst], dma=True)
    E("gpsimd", lambda e: e.dma_start(out=rperm_b[:], in_=rperm_d), writes=[Bconst], dma=True)
    E("sync", lambda e: e.dma_start(out=vecs[:], in_=vecs_d), writes=[Bconst], dma=True)
    E("sync", lambda e: e.dma_start(out=gates[:], in_=gates_d), writes=[Bconst], dma=True)
    E("sync", lambda e: e.dma_start(out=rope_c[:], in_=rope_cos_d), writes=[Bconst], dma=True)
    E("sync", lambda e: e.dma_start(out=rope_s[:], in_=rope_sin_d), writes=[Bconst], dma=True)

    def vcol(c0, n=1):
        return vecs[:, c0:c0 + n]

    def cmul(eng, out, a, b, tmp1, tmp2, bufs_r, bufs_w, rows=slice(0, 128)):
        r = rows
        E(eng, lambda e: e.tensor_tensor(out=tmp1[r, 0, :], in0=a[r, 0, :], in1=b[r, 0, :], op=ALU.mult), reads=bufs_r, writes=bufs_w)
        E(eng, lambda e: e.tensor_tensor(out=tmp1[r, 1, :], in0=a[r, 1, :], in1=b[r, 1, :], op=ALU.mult), reads=bufs_r, writes=bufs_w)
        E(eng, lambda e: e.tensor_tensor(out=tmp2[r, 0, :], in0=a[r, 0, :], in1=b[r, 1, :], op=ALU.mult), reads=bufs_r, writes=bufs_w)
        E(eng, lambda e: e.tensor_tensor(out=tmp2[r, 1, :], in0=a[r, 1, :], in1=b[r, 0, :], op=ALU.mult), reads=bufs_r, writes=bufs_w)
        E(eng, lambda e: e.tensor_tensor(out=out[r, 0, :], in0=tmp1[r, 0, :], in1=tmp1[r, 1, :], op=ALU.subtract), reads=bufs_r, writes=bufs_w)
        E(eng, lambda e: e.tensor_tensor(out=out[r, 1, :], in0=tmp2[r, 0, :], in1=tmp2[r, 1, :], op=ALU.add), reads=bufs_r, writes=bufs_w)

    ctx = dict(nc=nc, kb=kb, E=E, es=es, ps=ps, psbig=psbig, Bps=Bps, buf=buf, sbt=sbt, vcol=vcol, cmul=cmul,
               ones_b=ones_b, ones_f=ones_f, ident_f=ident_f, rperm_b=rperm_b, vecs=vecs, gates=gates, rope_c=rope_c, rope_s=rope_s,
               A8a=A8a, A8b=A8b, A8c=A8c, Sin_t=Sin_t, Bconst=Bconst, BA8=BA8, BSin=BSin)
    dr = dict(xT_ctx=xT_ctx, maskb=maskb_d, w_in=w_in_d, w_glu=w_glu_d, w_out=w_out_d, w_up=w_up_d, w_dn=w_dn_d, ssm=ssm_in,
              sel=sel_d, selT=selT_d, maskL=maskL_d, maskU=maskU_d, yT_out=yT_out,
              KT_s=KT_s, V_s=V_s, QT_s=QT_s, WS_s=WS_s, UG_s=UG_s, SF_s=SF_s, ZB_s=ZB_s, NS_s=NS_s, YA_s=YA_s,
              WOUT_b=WOUT_b, WUP_b=WUP_b, WDN_b=WDN_b, EST_s=EST_s)

    if "P" in stages:
        phase_weight_cast(ctx, dr)
    if "W" in stages:
        phase_ssm_weights(ctx, dr)
        kb.barrier()
    if "A" in stages:
        phase_proj(ctx, dr, own=False)
        kb.barrier()
    if "B" in stages:
        phase_proj(ctx, dr, own=True)
        kb.barrier()
    if "S" in stages:
        phase_ssm_scan(ctx, dr, own=False)
        kb.barrier()
        phase_ssm_carry(ctx, dr)
        kb.barrier()
        phase_ssm_scan(ctx, dr, own=True)
        kb.barrier()
    if "C" in stages:
        phase_ssm_out(ctx, dr)
        kb.barrier()
    if "D" in stages:
        phase_attention(ctx, dr)
        kb.barrier()
    if "E" in stages:
        phase_mlp(ctx, dr)
    kb.finish()
    es.close()
    return nc, kb


def load_x_block(c, xT_src, blk, xt, Bxt):
    E = c["E"]
    sl = slice(blk * BLK, (blk + 1) * BLK)
    src = xT_src.rearrange("(k p) t -> p k t", p=128)
    E("sync", lambda e: e.dma_start(out=xt[:, 0:8, :], in_=src[:, 0:8, sl]), writes=[Bxt], dma=True)
    E("sync", lambda e: e.dma_start(out=xt[:, 8:16, :], in_=src[:, 8:16, sl]), writes=[Bxt], dma=True)


def load_norm_block(c, st, xT_src, blk, xt, sq, ht, rstd, Bxt, Bsq, Bht, Brstd, gcol, ps_i=0, do_load=True):
    E, ps, Bps = c["E"], c["ps"], c["Bps"]
    ones_b, Bconst, vcol = c["ones_b"], c["Bconst"], c["vcol"]
    sl = slice(blk * BLK, (blk + 1) * BLK)
    src = xT_src.rearrange("(k p) t -> p k t", p=128)
    Bsq = Bsq if isinstance(Bsq, list) else [Bsq]
    if do_load:
        E("sync", lambda e: e.dma_start(out=xt[:, 0:8, :], in_=src[:, 0:8, sl]), writes=[Bxt], dma=True)
        E("sync", lambda e: e.dma_start(out=xt[:, 8:16, :], in_=src[:, 8:16, sl]), writes=[Bxt], dma=True)
    E("scalar", lambda e: e.activation(out=sq[:], in_=xt[:], func=AF.Square), reads=[Bxt], writes=Bsq)
    for k in range(16):
        E("tensor", lambda e, k=k: e.matmul(ps[ps_i][:], lhsT=ones_b[:], rhs=sq[:, k, :], start=(k == 0), stop=(k == 15)),
          reads=[Bconst] + Bsq, writes=[Bps[ps_i]], signal=(k == 15))
    E("scalar", lambda e: e.activation(out=rstd[:], in_=ps[ps_i][:], func=AF.Sqrt, scale=1.0 / 2048, bias=EPS),
      reads=[Bps[ps_i]], writes=[Brstd])
    E("vector", lambda e: e.reciprocal(out=rstd[:], in_=rstd[:]), reads=[Brstd], writes=[Brstd])
    for k in range(16):
        E("vector", lambda e, k=k: e.scalar_tensor_tensor(out=ht[:, k, :], in0=xt[:, k, :], scalar=vcol(gcol + k), in1=rstd[:],
                                                         op0=ALU.mult, op1=ALU.mult),
          reads=[Bxt, Brstd, Bconst], writes=[Bht])


def phase_proj(c, d, own):
    nc, E, ps, Bps, sbt, buf, vcol = c["nc"], c["E"], c["ps"], c["Bps"], c["sbt"], c["buf"], c["vcol"]
    ones_b, rperm_b, Bconst = c["ones_b"], c["rperm_b"], c["Bconst"]
    H = 8 if own else 2
    ncols = 1024 if own else 512
    col0 = 1024 if own else 2048
    nblk = NB_OWN if own else NB_CTX
    xT_src = d["xT_ctx"]
    rope_c, rope_s = c["rope_c"], c["rope_s"]
    gn = V_QN if own else V_KN
    out_s = d["QT_s"] if own else d["KT_s"]
    Bout = buf("QT_s" if own else "KT_s")
    BV = buf("V_s")
    with ExitStack() as st:
        W = sbt(st, "W_p", [128, 16, ncols], BF16)
        BW = Buf()
        for k in range(0, 16, 4):
            E("gpsimd", lambda e, k=k: e.dma_start(out=W[:, k:k + 4, :], in_=d["w_in"][:, k:k + 4, col0:col0 + ncols]), writes=[BW], dma=True)
        nx = 2 if own else 3
        xt = [sbt(st, "xt%d" % i, [128, 16, BLK], F32) for i in range(nx)]
        cs = [sbt(st, "cs%d" % i, [128, BLK], F32) for i in range(2)]
        sn = [sbt(st, "sn%d" % i, [128, BLK], F32) for i in range(2)]
        Bxt, Bcs = [Buf(), Buf(), Buf()], [Buf(), Buf()]
        for i in range(2):
            E("gpsimd", lambda e: e.tensor_copy(out=cs[i][64:128, :].rearrange("p (r c) -> p r c", c=64),
                                                in_=rope_c[64:128, None, 0:64].to_broadcast([64, 8, 64])), reads=[Bconst], writes=[Bcs[i]])
            E("gpsimd", lambda e: e.tensor_copy(out=sn[i][64:128, :].rearrange("p (r c) -> p r c", c=64),
                                                in_=rope_s[64:128, None, 0:64].to_broadcast([64, 8, 64])), reads=[Bconst], writes=[Bcs[i]])
        sq_one = sbt(st, "sq", [128, 16, BLK], BF16)
        sq2 = [sq_one, sq_one]
        ht2 = [sbt(st, "ht%d" % i, [128, 16, BLK], BF16) for i in range(2)]
        rstd2 = [sbt(st, "rstd%d" % i, [128, BLK], F32) for i in range(2)]
        Bsq_one = Buf()
        Bsq2, Bht2, Brstd2 = [Bsq_one, Bsq_one], [Buf(), Buf()], [Buf(), Buf()]
        sqh = [sbt(st, "sqh%d" % i, [128, BLK], BF16) for i in range(2)]
        rk = [sbt(st, "rk%d" % i, [128, BLK], F32) for i in range(2)]
        kn = [sbt(st, "kn%d" % i, [128, BLK], BF16) for i in range(2)]
        t1 = [sbt(st, "t1%d" % i, [128, BLK], F32) for i in range(2)]
        t2 = [sbt(st, "t2%d" % i, [128, BLK], F32) for i in range(2)]
        ko = [sbt(st, "ko%d" % i, [128, BLK], BF16) for i in range(2)]
        Bsqh, Brk, Bkn, Bt1, Bt2, Bko = [[Buf(), Buf()] for _ in range(6)]
        vt = [sbt(st, "vt%d" % i, [128, 4, 256], BF16) for i in range(2)] if not own else None
        Bvt = [Buf(), Buf()]
        for blk in range(nblk):
            b = blk % 2
            sl = slice(blk * BLK, (blk + 1) * BLK)
            E("gpsimd", lambda e: e.tensor_copy(out=cs[b][0:64, :].rearrange("p (r c) -> p r c", c=64),
                                                in_=rope_c[0:64, blk * 8:(blk + 1) * 8, None].to_broadcast([64, 8, 64])), reads=[Bconst], writes=[Bcs[b]])
            E("gpsimd", lambda e: e.tensor_copy(out=sn[b][0:64, :].rearrange("p (r c) -> p r c", c=64),
                                                in_=rope_s[0:64, blk * 8:(blk + 1) * 8, None].to_broadcast([64, 8, 64])), reads=[Bconst], writes=[Bcs[b]])
            sq, ht, rstd, Bsq, Bht, Brstd = sq2[b], ht2[b], rstd2[b], Bsq2[b], Bht2[b], Brstd2[b]
            if blk == 0:
                load_x_block(c, xT_src, 0, xt[0], Bxt[0])
                if nblk > 1:
                    load_x_block(c, xT_src, 1, xt[1], Bxt[1])
                load_norm_block(c, st, xT_src, 0, xt[0], sq2[0], ht2[0], rstd2[0], Bxt[0], Bsq2[0], Bht2[0], Brstd2[0], V_PREMIX, ps_i=0, do_load=False)
            if blk + 2 < nblk:
                load_x_block(c, xT_src, blk + 2, xt[(blk + 2) % nx], Bxt[(blk + 2) % nx])
            if blk + 1 < nblk:
                nb_ = (blk + 1) % 2
                load_norm_block(c, st, xT_src, blk + 1, xt[(blk + 1) % nx], sq2[nb_], ht2[nb_], rstd2[nb_], Bxt[(blk + 1) % nx], Bsq2[nb_], Bht2[nb_], Brstd2[nb_],
                                V_PREMIX, ps_i=0, do_load=False)
            for hd in range(H):
                i = hd % 2
                pk, pss, pr = 1 + i, 3 + i, 5 + i
                for k in range(16):
                    E("tensor", lambda e, k=k: e.matmul(ps[pk][:], lhsT=W[:, k, hd * 128:(hd + 1) * 128], rhs=ht[:, k, :],
                                                        start=(k == 0), stop=(k == 15)), reads=[BW, Bht], writes=[Bps[pk]], signal=(k == 15))
                E("scalar", lambda e: e.activation(out=sqh[i][:], in_=ps[pk][:], func=AF.Square), reads=[Bps[pk]], writes=[Bsqh[i]])
                E("tensor", lambda e: e.matmul(ps[pss][:], lhsT=ones_b[:], rhs=sqh[i][:], start=True, stop=True),
                  reads=[Bconst, Bsqh[i]], writes=[Bps[pss]])
                E("scalar", lambda e: e.activation(out=rk[i][:], in_=ps[pss][:], func=AF.Sqrt, scale=1.0 / 128, bias=EPS),
                  reads=[Bps[pss]], writes=[Brk[i]])
                E("vector", lambda e: e.reciprocal(out=rk[i][:], in_=rk[i][:]), reads=[Brk[i]], writes=[Brk[i]])
                E("vector", lambda e: e.scalar_tensor_tensor(out=kn[i][:], in0=ps[pk][:], scalar=vcol(gn), in1=rk[i][:],
                                                             op0=ALU.mult, op1=ALU.mult),
                  reads=[Bps[pk], Brk[i], Bconst], writes=[Bkn[i]])
                E("tensor", lambda e: e.matmul(ps[pr][:], lhsT=rperm_b[:], rhs=kn[i][:], start=True, stop=True),
                  reads=[Bconst, Bkn[i]], writes=[Bps[pr]])
                E("gpsimd", lambda e: e.tensor_tensor(out=t1[i][:], in0=kn[i][:], in1=cs[b][:], op=ALU.mult),
                  reads=[Bkn[i], Bcs[b]], writes=[Bt1[i]])
                E("vector", lambda e: e.tensor_tensor(out=t2[i][:], in0=ps[pr][:], in1=sn[b][:], op=ALU.mult),
                  reads=[Bps[pr], Bcs[b]], writes=[Bt2[i]])
                E("gpsimd", lambda e: e.tensor_tensor(out=ko[i][:], in0=t1[i][:], in1=t2[i][:], op=ALU.add),
                  reads=[Bt1[i], Bt2[i]], writes=[Bko[i]])
                E("sync", lambda e: e.dma_start(out=out_s[hd, :, sl], in_=ko[i][:]), reads=[Bko[i]], writes=[Bout], dma=True)
            if not own:
                for sub in range(4):
                    pv = 7
                    for k in range(16):
                        E("tensor", lambda e, k=k: e.matmul(ps[pv][:, 0:256], lhsT=ht[:, k, sub * 128:(sub + 1) * 128], rhs=W[:, k, 256:512],
                                                            start=(k == 0), stop=(k == 15)), reads=[BW, Bht], writes=[Bps[pv]], signal=(k == 15))
                    E("scalar", lambda e: e.copy(out=vt[b][:, sub, :], in_=ps[pv][:, 0:256]), reads=[Bps[pv]], writes=[Bvt[b]])
                for kvh in range(2):
                    E("sync", lambda e: e.dma_start(out=d["V_s"][kvh, :, blk * 4:(blk + 1) * 4, :], in_=vt[b][:, :, kvh * 128:(kvh + 1) * 128]),
                      reads=[Bvt[b]], writes=[BV], dma=True)


def phase_ssm_weights(c, d):
    nc, E, ps, Bps, sbt, buf, vcol = c["nc"], c["E"], c["ps"], c["Bps"], c["sbt"], c["buf"], c["vcol"]
    ident_f, Bconst = c["ident_f"], c["Bconst"]
    A8a, A8b, A8c, BA8 = c["A8a"], c["A8b"], c["A8c"], c["BA8"]
    S = d["ssm"]
    H0, H1 = slice(0, 64), slice(64, 128)
    with ExitStack() as st:
        T = lambda n, shape, dt=F32: sbt(st, n, shape, dt)
        ar, ai, ldt = T("ar", [128, 64]), T("ai", [128, 64]), T("ldt", [128, 64])
        bre, bim, cre, cim = T("bre", [128, 64, 16]), T("bim", [128, 64, 16]), T("cre", [128, 64, 16]), T("cim", [128, 64, 16])
        dug, mL, mU = T("dug", [128, 64]), T("mL", [128, 128]), T("mU", [128, 128])
        Bin = Buf()
        for t, k in ((ar, "ssm_ar"), (ai, "ssm_ai"), (ldt, "ssm_ldt"), (bre, "ssm_bre"), (bim, "ssm_bim"),
                     (cre, "ssm_cre"), (cim, "ssm_cim"), (dug, "ssm_dug")):
            E("sync", lambda e, t=t, k=k: e.dma_start(out=t[:], in_=S[k]), writes=[Bin], dma=True)
        E("sync", lambda e: e.dma_start(out=mL[:], in_=d["maskL"]), writes=[Bin], dma=True)
        E("sync", lambda e: e.dma_start(out=mU[:], in_=d["maskU"]), writes=[Bin], dma=True)
        Bw = Buf()
        V = lambda fn: E("vector", fn, reads=[Bin, Bw, Bconst], writes=[Bw])
        A = lambda fn: E("scalar", fn, reads=[Bin, Bw], writes=[Bw])
        dt_, lrdt, th, mag = T("dt_", [128, 64]), T("lrdt", [128, 64]), T("th", [128, 64]), T("mag", [128, 64])
        cc, ss, u1, u2 = T("cc", [128, 64]), T("ss", [128, 64]), T("u1", [128, 64]), T("u2", [128, 64])
        halfpi = T("halfpi", [128, 1])
        V(lambda e: e.memset(halfpi[:], float(np.pi / 2)))
        A(lambda e: e.activation(out=dt_[:], in_=ldt[:], func=AF.Exp))
        V(lambda e: e.tensor_tensor(out=lrdt[:], in0=ar[:], in1=dt_[:], op=ALU.mult))
        V(lambda e: e.tensor_tensor(out=th[:], in0=ai[:], in1=dt_[:], op=ALU.mult))
        A(lambda e: e.activation(out=mag[:], in_=lrdt[:], func=AF.Exp))
        A(lambda e: e.activation(out=ss[:], in_=th[:], func=AF.Sin, scale=1.0 / 32))
        A(lambda e: e.activation(out=cc[:], in_=th[:], func=AF.Sin, scale=1.0 / 32, bias=halfpi[:]))
        for _ in range(5):
            V(lambda e: e.tensor_tensor(out=u1[:], in0=cc[:], in1=cc[:], op=ALU.mult))
            V(lambda e: e.tensor_tensor(out=u2[:], in0=ss[:], in1=ss[:], op=ALU.mult))
            V(lambda e: e.scalar_tensor_tensor(out=ss[:], in0=cc[:], scalar=2.0, in1=ss[:], op0=ALU.mult, op1=ALU.mult))
            V(lambda e: e.tensor_tensor(out=cc[:], in0=u1[:], in1=u2[:], op=ALU.subtract))
        Lre, Lim = T("Lre", [128, 64, 9]), T("Lim", [128, 64, 9])
        Ire, Iim, inv = T("Ire", [128, 64, 9]), T("Iim", [128, 64, 9]), T("inv", [128, 64, 9])
        V(lambda e: e.memset(Lre[:, :, 0], 1.0))
        V(lambda e: e.memset(Lim[:, :, 0], 0.0))
        V(lambda e: e.tensor_tensor(out=Lre[:, :, 1], in0=mag[:], in1=cc[:], op=ALU.mult))
        V(lambda e: e.tensor_tensor(out=Lim[:, :, 1], in0=mag[:], in1=ss[:], op=ALU.mult))
        for k in range(2, 9):
            V(lambda e, k=k: e.tensor_tensor(out=u1[:], in0=Lre[:, :, k - 1], in1=Lre[:, :, 1], op=ALU.mult))
            V(lambda e, k=k: e.tensor_tensor(out=u2[:], in0=Lim[:, :, k - 1], in1=Lim[:, :, 1], op=ALU.mult))
            V(lambda e, k=k: e.tensor_tensor(out=Lre[:, :, k], in0=u1[:], in1=u2[:], op=ALU.subtract))
            V(lambda e, k=k: e.tensor_tensor(out=u1[:], in0=Lre[:, :, k - 1], in1=Lim[:, :, 1], op=ALU.mult))
            V(lambda e, k=k: e.tensor_tensor(out=u2[:], in0=Lim[:, :, k - 1], in1=Lre[:, :, 1], op=ALU.mult))
            V(lambda e, k=k: e.tensor_tensor(out=Lim[:, :, k], in0=u1[:], in1=u2[:], op=ALU.add))
        V(lambda e: e.tensor_tensor(out=inv[:], in0=Lre[:], in1=Lre[:], op=ALU.mult))
        V(lambda e: e.tensor_tensor(out=Ire[:], in0=Lim[:], in1=Lim[:], op=ALU.mult))
        V(lambda e: e.tensor_tensor(out=inv[:], in0=inv[:], in1=Ire[:], op=ALU.add))
        V(lambda e: e.reciprocal(out=inv[:], in_=inv[:]))
        V(lambda e: e.tensor_tensor(out=Ire[:], in0=Lre[:], in1=inv[:], op=ALU.mult))
        V(lambda e: e.scalar_tensor_tensor(out=Iim[:], in0=Lim[:], scalar=-1.0, in1=inv[:], op0=ALU.mult, op1=ALU.mult))
        E("vector", lambda e: e.tensor_copy(out=A8c[:, 0, :], in_=Lre[:, :, 8]), reads=[Bw], writes=[BA8])
        E("vector", lambda e: e.tensor_copy(out=A8c[:, 1, :], in_=Lim[:, :, 8]), reads=[Bw], writes=[BA8])
        E("vector", lambda e: e.tensor_copy(out=A8a[:, 0, :], in_=Lre[:, :, 8]), reads=[Bw], writes=[BA8])
        E("vector", lambda e: e.tensor_copy(out=A8a[:, 1, :], in_=Lre[:, :, 8]), reads=[Bw], writes=[BA8])
        E("vector", lambda e: e.tensor_scalar(out=A8b[:, 0, :], in0=Lim[:, :, 8], scalar1=-1.0, scalar2=None, op0=ALU.mult), reads=[Bw], writes=[BA8])
        E("vector", lambda e: e.tensor_copy(out=A8b[:, 1, :], in_=Lim[:, :, 8]), reads=[Bw], writes=[BA8])
        PCre, PCim, PGre, PGim = T("PCre", [128, 64, 8]), T("PCim", [128, 64, 8]), T("PGre", [128, 64, 8]), T("PGim", [128, 64, 8])
        for dst, src in ((PCre, Lre), (PCim, Lim), (PGre, Ire), (PGim, Iim)):
            V(lambda e, dst=dst, src=src: e.tensor_copy(out=dst[H0, :, :], in_=src[H0, :, 1:9]))
            V(lambda e, dst=dst, src=src: e.tensor_copy(out=dst[H1, :, :], in_=src[H1, :, 8:0:-1]))
        den, wre, wim = T("den", [128, 64]), T("wre", [128, 64]), T("wim", [128, 64])
        V(lambda e: e.tensor_tensor(out=den[:], in0=ar[:], in1=ar[:], op=ALU.mult))
        V(lambda e: e.tensor_tensor(out=u1[:], in0=ai[:], in1=ai[:], op=ALU.mult))
        V(lambda e: e.tensor_tensor(out=den[:], in0=den[:], in1=u1[:], op=ALU.add))
        V(lambda e: e.reciprocal(out=den[:], in_=den[:]))
        V(lambda e: e.tensor_scalar(out=u1[:], in0=Lre[:, :, 1], scalar1=-1.0, scalar2=None, op0=ALU.add))
        V(lambda e: e.tensor_tensor(out=wre[:], in0=u1[:], in1=ar[:], op=ALU.mult))
        V(lambda e: e.tensor_tensor(out=u2[:], in0=Lim[:, :, 1], in1=ai[:], op=ALU.mult))
        V(lambda e: e.tensor_tensor(out=wre[:], in0=wre[:], in1=u2[:], op=ALU.add))
        V(lambda e: e.tensor_tensor(out=wre[:], in0=wre[:], in1=den[:], op=ALU.mult))
        V(lambda e: e.tensor_tensor(out=wim[:], in0=Lim[:, :, 1], in1=ar[:], op=ALU.mult))
        V(lambda e: e.tensor_tensor(out=u2[:], in0=u1[:], in1=ai[:], op=ALU.mult))
        V(lambda e: e.tensor_tensor(out=wim[:], in0=wim[:], in1=u2[:], op=ALU.subtract))
        V(lambda e: e.tensor_tensor(out=wim[:], in0=wim[:], in1=den[:], op=ALU.mult))
        bbre, bbim, v1 = T("bbre", [128, 64, 16]), T("bbim", [128, 64, 16]), T("v1", [128, 64, 16])
        wre_b = wre[:, :, None].to_broadcast([128, 64, 16])
        wim_b = wim[:, :, None].to_broadcast([128, 64, 16])
        V(lambda e: e.tensor_tensor(out=bbre[:], in0=bre[:], in1=wre_b, op=ALU.mult))
        V(lambda e: e.tensor_tensor(out=v1[:], in0=bim[:], in1=wim_b, op=ALU.mult))
        V(lambda e: e.tensor_tensor(out=bbre[:], in0=bbre[:], in1=v1[:], op=ALU.subtract))
        V(lambda e: e.tensor_tensor(out=bbim[:], in0=bim[:], in1=wre_b, op=ALU.mult))
        V(lambda e: e.tensor_tensor(out=v1[:], in0=bre[:], in1=wim_b, op=ALU.mult))
        V(lambda e: e.tensor_tensor(out=bbim[:], in0=bbim[:], in1=v1[:], op=ALU.add))
        GB = 16
        Gre, Gim, Gimn = T("Gre", [128, GB, 8, 16]), T("Gim", [128, GB, 8, 16]), T("Gimn", [128, GB, 8, 16])
        CLre, CLim = T("CLre", [128, GB, 8, 16]), T("CLim", [128, GB, 8, 16])
        Wzre, Wzim = T("Wzre", [128, GB, 8, 16]), T("Wzim", [128, GB, 8, 16])
        w1, w2 = T("w1", [128, GB, 8, 16]), T("w2", [128, GB, 8, 16])
        stage = [T("stage%d" % i, [128, GB, 5, 128], BF16) for i in range(2)]
        Bstage = [Buf(), Buf()]
        tA, tB = T("tA", [128, 128]), T("tB", [128, 128])
        BtA = Buf()
        BWS = buf("WS_s")
        for gb in range(64 // GB):
            g0 = gb * GB
            gs = slice(g0, g0 + GB)
            sh4 = [128, GB, 8, 16]
            PGre_b = PGre[:, gs, :, None].to_broadcast(sh4)
            PGim_b = PGim[:, gs, :, None].to_broadcast(sh4)
            PCre_b = PCre[:, gs, :, None].to_broadcast(sh4)
            PCim_b = PCim[:, gs, :, None].to_broadcast(sh4)
            Bre_b = bbre[:, gs, None, :].to_broadcast(sh4)
            Bim_b = bbim[:, gs, None, :].to_broadcast(sh4)
            Cre_b = cre[:, gs, None, :].to_broadcast(sh4)
            Cim_b = cim[:, gs, None, :].to_broadcast(sh4)
            A8re_b = A8c[:, 0, gs, None, None].to_broadcast(sh4)
            A8im_b = A8c[:, 1, gs, None, None].to_broadcast(sh4)
            V2 = lambda fn: E("vector", fn, reads=[Bin, Bw, BA8, Bconst], writes=[Bw])
            V2(lambda e: e.tensor_tensor(out=w1[:], in0=PGre_b, in1=Bre_b, op=ALU.mult))
            V2(lambda e: e.tensor_tensor(out=w2[:], in0=PGim_b, in1=Bim_b, op=ALU.mult))
            V2(lambda e: e.tensor_tensor(out=Gre[:], in0=w1[:], in1=w2[:], op=ALU.subtract))
            V2(lambda e: e.tensor_tensor(out=w1[:], in0=PGre_b, in1=Bim_b, op=ALU.mult))
            V2(lambda e: e.tensor_tensor(out=w2[:], in0=PGim_b, in1=Bre_b, op=ALU.mult))
            V2(lambda e: e.tensor_tensor(out=Gim[:], in0=w1[:], in1=w2[:], op=ALU.add))
            V2(lambda e: e.tensor_scalar(out=Gimn[:], in0=Gim[:], scalar1=-1.0, scalar2=None, op0=ALU.mult))
            V2(lambda e: e.tensor_tensor(out=w1[:], in0=PCre_b, in1=Cre_b, op=ALU.mult))
            V2(lambda e: e.tensor_tensor(out=w2[:], in0=PCim_b, in1=Cim_b, op=ALU.mult))
            V2(lambda e: e.tensor_tensor(out=CLre[:], in0=w1[:], in1=w2[:], op=ALU.subtract))
            V2(lambda e: e.tensor_tensor(out=w1[:], in0=PCre_b, in1=Cim_b, op=ALU.mult))
            V2(lambda e: e.tensor_tensor(out=w2[:], in0=PCim_b, in1=Cre_b, op=ALU.mult))
            V2(lambda e: e.tensor_tensor(out=CLim[:], in0=w1[:], in1=w2[:], op=ALU.add))
            V2(lambda e: e.tensor_tensor(out=w1[:], in0=Gre[:], in1=A8re_b, op=ALU.mult))
            V2(lambda e: e.tensor_tensor(out=w2[:], in0=Gim[:], in1=A8im_b, op=ALU.mult))
            V2(lambda e: e.tensor_tensor(out=Wzre[:], in0=w1[:], in1=w2[:], op=ALU.subtract))
            V2(lambda e: e.tensor_tensor(out=w1[:], in0=Gim[:], in1=A8re_b, op=ALU.mult))
            V2(lambda e: e.tensor_tensor(out=w2[:], in0=Gre[:], in1=A8im_b, op=ALU.mult))
            V2(lambda e: e.tensor_tensor(out=Wzim[:], in0=w1[:], in1=w2[:], op=ALU.add))
            sg = stage[gb % 2]
            Bsg = Bstage[gb % 2]
            f2 = lambda t: t[:].rearrange("q g t h -> q g (t h)")
            E("gpsimd", lambda e: e.tensor_copy(out=sg[:, :, 3, :], in_=f2(CLre)), reads=[Bw], writes=[Bsg])
            E("gpsimd", lambda e: e.tensor_scalar(out=sg[:, :, 4, :], in0=f2(CLim), scalar1=-1.0, scalar2=None, op0=ALU.mult), reads=[Bw], writes=[Bsg])
            for gl in range(GB):
                g = g0 + gl
                f1 = lambda t: t[:, gl, :, :].rearrange("q t h -> q (t h)")
                E("tensor", lambda e: e.transpose(out=ps[0][:, 0:128], in_=f1(Wzre), identity=ident_f[:]), reads=[Bw, Bconst], writes=[Bps[0]], serial=True)
                E("tensor", lambda e: e.transpose(out=ps[1][:, 0:128], in_=f1(Wzim), identity=ident_f[:]), reads=[Bw, Bconst], writes=[Bps[1]], serial=True)
                E("scalar", lambda e: e.copy(out=sg[:, gl, 0, :], in_=ps[0][:, 0:128]), reads=[Bps[0]], writes=[Bsg])
                E("scalar", lambda e: e.copy(out=sg[:, gl, 1, :], in_=ps[1][:, 0:128]), reads=[Bps[1]], writes=[Bsg])
                for half, pi in ((H0, 2), (H1, 3)):
                    E("tensor", lambda e: e.matmul(ps[pi][:, 0:128], lhsT=f1(Gre)[half, :], rhs=f1(CLre)[half, :], start=True, stop=False),
                      reads=[Bw], writes=[Bps[pi]], serial=True)
                    E("tensor", lambda e: e.matmul(ps[pi][:, 0:128], lhsT=f1(Gimn)[half, :], rhs=f1(CLim)[half, :], start=False, stop=True),
                      reads=[Bw], writes=[Bps[pi]], serial=True)
                E("vector", lambda e: e.tensor_tensor(out=tA[:], in0=ps[2][:, 0:128], in1=mL[:], op=ALU.mult), reads=[Bps[2], Bin], writes=[BtA])
                E("vector", lambda e: e.tensor_tensor(out=tB[:], in0=ps[3][:, 0:128], in1=mU[:], op=ALU.mult), reads=[Bps[3], Bin], writes=[BtA])
                E("vector", lambda e: e.tensor_tensor(out=tA[:], in0=tA[:], in1=tB[:], op=ALU.add), reads=[BtA], writes=[BtA])
                E("vector", lambda e: e.scalar_tensor_tensor(out=sg[:, gl, 2, :], in0=ident_f[:], scalar=dug[:, g:g + 1], in1=tA[:],
                                                             op0=ALU.mult, op1=ALU.add), reads=[BtA, Bin, Bconst], writes=[Bsg])
            E("sync", lambda e: e.dma_start(out=d["WS_s"][:, gs, :, :], in_=sg[:]), reads=[Bsg], writes=[BWS], dma=True)


def _complex_sq(c, eng, t, tmp, bufs):
    E = c["E"]
    E(eng, lambda e: e.tensor_tensor(out=tmp[:, 0, :], in0=t[:, 0, :], in1=t[:, 0, :], op=ALU.mult), reads=bufs, writes=bufs)
    E(eng, lambda e: e.tensor_tensor(out=tmp[:, 1, :], in0=t[:, 1, :], in1=t[:, 1, :], op=ALU.mult), reads=bufs, writes=bufs)
    E(eng, lambda e: e.scalar_tensor_tensor(out=t[:, 1, :], in0=t[:, 0, :], scalar=2.0, in1=t[:, 1, :], op0=ALU.mult, op1=ALU.mult), reads=bufs, writes=bufs)
    E(eng, lambda e: e.tensor_tensor(out=t[:, 0, :], in0=tmp[:, 0, :], in1=tmp[:, 1, :], op=ALU.subtract), reads=bufs, writes=bufs)


def phase_ssm_scan(c, d, own):
    nc, E, ps, Bps, sbt, buf, vcol = c["nc"], c["E"], c["ps"], c["Bps"], c["sbt"], c["buf"], c["vcol"]
    Bconst, BA8, BSin = c["Bconst"], c["BA8"], c["BSin"]
    A8a, A8b, A8c, Sin_t = c["A8a"], c["A8b"], c["A8c"], c["Sin_t"]
    es = c["es"]
    H0, H1 = slice(0, 64), slice(64, 128)
    if "Est" not in c:
        c["Est"] = sbt(es, "Est", [128, 3, 2, 64], F32)
        c["BEst"] = Buf()
        c["A64"] = sbt(es, "A64", [128, 2, 64], F32)
        c["Aslot"] = sbt(es, "Aslot", [128, 2, 64], F32)
        c["BApow"] = Buf()
        tmpq = sbt(es, "tmpq", [128, 2, 64], F32)
        Bq = [c["BApow"], BA8]
        E("vector", lambda e: e.tensor_copy(out=c["A64"][:], in_=A8c[:]), reads=[BA8], writes=[c["BApow"]])
        for _ in range(6):
            _complex_sq(c, "vector", c["A64"], tmpq, [c["BApow"]])
        E("vector", lambda e: e.tensor_copy(out=c["Aslot"][:], in_=c["A64"][:]), reads=[c["BApow"]], writes=[c["BApow"]])
        n = NB_OWN
        while n > 1:
            _complex_sq(c, "vector", c["Aslot"], tmpq, [c["BApow"]])
            n //= 2
    Est, BEst, A64, BApow = c["Est"], c["BEst"], c["A64"], c["BApow"]
    with ExitStack() as st:
        Wu = sbt(st, "Wu", [128, 16, 1024], BF16)
        Sel = sbt(st, "Sel", [128, 64, 128], BF16)
        WZ = sbt(st, "WZ", [128, 64, 2, 128], BF16)
        BW = Buf()
        for k in range(0, 16, 4):
            E("gpsimd", lambda e, k=k: e.dma_start(out=Wu[:, k:k + 4, :], in_=d["w_in"][:, k:k + 4, 0:1024]), writes=[BW], dma=True)
        E("gpsimd", lambda e: e.dma_start(out=Sel[:], in_=d["sel"]), writes=[BW], dma=True)
        E("sync", lambda e: e.dma_start(out=WZ[:], in_=d["WS_s"][:, :, 0:2, :]), reads=[buf("WS_s")], writes=[BW], dma=True)
        xt = sbt(st, "xt", [128, 16, BLK], F32)
        scr = sbt(st, "scr", [128, 16, BLK], BF16)
        sq = scr
        ht = sbt(st, "ht", [128, 16, BLK], BF16)
        rstd = sbt(st, "rstd", [128, BLK], F32)
        Bxt, Bht, Brstd = Buf(), Buf(), Buf()
        uT = scr[:, 0:8, :]
        BuT = Buf()
        ug = scr[:, 8:16, :].rearrange("p a (b c) -> p (a b) c", c=64)
        Bug = Buf()
        Bsq = [BuT, Bug]
        Zbs = [sbt(st, "Zb%d" % i, [128, 64, 2, 64], BF16) for i in range(2)]
        BZbs = [Buf(), Buf()]
        nblk_done = [0]
        Sal = sbt(st, "Sal", [128, 64, 2, 64], BF16)
        BSal = Buf()
        S2 = sbt(st, "S2", [128, 2, 64], F32)
        q1 = sbt(st, "q1", [128, 2, 64], F32)
        q2 = sbt(st, "q2", [128, 2, 64], F32)
        BS2 = Buf()
        accb = sbt(st, "accb", [128, 2, 64], F32)
        Pw = sbt(st, "Pw", [128, 2, 64], F32)
        m1 = sbt(st, "m1", [128, 2, 64], F32)
        m2 = sbt(st, "m2", [128, 2, 64], F32)
        m3 = sbt(st, "m3", [128, 2, 64], F32)
        Bacc = Buf()
        slots = [0] if own else [1, 2, 3]
        jobs = [(slot, bi) for slot in slots for bi in range(NB_OWN)]

        def stage_a(n):
            slot, bi = jobs[n]
            load_norm_block(c, st, d["xT_ctx"], slot * NB_OWN + bi, xt, sq, ht, rstd, Bxt, Bsq, Bht, Brstd, V_PREMIX, ps_i=0)

        def stage_b(n):
            slot, bi = jobs[n]
            Zb, BZb = Zbs[n % 2], BZbs[n % 2]
            for j in range(8):
                pj = 1 + j % 2
                for k in range(16):
                    E("tensor", lambda e, k=k: e.matmul(ps[pj][:], lhsT=Wu[:, k, j * 128:(j + 1) * 128], rhs=ht[:, k, :],
                                                        start=(k == 0), stop=(k == 15)), reads=[BW, Bht], writes=[Bps[pj]], signal=(k == 15))
                E("scalar", lambda e: e.copy(out=uT[:, j, :], in_=ps[pj][:]), reads=[Bps[pj]], writes=[BuT])
            for j in range(8):
                pj = 3 + j % 2
                uv = uT[:, j, :].rearrange("p (c t) -> p c t", t=8)
                for gl in range(8):
                    for tau in range(8):
                        E("tensor", lambda e: e.matmul(ps[pj][:, gl * 64:(gl + 1) * 64], lhsT=Sel[:, gl * 8 + tau, :], rhs=uv[:, :, tau],
                                                       start=(tau == 0), stop=(tau == 7)), reads=[BW, BuT], writes=[Bps[pj]], signal=(tau == 7))
                E("scalar", lambda e: e.copy(out=ug[:, j * 8:(j + 1) * 8, :], in_=ps[pj][:].rearrange("p (g c) -> p g c", c=64)),
                  reads=[Bps[pj]], writes=[Bug])
            if own:
                E("sync", lambda e: e.dma_start(out=d["UG_s"][bi], in_=ug), reads=[Bug], writes=[buf("UG_s")], dma=True)
            for j in range(8):
                for ri in range(2):
                    pz = 5 + ((2 * j + ri) % 3)
                    for gl in range(8):
                        g = j * 8 + gl
                        E("tensor", lambda e: e.matmul(ps[pz][:, gl * 64:(gl + 1) * 64], lhsT=WZ[:, g, ri, :], rhs=ug[:, g, :],
                                                       start=True, stop=True), reads=[BW, Bug], writes=[Bps[pz]], signal=(gl == 7))
                    pv = ps[pz][:].rearrange("q (g c) -> q g c", c=64)
                    E("scalar", lambda e: e.copy(out=Zb[H0, :, ri, j * 8:(j + 1) * 8].rearrange("q c g -> q g c"), in_=pv[H0]),
                      reads=[Bps[pz]], writes=[BZb])
                    E("scalar", lambda e: e.copy(out=Zb[H1, :, ri, j * 8:(j + 1) * 8].rearrange("q c g -> q g c"), in_=pv[H1, :, ::-1]),
                      reads=[Bps[pz]], writes=[BZb])

        def stage_scan(n):
            slot, bi = jobs[n]
            Zb, BZb = Zbs[n % 2], BZbs[n % 2]
            if bi == 0:
                E("vector", lambda e: e.memset(S2[:], 0.0), writes=[BS2])
                if own:
                    E("vector", lambda e: e.tensor_copy(out=S2[H0], in_=Sin_t[H0]), reads=[BSin], writes=[BS2])
                else:
                    E("gpsimd", lambda e: e.memset(accb[:], 0.0), writes=[Bacc])
                    E("gpsimd", lambda e: e.memset(Pw[:], 0.0), writes=[Bacc])
                    E("gpsimd", lambda e: e.memset(Pw[:, 0, :], 1.0), writes=[Bacc])
            E("vector", lambda e: e.memset(S2[H1], 0.0), writes=[BS2])
            if own:
                E("vector", lambda e: e.tensor_copy(out=Sal[H0, 0, :, :], in_=S2[H0]), reads=[BS2], writes=[BSal])
            for i in range(64):
                E("vector", lambda e: e.tensor_tensor(out=q1[:], in0=S2[:], in1=A8a[:], op=ALU.mult), reads=[BS2, BA8], writes=[BS2])
                E("vector", lambda e: e.tensor_tensor(out=q2[:], in0=S2[:, ::-1, :], in1=A8b[:], op=ALU.mult), reads=[BS2, BA8], writes=[BS2])
                E("vector", lambda e: e.tensor_tensor(out=q1[:], in0=q1[:], in1=q2[:], op=ALU.add), reads=[BS2], writes=[BS2])
                E("vector", lambda e: e.tensor_tensor(out=S2[:], in0=q1[:], in1=Zb[:, i, :, :], op=ALU.add), reads=[BS2, BZb], writes=[BS2])
                if own and i < 63:
                    E("vector", lambda e: e.tensor_copy(out=Sal[H0, i + 1, :, :], in_=S2[H0]), reads=[BS2], writes=[BSal])
            if own:
                E("sync", lambda e: e.dma_start(out=d["SF_s"][bi], in_=Sal[H0]), reads=[BSal], writes=[buf("SF_s")], dma=True)
                E("sync", lambda e: e.dma_start(out=d["ZB_s"][bi], in_=Zb[H1]), reads=[BZb], writes=[buf("ZB_s")], dma=True)
            else:
                rb = [BS2, Bacc, BApow]
                c["cmul"]("gpsimd", m3, Pw, S2, m1, m2, rb, [Bacc], rows=H1)
                E("gpsimd", lambda e: e.tensor_tensor(out=accb[H1], in0=accb[H1], in1=m3[H1], op=ALU.add), reads=rb, writes=[Bacc])
                c["cmul"]("gpsimd", m3, Pw, A64, m1, m2, rb, [Bacc], rows=H1)
                E("gpsimd", lambda e: e.tensor_copy(out=Pw[H1], in_=m3[H1]), reads=rb, writes=[Bacc])
                if bi == NB_OWN - 1:
                    so = slot - 1
                    E("vector", lambda e: e.tensor_copy(out=Est[H0, so, :, :], in_=S2[H0]), reads=[BS2], writes=[BEst])
                    E("gpsimd", lambda e: e.tensor_copy(out=Est[H1, 2 - so, :, :], in_=accb[H1]), reads=[Bacc], writes=[BEst])

        for n in range(len(jobs)):
            stage_a(n)
            if n > 0:
                stage_scan(n - 1)
            stage_b(n)
        stage_scan(len(jobs) - 1)


def phase_ssm_carry(c, d):
    E, sbt, es = c["E"], c["sbt"], c["es"]
    Est, BEst, Aslot, BApow, Sin_t, BSin, gates, Bconst = c["Est"], c["BEst"], c["Aslot"], c["BApow"], c["Sin_t"], c["BSin"], c["gates"], c["Bconst"]
    with ExitStack() as st:
        m1 = sbt(st, "cm1", [128, 2, 64], F32)
        m2 = sbt(st, "cm2", [128, 2, 64], F32)
        T = sbt(st, "cT", [128, 2, 64], F32)
        Bt = Buf()
        E("vector", lambda e: e.memset(Sin_t[:], 0.0), writes=[BSin])
        rb = [BEst, BApow, BSin, Bt, Bconst]
        for s_ in range(3):
            c["cmul"]("vector", T, Aslot, Sin_t, m1, m2, rb, [Bt])
            E("vector", lambda e: e.tensor_tensor(out=T[:], in0=T[:], in1=Est[:, s_, :, :], op=ALU.add), reads=rb, writes=[Bt])
            E("vector", lambda e: e.tensor_tensor(out=T[:], in0=T[:], in1=Sin_t[:], op=ALU.subtract), reads=rb, writes=[Bt])
            E("vector", lambda e: e.scalar_tensor_tensor(out=Sin_t[:], in0=T[:], scalar=gates[:, s_:s_ + 1], in1=Sin_t[:], op0=ALU.mult, op1=ALU.add),
              reads=rb, writes=[BSin])
        if d["EST_s"] is not None:
            E("sync", lambda e: e.dma_start(out=d["EST_s"], in_=Sin_t[:]), reads=[BSin], dma=True)


def phase_ssm_out(c, d):
    nc, E, ps, Bps, sbt, buf, vcol = c["nc"], c["E"], c["ps"], c["Bps"], c["sbt"], c["buf"], c["vcol"]
    Bconst, BA8, BSin = c["Bconst"], c["BA8"], c["BSin"]
    A8a, A8b, Sin_t, ones_b = c["A8a"], c["A8b"], c["Sin_t"], c["ones_b"]
    H0, H1 = slice(0, 64), slice(64, 128)
    with ExitStack() as st:
        WM = sbt(st, "WM", [128, 64, 3, 128], BF16)
        SelT = sbt(st, "SelT", [128, 64, 128], BF16)
        Wg = sbt(st, "Wg", [128, 8, 1024], BF16)
        BW = Buf()
        E("sync", lambda e: e.dma_start(out=WM[:], in_=d["WS_s"][:, :, 2:5, :]), reads=[buf("WS_s")], writes=[BW], dma=True)
        E("gpsimd", lambda e: e.dma_start(out=SelT[:], in_=d["selT"]), writes=[BW], dma=True)
        E("gpsimd", lambda e: e.dma_start(out=Wg[:], in_=d["w_glu"]), writes=[BW], dma=True)
        ug = [sbt(st, "ugc%d" % i, [128, 64, 64], BF16) for i in range(2)]
        Bug = [Buf(), Buf()]
        SalN = sbt(st, "SalN", [128, 64, 2, 64], BF16)
        SalR = sbt(st, "SalR", [128, 64, 2, 64], BF16)
        ZbR = sbt(st, "ZbR", [128, 64, 2, 64], BF16)
        BSalN, BSalR, BZbR = Buf(), Buf(), Buf()
        S2 = sbt(st, "S2c", [128, 2, 64], F32)
        q1 = sbt(st, "q1c", [128, 2, 64], F32)
        q2 = sbt(st, "q2c", [128, 2, 64], F32)
        BS2 = Buf()
        yg = [sbt(st, "yg%d" % i, [128, 8, 64], BF16) for i in range(2)]
        Byg = [Buf(), Buf()]
        ysf = sbt(st, "ysf", [128, 8, BLK], F32)
        ysb = sbt(st, "ysb", [128, 8, BLK], BF16)
        Bysf, Bysb = Buf(), Buf()
        ta = [sbt(st, "ta%d" % i, [128, BLK], F32) for i in range(2)]
        tb = [sbt(st, "tb%d" % i, [128, BLK], F32) for i in range(2)]
        Bta, Btb = [Buf(), Buf()], [Buf(), Buf()]
        sqn = sbt(st, "sqn", [128, 8, BLK], BF16)
        nsb = sbt(st, "nsb", [128, 8, BLK], BF16)
        rstd = sbt(st, "rstdc", [128, BLK], F32)
        Bsqn, Bnsb, Brstd = Buf(), Buf(), Buf()
        E("vector", lambda e: e.memset(S2[:], 0.0), writes=[BS2])
        E("vector", lambda e: e.tensor_copy(out=S2[H1], in_=Sin_t[H1]), reads=[BSin], writes=[BS2])
        for it, bi in enumerate(range(NB_OWN - 1, -1, -1)):
            b = it % 2
            tok = slice(bi * BLK, (bi + 1) * BLK)
            E("sync", lambda e: e.dma_start(out=ug[b][:], in_=d["UG_s"][bi]), reads=[buf("UG_s")], writes=[Bug[b]], dma=True)
            E("sync", lambda e: e.dma_start(out=SalN[H0], in_=d["SF_s"][bi]), reads=[buf("SF_s")], writes=[BSalN], dma=True)
            E("sync", lambda e: e.dma_start(out=ZbR[H1], in_=d["ZB_s"][bi]), reads=[buf("ZB_s")], writes=[BZbR], dma=True)
            E("vector", lambda e: e.tensor_copy(out=SalR[H1, 0, :, :], in_=S2[H1]), reads=[BS2], writes=[BSalR])
            for i in range(64):
                E("vector", lambda e: e.tensor_tensor(out=q1[H1], in0=S2[H1], in1=A8a[H1], op=ALU.mult), reads=[BS2, BA8], writes=[BS2])
                E("vector", lambda e: e.tensor_tensor(out=q2[H1], in0=S2[H1, ::-1, :], in1=A8b[H1], op=ALU.mult), reads=[BS2, BA8], writes=[BS2])
                E("vector", lambda e: e.tensor_tensor(out=q1[H1], in0=q1[H1], in1=q2[H1], op=ALU.add), reads=[BS2], writes=[BS2])
                E("vector", lambda e: e.tensor_tensor(out=S2[H1], in0=q1[H1], in1=ZbR[H1, i, :, :], op=ALU.add), reads=[BS2, BZbR], writes=[BS2])
                if i < 63:
                    E("vector", lambda e: e.tensor_copy(out=SalR[H1, i + 1, :, :], in_=S2[H1]), reads=[BS2], writes=[BSalR])
            E("gpsimd", lambda e: e.tensor_copy(out=SalN[H1], in_=SalR[H1, ::-1, :, :]), reads=[BSalR], writes=[BSalN])
            for j in range(8):
                py = 1 + j % 2
                for gl in range(8):
                    g = j * 8 + gl
                    o = ps[py][:, gl * 64:(gl + 1) * 64]
                    E("tensor", lambda e: e.matmul(o, lhsT=WM[:, g, 0, :], rhs=ug[b][:, g, :], start=True, stop=False), reads=[BW, Bug[b]], writes=[Bps[py]], signal=False)
                    E("tensor", lambda e: e.matmul(o, lhsT=WM[:, g, 1, :], rhs=SalN[:, :, 0, g], start=False, stop=False), reads=[BW, BSalN], writes=[Bps[py]], signal=False)
                    E("tensor", lambda e: e.matmul(o, lhsT=WM[:, g, 2, :], rhs=SalN[:, :, 1, g], start=False, stop=True), reads=[BW, BSalN], writes=[Bps[py]])
                E("scalar", lambda e: e.copy(out=yg[j % 2][:], in_=ps[py][:].rearrange("p (g c) -> p g c", c=64)), reads=[Bps[py]], writes=[Byg[j % 2]])
                pf = 3 + j % 2
                pfv = ps[pf][:].rearrange("p (c t) -> p c t", t=8)
                for tau in range(8):
                    for gl in range(8):
                        E("tensor", lambda e: e.matmul(pfv[:, :, tau], lhsT=SelT[:, gl * 8 + tau, :], rhs=yg[j % 2][:, gl, :],
                                                       start=(gl == 0), stop=(gl == 7)), reads=[BW, Byg[j % 2]], writes=[Bps[pf]], signal=(gl == 7))
                a_, b_ = ta[j % 2], tb[j % 2]
                Ba, Bb = Bta[j % 2], Btb[j % 2]
                E("scalar", lambda e: e.activation(out=a_[:], in_=ps[pf][:], func=AF.Square), reads=[Bps[pf]], writes=[Ba])
                E("vector", lambda e: e.tensor_scalar(out=a_[:], in0=a_[:], scalar1=0.044715, scalar2=1.0, op0=ALU.mult, op1=ALU.add), reads=[Ba], writes=[Ba])
                E("vector", lambda e: e.tensor_tensor(out=a_[:], in0=a_[:], in1=ps[pf][:], op=ALU.mult), reads=[Ba, Bps[pf]], writes=[Ba])
                E("scalar", lambda e: e.activation(out=b_[:], in_=a_[:], func=AF.Sigmoid, scale=1.5957691216057308), reads=[Ba], writes=[Bb])
                E("vector", lambda e: e.tensor_tensor(out=ysf[:, j, :], in0=b_[:], in1=ps[pf][:], op=ALU.mult), reads=[Bb, Bps[pf]], writes=[Bysf])
                E("gpsimd", lambda e: e.tensor_copy(out=ysb[:, j, :], in_=ysf[:, j, :]), reads=[Bysf], writes=[Bysb])
            for j2 in range(8):
                pg = 5 + j2 % 2
                for j in range(8):
                    E("tensor", lambda e: e.matmul(ps[pg][:], lhsT=Wg[:, j, j2 * 128:(j2 + 1) * 128], rhs=ysb[:, j, :], start=(j == 0), stop=(j == 7)),
                      reads=[BW, Bysb], writes=[Bps[pg]], signal=(j == 7))
                b_, Bb = tb[j2 % 2], Btb[j2 % 2]
                E("scalar", lambda e: e.activation(out=b_[:], in_=ps[pg][:], func=AF.Sigmoid), reads=[Bps[pg]], writes=[Bb])
                E("vector", lambda e: e.tensor_tensor(out=ysf[:, j2, :], in0=ysf[:, j2, :], in1=b_[:], op=ALU.mult), reads=[Bb, Bysf], writes=[Bysf])
            E("scalar", lambda e: e.activation(out=sqn[:], in_=ysf[:], func=AF.Square), reads=[Bysf], writes=[Bsqn])
            for j in range(8):
                E("tensor", lambda e: e.matmul(ps[7][:], lhsT=ones_b[:], rhs=sqn[:, j, :], start=(j == 0), stop=(j == 7)), reads=[Bconst, Bsqn], writes=[Bps[7]], signal=(j == 7))
            E("scalar", lambda e: e.activation(out=rstd[:], in_=ps[7][:], func=AF.Sqrt, scale=1.0 / 1024, bias=EPS), reads=[Bps[7]], writes=[Brstd])
            E("vector", lambda e: e.reciprocal(out=rstd[:], in_=rstd[:]), reads=[Brstd], writes=[Brstd])
            for j in range(8):
                E("vector", lambda e: e.scalar_tensor_tensor(out=nsb[:, j, :], in0=ysf[:, j, :], scalar=vcol(V_SSMOUT + j), in1=rstd[:],
                                                             op0=ALU.mult, op1=ALU.mult), reads=[Bysf, Brstd, Bconst], writes=[Bnsb])
            E("sync", lambda e: e.dma_start(out=d["NS_s"][:, :, tok], in_=nsb[:]), reads=[Bnsb], writes=[buf("NS_s")], dma=True)


def phase_attention(c, d):
    nc, E, ps, Bps, sbt, buf = c["nc"], c["E"], c["ps"], c["Bps"], c["sbt"], c["buf"]
    psbig, ones_b, Bconst = c["psbig"], c["ones_b"], c["Bconst"]
    scale = 1.0 / float(np.sqrt(128.0))
    QW = 2 * BLK
    with ExitStack() as st:
        KT = sbt(st, "KT", [128, NCTX], BF16)
        Vt = sbt(st, "Vt", [128, NKT, 128], BF16)
        mk = sbt(st, "mk", [128, NKT], F32)
        BKV, Bmk = Buf(), Buf()
        E("sync", lambda e: e.dma_start(out=mk[:], in_=d["maskb"]), writes=[Bmk], dma=True)
        qt = [sbt(st, "qt%d" % i, [128, QW], BF16) for i in range(2)]
        Bqt = [Buf(), Buf()]
        NP = 4
        pT = [sbt(st, "pT%d" % i, [128, QW], BF16) for i in range(NP)]
        BpT = [Buf() for _ in range(NP)]
        acc = [sbt(st, "acc%d" % i, [128, QW], F32) for i in range(2)]
        Bacc = [Buf(), Buf()]
        rec = sbt(st, "rec", [128, QW], F32)
        Brec = Buf()
        dhi = sbt(st, "dhi", [128, QW], BF16)
        dlo = sbt(st, "dlo", [128, QW], BF16)
        Bdh = Buf()
        yo = [sbt(st, "yo%d" % i, [128, QW], F32) for i in range(2)]
        Byo = [Buf(), Buf()]
        BS = [[Bps[0], Bps[1]], [Bps[2], Bps[3]]]
        BO = [Bps[4], Bps[5]]
        BD = [Bps[6], Bps[7]]
        it = 0
        for kvh in range(2):
            nchunk = 4
            for q in range(nchunk):
                cs_ = slice(q * NCTX // nchunk, (q + 1) * NCTX // nchunk)
                ks_ = slice(q * NKT // nchunk, (q + 1) * NKT // nchunk)
                E("sync", lambda e: e.dma_start(out=KT[:, cs_], in_=d["KT_s"][kvh, :, cs_]), reads=[buf("KT_s")], writes=[BKV], dma=True)
                E("sync", lambda e: e.dma_start(out=Vt[:, ks_, :], in_=d["V_s"][kvh, :, ks_, :]), reads=[buf("V_s")], writes=[BKV], dma=True)
            for qh in range(4):
                head = kvh * 4 + qh
                for qb in range(NB_OWN // 2):
                    b = it % 2
                    tok = slice(qb * QW, (qb + 1) * QW)
                    E("sync", lambda e: e.dma_start(out=qt[b][:], in_=d["QT_s"][head, :, tok]), reads=[buf("QT_s")], writes=[Bqt[b]], dma=True)

                    def s_mm(kt):
                        sb_ = kt % 2
                        for hf in range(2):
                            E("tensor", lambda e: e.matmul(psbig[sb_][:, hf * BLK:(hf + 1) * BLK], lhsT=KT[:, kt * 128:(kt + 1) * 128],
                                                           rhs=qt[b][:, hf * BLK:(hf + 1) * BLK], start=True, stop=True),
                              reads=[BKV, Bqt[b]], writes=[BS[sb_][hf]], signal=(hf == 1))
                    s_mm(0)
                    nd = [0, 0]
                    for kt in range(NKT):
                        if kt + 1 < NKT:
                            s_mm(kt + 1)
                        sb_ = kt % 2
                        r = kt % NP
                        E("scalar", lambda e: e.activation(out=pT[r][:], in_=psbig[sb_][:], func=AF.Exp, bias=mk[:, kt:kt + 1], scale=scale),
                          reads=BS[sb_] + [Bmk], writes=[BpT[r]])
                        for hf in range(2):
                            E("tensor", lambda e: e.matmul(psbig[2][:, hf * BLK:(hf + 1) * BLK], lhsT=Vt[:, kt, :], rhs=pT[r][:, hf * BLK:(hf + 1) * BLK],
                                                           start=(kt == 0), stop=(kt == NKT - 1)),
                              reads=[BKV, BpT[r]], writes=[BO[hf]], signal=(kt == NKT - 1 and hf == 1))
                        if kt % 4 == 0:
                            for hf in range(2):
                                hs = slice(hf * BLK, (hf + 1) * BLK)
                                E("tensor", lambda e: e.matmul(psbig[3][:, hs], lhsT=ones_b[:], rhs=pT[r][:, hs], start=(kt == 0), stop=False),
                                  reads=[Bconst, BpT[r]], writes=[BD[hf]], signal=False)
                        else:
                            a = (kt - kt // 4) % 2
                            eng = "vector" if a == 0 else "gpsimd"
                            if nd[a] == 0:
                                E(eng, lambda e: e.tensor_copy(out=acc[a][:], in_=pT[r][:]), reads=[BpT[r]], writes=[Bacc[a]])
                            else:
                                E(eng, lambda e: e.tensor_tensor(out=acc[a][:], in0=acc[a][:], in1=pT[r][:], op=ALU.add), reads=[BpT[r], Bacc[a]], writes=[Bacc[a]])
                            nd[a] += 1
                    E("vector", lambda e: e.tensor_tensor(out=acc[0][:], in0=acc[0][:], in1=acc[1][:], op=ALU.add), reads=[Bacc[0], Bacc[1]], writes=[Bacc[0]])
                    E("vector", lambda e: e.tensor_copy(out=dhi[:], in_=acc[0][:]), reads=[Bacc[0]], writes=[Bdh])
                    E("vector", lambda e: e.tensor_tensor(out=acc[0][:], in0=acc[0][:], in1=dhi[:], op=ALU.subtract), reads=[Bdh, Bacc[0]], writes=[Bacc[0]])
                    E("vector", lambda e: e.tensor_copy(out=dlo[:], in_=acc[0][:]), reads=[Bacc[0]], writes=[Bdh])
                    for hf in range(2):
                        hs = slice(hf * BLK, (hf + 1) * BLK)
                        E("tensor", lambda e: e.matmul(psbig[3][:, hs], lhsT=ones_b[:], rhs=dhi[:, hs], start=False, stop=False), reads=[Bconst, Bdh], writes=[BD[hf]], signal=False)
                        E("tensor", lambda e: e.matmul(psbig[3][:, hs], lhsT=ones_b[:], rhs=dlo[:, hs], start=False, stop=True), reads=[Bconst, Bdh], writes=[BD[hf]])
                    E("vector", lambda e: e.reciprocal(out=rec[:], in_=psbig[3][:]), reads=BD, writes=[Brec])
                    E("vector", lambda e: e.tensor_tensor(out=yo[b][:], in0=psbig[2][:], in1=rec[:], op=ALU.mult), reads=BO + [Brec], writes=[Byo[b]])
                    E("sync", lambda e: e.dma_start(out=d["YA_s"][head, :, tok], in_=yo[b][:]), reads=[Byo[b]], writes=[buf("YA_s")], dma=True)
                    it += 1


def phase_weight_cast(c, d):
    E, buf = c["E"], c["buf"]
    for f in range(16):
        E("gpsimd", lambda e: e.dma_start(out=d["WOUT_b"][f].rearrange("p k c -> p (k c)"), in_=d["w_out"][f].rearrange("p k c -> p (k c)")),
          writes=[buf("WOUT_b")], dma=True)
    for f in range(64):
        E("gpsimd", lambda e: e.dma_start(out=d["WUP_b"][f].rearrange("p k c -> p (k c)"), in_=d["w_up"][f].rearrange("p k c -> p (k c)")),
          writes=[buf("WUP_b")], dma=True)
    for f in range(16):
        for h in range(2):
            E("gpsimd", lambda e: e.dma_start(out=d["WDN_b"][f, h].rearrange("p k c -> p (k c)"), in_=d["w_dn"][f, h].rearrange("p k c -> p (k c)")),
              writes=[buf("WDN_b")], dma=True)


def phase_mlp(c, d):
    nc, E, ps, Bps, sbt, buf, vcol = c["nc"], c["E"], c["ps"], c["Bps"], c["sbt"], c["buf"], c["vcol"]
    ones_b, Bconst = c["ones_b"], c["Bconst"]
    with ExitStack() as st:
        XT = sbt(st, "XT", [128, 16, BLK], F32)
        MT = sbt(st, "MT", [128, 16, BLK], F32)
        ACTb = sbt(st, "ACTb", [128, 16, BLK], BF16)
        AT = sbt(st, "AT", [128, 32, BLK], BF16)
        YA = sbt(st, "YA", [128, 8, BLK], F32)
        BXT, BMT, BACT, BAT, BYA = Buf(), Buf(), Buf(), Buf(), Buf()
        rstd = sbt(st, "rstde", [128, BLK], F32)
        Brstd = Buf()
        tmp = [sbt(st, "tmpe%d" % i, [128, BLK], F32) for i in range(2)]
        Btmp = [Buf(), Buf()]
        wo = [sbt(st, "wo%d" % i, [128, 16, 128], BF16) for i in range(3)]
        Bwo = [Buf() for _ in range(3)]
        wu = [sbt(st, "wu%d" % i, [128, 16, 128], BF16) for i in range(3)]
        Bwu = [Buf() for _ in range(3)]
        wd = [sbt(st, "wd%d" % i, [128, 32, 128], BF16) for i in range(3)]
        Bwd = [Buf() for _ in range(3)]
        src = d["xT_ctx"].rearrange("(k p) t -> p k t", p=128)
        SQ = AT[:, 0:16, :]

        def stats(srct, Bsrc, nk, denom, pi):
            E("scalar", lambda e: e.activation(out=SQ[:, 0:nk, :], in_=srct, func=AF.Square), reads=[Bsrc], writes=[BAT])
            for k in range(nk):
                E("tensor", lambda e, k=k: e.matmul(ps[pi][:], lhsT=ones_b[:], rhs=SQ[:, k, :], start=(k == 0), stop=(k == nk - 1)),
                  reads=[Bconst, BAT], writes=[Bps[pi]], signal=(k == nk - 1))
            E("scalar", lambda e: e.activation(out=rstd[:], in_=ps[pi][:], func=AF.Sqrt, scale=1.0 / denom, bias=EPS), reads=[Bps[pi]], writes=[Brstd])
            E("vector", lambda e: e.reciprocal(out=rstd[:], in_=rstd[:]), reads=[Brstd], writes=[Brstd])

        for blk in range(NB_OWN):
            tok = slice(blk * BLK, (blk + 1) * BLK)
            E("sync", lambda e: e.dma_start(out=XT[:, 0:8, :], in_=src[:, 0:8, tok]), writes=[BXT], dma=True)
            E("sync", lambda e: e.dma_start(out=XT[:, 8:16, :], in_=src[:, 8:16, tok]), writes=[BXT], dma=True)
            E("sync", lambda e: e.dma_start(out=ACTb[:, 0:8, :], in_=d["NS_s"][:, :, tok]), reads=[buf("NS_s")], writes=[BACT], dma=True)
            E("sync", lambda e: e.dma_start(out=YA[:], in_=d["YA_s"][:, :, tok].rearrange("h p t -> p h t")), reads=[buf("YA_s")], writes=[BYA], dma=True)
            stats(YA[:], BYA, 8, 1024.0, 0)
            for j in range(8):
                E("vector", lambda e: e.scalar_tensor_tensor(out=ACTb[:, 8 + j, :], in0=YA[:, j, :], scalar=vcol(V_ATTOUT + j), in1=rstd[:],
                                                             op0=ALU.mult, op1=ALU.mult), reads=[BYA, Brstd, Bconst], writes=[BACT])
            for dt in range(16):
                w, Bw = wo[dt % 3], Bwo[dt % 3]
                E("sync", lambda e: e.dma_start(out=w[:], in_=d["WOUT_b"][dt]), reads=[buf("WOUT_b")], writes=[Bw], dma=True)
                pi = 1 + dt % 2
                for k in range(16):
                    E("tensor", lambda e, k=k: e.matmul(ps[pi][:], lhsT=w[:, k, :], rhs=ACTb[:, k, :], start=(k == 0), stop=(k == 15)),
                      reads=[Bw, BACT], writes=[Bps[pi]], signal=(k == 15))
                E("scalar", lambda e: e.copy(out=MT[:, dt, :], in_=ps[pi][:]), reads=[Bps[pi]], writes=[BMT])
            stats(MT[:], BMT, 16, 2048.0, 0)
            for k in range(16):
                t, Bt = tmp[k % 2], Btmp[k % 2]
                E("vector", lambda e: e.scalar_tensor_tensor(out=t[:], in0=MT[:, k, :], scalar=vcol(V_POSTMIX + k), in1=rstd[:],
                                                             op0=ALU.mult, op1=ALU.mult), reads=[BMT, Brstd, Bconst], writes=[Bt])
                E("gpsimd", lambda e: e.tensor_tensor(out=XT[:, k, :], in0=XT[:, k, :], in1=t[:], op=ALU.add), reads=[Bt, BXT], writes=[BXT])
            stats(XT[:], BXT, 16, 2048.0, 0)
            for k in range(16):
                E("vector", lambda e: e.scalar_tensor_tensor(out=ACTb[:, k, :], in0=XT[:, k, :], scalar=vcol(V_PREMLP + k), in1=rstd[:],
                                                             op0=ALU.mult, op1=ALU.mult), reads=[BXT, Brstd, Bconst], writes=[BACT])
            for half in range(2):
                for f in range(32):
                    ff = half * 32 + f
                    w, Bw = wu[ff % 3], Bwu[ff % 3]
                    E("sync", lambda e: e.dma_start(out=w[:], in_=d["WUP_b"][ff]), reads=[buf("WUP_b")], writes=[Bw], dma=True)
                    pi = 3 + ff % 2
                    for k in range(16):
                        E("tensor", lambda e, k=k: e.matmul(ps[pi][:], lhsT=w[:, k, :], rhs=ACTb[:, k, :], start=(k == 0), stop=(k == 15)),
                          reads=[Bw, BACT], writes=[Bps[pi]], signal=(k == 15))
                    t, Bt = tmp[ff % 2], Btmp[ff % 2]
                    E("scalar", lambda e: e.activation(out=t[:], in_=ps[pi][:], func=AF.Relu), reads=[Bps[pi]], writes=[Bt])
                    E("gpsimd" if ff % 2 else "vector", lambda e: e.tensor_tensor(out=AT[:, f, :], in0=t[:], in1=t[:], op=ALU.mult), reads=[Bt], writes=[BAT])
                for dt in range(16):
                    i3 = (half * 16 + dt) % 3
                    w, Bw = wd[i3], Bwd[i3]
                    E("gpsimd", lambda e: e.dma_start(out=w[:], in_=d["WDN_b"][dt, half]), reads=[buf("WDN_b")], writes=[Bw], dma=True)
                    pi = 5 + dt % 2
                    for f in range(32):
                        E("tensor", lambda e, f=f: e.matmul(ps[pi][:], lhsT=w[:, f, :], rhs=AT[:, f, :], start=(f == 0), stop=(f == 31)),
                          reads=[Bw, BAT], writes=[Bps[pi]], signal=(f == 31))
                    if half == 0:
                        E("scalar", lambda e: e.copy(out=MT[:, dt, :], in_=ps[pi][:]), reads=[Bps[pi]], writes=[BMT])
                    else:
                        E("vector", lambda e: e.tensor_tensor(out=MT[:, dt, :], in0=MT[:, dt, :], in1=ps[pi][:], op=ALU.add), reads=[Bps[pi], BMT], writes=[BMT])
            stats(MT[:], BMT, 16, 2048.0, 0)
            for k in range(16):
                t, Bt = tmp[k % 2], Btmp[k % 2]
                E("vector", lambda e: e.scalar_tensor_tensor(out=t[:], in0=MT[:, k, :], scalar=vcol(V_POSTMLP + k), in1=rstd[:],
                                                             op0=ALU.mult, op1=ALU.mult), reads=[BMT, Brstd, Bconst], writes=[Bt])
                E("gpsimd", lambda e: e.tensor_tensor(out=XT[:, k, :], in0=XT[:, k, :], in1=t[:], op=ALU.add), reads=[Bt, BXT], writes=[BXT])
            dst = d["yT_out"].rearrange("(k p) t -> p k t", p=128)
            E("sync", lambda e: e.dma_start(out=dst[:, 0:8, tok], in_=XT[:, 0:8, :]), reads=[BXT], dma=True)
            E("sync", lambda e: e.dma_start(out=dst[:, 8:16, tok], in_=XT[:, 8:16, :]), reads=[BXT], dma=True)


_STAGES = os.environ.get("MK_STAGES", "PWABSCDE")


def run_cores(inputs, stages=_STAGES, debug=()):
    sh = _prep_shared(inputs)
    in_maps = [_prep_core(inputs, c, sh) for c in range(8)]
    nc, kb = build_program(stages, debug)
    res = run_bass_kernel_spmd(nc, in_maps, core_ids=list(range(8)))
    return res, kb


def kernel(**inputs):
    res, _ = run_cores(inputs)
    yp = np.stack([np.ascontiguousarray(res.results[c]["yT_out"].T) for c in range(4)], axis=0)
    ys = np.concatenate([res.results[4 + j]["yT_out"].T for j in range(4)], axis=0)[None]
    return (np.ascontiguousarray(yp.astype(np.float32)), np.ascontiguousarray(ys.astype(np.float32)))
```
